# Optimizing a Trainium2 kernel written in Bass

```python
import jax, jax.numpy as jnp
from jax import lax
import numpy as np

D_MODEL = 1024
BATCH = 16
SEQ = 4096
DEPTH = 1

D_MIX = D_MODEL
A_HEADS = 8
A_DK = 64
A_DV = 64
A_KWIDTH = A_HEADS * A_DK
A_WIDTH = A_HEADS * A_DV
CHUNK = 64
B_HEADS = 4
B_NOPE = 128
B_ROPE = 64
B_V = 128
B_WIDTH = B_HEADS * B_V
Q_LORA = 384
KV_LORA = 256
ROPE_THETA = 10000.0
Q_BLOCK = 128
D_FF = ((8 * D_MODEL + 3 * 256 - 1) // (3 * 256)) * 256
EPS = 1e-6
IN_SPLITS = (A_KWIDTH, A_WIDTH, A_KWIDTH, A_KWIDTH, A_WIDTH, Q_LORA, KV_LORA, B_ROPE)
D_IN = A_KWIDTH * 3 + A_WIDTH * 2 + Q_LORA + KV_LORA + B_ROPE

kernel_name = 'hybrid_hgrn2_mla_encoder_block'


def _rmsnorm(x, g):
    xf = x.astype(jnp.float32)
    y = xf * lax.rsqrt(jnp.mean(xf * xf, axis=-1, keepdims=True) + EPS)
    return (y * g.astype(jnp.float32)).astype(x.dtype)


def _rope_tables(seq):
    inv = 1.0 / (ROPE_THETA ** (jnp.arange(0, B_ROPE, 2, dtype=jnp.float32) / B_ROPE))
    ang = jnp.arange(seq, dtype=jnp.float32)[:, None] * inv[None, :]
    return jnp.cos(ang), jnp.sin(ang)


def _apply_rope(x, cos, sin):
    xf = x.astype(jnp.float32)
    x1, x2 = jnp.split(xf, 2, axis=-1)
    out = jnp.concatenate([x1 * cos - x2 * sin, x1 * sin + x2 * cos], axis=-1)
    return out.astype(x.dtype)


def _gla_chunkwise(q, k, v, log_f):
    bsz, nh, seq, dk = q.shape
    dv = v.shape[-1]
    n = seq // CHUNK
    q = q.reshape(bsz, nh, n, CHUNK, dk)
    k = k.reshape(bsz, nh, n, CHUNK, dk)
    log_f = log_f.reshape(bsz, nh, n, CHUNK, dk)
    v = v.reshape(bsz, nh, n, CHUNK, dv)
    cum = jnp.cumsum(log_f, axis=3)
    last = cum[:, :, :, -1:, :]
    q_dec = q * jnp.exp(cum)
    k_inv = k * jnp.exp(-cum)
    k_to_end = k * jnp.exp(last - cum)
    mask = jnp.tril(jnp.ones((CHUNK, CHUNK), dtype=bool))
    scores = jnp.einsum('bhnid,bhnjd->bhnij', q_dec, k_inv)
    o_intra = jnp.einsum('bhnij,bhnje->bhnie', jnp.where(mask, scores, 0.0), v)
    u = jnp.einsum('bhnjd,bhnje->bhnde', k_to_end, v)
    decay = jnp.exp(last[:, :, :, 0, :])

    def step(s, xs):
        d, u_n = xs
        return d[..., None] * s + u_n, s

    s0 = jnp.zeros((bsz, nh, dk, dv), dtype=q.dtype)
    _, s_prev = lax.scan(step, s0, (jnp.moveaxis(decay, 2, 0), jnp.moveaxis(u, 2, 0)))
    s_prev = jnp.moveaxis(s_prev, 0, 2)
    o_inter = jnp.einsum('bhnid,bhnde->bhnie', q_dec, s_prev)
    return (o_intra + o_inter).reshape(bsz, nh, seq, dv)


def _hgrn2_group(hq, hi, hf_fwd, hf_bwd, hg, lb, norm_g):
    bsz, seq, _ = hq.shape
    f32 = jnp.float32

    def to_heads(a, d):
        return a.astype(f32).reshape(bsz, seq, A_HEADS, d).transpose(0, 2, 1, 3)

    q = to_heads(jax.nn.silu(hq), A_DK)
    v = to_heads(hi, A_DV)

    def direction(pre, lb_dir, reverse):
        lb_h = lb_dir.astype(f32).reshape(A_HEADS, 1, A_DK)
        z = to_heads(pre, A_DK)
        log_f = jnp.log(lb_h + (1.0 - lb_h) * jax.nn.sigmoid(z))
        k = (1.0 - lb_h) * jax.nn.sigmoid(-z)
        if reverse:
            o = _gla_chunkwise(jnp.flip(q, 2), jnp.flip(k, 2), jnp.flip(v, 2), jnp.flip(log_f, 2))
            return jnp.flip(o, 2)
        return _gla_chunkwise(q, k, v, log_f)

    o = direction(hf_fwd, lb[0], False) + direction(hf_bwd, lb[1], True)
    o = o.transpose(0, 2, 1, 3)
    o = o * lax.rsqrt(jnp.mean(o * o, axis=-1, keepdims=True) + EPS)
    o = o * norm_g.astype(f32).reshape(A_HEADS, A_DV)
    o = o.reshape(bsz, seq, A_WIDTH) * jax.nn.silu(hg.astype(f32))
    return o.astype(hq.dtype)


def _mla_group(c_q, c_kv, k_rope, g_qa, w_qb, g_kva, w_kvb, g_out):
    bsz, seq, _ = c_q.shape
    cos, sin = _rope_tables(seq)
    q = (_rmsnorm(c_q, g_qa) @ w_qb).reshape(bsz, seq, B_HEADS, B_NOPE + B_ROPE)
    q_nope, q_rope = q[..., :B_NOPE], q[..., B_NOPE:]
    kv = (_rmsnorm(c_kv, g_kva) @ w_kvb).reshape(bsz, seq, B_HEADS, B_NOPE + B_V)
    k_nope, v = kv[..., :B_NOPE], kv[..., B_NOPE:]
    q_rope = _apply_rope(q_rope, cos[:, None, :], sin[:, None, :])
    k_rope = _apply_rope(k_rope, cos, sin)
    scale = (B_NOPE + B_ROPE) ** -0.5
    n_blk = seq // Q_BLOCK
    qn_blocks = q_nope.reshape(bsz, n_blk, Q_BLOCK, B_HEADS, B_NOPE).swapaxes(0, 1)
    qr_blocks = q_rope.reshape(bsz, n_blk, Q_BLOCK, B_HEADS, B_ROPE).swapaxes(0, 1)

    def attend(blk):
        qn, qr = blk
        s = (jnp.einsum('bqhd,bkhd->bhqk', qn, k_nope)
             + jnp.einsum('bqhr,bkr->bhqk', qr, k_rope))
        p = jax.nn.softmax(s.astype(jnp.float32) * scale, axis=-1)
        return jnp.einsum('bhqk,bkhe->bqhe', p.astype(v.dtype), v)

    o = lax.map(attend, (qn_blocks, qr_blocks))
    o = o.swapaxes(0, 1).reshape(bsz, seq, B_WIDTH)
    return _rmsnorm(o, g_out)


def setup_inputs(seed: int = 0) -> dict:
    key = jax.random.key(seed)
    ks = jax.random.split(key, 20)
    f32 = jnp.float32

    def nrm(k, shape, fan_in):
        return jax.random.normal(k, shape, f32) * (fan_in ** -0.5)

    def gain(k, shape):
        return 1.0 + 0.02 * jax.random.normal(k, shape, f32)

    return {
        'x': jax.random.normal(ks[0], (BATCH, SEQ, D_MODEL), f32),
        'norm1_g': gain(ks[1], (DEPTH, D_MODEL)),
        'w_in': nrm(ks[2], (DEPTH, D_MODEL, D_IN), D_MODEL),
        'lb_logits': 0.1 * jax.random.normal(ks[3], (2, DEPTH + 1, A_KWIDTH), f32),
        'hgrn_norm_g': gain(ks[4], (DEPTH, A_WIDTH)),
        'q_a_norm_g': gain(ks[5], (DEPTH, Q_LORA)),
        'w_q_b': nrm(ks[6], (DEPTH, Q_LORA, B_HEADS * (B_NOPE + B_ROPE)), Q_LORA),
        'kv_a_norm_g': gain(ks[7], (DEPTH, KV_LORA)),
        'w_kv_b': nrm(ks[8], (DEPTH, KV_LORA, B_HEADS * (B_NOPE + B_V)), KV_LORA),
        'mla_norm_g': gain(ks[9], (DEPTH, B_WIDTH)),
        'w_out': nrm(ks[10], (DEPTH, D_MIX, D_MODEL), D_MIX),
        'norm2_g': gain(ks[11], (DEPTH, D_MODEL)),
        'w_gate': nrm(ks[12], (DEPTH, D_MODEL, D_FF), D_MODEL),
        'w_up': nrm(ks[13], (DEPTH, D_MODEL, D_FF), D_MODEL),
        'w_down': nrm(ks[14], (DEPTH, D_FF, D_MODEL), D_FF),
        'final_norm_g': gain(ks[15], (D_MODEL,)),
    }


def reference(x, norm1_g, w_in, lb_logits, hgrn_norm_g, q_a_norm_g, w_q_b, kv_a_norm_g,
              w_kv_b, mla_norm_g, w_out, norm2_g, w_gate, w_up, w_down, final_norm_g):
    p = jax.nn.softmax(lb_logits.astype(jnp.float32), axis=1)
    lower_bounds = jnp.cumsum(p, axis=1)[:, :DEPTH]
    split_at = [int(v) for v in np.cumsum(IN_SPLITS)[:-1]]
    for l in range(DEPTH):
        h = _rmsnorm(x, norm1_g[l])
        proj = h @ w_in[l]
        hq, hi, hf_fwd, hf_bwd, hg, c_q, c_kv, k_r = jnp.split(proj, split_at, axis=-1)
        y_a = _hgrn2_group(hq, hi, hf_fwd, hf_bwd, hg, lower_bounds[:, l], hgrn_norm_g[l])
        y_b = _mla_group(c_q, c_kv, k_r, q_a_norm_g[l], w_q_b[l], kv_a_norm_g[l],
                         w_kv_b[l], mla_norm_g[l])
        x = x + jnp.concatenate([y_a, y_b], axis=-1) @ w_out[l]
        h = _rmsnorm(x, norm2_g[l])
        x = x + (jax.nn.silu(h @ w_gate[l]) * (h @ w_up[l])) @ w_down[l]
    return _rmsnorm(x, final_norm_g)
```

```python
import contextlib
import os
import numpy as np
import ml_dtypes
import concourse.bass as bass
import concourse.mybir as mybir
from concourse.bass_utils import run_bass_kernel_spmd

F32 = mybir.dt.float32
BF16 = mybir.dt.bfloat16
AF = mybir.ActivationFunctionType
ALU = mybir.AluOpType
AX = mybir.AxisListType

D = 1024
DFF = 2816
NFB = DFF // 128
EPS = 1e-6
NCORES = 8
W1C = 3456
SCALE = 192 ** -0.5


class Buf:
    __slots__ = ("name", "w", "r", "x")

    def __init__(self, name=""):
        self.name = name
        self.w = None
        self.r = {}
        self.x = len(name) > 1 and name[0] == "p" and (name[1].isupper() or name.startswith(("pmm", "pxt")))


class Stage:
    ENGS = ("pe", "act", "dve", "pool", "sp")

    def __init__(self, nc, name, n_dma_sems=16):
        self.nc = nc
        self.name = name
        self.ops = {e: [] for e in self.ENGS}
        self.cnt = {e: 0 for e in ("pe", "act", "dve", "pool")}
        self.waited = {e: {} for e in self.ENGS}
        self.n_dma = n_dma_sems
        self.dma_cnt = [0] * n_dma_sems
        self.dma_rr = 0
        self.sems = {}

    def _need(self, eng, ev, waits):
        if ev is None:
            return
        key, val = ev
        if key == "pe" and eng == "pe":
            return
        if self.waited[eng].get(key, 0) >= val:
            return
        self.waited[eng][key] = val
        waits.append((key, val))

    def _deps(self, eng, reads, writes):
        waits = []
        for b in reads:
            self._need(eng, b.w, waits)
            if b.x:
                for k, v in b.r.items():
                    if k != eng:
                        self._need(eng, (k, v), waits)
        for b in writes:
            self._need(eng, b.w, waits)
            for k, v in b.r.items():
                self._need(eng, (k, v), waits)
        return waits

    def _commit(self, ev, reads, writes):
        k, v = ev
        for b in reads:
            if b.r.get(k, 0) < v:
                b.r[k] = v
        for b in writes:
            b.w = ev
            b.r = {}

    def op(self, eng, fn, reads=(), writes=()):
        waits = self._deps(eng, reads, writes)
        self.cnt[eng] += 1
        ev = (eng, self.cnt[eng])
        self.ops[eng].append((waits, fn, (eng, 1)))
        self._commit(ev, reads, writes)
        return ev

    def dma(self, fn, reads=(), writes=(), queue="sp"):
        waits = self._deps(queue, reads, writes)
        k = self.dma_rr
        self.dma_rr = (self.dma_rr + 1) % self.n_dma
        key = "dma%d" % k
        if self.dma_cnt[k] > 0:
            self._need(queue, (key, self.dma_cnt[k]), waits)
        self.dma_cnt[k] += 16
        ev = (key, self.dma_cnt[k])
        self.ops[queue].append((waits, fn, (key, 16)))
        self._commit(ev, reads, writes)
        return ev

    def finish(self, eng="sp"):
        waits = []
        for k in range(self.n_dma):
            if self.dma_cnt[k] > 0:
                self._need(eng, ("dma%d" % k, self.dma_cnt[k]), waits)
        if waits:
            self.ops[eng].append((waits, None, None))

    def emit(self):
        nc = self.nc
        with contextlib.ExitStack() as st:
            for e in ("pe", "act", "dve", "pool"):
                self.sems[e] = st.enter_context(nc.semaphore("%s_%s" % (self.name, e)))
            for k in range(self.n_dma):
                self.sems["dma%d" % k] = st.enter_context(nc.semaphore("%s_d%d" % (self.name, k)))
            block = st.enter_context(nc.Block())
            sems = self.sems

            def run(h, lst):
                for waits, fn, inc in lst:
                    for key, val in waits:
                        h.wait_ge(sems[key], val)
                    if fn is not None:
                        fn(h).then_inc(sems[inc[0]], inc[1])

            if self.ops["sp"]:
                @block.sync
                def _(h):
                    run(h, self.ops["sp"])
            if self.ops["pe"]:
                @block.tensor
                def _(h):
                    run(h, self.ops["pe"])
            if self.ops["act"]:
                @block.scalar
                def _(h):
                    run(h, self.ops["act"])
            if self.ops["dve"]:
                @block.vector
                def _(h):
                    run(h, self.ops["dve"])
            if self.ops["pool"]:
                @block.gpsimd
                def _(h):
                    run(h, self.ops["pool"])


class Ring:
    def __init__(self, tens, n, name):
        self.t = tens
        self.n = n
        self.i = 0
        self.bufs = [Buf("%s%d" % (name, k)) for k in range(n)]

    def get(self):
        k = self.i
        self.i = (self.i + 1) % self.n
        return self.t[:, k], self.bufs[k]


def bc(ap, shape):
    return ap.to_broadcast(shape)


def build_nc(n_seq, L, debug=False, stages=(1, 2, 3, 4)):
    T = n_seq * L
    NT = T // 128
    NS = T // 512
    NCH = T // 128
    nc = bass.Bass("TRN2", target_bir_lowering=False)

    def din(name, shape, dt=F32):
        return nc.dram_tensor(name, list(shape), dt, kind="ExternalInput").ap()

    skind = "ExternalOutput" if debug else "Internal"

    def dscr(name, shape, dt):
        return nc.dram_tensor(name, list(shape), dt, kind=skind).ap()

    x = din("x", [T, D])
    out = nc.dram_tensor("out", [T, D], F32, kind="ExternalOutput").ap()
    w_in = din("w_in", [128, 8, W1C])
    g1 = din("g1", [128, 8])
    lbl = din("lbl", [128, 2, 2, 4])
    w_qb = din("w_qb", [128, 3, 1024])
    gqa = din("gqa", [128, 3])
    w_kvb = din("w_kvb", [128, 2, 1024])
    gkva = din("gkva", [128, 2])
    w_out = din("w_out", [128, 8, D])
    gout = din("gout", [128, 8])
    w_gate = din("w_gate", [128, 8, DFF])
    w_up = din("w_up", [128, 8, DFF])
    g2 = din("g2", [128, 8])
    w_down = din("w_down", [128, NFB, D])
    gfin = din("gfin", [1, D])
    c_ident = din("c_ident", [128, 128], BF16)
    c_ones = din("c_ones", [128, 128], BF16)
    c_cos = din("c_cos", [128, L])
    c_sin = din("c_sin", [128, L])
    c_rmask = din("c_rmask", [128, 512])
    c_maskf = din("c_maskf", [128, 128])
    c_maskb = din("c_maskb", [128, 128])

    qdT_s = dscr("qdT_s", [2, 4, 128, T], BF16)
    kiT_s = dscr("kiT_s", [2, 4, 128, T], BF16)
    scal_s = dscr("scal_s", [128, 2, 3, 4, NCH], F32)
    v_s = dscr("v_s", [T, 512], BF16)
    g_s = dscr("g_s", [T, 512], F32)
    qnT_s = dscr("qnT_s", [4, 128, T], BF16)
    qrT_s = dscr("qrT_s", [2, 128, T], BF16)
    knT_s = dscr("knT_s", [4, 128, T], BF16)
    krT_s = dscr("krT_s", [128, T], BF16)
    vm_s = dscr("vm_s", [T, 512], BF16)
    yT_s = dscr("yT_s", [8, 128, T], BF16)
    of_s = dscr("of_s", [T, 512], F32)

    def stage1():
        with contextlib.ExitStack() as es:
            def sb(name, shape, dt=F32):
                return es.enter_context(nc.sbuf_tensor("s1_" + name, list(shape), dt))

            def ps(name, shape, dt=F32):
                return es.enter_context(nc.psum_tensor("s1_" + name, list(shape), dt))

            st = Stage(nc, "s1")
            w1 = sb("w1", [128, 8, W1C], BF16)
            wq = sb("wq", [128, 3, 1024], BF16)
            wkv = sb("wkv", [128, 2, 1024], BF16)
            stg = sb("stg", [128, 2, 1024], F32)
            g1t = sb("g1t", [128, 8]); gqat = sb("gqat", [128, 3]); gkvat = sb("gkvat", [128, 2])
            lblt = sb("lblt", [128, 2, 2, 4])
            lbt = sb("lbt", [128, 2, 4]); omlt = sb("omlt", [128, 2, 4])
            fa = sb("fa", [128, 2, 4]); fb_ = sb("fb", [128, 2, 4]); nfb = sb("nfb", [128, 2, 4])
            ident = sb("ident", [128, 128], BF16)
            rmask = sb("rmask", [128, 512])
            xt = sb("xt", [128, 4, D]); xjunk = sb("xjunk", [128, 2, D], BF16)
            xn = sb("xn", [128, 2, D], BF16)
            hT = sb("hT", [128, 2, 8, 512], BF16)
            ssx = sb("ssx", [128, 2, 4]); lnx = sb("lnx", [128, 2, 4]); rsx = sb("rsx", [128, 2, 4])
            cst = sb("cst", [128, 2, 512]); snt = sb("snt", [128, 2, 512])
            qT = sb("qT", [128, 4, 512])
            th = sb("th", [128, 8, 512])
            tmpf = sb("tmpf", [128, 8, 512])
            tmpb = sb("tmpb", [128, 8, 512], BF16)
            gt = sb("gt", [128, 2, 512])
            cqf = sb("cqf", [128, 4, 640]); cqn = sb("cqn", [128, 4, 640], BF16)
            ssq = sb("ssq", [128, 8]); lnq = sb("lnq", [128, 8]); rsq = sb("rsq", [128, 8])
            cqT = sb("cqT", [128, 3, 512], BF16); ckvT = sb("ckvT", [128, 2, 512], BF16)
            scal = sb("scal", [128, 2, 3, 4, NCH])
            pxt = ps("pxt", [128, 2, 8, 128], BF16)
            pmm = ps("pmm", [128, 6, 512])

            B = {}
            def b(name):
                if name not in B:
                    B[name] = Buf(name)
                return B[name]

            ring_f = Ring(tmpf, 8, "tmpf")
            ring_b = Ring(tmpb, 8, "tmpb")
            ring_p = Ring(pmm, 6, "pmm")
            ring_g = Ring(gt, 2, "gt")
            pxb = [Buf("pxt0"), Buf("pxt1")]
            hTb = [Buf("hT0"), Buf("hT1")]
            xtb = [Buf("xt%d" % i) for i in range(4)]
            xnb = [Buf("xn0"), Buf("xn1")]

            for (dst, src, nm) in ((g1t, g1, "g1t"), (gqat, gqa, "gqat"), (gkvat, gkva, "gkvat"),
                                   (lblt, lbl, "lblt"), (ident, c_ident, "ident"), (rmask, c_rmask, "rmask")):
                st.dma(lambda h, dst=dst, src=src: h.dma_start(out=dst[:], in_=src), writes=[b(nm)])
            st.op("dve", lambda h: h.tensor_tensor(out=lbt[:], in0=lblt[:, :, 1, :], in1=lblt[:, :, 0, :], op=ALU.subtract),
                  reads=[b("lblt")], writes=[b("lbt")])
            st.op("act", lambda h: h.activation(out=lbt[:], in_=lbt[:], func=AF.Exp), reads=[b("lbt")], writes=[b("lbt")])
            st.op("dve", lambda h: h.tensor_scalar(out=lbt[:], in0=lbt[:], scalar1=1.0, scalar2=None, op0=ALU.add),
                  reads=[b("lbt")], writes=[b("lbt")])
            st.op("dve", lambda h: h.reciprocal(out=lbt[:], in_=lbt[:]), reads=[b("lbt")], writes=[b("lbt")])
            st.op("dve", lambda h: h.tensor_scalar(out=omlt[:], in0=lbt[:], scalar1=-1.0, scalar2=1.0, op0=ALU.mult, op1=ALU.add),
                  reads=[b("lbt")], writes=[b("omlt")])
            st.op("dve", lambda h: h.tensor_scalar(out=fb_[:], in0=omlt[:], scalar1=0.5, scalar2=None, op0=ALU.mult),
                  reads=[b("omlt")], writes=[b("fb")])
            st.op("dve", lambda h: h.tensor_tensor(out=fa[:], in0=lbt[:], in1=fb_[:], op=ALU.add),
                  reads=[b("lbt"), b("fb")], writes=[b("fa")])
            st.op("dve", lambda h: h.tensor_scalar(out=nfb[:], in0=fb_[:], scalar1=-1.0, scalar2=None, op0=ALU.mult),
                  reads=[b("fb")], writes=[b("nfb")])

            pcnt = [0]

            def prep(dst3, src3, nchunk, ncols, gain, eng_cycle, name):
                for c in range(nchunk):
                    for c0 in range(0, ncols, 1024):
                        c1 = min(ncols, c0 + 1024)
                        slot = pcnt[0] % 2
                        eng = eng_cycle[pcnt[0] % len(eng_cycle)]
                        pcnt[0] += 1
                        sbuf = b("stg%d" % slot)
                        st.dma(lambda h, c=c, slot=slot, c0=c0, c1=c1: h.dma_start(out=stg[:, slot, 0:c1 - c0], in_=src3[:, c, c0:c1]),
                               writes=[sbuf])
                        st.op(eng, lambda h, c=c, slot=slot, c0=c0, c1=c1: h.tensor_scalar(
                            out=dst3[:, c, c0:c1], in0=stg[:, slot, 0:c1 - c0], scalar1=gain[:, c:c + 1], scalar2=None, op0=ALU.mult),
                            reads=[sbuf, b(name + "_g")], writes=[b(name)])
            B["w1_g"] = b("g1t"); B["wq_g"] = b("gqat"); B["wkv_g"] = b("gkvat")
            prep(w1, w_in, 8, W1C, g1t, ["dve", "pool"], "w1")
            prep(wq, w_qb, 3, 1024, gqat, ["dve", "pool"], "wq")
            prep(wkv, w_kvb, 2, 1024, gkvat, ["dve", "pool"], "wkv")
            v1 = w1[:, :, 3328:3456].rearrange("p c (g r) -> p c g r", r=64)[:, :, :, 0:32]
            st.op("dve", lambda h: h.tensor_scalar(out=v1, in0=v1, scalar1=-1.0, scalar2=None, op0=ALU.mult),
                  reads=[b("w1")], writes=[b("w1")])
            v2 = wq[:, :, 768:1024].rearrange("p c (g r) -> p c g r", r=64)[:, :, :, 0:32]
            st.op("dve", lambda h: h.tensor_scalar(out=v2, in0=v2, scalar1=-1.0, scalar2=None, op0=ALU.mult),
                  reads=[b("wq")], writes=[b("wq")])

            def xnorm_part1(s):
                slot = s % 2
                for j in range(4):
                    k = j
                    t0 = s * 512 + j * 128
                    st.dma(lambda h, k=k, t0=t0: h.dma_start(out=xt[:, k, :], in_=x[t0:t0 + 128, :]), writes=[xtb[k]])
                    st.op("act", lambda h, k=k, j=j, slot=slot: h.activation(out=xjunk[:, j % 2, :], in_=xt[:, k, :], func=AF.Square,
                                                                             accum_out=ssx[:, slot, j:j + 1]),
                          reads=[xtb[k]], writes=[b("ssx%d_%d" % (slot, j)), b("xjunk%d" % (j % 2))])
                return [0, 1, 2, 3]

            def xnorm_part2(s, ks):
                slot = s % 2
                st.op("act", lambda h: h.activation(out=lnx[:, slot, :], in_=ssx[:, slot, :], func=AF.Ln, scale=1.0 / D, bias=EPS),
                      reads=[b("ssx%d_%d" % (slot, j)) for j in range(4)], writes=[b("lnx%d" % slot)])
                st.op("act", lambda h: h.activation(out=rsx[:, slot, :], in_=lnx[:, slot, :], func=AF.Exp, scale=-0.5),
                      reads=[b("lnx%d" % slot)], writes=[b("rsx%d" % slot)])
                for j in range(4):
                    k = ks[j]
                    n = j % 2
                    st.op("dve", lambda h, k=k, j=j, n=n: h.tensor_scalar(out=xn[:, n, :], in0=xt[:, k, :], scalar1=rsx[:, slot, j:j + 1],
                                                                        scalar2=None, op0=ALU.mult),
                          reads=[xtb[k], b("rsx%d" % slot)], writes=[xnb[n]])
                    for c in range(8):
                        st.op("pe", lambda h, n=n, c=c: h.transpose(out=pxt[:, n, c, :], in_=xn[:, n, c * 128:(c + 1) * 128], identity=ident[:]),
                              reads=[xnb[n], b("ident")], writes=[pxb[n]])
                    st.op("act", lambda h, n=n, j=j: h.activation(out=hT[:, slot, :, j * 128:(j + 1) * 128], in_=pxt[:, n, :, :], func=AF.Copy),
                          reads=[pxb[n]], writes=[hTb[slot]])

            def mm_group(lhs_fn, rhs_fn, nk, reads, nfree=512):
                pt, pb = ring_p.get()
                pt = pt[:, 0:nfree]
                for c in range(nk):
                    st.op("pe", lambda h, c=c, pt=pt: h.matmul(pt, lhsT=lhs_fn(c), rhs=rhs_fn(c), start=(c == 0), stop=(c == nk - 1)),
                          reads=reads, writes=[pb])
                return pt, pb

            ks0 = xnorm_part1(0)
            xnorm_part2(0, ks0)
            def super_tile(s):
                slot = s % 2
                t0 = s * 512
                pos0 = t0 % L
                c0 = s * 4
                hs = hTb[slot]
                st.dma(lambda h, pos0=pos0: h.dma_start(out=cst[:, slot, :], in_=c_cos[:, pos0:pos0 + 512]), writes=[b("cst%d" % slot)])
                st.dma(lambda h, pos0=pos0: h.dma_start(out=snt[:, slot, :], in_=c_sin[:, pos0:pos0 + 512]), writes=[b("snt%d" % slot)])
                ks_next = xnorm_part1(s + 1) if s + 1 < NS else None
                for blk in range(4):
                    pt, pb = mm_group(lambda c, blk=blk: w1[:, c, blk * 128:(blk + 1) * 128], lambda c: hT[:, slot, c, :], 8, [hs, b("w1")])
                    st.op("act", lambda h, pt=pt, blk=blk: h.activation(out=qT[:, blk, :], in_=pt, func=AF.Silu),
                          reads=[pb], writes=[b("qT%d" % blk)])
                for db in range(8):
                    col = 512 + db * 128
                    pt, pb = mm_group(lambda c, col=col: w1[:, c, col:col + 128], lambda c: hT[:, slot, c, :], 8, [hs, b("w1")])
                    st.op("act", lambda h, pt=pt, db=db: h.activation(out=th[:, db, :], in_=pt, func=AF.Tanh, scale=0.5),
                          reads=[pb], writes=[b("th%d" % db)])
                for j in range(4):
                    tt = t0 + j * 128
                    pt, pb = mm_group(lambda c, j=j: hT[:, slot, c, j * 128:(j + 1) * 128], lambda c: w1[:, c, 1536:2048], 8, [hs, b("w1")])
                    ot, ob = ring_b.get()
                    st.op("dve", lambda h, pt=pt, ot=ot: h.tensor_copy(out=ot, in_=pt), reads=[pb], writes=[ob])
                    st.dma(lambda h, ot=ot, tt=tt: h.dma_start(out=v_s[tt:tt + 128, :], in_=ot), reads=[ob])
                    pt, pb = mm_group(lambda c, j=j: hT[:, slot, c, j * 128:(j + 1) * 128], lambda c: w1[:, c, 2048:2560], 8, [hs, b("w1")])
                    ot, ob = ring_g.get()
                    st.op("act", lambda h, pt=pt, ot=ot: h.activation(out=ot, in_=pt, func=AF.Silu), reads=[pb], writes=[ob])
                    st.dma(lambda h, ot=ot, tt=tt: h.dma_start(out=g_s[tt:tt + 128, :], in_=ot), reads=[ob])
                for j in range(4):
                    pt, pb = mm_group(lambda c, j=j: hT[:, slot, c, j * 128:(j + 1) * 128], lambda c: w1[:, c, 2560:2944], 8, [hs, b("w1")], nfree=384)
                    st.op("act", lambda h, pt=pt, j=j: h.activation(out=xjunk[:, 0, 0:384], in_=pt[:, 0:384], func=AF.Square, accum_out=ssq[:, j:j + 1]),
                          reads=[pb], writes=[b("ssq%d" % j), b("xjunk0")])
                    st.op("dve", lambda h, pt=pt, j=j: h.tensor_copy(out=cqf[:, j, 0:384], in_=pt[:, 0:384]), reads=[pb], writes=[b("cqf%d" % j)])
                    pt, pb = mm_group(lambda c, j=j: hT[:, slot, c, j * 128:(j + 1) * 128], lambda c: w1[:, c, 2944:3200], 8, [hs, b("w1")], nfree=256)
                    st.op("act", lambda h, pt=pt, j=j: h.activation(out=xjunk[:, 1, 0:256], in_=pt[:, 0:256], func=AF.Square, accum_out=ssq[:, 4 + j:5 + j]),
                          reads=[pb], writes=[b("ssq%d" % (4 + j)), b("xjunk1")])
                    st.op("dve", lambda h, pt=pt, j=j: h.tensor_copy(out=cqf[:, j, 384:640], in_=pt[:, 0:256]), reads=[pb], writes=[b("cqf%d" % j)])
                pt1, pb1 = mm_group(lambda c: w1[:, c, 3200:3328], lambda c: hT[:, slot, c, :], 8, [hs, b("w1")])
                pt2, pb2 = mm_group(lambda c: w1[:, c, 3328:3456], lambda c: hT[:, slot, c, :], 8, [hs, b("w1")])

                def rope(pt1, pb1, pt2, pb2, dst_fn):
                    f1, fb1 = ring_f.get()
                    f2, fb2 = ring_f.get()
                    ot, ob = ring_b.get()
                    st.op("dve", lambda h: h.tensor_tensor(out=f1, in0=pt1, in1=cst[:, slot, :], op=ALU.mult),
                          reads=[pb1, b("cst%d" % slot)], writes=[fb1])
                    st.op("dve", lambda h: h.tensor_tensor(out=f2, in0=pt2, in1=snt[:, slot, :], op=ALU.mult),
                          reads=[pb2, b("snt%d" % slot)], writes=[fb2])
                    st.op("pool", lambda h: h.tensor_tensor(out=ot, in0=f1, in1=f2, op=ALU.add), reads=[fb1, fb2], writes=[ob])
                    st.dma(lambda h: h.dma_start(out=dst_fn(), in_=ot), reads=[ob])
                rope(pt1, pb1, pt2, pb2, lambda: krT_s[:, t0:t0 + 512])

                for db in range(8):
                    d, blk = db // 4, db % 4
                    thb = b("th%d" % db)
                    ft, fbuf = ring_f.get()
                    k1, k1b = ring_f.get()
                    lf, lfb = ring_f.get()
                    st.op("pool", lambda h, ft=ft, db=db, d=d, blk=blk: h.tensor_scalar(
                        out=ft, in0=th[:, db, :], scalar1=fb_[:, d, blk:blk + 1], scalar2=fa[:, d, blk:blk + 1], op0=ALU.mult, op1=ALU.add),
                        reads=[thb, b("fb"), b("fa")], writes=[fbuf])
                    st.op("pool", lambda h, k1=k1, db=db, d=d, blk=blk: h.tensor_scalar(
                        out=k1, in0=th[:, db, :], scalar1=nfb[:, d, blk:blk + 1], scalar2=fb_[:, d, blk:blk + 1], op0=ALU.mult, op1=ALU.add),
                        reads=[thb, b("fb"), b("nfb")], writes=[k1b])
                    st.op("act", lambda h, lf=lf, ft=ft: h.activation(out=lf, in_=ft, func=AF.Ln), reads=[fbuf], writes=[lfb])
                    cumt, cumb = ring_f.get()
                    cct, ccb = ring_f.get()
                    st.op("dve", lambda h, lf=lf, cumt=cumt: h.tensor_tensor_scan(out=cumt, data0=rmask[:], data1=lf, initial=0.0,
                                                                                  op0=ALU.mult, op1=ALU.add),
                          reads=[lfb, b("rmask")], writes=[cumb])
                    cumv = cumt.rearrange("p (c t) -> p c t", t=128)
                    ccv = cct.rearrange("p (c t) -> p c t", t=128)
                    st.op("dve", lambda h, cumv=cumv, ccv=ccv: h.tensor_tensor(out=ccv, in0=cumv, in1=bc(cumv[:, :, 63:64], [128, 4, 128]), op=ALU.subtract),
                          reads=[cumb], writes=[ccb])
                    Xi, Yi = (1, 2) if d == 0 else (2, 1)
                    st.op("act", lambda h, d=d, blk=blk, cumv=cumv: h.activation(out=scal[:, d, 0, blk, c0:c0 + 4], in_=cumv[:, :, 127], func=AF.Exp),
                          reads=[cumb], writes=[b("scal")])
                    st.op("act", lambda h, d=d, blk=blk, ccv=ccv, Xi=Xi: h.activation(out=scal[:, d, Xi, blk, c0:c0 + 4], in_=ccv[:, :, 127], func=AF.Exp),
                          reads=[ccb], writes=[b("scal")])
                    st.op("act", lambda h, d=d, blk=blk, cumv=cumv, Yi=Yi: h.activation(out=scal[:, d, Yi, blk, c0:c0 + 4], in_=cumv[:, :, 63], func=AF.Exp),
                          reads=[cumb], writes=[b("scal")])
                    if d == 0:
                        src = cct
                        srcb = ccb
                    else:
                        t2, t2b = ring_f.get()
                        st.op("pool", lambda h, t2=t2, lf=lf, cct=cct: h.tensor_tensor(out=t2, in0=lf, in1=cct, op=ALU.subtract),
                              reads=[lfb, ccb], writes=[t2b])
                        src, srcb = t2, t2b
                    ea, eab = ring_f.get()
                    eb, ebb = ring_f.get()
                    st.op("act", lambda h, ea=ea, src=src: h.activation(out=ea, in_=src, func=AF.Exp), reads=[srcb], writes=[eab])
                    st.op("act", lambda h, eb=eb, src=src: h.activation(out=eb, in_=src, func=AF.Exp, scale=-1.0), reads=[srcb], writes=[ebb])
                    o1, o1b = ring_b.get()
                    o2, o2b = ring_b.get()
                    st.op("dve", lambda h, o1=o1, ea=ea, blk=blk: h.tensor_tensor(out=o1, in0=qT[:, blk, :], in1=ea, op=ALU.mult),
                          reads=[b("qT%d" % blk), eab], writes=[o1b])
                    st.op("dve", lambda h, o2=o2, eb=eb, k1=k1: h.tensor_tensor(out=o2, in0=k1, in1=eb, op=ALU.mult),
                          reads=[k1b, ebb], writes=[o2b])
                    st.dma(lambda h, o1=o1, d=d, blk=blk: h.dma_start(out=qdT_s[d, blk, :, t0:t0 + 512], in_=o1), reads=[o1b])
                    st.dma(lambda h, o2=o2, d=d, blk=blk: h.dma_start(out=kiT_s[d, blk, :, t0:t0 + 512], in_=o2), reads=[o2b])
                st.op("act", lambda h: h.activation(out=lnq[:, 0:4], in_=ssq[:, 0:4], func=AF.Ln, scale=1.0 / 384, bias=EPS),
                      reads=[b("ssq%d" % i) for i in range(8)], writes=[b("lnq")])
                st.op("act", lambda h: h.activation(out=lnq[:, 4:8], in_=ssq[:, 4:8], func=AF.Ln, scale=1.0 / 256, bias=EPS),
                      reads=[b("ssq%d" % i) for i in range(8)], writes=[b("lnq")])
                st.op("act", lambda h: h.activation(out=rsq[:], in_=lnq[:], func=AF.Exp, scale=-0.5), reads=[b("lnq")], writes=[b("rsq")])
                for j in range(4):
                    st.op("dve", lambda h, j=j: h.tensor_scalar(out=cqn[:, j, 0:384], in0=cqf[:, j, 0:384], scalar1=rsq[:, j:j + 1], scalar2=None, op0=ALU.mult),
                          reads=[b("cqf%d" % j), b("rsq")], writes=[b("cqn%d" % j)])
                    st.op("dve", lambda h, j=j: h.tensor_scalar(out=cqn[:, j, 384:640], in0=cqf[:, j, 384:640], scalar1=rsq[:, 4 + j:5 + j], scalar2=None, op0=ALU.mult),
                          reads=[b("cqf%d" % j), b("rsq")], writes=[b("cqn%d" % j)])
                if ks_next is not None:
                    xnorm_part2(s + 1, ks_next)
                for j in range(4):
                    n = j % 2
                    for c in range(5):
                        st.op("pe", lambda h, n=n, c=c, j=j: h.transpose(out=pxt[:, n, c, :], in_=cqn[:, j, c * 128:(c + 1) * 128], identity=ident[:]),
                              reads=[b("cqn%d" % j), b("ident")], writes=[pxb[n]])
                    st.op("act", lambda h, n=n, j=j: h.activation(out=cqT[:, :, j * 128:(j + 1) * 128], in_=pxt[:, n, 0:3, :], func=AF.Copy),
                          reads=[pxb[n]], writes=[b("cqT")])
                    st.op("act", lambda h, n=n, j=j: h.activation(out=ckvT[:, :, j * 128:(j + 1) * 128], in_=pxt[:, n, 3:5, :], func=AF.Copy),
                          reads=[pxb[n]], writes=[b("ckvT")])
                for hh in range(4):
                    pt, pb = mm_group(lambda c, hh=hh: wq[:, c, hh * 128:(hh + 1) * 128], lambda c: cqT[:, c, :], 3, [b("cqT"), b("wq")])
                    ot, ob = ring_b.get()
                    st.op("act", lambda h, pt=pt, ot=ot: h.activation(out=ot, in_=pt, func=AF.Copy), reads=[pb], writes=[ob])
                    st.dma(lambda h, ot=ot, hh=hh: h.dma_start(out=qnT_s[hh, :, t0:t0 + 512], in_=ot), reads=[ob])
                for pr in range(2):
                    pt1, pb1 = mm_group(lambda c, pr=pr: wq[:, c, 512 + pr * 128:640 + pr * 128], lambda c: cqT[:, c, :], 3, [b("cqT"), b("wq")])
                    pt2, pb2 = mm_group(lambda c, pr=pr: wq[:, c, 768 + pr * 128:896 + pr * 128], lambda c: cqT[:, c, :], 3, [b("cqT"), b("wq")])
                    rope(pt1, pb1, pt2, pb2, lambda pr=pr: qrT_s[pr, :, t0:t0 + 512])
                for hh in range(4):
                    pt, pb = mm_group(lambda c, hh=hh: wkv[:, c, hh * 128:(hh + 1) * 128], lambda c: ckvT[:, c, :], 2, [b("ckvT"), b("wkv")])
                    ot, ob = ring_b.get()
                    st.op("act", lambda h, pt=pt, ot=ot: h.activation(out=ot, in_=pt, func=AF.Copy), reads=[pb], writes=[ob])
                    st.dma(lambda h, ot=ot, hh=hh: h.dma_start(out=knT_s[hh, :, t0:t0 + 512], in_=ot), reads=[ob])
                for j in range(4):
                    tt = t0 + j * 128
                    pt, pb = mm_group(lambda c, j=j: ckvT[:, c, j * 128:(j + 1) * 128], lambda c: wkv[:, c, 512:1024], 2, [b("ckvT"), b("wkv")])
                    ot, ob = ring_b.get()
                    st.op("dve", lambda h, pt=pt, ot=ot: h.tensor_copy(out=ot, in_=pt), reads=[pb], writes=[ob])
                    st.dma(lambda h, ot=ot, tt=tt: h.dma_start(out=vm_s[tt:tt + 128, :], in_=ot), reads=[ob])
            for s in range(NS):
                super_tile(s)
            st.dma(lambda h: h.dma_start(out=scal_s, in_=scal[:]), reads=[b("scal")])
            st.finish()
            st.emit()

    def stage2():
        NK = L // 128
        NQ = L // 512
        with contextlib.ExitStack() as es:
            def sb(name, shape, dt=F32):
                return es.enter_context(nc.sbuf_tensor("s2_" + name, list(shape), dt))

            def ps(name, shape, dt=F32):
                return es.enter_context(nc.psum_tensor("s2_" + name, list(shape), dt))

            st = Stage(nc, "s2")
            knT = sb("knT", [128, 4, L], BF16)
            vm = sb("vm", [128, NK, 512], BF16)
            krT = sb("krT", [128, L], BF16)
            qn = sb("qn", [128, 2, 4, 512], BF16)
            qr = sb("qr", [128, 2, 2, 512], BF16)
            ones = sb("ones", [128, 128], BF16)
            Pt = sb("Pt", [128, 4, 512], BF16)
            oT = sb("oT", [128, 2, 4, 512])
            rden = sb("rden", [128, 2, 512])
            sq = sb("sq", [128, 4, 512], BF16)
            lnv = sb("lnv", [128, 512]); rstd = sb("rstd", [128, 512])
            yb = sb("yb", [128, 4, 512], BF16)
            pS = ps("pS", [128, 3, 512])
            pO = ps("pO", [128, 2, 512])
            pD = ps("pD", [128, 1, 512])
            pSS = ps("pSS", [128, 512])
            acc = sb("acc", [128, 2, 512])
            onesf = sb("onesf", [128, 128])
            B = {}

            def b(name):
                if name not in B:
                    B[name] = Buf(name)
                return B[name]
            ring_P = Ring(Pt, 4, "Pt")
            ring_S = Ring(pS, 3, "pS")
            ring_y = Ring(yb, 4, "yb")
            st.dma(lambda h: h.dma_start(out=ones[:], in_=c_ones), writes=[b("ones")])
            st.op("pool", lambda h: h.memset(onesf[:], 1.0), writes=[b("onesf")])

            items = []
            for seq in range(n_seq):
                for qb in range(NQ):
                    for hh in range(4):
                        for kc in range(NK):
                            items.append((seq, qb, hh, kc))

            def loads_seq(seq):
                s0 = seq * L
                for hh in range(4):
                    st.dma(lambda h, hh=hh: h.dma_start(out=knT[:, hh, :], in_=knT_s[hh, :, s0:s0 + L]), writes=[b("knT")])
                st.dma(lambda h: h.dma_start(out=krT[:], in_=krT_s[:, s0:s0 + L]), writes=[b("krT")])
                st.dma(lambda h: h.dma_start(out=vm[:], in_=vm_s[s0:s0 + L, :].rearrange("(k p) f -> p k f", p=128)), writes=[b("vm")])

            def loads_q(seq, qb):
                slot = (seq * NQ + qb) % 2
                tq = seq * L + qb * 512
                st.dma(lambda h: h.dma_start(out=qn[:, slot], in_=qnT_s[:, :, tq:tq + 512].rearrange("h p t -> p h t")), writes=[b("qn%d" % slot)])
                st.dma(lambda h: h.dma_start(out=qr[:, slot], in_=qrT_s[:, :, tq:tq + 512].rearrange("h p t -> p h t")), writes=[b("qr%d" % slot)])

            def qk(item):
                seq, qb, hh, kc = item
                slot = (seq * NQ + qb) % 2
                hp, pr = hh % 2, hh // 2
                pt, pb = ring_S.get()
                st.op("pe", lambda h: h.matmul(pt, lhsT=knT[:, hh, kc * 128:(kc + 1) * 128], rhs=qn[:, slot, hh, :], start=True, stop=False),
                      reads=[b("knT"), b("qn%d" % slot)], writes=[pb])
                st.op("pe", lambda h: h.matmul(pt, lhsT=krT[hp * 64:(hp + 1) * 64, kc * 128:(kc + 1) * 128],
                                               rhs=qr[hp * 64:(hp + 1) * 64, slot, pr, :], start=False, stop=True),
                      reads=[b("krT"), b("qr%d" % slot)], writes=[pb])
                return pt, pb

            def finish_head(seq, qb, hh, oslot):
                qslot = (seq * NQ + qb) % 2
                st.op("pe", lambda h: h.matmul(pD[:, 0, :], lhsT=onesf[:], rhs=acc[:, oslot, :], start=True, stop=True),
                      reads=[b("onesf"), b("acc%d" % oslot)], writes=[b("pD0")])
                st.op("dve", lambda h: h.reciprocal(out=rden[:, oslot, :], in_=pD[:, 0, :]), reads=[b("pD0")], writes=[b("rden%d" % oslot)])
                st.op("dve", lambda h: h.tensor_tensor(out=oT[:, qslot, hh, :], in0=pO[:, oslot, :], in1=rden[:, oslot, :], op=ALU.mult),
                      reads=[b("pO%d" % oslot), b("rden%d" % oslot)], writes=[b("oT%d_%d" % (qslot, hh))])

            def finish_q(seq, qb):
                qslot = (seq * NQ + qb) % 2
                tq = seq * L + qb * 512
                for hh in range(4):
                    st.op("act", lambda h, hh=hh: h.activation(out=sq[:, hh, :], in_=oT[:, qslot, hh, :], func=AF.Square),
                          reads=[b("oT%d_%d" % (qslot, hh))], writes=[b("sq%d" % hh)])
                for hh in range(4):
                    st.op("pe", lambda h, hh=hh: h.matmul(pSS[:], lhsT=ones[:], rhs=sq[:, hh, :], start=(hh == 0), stop=(hh == 3)),
                          reads=[b("ones"), b("sq%d" % hh)], writes=[b("pSS")])
                st.op("act", lambda h: h.activation(out=lnv[:], in_=pSS[:], func=AF.Ln, scale=1.0 / 512, bias=EPS), reads=[b("pSS")], writes=[b("lnv")])
                st.op("act", lambda h: h.activation(out=rstd[:], in_=lnv[:], func=AF.Exp, scale=-0.5), reads=[b("lnv")], writes=[b("rstd")])
                for hh in range(4):
                    yt, ybuf = ring_y.get()
                    st.op("dve", lambda h, hh=hh, yt=yt: h.tensor_tensor(out=yt, in0=oT[:, qslot, hh, :], in1=rstd[:], op=ALU.mult),
                          reads=[b("oT%d_%d" % (qslot, hh)), b("rstd")], writes=[ybuf])
                    st.dma(lambda h, hh=hh, yt=yt: h.dma_start(out=yT_s[4 + hh, :, tq:tq + 512], in_=yt), reads=[ybuf])

            pend = None
            ocnt = 0
            for idx, item in enumerate(items):
                seq, qb, hh, kc = item
                if idx == 0:
                    loads_seq(seq)
                if hh == 0 and kc == 0:
                    if idx == 0:
                        loads_q(seq, 0)
                    nxt = (seq, qb + 1) if qb + 1 < NQ else ((seq + 1, 0) if seq + 1 < n_seq else None)
                    if nxt is not None and nxt[0] == seq:
                        loads_q(*nxt)
                if idx == 0:
                    pend = qk(item)
                pt, pb = pend
                oslot = ocnt % 2
                Pa, Pb = ring_P.get()
                st.op("act", lambda h, Pa=Pa, pt=pt: h.activation(out=Pa, in_=pt, func=AF.Exp, scale=SCALE), reads=[pb], writes=[Pb])
                if idx + 1 < len(items):
                    nseq, nqb, nhh, nkc = items[idx + 1]
                    if nseq != seq:
                        pass
                    if (nseq, nqb) != (seq, qb) and nseq != seq:
                        pend = None
                    else:
                        pend = qk(items[idx + 1])
                st.op("pe", lambda h, Pa=Pa, oslot=oslot, hh=hh, kc=kc: h.matmul(pO[:, oslot, :], lhsT=vm[:, kc, hh * 128:(hh + 1) * 128], rhs=Pa,
                                                                           start=(kc == 0), stop=(kc == NK - 1)),
                      reads=[b("vm"), Pb], writes=[b("pO%d" % oslot)])
                if kc == 0:
                    st.op("dve", lambda h, Pa=Pa, oslot=oslot: h.tensor_copy(out=acc[:, oslot, :], in_=Pa), reads=[Pb], writes=[b("acc%d" % oslot)])
                else:
                    st.op("dve", lambda h, Pa=Pa, oslot=oslot: h.tensor_tensor(out=acc[:, oslot, :], in0=acc[:, oslot, :], in1=Pa, op=ALU.add),
                          reads=[Pb, b("acc%d" % oslot)], writes=[b("acc%d" % oslot)])
                if kc == NK - 1:
                    finish_head(seq, qb, hh, oslot)
                    ocnt += 1
                    if hh == 3:
                        finish_q(seq, qb)
                if pend is None and idx + 1 < len(items):
                    nseq, nqb, nhh, nkc = items[idx + 1]
                    loads_seq(nseq)
                    loads_q(nseq, 0)
                    pend = qk(items[idx + 1])
            st.finish()
            st.emit()

    def stage3():
        NCs = L // 128
        with contextlib.ExitStack() as es:
            def sb(name, shape, dt=F32):
                return es.enter_context(nc.sbuf_tensor("s3_" + name, list(shape), dt))

            def ps(name, shape, dt=F32):
                return es.enter_context(nc.psum_tensor("s3_" + name, list(shape), dt))

            st = Stage(nc, "s3")
            scal = sb("scal", [128, 2, 3, 4, NCH])
            masks = sb("masks", [128, 2, 128])
            ident = sb("ident", [128, 128], BF16)
            qd = sb("qd", [128, 2 * n_seq, 4, 512], BF16)
            ki = sb("ki", [128, 2 * n_seq, 4, 512], BF16)
            vt = sb("vt", [128, 2 * n_seq, 4, 512], BF16)
            kitok = sb("kitok", [128, 2, 512], BF16)
            scT = sb("scT", [128, 2, 8, 128], BF16)
            S_all = sb("S", [128, n_seq, 4, 128]); Sp_all = sb("Sp", [128, n_seq, 4, 128], BF16)
            t1_all = sb("t1", [128, n_seq, 4, 128]); t2_all = sb("t2", [128, n_seq, 4, 128])
            ofw = sb("ofw", [128, 4, 512])
            gtl = sb("gtl", [128, 4, 512])
            osum_a = sb("osum", [128, 2, 512]); sqt_a = sb("sqt", [128, 2, 512]); yn_a = sb("yn", [128, 2, 512])
            ss8_a = sb("ss8", [128, 2, 8]); ln8_a = sb("ln8", [128, 2, 8]); rs8_a = sb("rs8", [128, 2, 8])
            ya = sb("ya", [128, 2, 512], BF16)
            yaT = sb("yaT", [128, 2, 4, 128], BF16)
            pT = ps("pT", [128, 2, 1024], BF16)
            pSc = ps("pSc", [128, 2, 4, 128])
            pOo = ps("pOo", [128, 2, 512])
            pU = ps("pU", [128, 1, 4, 128])
            pY = ps("pY", [128, 8, 128], BF16)
            B = {}

            def b(name):
                if name not in B:
                    B[name] = Buf(name)
                return B[name]
            ring_of = Ring(ofw, 4, "ofw")
            ring_g = Ring(gtl, 4, "gtl")
            st.dma(lambda h: h.dma_start(out=scal[:], in_=scal_s), writes=[b("scal")])
            st.dma(lambda h: h.dma_start(out=masks[:, 0, :], in_=c_maskf), writes=[b("masks")])
            st.dma(lambda h: h.dma_start(out=masks[:, 1, :], in_=c_maskb), writes=[b("masks")])
            st.dma(lambda h: h.dma_start(out=ident[:], in_=c_ident), writes=[b("ident")])

            def run_dir(seq, d):
                order = list(range(NCs)) if d == 0 else list(range(NCs - 1, -1, -1))
                S = S_all[:, seq]; Sp = Sp_all[:, seq]
                st.op("pool", lambda h: h.memset(S, 0.0), writes=[b("S%d" % seq)])
                st.op("pool", lambda h: h.memset(Sp, 0.0), writes=[b("Sp%d" % seq)])
                cur_group = [None, None]
                gcount = [0]

                def load_group(gi):
                    slot = seq * 2 + gcount[0] % 2
                    gcount[0] += 1
                    tg = seq * L + gi * 512
                    st.dma(lambda h: h.dma_start(out=qd[:, slot], in_=qdT_s[d, :, :, tg:tg + 512].rearrange("k p t -> p k t")), writes=[b("qd%d" % slot)])
                    st.dma(lambda h: h.dma_start(out=ki[:, slot], in_=kiT_s[d, :, :, tg:tg + 512].rearrange("k p t -> p k t")), writes=[b("ki%d" % slot)])
                    st.dma(lambda h: h.dma_start(out=vt[:, slot], in_=v_s[tg:tg + 512, :].rearrange("(j p) f -> p j f", p=128)), writes=[b("vt%d" % slot)])
                    return slot

                for pos, n in enumerate(order):
                    gi = n // 4
                    if cur_group[0] != gi:
                        cur_group[0] = gi
                        cur_group[1] = load_group(gi)
                    slot = cur_group[1]
                    j = n % 4
                    cg = seq * NCs + n
                    t0 = cg * 128
                    cs = seq % 2
                    chunk(seq, d, n, pos, order, slot, j, cg, t0, cs)
                    yield

            def chunk(seq, d, n, pos, order, slot, j, cg, t0, cs):
                S = S_all[:, seq]; Sp = Sp_all[:, seq]; t1 = t1_all[:, seq]; t2 = t2_all[:, seq]
                osum = osum_a[:, cs]; sqt = sqt_a[:, cs]; yn = yn_a[:, cs]
                ss8 = ss8_a[:, cs]; ln8 = ln8_a[:, cs]; rs8 = rs8_a[:, cs]
                bS, bSp, bt1, bt2 = b("S%d" % seq), b("Sp%d" % seq), b("t1%d" % seq), b("t2%d" % seq)
                bos, bsq, byn, bss, bln, brs = (b("%s%d" % (nm, cs)) for nm in ("osum", "sqt", "yn", "ss8", "ln8", "rs8"))
                qd_c = qd[:, slot, :, j * 128:(j + 1) * 128]
                ki_c = ki[:, slot, :, j * 128:(j + 1) * 128]
                v_c = vt[:, slot, j, :]
                rq, rk, rv = b("qd%d" % slot), b("ki%d" % slot), b("vt%d" % slot)
                if d == 1:
                    oft, ofb = ring_of.get()
                    gtt, gtb = ring_g.get()
                    st.dma(lambda h: h.dma_start(out=oft, in_=of_s[t0:t0 + 128, :]), reads=[b("ofs%d" % cg)], writes=[ofb])
                    st.dma(lambda h: h.dma_start(out=gtt, in_=g_s[t0:t0 + 128, :]), writes=[gtb])
                for bk in range(4):
                    st.op("pe", lambda h, bk=bk: h.transpose(out=pT[:, cs, bk * 128:(bk + 1) * 128], in_=ki_c[:, bk, :], identity=ident[:]),
                          reads=[rk, b("ident")], writes=[b("pT%d" % cs)])
                st.op("act", lambda h: h.activation(out=kitok[:, cs, :], in_=pT[:, cs, 0:512], func=AF.Copy), reads=[b("pT%d" % cs)], writes=[b("kitok%d" % cs)])
                for hh in range(8):
                    bk, hp = hh // 2, hh % 2
                    st.op("pe", lambda h, hh=hh, bk=bk, hp=hp: h.matmul(pSc[:, hp, bk, :], lhsT=ki_c[hp * 64:(hp + 1) * 64, bk, :],
                                                                        rhs=qd_c[hp * 64:(hp + 1) * 64, bk, :], start=True, stop=True),
                          reads=[rk, rq], writes=[b("pSc%d" % hp)])
                scv = scT[:, cs].rearrange("p (k two) t -> p k two t", two=2)
                for half in range(2):
                    st.op("dve", lambda h, half=half: h.tensor_tensor(out=scv[:, :, half, :], in0=pSc[:, half],
                                                                      in1=bc(masks[:, d:d + 1, :], [128, 4, 128]), op=ALU.mult),
                          reads=[b("pSc%d" % half), b("masks")], writes=[b("scT%d_%d" % (cs, half))])
                for hh in range(8):
                    bk, hp = hh // 2, hh % 2
                    st.op("pe", lambda h, hh=hh: h.matmul(pOo[:, cs, hh * 64:(hh + 1) * 64], lhsT=scT[:, cs, hh, :], rhs=v_c[:, hh * 64:(hh + 1) * 64],
                                                          start=True, stop=False),
                          reads=[b("scT%d_%d" % (cs, hh % 2)), rv], writes=[b("pOo%d" % cs)])
                    st.op("pe", lambda h, hh=hh, bk=bk, hp=hp: h.matmul(pOo[:, cs, hh * 64:(hh + 1) * 64], lhsT=qd_c[hp * 64:(hp + 1) * 64, bk, :],
                                                                        rhs=Sp[hp * 64:(hp + 1) * 64, bk, hp * 64:(hp + 1) * 64], start=False, stop=True),
                          reads=[rq, bSp], writes=[b("pOo%d" % cs)])
                for bk in range(4):
                    st.op("pe", lambda h, bk=bk: h.matmul(pU[:, 0, bk, :], lhsT=kitok[:, cs, bk * 128:(bk + 1) * 128], rhs=v_c[:, bk * 128:(bk + 1) * 128],
                                                          start=True, stop=True),
                          reads=[b("kitok%d" % cs), rv], writes=[b("pU0")])
                if pos + 1 < len(order):
                    cgn = seq * NCs + order[pos + 1]
                    Abc = bc(scal[:, d, 0, :, cg:cg + 1], [128, 4, 128])
                    Bbc = bc(scal[:, d, 1, :, cg:cg + 1], [128, 4, 128])
                    Cbc = bc(scal[:, d, 2, :, cgn:cgn + 1], [128, 4, 128])
                    st.op("pool", lambda h: h.tensor_tensor(out=t1, in0=S, in1=Abc, op=ALU.mult), reads=[bS, b("scal")], writes=[bt1])
                    st.op("dve", lambda h: h.tensor_tensor(out=t2, in0=pU[:, 0], in1=Bbc, op=ALU.mult), reads=[b("pU0"), b("scal")], writes=[bt2])
                    st.op("dve", lambda h: h.tensor_tensor(out=S, in0=t1, in1=t2, op=ALU.add), reads=[bt1, bt2], writes=[bS])
                    st.op("dve", lambda h: h.tensor_tensor(out=Sp, in0=S, in1=Cbc, op=ALU.mult), reads=[bS, b("scal")], writes=[bSp])
                if d == 0:
                    oft, ofb = ring_of.get()
                    st.op("act", lambda h: h.activation(out=oft, in_=pOo[:, cs, :], func=AF.Copy), reads=[b("pOo%d" % cs)], writes=[ofb])
                    st.dma(lambda h: h.dma_start(out=of_s[t0:t0 + 128, :], in_=oft), reads=[ofb], writes=[b("ofs%d" % cg)])
                else:
                    st.op("dve", lambda h: h.tensor_tensor(out=osum, in0=pOo[:, cs, :], in1=oft, op=ALU.add), reads=[b("pOo%d" % cs), ofb], writes=[bos])
                    st.op("act", lambda h: h.activation(out=sqt, in_=osum, func=AF.Square), reads=[bos], writes=[bsq])
                    st.op("dve", lambda h: h.tensor_reduce(out=ss8, in_=sqt.rearrange("p (h e) -> p h e", e=64), axis=AX.X, op=ALU.add),
                          reads=[bsq], writes=[bss])
                    st.op("act", lambda h: h.activation(out=ln8, in_=ss8, func=AF.Ln, scale=1.0 / 64, bias=EPS), reads=[bss], writes=[bln])
                    st.op("act", lambda h: h.activation(out=rs8, in_=ln8, func=AF.Exp, scale=-0.5), reads=[bln], writes=[brs])
                    st.op("dve", lambda h: h.tensor_tensor(out=yn.rearrange("p (h e) -> p h e", e=64), in0=osum.rearrange("p (h e) -> p h e", e=64),
                                                           in1=bc(rs8.unsqueeze(2), [128, 8, 64]), op=ALU.mult),
                          reads=[bos, brs], writes=[byn])
                    st.op("pool", lambda h: h.tensor_tensor(out=ya[:, cs, :], in0=yn, in1=gtt, op=ALU.mult), reads=[byn, gtb], writes=[b("ya%d" % cs)])
                    for bk in range(4):
                        st.op("pe", lambda h, bk=bk: h.transpose(out=pY[:, bk, :], in_=ya[:, cs, bk * 128:(bk + 1) * 128], identity=ident[:]),
                              reads=[b("ya%d" % cs), b("ident")], writes=[b("pY")])
                    st.op("act", lambda h: h.activation(out=yaT[:, cs], in_=pY[:, 0:4, :], func=AF.Copy), reads=[b("pY")], writes=[b("yaT%d" % cs)])
                    st.dma(lambda h: h.dma_start(out=yT_s[0:4, :, t0:t0 + 128].rearrange("k p t -> p k t"), in_=yaT[:, cs]), reads=[b("yaT%d" % cs)])

            for d in range(2):
                gens = [run_dir(seq, d) for seq in range(n_seq)]
                live = list(gens)
                while live:
                    for g in list(live):
                        try:
                            next(g)
                        except StopIteration:
                            live.remove(g)
            st.finish()
            st.emit()

    def stage4():
        TT = 256
        NTL = T // TT
        with contextlib.ExitStack() as es:
            def sb(name, shape, dt=F32):
                return es.enter_context(nc.sbuf_tensor("s4_" + name, list(shape), dt))

            def ps(name, shape, dt=F32):
                return es.enter_context(nc.psum_tensor("s4_" + name, list(shape), dt))

            st = Stage(nc, "s4")
            wo = sb("wo", [128, 8, D], BF16)
            wg = sb("wg", [128, 8, DFF], BF16)
            wu = sb("wu", [128, 8, DFF], BF16)
            wd = sb("wd", [128, NFB, D], BF16)
            goutt = sb("goutt", [128, 8]); g2t = sb("g2t", [128, 8])
            gfb = sb("gfb", [128, D])
            ident = sb("ident", [128, 128], BF16)
            xt = sb("xt", [128, 2, 2, D])
            yT = sb("yT", [128, 2, 8, TT], BF16)
            h2n = sb("h2n", [128, 2, D], BF16)
            h2T = sb("h2T", [128, 8, TT], BF16)
            aT = sb("aT", [128, NFB, TT], BF16)
            sg = sb("sg", [128, 2, TT])
            junk = sb("junk", [128, D], BF16)
            ss = sb("ss", [128, 4]); lnt = sb("lnt", [128, 4]); rs = sb("rs", [128, 4])
            pxt = ps("pxt", [128, 8, 128], BF16)
            pG = ps("pG", [128, 2, 512])
            pUu = ps("pUu", [128, 2, 512])
            pA = ps("pA", [128, 2, 512])
            B = {}

            def b(name):
                if name not in B:
                    B[name] = Buf(name)
                return B[name]
            ring_A = Ring(pA, 2, "pA")
            for (dst, src, nm) in ((goutt, gout, "goutt"), (g2t, g2, "g2t"), (ident, c_ident, "ident")):
                st.dma(lambda h, dst=dst, src=src: h.dma_start(out=dst[:], in_=src), writes=[b(nm)])
            if os.environ.get("S4_NOBC"):
                for pp in range(0, 128, 32):
                    pass
                st.op("pool", lambda h: h.memset(gfb[:], 1.0), writes=[b("gfb")])
            else:
                st.dma(lambda h: h.dma_start(out=gfb[:], in_=gfin.partition_broadcast(128)), writes=[b("gfb")])
            stg = xt[:].rearrange("p a b d -> p (a b) d")
            pcnt = [0]

            def prep(dst3, src3, nchunk, ncols, gain, name, gname):
                for c in range(nchunk):
                    for c0 in range(0, ncols, 1024):
                        c1 = min(ncols, c0 + 1024)
                        slot = pcnt[0] % 4
                        eng = (("dve", "pool") if os.environ.get("S4_NOACT") else ("dve", "pool", "act"))[pcnt[0] % (2 if os.environ.get("S4_NOACT") else 3)]
                        pcnt[0] += 1
                        sbuf = b("xt%d" % slot)
                        st.dma(lambda h, c=c, slot=slot, c0=c0, c1=c1: h.dma_start(out=stg[:, slot, 0:c1 - c0], in_=src3[:, c, c0:c1]), writes=[sbuf])
                        rd = [sbuf] + ([b(gname)] if gain is not None else [])
                        if eng == "act":
                            if gain is None:
                                st.op("act", lambda h, c=c, slot=slot, c0=c0, c1=c1: h.activation(out=dst3[:, c, c0:c1], in_=stg[:, slot, 0:c1 - c0], func=AF.Copy),
                                      reads=rd, writes=[b(name)])
                            else:
                                st.op("act", lambda h, c=c, slot=slot, c0=c0, c1=c1: h.activation(out=dst3[:, c, c0:c1], in_=stg[:, slot, 0:c1 - c0], func=AF.Copy,
                                                                                               scale=gain[:, c:c + 1]),
                                      reads=rd, writes=[b(name)])
                        else:
                            if gain is None:
                                st.op(eng, lambda h, c=c, slot=slot, c0=c0, c1=c1: h.tensor_copy(out=dst3[:, c, c0:c1], in_=stg[:, slot, 0:c1 - c0]),
                                      reads=rd, writes=[b(name)])
                            else:
                                st.op(eng, lambda h, c=c, slot=slot, c0=c0, c1=c1: h.tensor_scalar(out=dst3[:, c, c0:c1], in0=stg[:, slot, 0:c1 - c0],
                                                                                                scalar1=gain[:, c:c + 1], scalar2=None, op0=ALU.mult),
                                      reads=rd, writes=[b(name)])
            prep(wo, w_out, 8, D, goutt, "wo", "goutt")
            prep(wg, w_gate, 8, DFF, g2t, "wg", "g2t")
            prep(wu, w_up, 8, DFF, g2t, "wu", "g2t")
            prep(wd, w_down, NFB, D, None, "wd", None)

            def loads(i):
                slot = i % 2
                t0 = i * TT
                st.dma(lambda h: h.dma_start(out=xt[:, slot], in_=x[t0:t0 + TT, :].rearrange("(s p) d -> p s d", p=128)),
                       writes=[b("xt%d" % (slot * 2)), b("xt%d" % (slot * 2 + 1))])
                st.dma(lambda h: h.dma_start(out=yT[:, slot], in_=yT_s[:, :, t0:t0 + TT].rearrange("c p t -> p c t")), writes=[b("yT%d" % slot)])

            def rms_rstd(i, sub, which):
                slot = i % 2
                col = which * 2 + sub
                xb = b("xt%d" % (slot * 2 + sub))
                st.op("act", lambda h: h.activation(out=junk[:], in_=xt[:, slot, sub, :], func=AF.Square, accum_out=ss[:, col:col + 1]),
                      reads=[xb], writes=[b("junk"), b("ss%d" % col)])
                st.op("act", lambda h: h.activation(out=lnt[:, col:col + 1], in_=ss[:, col:col + 1], func=AF.Ln, scale=1.0 / D, bias=EPS),
                      reads=[b("ss%d" % col)], writes=[b("ln%d" % col)])
                st.op("act", lambda h: h.activation(out=rs[:, col:col + 1], in_=lnt[:, col:col + 1], func=AF.Exp, scale=-0.5),
                      reads=[b("ln%d" % col)], writes=[b("rs%d" % col)])
                return col

            def tile(i):
                slot = i % 2
                t0 = i * TT
                if i + 1 < NTL:
                    loads(i + 1)
                for sub in range(2):
                    xb = b("xt%d" % (slot * 2 + sub))
                    for half in range(2):
                        pt, pb = ring_A.get()
                        for c in range(8):
                            st.op("pe", lambda h, c=c, pt=pt, sub=sub, half=half: h.matmul(pt, lhsT=yT[:, slot, c, sub * 128:(sub + 1) * 128],
                                                                                          rhs=wo[:, c, half * 512:(half + 1) * 512], start=(c == 0), stop=(c == 7)),
                                  reads=[b("yT%d" % slot), b("wo")], writes=[pb])
                        st.op("dve", lambda h, pt=pt, sub=sub, half=half: h.tensor_tensor(out=xt[:, slot, sub, half * 512:(half + 1) * 512],
                                                                                         in0=xt[:, slot, sub, half * 512:(half + 1) * 512], in1=pt, op=ALU.add),
                              reads=[pb, xb], writes=[xb])
                for sub in range(2):
                    xb = b("xt%d" % (slot * 2 + sub))
                    col = rms_rstd(i, sub, 0)
                    st.op("dve", lambda h, sub=sub, col=col: h.tensor_scalar(out=h2n[:, sub, :], in0=xt[:, slot, sub, :], scalar1=rs[:, col:col + 1],
                                                                            scalar2=None, op0=ALU.mult),
                          reads=[xb, b("rs%d" % col)], writes=[b("h2n%d" % sub)])
                    for c in range(8):
                        st.op("pe", lambda h, sub=sub, c=c: h.transpose(out=pxt[:, c, :], in_=h2n[:, sub, c * 128:(c + 1) * 128], identity=ident[:]),
                              reads=[b("h2n%d" % sub), b("ident")], writes=[b("pxt")])
                    st.op("dve", lambda h, sub=sub: h.tensor_copy(out=h2T[:, :, sub * 128:(sub + 1) * 128], in_=pxt[:]), reads=[b("pxt")], writes=[b("h2T")])
                for fb in range(NFB):
                    gs = fb % 2
                    for c in range(8):
                        st.op("pe", lambda h, c=c, fb=fb, gs=gs: h.matmul(pG[:, gs, 0:TT], lhsT=wg[:, c, fb * 128:(fb + 1) * 128], rhs=h2T[:, c, :],
                                                                          start=(c == 0), stop=(c == 7)),
                              reads=[b("wg"), b("h2T")], writes=[b("pG%d" % gs)])
                    for c in range(8):
                        st.op("pe", lambda h, c=c, fb=fb, gs=gs: h.matmul(pUu[:, gs, 0:TT], lhsT=wu[:, c, fb * 128:(fb + 1) * 128], rhs=h2T[:, c, :],
                                                                          start=(c == 0), stop=(c == 7)),
                              reads=[b("wu"), b("h2T")], writes=[b("pU%d" % gs)])
                    st.op("act", lambda h, gs=gs: h.activation(out=sg[:, gs, :], in_=pG[:, gs, 0:TT], func=AF.Silu), reads=[b("pG%d" % gs)], writes=[b("sg%d" % gs)])
                    st.op("dve", lambda h, gs=gs, fb=fb: h.tensor_tensor(out=aT[:, fb, :], in0=sg[:, gs, :], in1=pUu[:, gs, 0:TT], op=ALU.mult),
                          reads=[b("sg%d" % gs), b("pU%d" % gs)], writes=[b("aT")])
                for sub in range(2):
                    xb = b("xt%d" % (slot * 2 + sub))
                    for half in range(2):
                        pt, pb = ring_A.get()
                        for fb in range(NFB):
                            st.op("pe", lambda h, fb=fb, pt=pt, sub=sub, half=half: h.matmul(pt, lhsT=aT[:, fb, sub * 128:(sub + 1) * 128],
                                                                                            rhs=wd[:, fb, half * 512:(half + 1) * 512], start=(fb == 0), stop=(fb == NFB - 1)),
                                  reads=[b("aT"), b("wd")], writes=[pb])
                        st.op("dve", lambda h, pt=pt, sub=sub, half=half: h.tensor_tensor(out=xt[:, slot, sub, half * 512:(half + 1) * 512],
                                                                                         in0=xt[:, slot, sub, half * 512:(half + 1) * 512], in1=pt, op=ALU.add),
                              reads=[pb, xb], writes=[xb])
                for sub in range(2):
                    xb = b("xt%d" % (slot * 2 + sub))
                    col = rms_rstd(i, sub, 1)
                    st.op("dve", lambda h, sub=sub, col=col: h.scalar_tensor_tensor(out=xt[:, slot, sub, :], in0=xt[:, slot, sub, :], scalar=rs[:, col:col + 1],
                                                                                   in1=gfb[:], op0=ALU.mult, op1=ALU.mult),
                          reads=[xb, b("rs%d" % col), b("gfb")], writes=[xb])
                    st.dma(lambda h, sub=sub: h.dma_start(out=out[t0 + sub * 128:t0 + (sub + 1) * 128, :], in_=xt[:, slot, sub, :]), reads=[xb])

            loads(0)
            for i in range(NTL):
                tile(i)
            st.finish()
            st.emit()

    if 1 in stages:
        stage1()
    if 2 in stages:
        stage2()
    if 3 in stages:
        stage3()
    if 4 in stages:
        stage4()
    return nc


def _pcn(w, rows):
    n = w.shape[1]
    return np.ascontiguousarray(w.reshape(rows // 128, 128, n).transpose(1, 0, 2))


def _pc(g):
    return np.ascontiguousarray(g.reshape(-1, 128).T)


def layout_inputs(inp, L):
    f32 = np.float32
    w_in = np.asarray(inp["w_in"][0], f32)
    hq, hi, hff, hfb, hg, cq, ckv, kr = np.split(w_in, np.cumsum([512, 512, 512, 512, 512, 384, 256])[:], axis=1)
    krot = np.concatenate([kr[:, 32:64], kr[:, 0:32]], axis=1)
    w1 = np.concatenate([hq, hff, hfb, hi, hg, cq, ckv, kr, kr, krot, krot], axis=1)
    assert w1.shape[1] == W1C
    wqb = np.asarray(inp["w_q_b"][0], f32).reshape(384, 4, 192)
    nope = wqb[:, :, 0:128].reshape(384, 512)
    rp = wqb[:, :, 128:192]
    rope = rp.reshape(384, 256)
    rot = np.concatenate([rp[:, :, 32:64], rp[:, :, 0:32]], axis=2).reshape(384, 256)
    wq = np.concatenate([nope, rope, rot], axis=1)
    wkvb = np.asarray(inp["w_kv_b"][0], f32).reshape(256, 4, 256)
    wkv = np.concatenate([wkvb[:, :, 0:128].reshape(256, 512), wkvb[:, :, 128:256].reshape(256, 512)], axis=1)
    lbl = np.asarray(inp["lb_logits"], f32)
    lbl_l = np.ascontiguousarray(lbl.reshape(2, 2, 4, 128).transpose(3, 0, 1, 2))
    gout = np.concatenate([np.asarray(inp["hgrn_norm_g"][0], f32), np.asarray(inp["mla_norm_g"][0], f32)])
    inv = 1.0 / (10000.0 ** (np.arange(0, 64, 2, dtype=np.float32) / 64.0))
    ang = np.arange(L, dtype=np.float32)[None, :] * inv[:, None].astype(np.float32)
    cos = np.cos(ang).astype(f32)
    sin = np.sin(ang).astype(f32)
    c_cos = np.ascontiguousarray(np.tile(cos, (4, 1)))
    c_sin = np.ascontiguousarray(np.tile(sin, (4, 1)))
    rmask = np.ones((128, 512), f32)
    rmask[:, 0::128] = 0.0
    jj = np.arange(128)[:, None]
    ii = np.arange(128)[None, :]
    d = {
        "w_in": _pcn(w1, 1024), "g1": _pc(np.asarray(inp["norm1_g"][0], f32)), "lbl": lbl_l,
        "w_qb": _pcn(wq, 384), "gqa": _pc(np.asarray(inp["q_a_norm_g"][0], f32)),
        "w_kvb": _pcn(wkv, 256), "gkva": _pc(np.asarray(inp["kv_a_norm_g"][0], f32)),
        "w_out": _pcn(np.asarray(inp["w_out"][0], f32), 1024), "gout": _pc(gout),
        "w_gate": _pcn(np.asarray(inp["w_gate"][0], f32), 1024), "w_up": _pcn(np.asarray(inp["w_up"][0], f32), 1024),
        "g2": _pc(np.asarray(inp["norm2_g"][0], f32)),
        "w_down": _pcn(np.asarray(inp["w_down"][0], f32), DFF),
        "gfin": np.asarray(inp["final_norm_g"], f32).reshape(1, D),
        "c_ident": np.eye(128).astype(ml_dtypes.bfloat16), "c_ones": np.ones((128, 128), ml_dtypes.bfloat16),
        "c_cos": c_cos, "c_sin": c_sin, "c_rmask": rmask,
        "c_maskf": (jj <= ii).astype(f32), "c_maskb": (jj >= ii).astype(f32),
    }
    return d


_NC_CACHE = {}


def kernel(**inputs):
    x = np.asarray(inputs["x"], np.float32)
    Bt, L, _ = x.shape
    n_seq = Bt // NCORES
    key = (n_seq, L)
    if key not in _NC_CACHE:
        _NC_CACHE[key] = build_nc(n_seq, L)
    nc = _NC_CACHE[key]
    shared = layout_inputs(inputs, L)
    in_maps = []
    for c in range(NCORES):
        m = dict(shared)
        m["x"] = np.ascontiguousarray(x[c * n_seq:(c + 1) * n_seq].reshape(n_seq * L, D))
        in_maps.append(m)
    res = run_bass_kernel_spmd(nc, in_maps, core_ids=list(range(NCORES)))
    out = np.stack([r["out"].reshape(n_seq, L, D) for r in res.results], axis=0)
    return out.reshape(Bt, L, D).astype(np.float32)
```

```python
import contextlib
import os
import numpy as np
import ml_dtypes
import concourse.bass as bass
import concourse.mybir as mybir
from concourse.bass_utils import run_bass_kernel_spmd

F32 = mybir.dt.float32
BF16 = mybir.dt.bfloat16
AF = mybir.ActivationFunctionType
ALU = mybir.AluOpType
AX = mybir.AxisListType

D = 1024
DFF = 2816
NFB = DFF // 128
EPS = 1e-6
NCORES = 8
W1C = 3456
SCALE = 192 ** -0.5


class Buf:
    __slots__ = ("name", "w", "r", "x")

    def __init__(self, name=""):
        self.name = name
        self.w = None
        self.r = {}
        self.x = len(name) > 1 and name[0] == "p" and (name[1].isupper() or name.startswith(("pmm", "pxt")))


class Stage:
    ENGS = ("pe", "act", "dve", "pool", "sp")

    def __init__(self, nc, name, n_dma_sems=16):
        self.nc = nc
        self.name = name
        self.ops = {e: [] for e in self.ENGS}
        self.cnt = {e: 0 for e in ("pe", "act", "dve", "pool")}
        self.waited = {e: {} for e in self.ENGS}
        self.n_dma = n_dma_sems
        self.dma_cnt = [0] * n_dma_sems
        self.dma_rr = 0
        self.sems = {}

    def _need(self, eng, ev, waits):
        if ev is None:
            return
        key, val = ev
        if key == "pe" and eng == "pe":
            return
        if self.waited[eng].get(key, 0) >= val:
            return
        self.waited[eng][key] = val
        waits.append((key, val))

    def _deps(self, eng, reads, writes):
        waits = []
        for b in reads:
            self._need(eng, b.w, waits)
            if b.x:
                for k, v in b.r.items():
                    if k != eng:
                        self._need(eng, (k, v), waits)
        for b in writes:
            self._need(eng, b.w, waits)
            for k, v in b.r.items():
                self._need(eng, (k, v), waits)
        return waits

    def _commit(self, ev, reads, writes):
        k, v = ev
        for b in reads:
            if b.r.get(k, 0) < v:
                b.r[k] = v
        for b in writes:
            b.w = ev
            b.r = {}

    def op(self, eng, fn, reads=(), writes=()):
        waits = self._deps(eng, reads, writes)
        self.cnt[eng] += 1
        ev = (eng, self.cnt[eng])
        self.ops[eng].append((waits, fn, (eng, 1)))
        self._commit(ev, reads, writes)
        return ev

    def dma(self, fn, reads=(), writes=(), queue="sp"):
        waits = self._deps(queue, reads, writes)
        k = self.dma_rr
        self.dma_rr = (self.dma_rr + 1) % self.n_dma
        key = "dma%d" % k
        if self.dma_cnt[k] > 0:
            self._need(queue, (key, self.dma_cnt[k]), waits)
        self.dma_cnt[k] += 16
        ev = (key, self.dma_cnt[k])
        self.ops[queue].append((waits, fn, (key, 16)))
        self._commit(ev, reads, writes)
        return ev

    def finish(self, eng="sp"):
        waits = []
        for k in range(self.n_dma):
            if self.dma_cnt[k] > 0:
                self._need(eng, ("dma%d" % k, self.dma_cnt[k]), waits)
        if waits:
            self.ops[eng].append((waits, None, None))

    def emit(self):
        nc = self.nc
        with contextlib.ExitStack() as st:
            for e in ("pe", "act", "dve", "pool"):
                self.sems[e] = st.enter_context(nc.semaphore("%s_%s" % (self.name, e)))
            for k in range(self.n_dma):
                self.sems["dma%d" % k] = st.enter_context(nc.semaphore("%s_d%d" % (self.name, k)))
            block = st.enter_context(nc.Block())
            sems = self.sems

            def run(h, lst):
                for waits, fn, inc in lst:
                    for key, val in waits:
                        h.wait_ge(sems[key], val)
                    if fn is not None:
                        fn(h).then_inc(sems[inc[0]], inc[1])

            if self.ops["sp"]:
                @block.sync
                def _(h):
                    run(h, self.ops["sp"])
            if self.ops["pe"]:
                @block.tensor
                def _(h):
                    run(h, self.ops["pe"])
            if self.ops["act"]:
                @block.scalar
                def _(h):
                    run(h, self.ops["act"])
            if self.ops["dve"]:
                @block.vector
                def _(h):
                    run(h, self.ops["dve"])
            if self.ops["pool"]:
                @block.gpsimd
                def _(h):
                    run(h, self.ops["pool"])


class Ring:
    def __init__(self, tens, n, name):
        self.t = tens
        self.n = n
        self.i = 0
        self.bufs = [Buf("%s%d" % (name, k)) for k in range(n)]

    def get(self):
        k = self.i
        self.i = (self.i + 1) % self.n
        return self.t[:, k], self.bufs[k]


def bc(ap, shape):
    return ap.to_broadcast(shape)


def build_nc(n_seq, L, debug=False, stages=(1, 2, 3, 4)):
    T = n_seq * L
    NT = T // 128
    NS = T // 512
    NCH = T // 128
    nc = bass.Bass("TRN2", target_bir_lowering=False)

    def din(name, shape, dt=F32):
        return nc.dram_tensor(name, list(shape), dt, kind="ExternalInput").ap()

    skind = "ExternalOutput" if debug else "Internal"

    def dscr(name, shape, dt):
        return nc.dram_tensor(name, list(shape), dt, kind=skind).ap()

    x = din("x", [T, D])
    out = nc.dram_tensor("out", [T, D], F32, kind="ExternalOutput").ap()
    w_in = din("w_in", [128, 8, W1C])
    g1 = din("g1", [128, 8])
    lbl = din("lbl", [128, 2, 2, 4])
    w_qb = din("w_qb", [128, 3, 1024])
    gqa = din("gqa", [128, 3])
    w_kvb = din("w_kvb", [128, 2, 1024])
    gkva = din("gkva", [128, 2])
    w_out = din("w_out", [128, 8, D])
    gout = din("gout", [128, 8])
    w_gate = din("w_gate", [128, 8, DFF])
    w_up = din("w_up", [128, 8, DFF])
    g2 = din("g2", [128, 8])
    w_down = din("w_down", [128, NFB, D])
    gfin = din("gfin", [1, D])
    c_ident = din("c_ident", [128, 128], BF16)
    c_ones = din("c_ones", [128, 128], BF16)
    c_cos = din("c_cos", [128, L])
    c_sin = din("c_sin", [128, L])
    c_rmask = din("c_rmask", [128, 512])
    c_maskf = din("c_maskf", [128, 128])
    c_maskb = din("c_maskb", [128, 128])

    qdT_s = dscr("qdT_s", [2, 4, 128, T], BF16)
    kiT_s = dscr("kiT_s", [2, 4, 128, T], BF16)
    scal_s = dscr("scal_s", [128, 2, 3, 4, NCH], F32)
    v_s = dscr("v_s", [T, 512], BF16)
    g_s = dscr("g_s", [T, 512], F32)
    qnT_s = dscr("qnT_s", [4, 128, T], BF16)
    qrT_s = dscr("qrT_s", [2, 128, T], BF16)
    knT_s = dscr("knT_s", [4, 128, T], BF16)
    krT_s = dscr("krT_s", [128, T], BF16)
    vm_s = dscr("vm_s", [T, 512], BF16)
    yT_s = dscr("yT_s", [8, 128, T], BF16)
    of_s = dscr("of_s", [T, 512], F32)

    def stage1():
        with contextlib.ExitStack() as es:
            def sb(name, shape, dt=F32):
                return es.enter_context(nc.sbuf_tensor("s1_" + name, list(shape), dt))

            def ps(name, shape, dt=F32):
                return es.enter_context(nc.psum_tensor("s1_" + name, list(shape), dt))

            st = Stage(nc, "s1")
            w1 = sb("w1", [128, 8, W1C], BF16)
            wq = sb("wq", [128, 3, 1024], BF16)
            wkv = sb("wkv", [128, 2, 1024], BF16)
            stg = sb("stg", [128, 2, 1024], F32)
            g1t = sb("g1t", [128, 8]); gqat = sb("gqat", [128, 3]); gkvat = sb("gkvat", [128, 2])
            lblt = sb("lblt", [128, 2, 2, 4])
            lbt = sb("lbt", [128, 2, 4]); omlt = sb("omlt", [128, 2, 4])
            fa = sb("fa", [128, 2, 4]); fb_ = sb("fb", [128, 2, 4]); nfb = sb("nfb", [128, 2, 4])
            ident = sb("ident", [128, 128], BF16)
            rmask = sb("rmask", [128, 512])
            xt = sb("xt", [128, 4, D]); xjunk = sb("xjunk", [128, 2, D], BF16)
            xn = sb("xn", [128, 2, D], BF16)
            hT = sb("hT", [128, 2, 8, 512], BF16)
            ssx = sb("ssx", [128, 2, 4]); lnx = sb("lnx", [128, 2, 4]); rsx = sb("rsx", [128, 2, 4])
            cst = sb("cst", [128, 2, 512]); snt = sb("snt", [128, 2, 512])
            qT = sb("qT", [128, 4, 512])
            th = sb("th", [128, 8, 512])
            tmpf = sb("tmpf", [128, 8, 512])
            tmpb = sb("tmpb", [128, 8, 512], BF16)
            gt = sb("gt", [128, 2, 512])
            cqf = sb("cqf", [128, 4, 640]); cqn = sb("cqn", [128, 4, 640], BF16)
            ssq = sb("ssq", [128, 8]); lnq = sb("lnq", [128, 8]); rsq = sb("rsq", [128, 8])
            cqT = sb("cqT", [128, 3, 512], BF16); ckvT = sb("ckvT", [128, 2, 512], BF16)
            scal = sb("scal", [128, 2, 3, 4, NCH])
            pxt = ps("pxt", [128, 2, 8, 128], BF16)
            pmm = ps("pmm", [128, 6, 512])

            B = {}
            def b(name):
                if name not in B:
                    B[name] = Buf(name)
                return B[name]

            ring_f = Ring(tmpf, 8, "tmpf")
            ring_b = Ring(tmpb, 8, "tmpb")
            ring_p = Ring(pmm, 6, "pmm")
            ring_g = Ring(gt, 2, "gt")
            pxb = [Buf("pxt0"), Buf("pxt1")]
            hTb = [Buf("hT0"), Buf("hT1")]
            xtb = [Buf("xt%d" % i) for i in range(4)]
            xnb = [Buf("xn0"), Buf("xn1")]

            for (dst, src, nm) in ((g1t, g1, "g1t"), (gqat, gqa, "gqat"), (gkvat, gkva, "gkvat"),
                                   (lblt, lbl, "lblt"), (ident, c_ident, "ident"), (rmask, c_rmask, "rmask")):
                st.dma(lambda h, dst=dst, src=src: h.dma_start(out=dst[:], in_=src), writes=[b(nm)])
            st.op("dve", lambda h: h.tensor_tensor(out=lbt[:], in0=lblt[:, :, 1, :], in1=lblt[:, :, 0, :], op=ALU.subtract),
                  reads=[b("lblt")], writes=[b("lbt")])
            st.op("act", lambda h: h.activation(out=lbt[:], in_=lbt[:], func=AF.Exp), reads=[b("lbt")], writes=[b("lbt")])
            st.op("dve", lambda h: h.tensor_scalar(out=lbt[:], in0=lbt[:], scalar1=1.0, scalar2=None, op0=ALU.add),
                  reads=[b("lbt")], writes=[b("lbt")])
            st.op("dve", lambda h: h.reciprocal(out=lbt[:], in_=lbt[:]), reads=[b("lbt")], writes=[b("lbt")])
            st.op("dve", lambda h: h.tensor_scalar(out=omlt[:], in0=lbt[:], scalar1=-1.0, scalar2=1.0, op0=ALU.mult, op1=ALU.add),
                  reads=[b("lbt")], writes=[b("omlt")])
            st.op("dve", lambda h: h.tensor_scalar(out=fb_[:], in0=omlt[:], scalar1=0.5, scalar2=None, op0=ALU.mult),
                  reads=[b("omlt")], writes=[b("fb")])
            st.op("dve", lambda h: h.tensor_tensor(out=fa[:], in0=lbt[:], in1=fb_[:], op=ALU.add),
                  reads=[b("lbt"), b("fb")], writes=[b("fa")])
            st.op("dve", lambda h: h.tensor_scalar(out=nfb[:], in0=fb_[:], scalar1=-1.0, scalar2=None, op0=ALU.mult),
                  reads=[b("fb")], writes=[b("nfb")])

            pcnt = [0]

            def prep(dst3, src3, nchunk, ncols, gain, eng_cycle, name):
                for c in range(nchunk):
                    for c0 in range(0, ncols, 1024):
                        c1 = min(ncols, c0 + 1024)
                        slot = pcnt[0] % 2
                        eng = eng_cycle[pcnt[0] % len(eng_cycle)]
                        pcnt[0] += 1
                        sbuf = b("stg%d" % slot)
                        st.dma(lambda h, c=c, slot=slot, c0=c0, c1=c1: h.dma_start(out=stg[:, slot, 0:c1 - c0], in_=src3[:, c, c0:c1]),
                               writes=[sbuf])
                        st.op(eng, lambda h, c=c, slot=slot, c0=c0, c1=c1: h.tensor_scalar(
                            out=dst3[:, c, c0:c1], in0=stg[:, slot, 0:c1 - c0], scalar1=gain[:, c:c + 1], scalar2=None, op0=ALU.mult),
                            reads=[sbuf, b(name + "_g")], writes=[b(name)])
            B["w1_g"] = b("g1t"); B["wq_g"] = b("gqat"); B["wkv_g"] = b("gkvat")
            prep(w1, w_in, 8, W1C, g1t, ["dve", "pool"], "w1")
            prep(wq, w_qb, 3, 1024, gqat, ["dve", "pool"], "wq")
            prep(wkv, w_kvb, 2, 1024, gkvat, ["dve", "pool"], "wkv")
            v1 = w1[:, :, 3328:3456].rearrange("p c (g r) -> p c g r", r=64)[:, :, :, 0:32]
            st.op("dve", lambda h: h.tensor_scalar(out=v1, in0=v1, scalar1=-1.0, scalar2=None, op0=ALU.mult),
                  reads=[b("w1")], writes=[b("w1")])
            v2 = wq[:, :, 768:1024].rearrange("p c (g r) -> p c g r", r=64)[:, :, :, 0:32]
            st.op("dve", lambda h: h.tensor_scalar(out=v2, in0=v2, scalar1=-1.0, scalar2=None, op0=ALU.mult),
                  reads=[b("wq")], writes=[b("wq")])

            def xnorm_part1(s):
                slot = s % 2
                for j in range(4):
                    k = j
                    t0 = s * 512 + j * 128
                    st.dma(lambda h, k=k, t0=t0: h.dma_start(out=xt[:, k, :], in_=x[t0:t0 + 128, :]), writes=[xtb[k]])
                    st.op("act", lambda h, k=k, j=j, slot=slot: h.activation(out=xjunk[:, j % 2, :], in_=xt[:, k, :], func=AF.Square,
                                                                             accum_out=ssx[:, slot, j:j + 1]),
                          reads=[xtb[k]], writes=[b("ssx%d_%d" % (slot, j)), b("xjunk%d" % (j % 2))])
                return [0, 1, 2, 3]

            def xnorm_part2(s, ks):
                slot = s % 2
                st.op("act", lambda h: h.activation(out=lnx[:, slot, :], in_=ssx[:, slot, :], func=AF.Ln, scale=1.0 / D, bias=EPS),
                      reads=[b("ssx%d_%d" % (slot, j)) for j in range(4)], writes=[b("lnx%d" % slot)])
                st.op("act", lambda h: h.activation(out=rsx[:, slot, :], in_=lnx[:, slot, :], func=AF.Exp, scale=-0.5),
                      reads=[b("lnx%d" % slot)], writes=[b("rsx%d" % slot)])
                for j in range(4):
                    k = ks[j]
                    n = j % 2
                    st.op("dve", lambda h, k=k, j=j, n=n: h.tensor_scalar(out=xn[:, n, :], in0=xt[:, k, :], scalar1=rsx[:, slot, j:j + 1],
                                                                        scalar2=None, op0=ALU.mult),
                          reads=[xtb[k], b("rsx%d" % slot)], writes=[xnb[n]])
                    for c in range(8):
                        st.op("pe", lambda h, n=n, c=c: h.transpose(out=pxt[:, n, c, :], in_=xn[:, n, c * 128:(c + 1) * 128], identity=ident[:]),
                              reads=[xnb[n], b("ident")], writes=[pxb[n]])
                    st.op("act", lambda h, n=n, j=j: h.activation(out=hT[:, slot, :, j * 128:(j + 1) * 128], in_=pxt[:, n, :, :], func=AF.Copy),
                          reads=[pxb[n]], writes=[hTb[slot]])

            def mm_group(lhs_fn, rhs_fn, nk, reads, nfree=512):
                pt, pb = ring_p.get()
                pt = pt[:, 0:nfree]
                for c in range(nk):
                    st.op("pe", lambda h, c=c, pt=pt: h.matmul(pt, lhsT=lhs_fn(c), rhs=rhs_fn(c), start=(c == 0), stop=(c == nk - 1)),
                          reads=reads, writes=[pb])
                return pt, pb

            ks0 = xnorm_part1(0)
            xnorm_part2(0, ks0)
            def super_tile(s):
                slot = s % 2
                t0 = s * 512
                pos0 = t0 % L
                c0 = s * 4
                hs = hTb[slot]
                st.dma(lambda h, pos0=pos0: h.dma_start(out=cst[:, slot, :], in_=c_cos[:, pos0:pos0 + 512]), writes=[b("cst%d" % slot)])
                st.dma(lambda h, pos0=pos0: h.dma_start(out=snt[:, slot, :], in_=c_sin[:, pos0:pos0 + 512]), writes=[b("snt%d" % slot)])
                ks_next = xnorm_part1(s + 1) if s + 1 < NS else None
                for blk in range(4):
                    pt, pb = mm_group(lambda c, blk=blk: w1[:, c, blk * 128:(blk + 1) * 128], lambda c: hT[:, slot, c, :], 8, [hs, b("w1")])
                    st.op("act", lambda h, pt=pt, blk=blk: h.activation(out=qT[:, blk, :], in_=pt, func=AF.Silu),
                          reads=[pb], writes=[b("qT%d" % blk)])
                for db in range(8):
                    col = 512 + db * 128
                    pt, pb = mm_group(lambda c, col=col: w1[:, c, col:col + 128], lambda c: hT[:, slot, c, :], 8, [hs, b("w1")])
                    st.op("act", lambda h, pt=pt, db=db: h.activation(out=th[:, db, :], in_=pt, func=AF.Tanh, scale=0.5),
                          reads=[pb], writes=[b("th%d" % db)])
                for j in range(4):
                    tt = t0 + j * 128
                    pt, pb = mm_group(lambda c, j=j: hT[:, slot, c, j * 128:(j + 1) * 128], lambda c: w1[:, c, 1536:2048], 8, [hs, b("w1")])
                    ot, ob = ring_b.get()
                    st.op("dve", lambda h, pt=pt, ot=ot: h.tensor_copy(out=ot, in_=pt), reads=[pb], writes=[ob])
                    st.dma(lambda h, ot=ot, tt=tt: h.dma_start(out=v_s[tt:tt + 128, :], in_=ot), reads=[ob])
                    pt, pb = mm_group(lambda c, j=j: hT[:, slot, c, j * 128:(j + 1) * 128], lambda c: w1[:, c, 2048:2560], 8, [hs, b("w1")])
                    ot, ob = ring_g.get()
                    st.op("act", lambda h, pt=pt, ot=ot: h.activation(out=ot, in_=pt, func=AF.Silu), reads=[pb], writes=[ob])
                    st.dma(lambda h, ot=ot, tt=tt: h.dma_start(out=g_s[tt:tt + 128, :], in_=ot), reads=[ob])
                for j in range(4):
                    pt, pb = mm_group(lambda c, j=j: hT[:, slot, c, j * 128:(j + 1) * 128], lambda c: w1[:, c, 2560:2944], 8, [hs, b("w1")], nfree=384)
                    st.op("act", lambda h, pt=pt, j=j: h.activation(out=xjunk[:, 0, 0:384], in_=pt[:, 0:384], func=AF.Square, accum_out=ssq[:, j:j + 1]),
                          reads=[pb], writes=[b("ssq%d" % j), b("xjunk0")])
                    st.op("dve", lambda h, pt=pt, j=j: h.tensor_copy(out=cqf[:, j, 0:384], in_=pt[:, 0:384]), reads=[pb], writes=[b("cqf%d" % j)])
                    pt, pb = mm_group(lambda c, j=j: hT[:, slot, c, j * 128:(j + 1) * 128], lambda c: w1[:, c, 2944:3200], 8, [hs, b("w1")], nfree=256)
                    st.op("act", lambda h, pt=pt, j=j: h.activation(out=xjunk[:, 1, 0:256], in_=pt[:, 0:256], func=AF.Square, accum_out=ssq[:, 4 + j:5 + j]),
                          reads=[pb], writes=[b("ssq%d" % (4 + j)), b("xjunk1")])
                    st.op("dve", lambda h, pt=pt, j=j: h.tensor_copy(out=cqf[:, j, 384:640], in_=pt[:, 0:256]), reads=[pb], writes=[b("cqf%d" % j)])
                pt1, pb1 = mm_group(lambda c: w1[:, c, 3200:3328], lambda c: hT[:, slot, c, :], 8, [hs, b("w1")])
                pt2, pb2 = mm_group(lambda c: w1[:, c, 3328:3456], lambda c: hT[:, slot, c, :], 8, [hs, b("w1")])

                def rope(pt1, pb1, pt2, pb2, dst_fn):
                    f1, fb1 = ring_f.get()
                    f2, fb2 = ring_f.get()
                    ot, ob = ring_b.get()
                    st.op("dve", lambda h: h.tensor_tensor(out=f1, in0=pt1, in1=cst[:, slot, :], op=ALU.mult),
                          reads=[pb1, b("cst%d" % slot)], writes=[fb1])
                    st.op("dve", lambda h: h.tensor_tensor(out=f2, in0=pt2, in1=snt[:, slot, :], op=ALU.mult),
                          reads=[pb2, b("snt%d" % slot)], writes=[fb2])
                    st.op("pool", lambda h: h.tensor_tensor(out=ot, in0=f1, in1=f2, op=ALU.add), reads=[fb1, fb2], writes=[ob])
                    st.dma(lambda h: h.dma_start(out=dst_fn(), in_=ot), reads=[ob])
                rope(pt1, pb1, pt2, pb2, lambda: krT_s[:, t0:t0 + 512])

                for db in range(8):
                    d, blk = db // 4, db % 4
                    thb = b("th%d" % db)
                    ft, fbuf = ring_f.get()
                    k1, k1b = ring_f.get()
                    lf, lfb = ring_f.get()
                    st.op("pool", lambda h, ft=ft, db=db, d=d, blk=blk: h.tensor_scalar(
                        out=ft, in0=th[:, db, :], scalar1=fb_[:, d, blk:blk + 1], scalar2=fa[:, d, blk:blk + 1], op0=ALU.mult, op1=ALU.add),
                        reads=[thb, b("fb"), b("fa")], writes=[fbuf])
                    st.op("pool", lambda h, k1=k1, db=db, d=d, blk=blk: h.tensor_scalar(
                        out=k1, in0=th[:, db, :], scalar1=nfb[:, d, blk:blk + 1], scalar2=fb_[:, d, blk:blk + 1], op0=ALU.mult, op1=ALU.add),
                        reads=[thb, b("fb"), b("nfb")], writes=[k1b])
                    st.op("act", lambda h, lf=lf, ft=ft: h.activation(out=lf, in_=ft, func=AF.Ln), reads=[fbuf], writes=[lfb])
                    cumt, cumb = ring_f.get()
                    cct, ccb = ring_f.get()
                    st.op("dve", lambda h, lf=lf, cumt=cumt: h.tensor_tensor_scan(out=cumt, data0=rmask[:], data1=lf, initial=0.0,
                                                                                  op0=ALU.mult, op1=ALU.add),
                          reads=[lfb, b("rmask")], writes=[cumb])
                    cumv = cumt.rearrange("p (c t) -> p c t", t=128)
                    ccv = cct.rearrange("p (c t) -> p c t", t=128)
                    st.op("dve", lambda h, cumv=cumv, ccv=ccv: h.tensor_tensor(out=ccv, in0=cumv, in1=bc(cumv[:, :, 63:64], [128, 4, 128]), op=ALU.subtract),
                          reads=[cumb], writes=[ccb])
                    Xi, Yi = (1, 2) if d == 0 else (2, 1)
                    st.op("act", lambda h, d=d, blk=blk, cumv=cumv: h.activation(out=scal[:, d, 0, blk, c0:c0 + 4], in_=cumv[:, :, 127], func=AF.Exp),
                          reads=[cumb], writes=[b("scal")])
                    st.op("act", lambda h, d=d, blk=blk, ccv=ccv, Xi=Xi: h.activation(out=scal[:, d, Xi, blk, c0:c0 + 4], in_=ccv[:, :, 127], func=AF.Exp),
                          reads=[ccb], writes=[b("scal")])
                    st.op("act", lambda h, d=d, blk=blk, cumv=cumv, Yi=Yi: h.activation(out=scal[:, d, Yi, blk, c0:c0 + 4], in_=cumv[:, :, 63], func=AF.Exp),
                          reads=[cumb], writes=[b("scal")])
                    if d == 0:
                        src = cct
                        srcb = ccb
                    else:
                        t2, t2b = ring_f.get()
                        st.op("pool", lambda h, t2=t2, lf=lf, cct=cct: h.tensor_tensor(out=t2, in0=lf, in1=cct, op=ALU.subtract),
                              reads=[lfb, ccb], writes=[t2b])
                        src, srcb = t2, t2b
                    ea, eab = ring_f.get()
                    eb, ebb = ring_f.get()
                    st.op("act", lambda h, ea=ea, src=src: h.activation(out=ea, in_=src, func=AF.Exp), reads=[srcb], writes=[eab])
                    st.op("act", lambda h, eb=eb, src=src: h.activation(out=eb, in_=src, func=AF.Exp, scale=-1.0), reads=[srcb], writes=[ebb])
                    o1, o1b = ring_b.get()
                    o2, o2b = ring_b.get()
                    st.op("dve", lambda h, o1=o1, ea=ea, blk=blk: h.tensor_tensor(out=o1, in0=qT[:, blk, :], in1=ea, op=ALU.mult),
                          reads=[b("qT%d" % blk), eab], writes=[o1b])
                    st.op("dve", lambda h, o2=o2, eb=eb, k1=k1: h.tensor_tensor(out=o2, in0=k1, in1=eb, op=ALU.mult),
                          reads=[k1b, ebb], writes=[o2b])
                    st.dma(lambda h, o1=o1, d=d, blk=blk: h.dma_start(out=qdT_s[d, blk, :, t0:t0 + 512], in_=o1), reads=[o1b])
                    st.dma(lambda h, o2=o2, d=d, blk=blk: h.dma_start(out=kiT_s[d, blk, :, t0:t0 + 512], in_=o2), reads=[o2b])
                st.op("act", lambda h: h.activation(out=lnq[:, 0:4], in_=ssq[:, 0:4], func=AF.Ln, scale=1.0 / 384, bias=EPS),
                      reads=[b("ssq%d" % i) for i in range(8)], writes=[b("lnq")])
                st.op("act", lambda h: h.activation(out=lnq[:, 4:8], in_=ssq[:, 4:8], func=AF.Ln, scale=1.0 / 256, bias=EPS),
                      reads=[b("ssq%d" % i) for i in range(8)], writes=[b("lnq")])
                st.op("act", lambda h: h.activation(out=rsq[:], in_=lnq[:], func=AF.Exp, scale=-0.5), reads=[b("lnq")], writes=[b("rsq")])
                for j in range(4):
                    st.op("dve", lambda h, j=j: h.tensor_scalar(out=cqn[:, j, 0:384], in0=cqf[:, j, 0:384], scalar1=rsq[:, j:j + 1], scalar2=None, op0=ALU.mult),
                          reads=[b("cqf%d" % j), b("rsq")], writes=[b("cqn%d" % j)])
                    st.op("dve", lambda h, j=j: h.tensor_scalar(out=cqn[:, j, 384:640], in0=cqf[:, j, 384:640], scalar1=rsq[:, 4 + j:5 + j], scalar2=None, op0=ALU.mult),
                          reads=[b("cqf%d" % j), b("rsq")], writes=[b("cqn%d" % j)])
                if ks_next is not None:
                    xnorm_part2(s + 1, ks_next)
                for j in range(4):
                    n = j % 2
                    for c in range(5):
                        st.op("pe", lambda h, n=n, c=c, j=j: h.transpose(out=pxt[:, n, c, :], in_=cqn[:, j, c * 128:(c + 1) * 128], identity=ident[:]),
                              reads=[b("cqn%d" % j), b("ident")], writes=[pxb[n]])
                    st.op("act", lambda h, n=n, j=j: h.activation(out=cqT[:, :, j * 128:(j + 1) * 128], in_=pxt[:, n, 0:3, :], func=AF.Copy),
                          reads=[pxb[n]], writes=[b("cqT")])
                    st.op("act", lambda h, n=n, j=j: h.activation(out=ckvT[:, :, j * 128:(j + 1) * 128], in_=pxt[:, n, 3:5, :], func=AF.Copy),
                          reads=[pxb[n]], writes=[b("ckvT")])
                for hh in range(4):
                    pt, pb = mm_group(lambda c, hh=hh: wq[:, c, hh * 128:(hh + 1) * 128], lambda c: cqT[:, c, :], 3, [b("cqT"), b("wq")])
                    ot, ob = ring_b.get()
                    st.op("act", lambda h, pt=pt, ot=ot: h.activation(out=ot, in_=pt, func=AF.Copy), reads=[pb], writes=[ob])
                    st.dma(lambda h, ot=ot, hh=hh: h.dma_start(out=qnT_s[hh, :, t0:t0 + 512], in_=ot), reads=[ob])
                for pr in range(2):
                    pt1, pb1 = mm_group(lambda c, pr=pr: wq[:, c, 512 + pr * 128:640 + pr * 128], lambda c: cqT[:, c, :], 3, [b("cqT"), b("wq")])
                    pt2, pb2 = mm_group(lambda c, pr=pr: wq[:, c, 768 + pr * 128:896 + pr * 128], lambda c: cqT[:, c, :], 3, [b("cqT"), b("wq")])
                    rope(pt1, pb1, pt2, pb2, lambda pr=pr: qrT_s[pr, :, t0:t0 + 512])
                for hh in range(4):
                    pt, pb = mm_group(lambda c, hh=hh: wkv[:, c, hh * 128:(hh + 1) * 128], lambda c: ckvT[:, c, :], 2, [b("ckvT"), b("wkv")])
                    ot, ob = ring_b.get()
                    st.op("act", lambda h, pt=pt, ot=ot: h.activation(out=ot, in_=pt, func=AF.Copy), reads=[pb], writes=[ob])
                    st.dma(lambda h, ot=ot, hh=hh: h.dma_start(out=knT_s[hh, :, t0:t0 + 512], in_=ot), reads=[ob])
                for j in range(4):
                    tt = t0 + j * 128
                    pt, pb = mm_group(lambda c, j=j: ckvT[:, c, j * 128:(j + 1) * 128], lambda c: wkv[:, c, 512:1024], 2, [b("ckvT"), b("wkv")])
                    ot, ob = ring_b.get()
                    st.op("dve", lambda h, pt=pt, ot=ot: h.tensor_copy(out=ot, in_=pt), reads=[pb], writes=[ob])
                    st.dma(lambda h, ot=ot, tt=tt: h.dma_start(out=vm_s[tt:tt + 128, :], in_=ot), reads=[ob])
            for s in range(NS):
                super_tile(s)
            st.dma(lambda h: h.dma_start(out=scal_s, in_=scal[:]), reads=[b("scal")])
            st.finish()
            st.emit()

    def stage2():
        NK = L // 128
        NQ = L // 512
        DEN = os.environ.get("S2_DEN", "pe")
        ROPE128 = os.environ.get("S2_ROPE", "k128") == "k128"
        NPS = int(os.environ.get("S2_NPS", "4"))
        NP = int(os.environ.get("S2_NP", "6"))
        with contextlib.ExitStack() as es:
            def sb(name, shape, dt=F32):
                return es.enter_context(nc.sbuf_tensor("s2_" + name, list(shape), dt))

            def ps(name, shape, dt=F32):
                return es.enter_context(nc.psum_tensor("s2_" + name, list(shape), dt))

            st = Stage(nc, "s2")
            knT = sb("knT", [128, 4, L], BF16)
            vm = sb("vm", [128, NK, 512], BF16)
            krT = sb("krT", [128, L], BF16)
            qn = sb("qn", [128, 2, 4, 512], BF16)
            qr = sb("qr", [128, 2, 2, 512], BF16)
            ones = sb("ones", [128, 128], BF16)
            Pt = sb("Pt", [128, NP, 512], BF16)
            qrz = sb("qrz", [128, 2, 4, 512], BF16)
            hm = sb("hm", [128, 2])
            accP = sb("accP", [128, 2, 512])
            oT = sb("oT", [128, 2, 4, 512])
            rden = sb("rden", [128, 2, 512])
            sq = sb("sq", [128, 4, 512], BF16)
            lnv = sb("lnv", [128, 512]); rstd = sb("rstd", [128, 512])
            yb = sb("yb", [128, 4, 512], BF16)
            pS = ps("pS", [128, NPS, 512])
            pO = ps("pO", [128, 2, 512])
            pD = ps("pD", [128, 1, 512])
            pSS = ps("pSS", [128, 512])
            acc = sb("acc", [128, 2, 512])
            onesf = sb("onesf", [128, 128])
            B = {}

            def b(name):
                if name not in B:
                    B[name] = Buf(name)
                return B[name]
            ring_P = Ring(Pt, NP, "Pt")
            ring_S = Ring(pS, NPS, "pS")
            ring_y = Ring(yb, 4, "yb")
            st.dma(lambda h: h.dma_start(out=ones[:], in_=c_ones), writes=[b("ones")])
            st.op("pool", lambda h: h.memset(onesf[:], 1.0), writes=[b("onesf")])
            st.op("pool", lambda h: h.memset(hm[:], 0.0), writes=[b("hm")])
            st.op("pool", lambda h: h.memset(hm[0:64, 0:1], 1.0), writes=[b("hm")])
            st.op("pool", lambda h: h.memset(hm[64:128, 1:2], 1.0), writes=[b("hm")])

            items = []
            for seq in range(n_seq):
                for qb in range(NQ):
                    for hh in range(4):
                        for kc in range(NK):
                            items.append((seq, qb, hh, kc))

            def loads_seq(seq):
                s0 = seq * L
                for hh in range(4):
                    st.dma(lambda h, hh=hh: h.dma_start(out=knT[:, hh, :], in_=knT_s[hh, :, s0:s0 + L]), writes=[b("knT")])
                st.dma(lambda h: h.dma_start(out=krT[:], in_=krT_s[:, s0:s0 + L]), writes=[b("krT")])
                st.dma(lambda h: h.dma_start(out=vm[:], in_=vm_s[s0:s0 + L, :].rearrange("(k p) f -> p k f", p=128)), writes=[b("vm")])

            def loads_q(seq, qb):
                slot = (seq * NQ + qb) % 2
                tq = seq * L + qb * 512
                st.dma(lambda h: h.dma_start(out=qn[:, slot], in_=qnT_s[:, :, tq:tq + 512].rearrange("h p t -> p h t")), writes=[b("qn%d" % slot)])
                st.dma(lambda h: h.dma_start(out=qr[:, slot], in_=qrT_s[:, :, tq:tq + 512].rearrange("h p t -> p h t")), writes=[b("qr%d" % slot)])
                if ROPE128:
                    for hh in range(4):
                        st.op("pool", lambda h, hh=hh: h.tensor_scalar(out=qrz[:, slot, hh, :], in0=qr[:, slot, hh // 2, :], scalar1=hm[:, hh % 2:hh % 2 + 1],
                                                                        scalar2=None, op0=ALU.mult),
                              reads=[b("qr%d" % slot), b("hm")], writes=[b("qrz%d" % slot)])

            def qk(item):
                seq, qb, hh, kc = item
                slot = (seq * NQ + qb) % 2
                hp, pr = hh % 2, hh // 2
                pt, pb = ring_S.get()
                st.op("pe", lambda h: h.matmul(pt, lhsT=knT[:, hh, kc * 128:(kc + 1) * 128], rhs=qn[:, slot, hh, :], start=True, stop=False),
                      reads=[b("knT"), b("qn%d" % slot)], writes=[pb])
                if ROPE128:
                    st.op("pe", lambda h: h.matmul(pt, lhsT=krT[:, kc * 128:(kc + 1) * 128], rhs=qrz[:, slot, hh, :], start=False, stop=True),
                          reads=[b("krT"), b("qrz%d" % slot)], writes=[pb])
                else:
                    st.op("pe", lambda h: h.matmul(pt, lhsT=krT[hp * 64:(hp + 1) * 64, kc * 128:(kc + 1) * 128],
                                                   rhs=qr[hp * 64:(hp + 1) * 64, slot, pr, :], start=False, stop=True),
                          reads=[b("krT"), b("qr%d" % slot)], writes=[pb])
                return pt, pb

            def finish_head(seq, qb, hh, oslot):
                qslot = (seq * NQ + qb) % 2
                if DEN == "dve":
                    st.op("pe", lambda h: h.matmul(pD[:, 0, :], lhsT=onesf[:], rhs=acc[:, oslot, :], start=True, stop=True),
                          reads=[b("onesf"), b("acc%d" % oslot)], writes=[b("pD0")])
                elif DEN == "split":
                    st.op("pe", lambda h: h.matmul(pD[:, 0, :], lhsT=onesf[:], rhs=acc[:, oslot, :], start=True, stop=False),
                          reads=[b("onesf"), b("acc%d" % oslot)], writes=[b("pD0")])
                    st.op("pe", lambda h: h.matmul(pD[:, 0, :], lhsT=onesf[:], rhs=accP[:, oslot, :], start=False, stop=True),
                          reads=[b("onesf"), b("accP%d" % oslot)], writes=[b("pD0")])
                st.op("dve", lambda h: h.reciprocal(out=rden[:, oslot, :], in_=pD[:, 0, :]), reads=[b("pD0")], writes=[b("rden%d" % oslot)])
                st.op("dve", lambda h: h.tensor_tensor(out=oT[:, qslot, hh, :], in0=pO[:, oslot, :], in1=rden[:, oslot, :], op=ALU.mult),
                      reads=[b("pO%d" % oslot), b("rden%d" % oslot)], writes=[b("oT%d_%d" % (qslot, hh))])

            def finish_q(seq, qb):
                qslot = (seq * NQ + qb) % 2
                tq = seq * L + qb * 512
                for hh in range(4):
                    st.op("act", lambda h, hh=hh: h.activation(out=sq[:, hh, :], in_=oT[:, qslot, hh, :], func=AF.Square),
                          reads=[b("oT%d_%d" % (qslot, hh))], writes=[b("sq%d" % hh)])
                for hh in range(4):
                    st.op("pe", lambda h, hh=hh: h.matmul(pSS[:], lhsT=ones[:], rhs=sq[:, hh, :], start=(hh == 0), stop=(hh == 3)),
                          reads=[b("ones"), b("sq%d" % hh)], writes=[b("pSS")])
                st.op("act", lambda h: h.activation(out=lnv[:], in_=pSS[:], func=AF.Ln, scale=1.0 / 512, bias=EPS), reads=[b("pSS")], writes=[b("lnv")])
                st.op("act", lambda h: h.activation(out=rstd[:], in_=lnv[:], func=AF.Exp, scale=-0.5), reads=[b("lnv")], writes=[b("rstd")])
                for hh in range(4):
                    yt, ybuf = ring_y.get()
                    st.op("dve", lambda h, hh=hh, yt=yt: h.tensor_tensor(out=yt, in0=oT[:, qslot, hh, :], in1=rstd[:], op=ALU.mult),
                          reads=[b("oT%d_%d" % (qslot, hh)), b("rstd")], writes=[ybuf])
                    st.dma(lambda h, hh=hh, yt=yt: h.dma_start(out=yT_s[4 + hh, :, tq:tq + 512], in_=yt), reads=[ybuf])

            import collections
            LA = int(os.environ.get("S2_LA", "2"))
            pending = collections.deque()
            nq = [0]

            def ensure(upto, seq):
                while nq[0] < min(upto, len(items)) and items[nq[0]][0] == seq:
                    pending.append(qk(items[nq[0]]))
                    nq[0] += 1

            ocnt = 0
            for idx, item in enumerate(items):
                seq, qb, hh, kc = item
                if qb == 0 and hh == 0 and kc == 0:
                    loads_seq(seq)
                    loads_q(seq, 0)
                if hh == 0 and kc == 0 and qb + 1 < NQ:
                    loads_q(seq, qb + 1)
                ensure(idx + 1 + LA, seq)
                pt, pb = pending.popleft()
                oslot = ocnt % 2
                Pa, Pb = ring_P.get()
                st.op("act", lambda h, Pa=Pa, pt=pt: h.activation(out=Pa, in_=pt, func=AF.Exp, scale=SCALE), reads=[pb], writes=[Pb])
                st.op("pe", lambda h, Pa=Pa, oslot=oslot, hh=hh, kc=kc: h.matmul(pO[:, oslot, :], lhsT=vm[:, kc, hh * 128:(hh + 1) * 128], rhs=Pa,
                                                                           start=(kc == 0), stop=(kc == NK - 1)),
                      reads=[b("vm"), Pb], writes=[b("pO%d" % oslot)])
                if DEN == "pe":
                    st.op("pe", lambda h, Pa=Pa, kc=kc: h.matmul(pD[:, 0, :], lhsT=ones[:], rhs=Pa, start=(kc == 0), stop=(kc == NK - 1)),
                          reads=[b("ones"), Pb], writes=[b("pD0")])
                else:
                    on_pool = (DEN == "split" and kc % 4 == 3)
                    eng, at, an, first = ("pool", accP, "accP", kc == 3) if on_pool else ("dve", acc, "acc", kc == 0)
                    if first:
                        st.op(eng, lambda h, Pa=Pa, oslot=oslot, at=at: h.tensor_copy(out=at[:, oslot, :], in_=Pa), reads=[Pb], writes=[b("%s%d" % (an, oslot))])
                    else:
                        st.op(eng, lambda h, Pa=Pa, oslot=oslot, at=at: h.tensor_tensor(out=at[:, oslot, :], in0=at[:, oslot, :], in1=Pa, op=ALU.add),
                              reads=[Pb, b("%s%d" % (an, oslot))], writes=[b("%s%d" % (an, oslot))])
                if kc == NK - 1:
                    finish_head(seq, qb, hh, oslot)
                    ocnt += 1
                    if hh == 3:
                        finish_q(seq, qb)
            st.finish()
            st.emit()

    def stage3():
        NCs = L // 128
        with contextlib.ExitStack() as es:
            def sb(name, shape, dt=F32):
                return es.enter_context(nc.sbuf_tensor("s3_" + name, list(shape), dt))

            def ps(name, shape, dt=F32):
                return es.enter_context(nc.psum_tensor("s3_" + name, list(shape), dt))

            st = Stage(nc, "s3")
            scal = sb("scal", [128, 2, 3, 4, NCH])
            masks = sb("masks", [128, 2, 128])
            ident = sb("ident", [128, 128], BF16)
            qd = sb("qd", [128, 2 * n_seq, 4, 512], BF16)
            ki = sb("ki", [128, 2 * n_seq, 4, 512], BF16)
            vt = sb("vt", [128, 2 * n_seq, 4, 512], BF16)
            kitok = sb("kitok", [128, 2, 512], BF16)
            scT = sb("scT", [128, 2, 8, 128], BF16)
            S_all = sb("S", [128, n_seq, 4, 128]); Sp_all = sb("Sp", [128, n_seq, 4, 128], BF16)
            t1_all = sb("t1", [128, n_seq, 4, 128]); t2_all = sb("t2", [128, n_seq, 4, 128])
            ofw = sb("ofw", [128, 4, 512])
            gtl = sb("gtl", [128, 4, 512])
            osum_a = sb("osum", [128, 2, 512]); sqt_a = sb("sqt", [128, 2, 512]); yn_a = sb("yn", [128, 2, 512])
            ss8_a = sb("ss8", [128, 2, 8]); ln8_a = sb("ln8", [128, 2, 8]); rs8_a = sb("rs8", [128, 2, 8])
            ya = sb("ya", [128, 2, 512], BF16)
            yaT = sb("yaT", [128, 2, 4, 128], BF16)
            pT = ps("pT", [128, 2, 1024], BF16)
            pSc = ps("pSc", [128, 2, 4, 128])
            pOo = ps("pOo", [128, 2, 512])
            pU = ps("pU", [128, 1, 4, 128])
            pY = ps("pY", [128, 8, 128], BF16)
            B = {}

            def b(name):
                if name not in B:
                    B[name] = Buf(name)
                return B[name]
            ring_of = Ring(ofw, 4, "ofw")
            ring_g = Ring(gtl, 4, "gtl")
            st.dma(lambda h: h.dma_start(out=scal[:], in_=scal_s), writes=[b("scal")])
            st.dma(lambda h: h.dma_start(out=masks[:, 0, :], in_=c_maskf), writes=[b("masks")])
            st.dma(lambda h: h.dma_start(out=masks[:, 1, :], in_=c_maskb), writes=[b("masks")])
            st.dma(lambda h: h.dma_start(out=ident[:], in_=c_ident), writes=[b("ident")])

            def run_dir(seq, d):
                order = list(range(NCs)) if d == 0 else list(range(NCs - 1, -1, -1))
                S = S_all[:, seq]; Sp = Sp_all[:, seq]
                st.op("pool", lambda h: h.memset(S, 0.0), writes=[b("S%d" % seq)])
                st.op("pool", lambda h: h.memset(Sp, 0.0), writes=[b("Sp%d" % seq)])
                cur_group = [None, None]
                gcount = [0]

                def load_group(gi):
                    slot = seq * 2 + gcount[0] % 2
                    gcount[0] += 1
                    tg = seq * L + gi * 512
                    st.dma(lambda h: h.dma_start(out=qd[:, slot], in_=qdT_s[d, :, :, tg:tg + 512].rearrange("k p t -> p k t")), writes=[b("qd%d" % slot)])
                    st.dma(lambda h: h.dma_start(out=ki[:, slot], in_=kiT_s[d, :, :, tg:tg + 512].rearrange("k p t -> p k t")), writes=[b("ki%d" % slot)])
                    st.dma(lambda h: h.dma_start(out=vt[:, slot], in_=v_s[tg:tg + 512, :].rearrange("(j p) f -> p j f", p=128)), writes=[b("vt%d" % slot)])
                    return slot

                for pos, n in enumerate(order):
                    gi = n // 4
                    if cur_group[0] != gi:
                        cur_group[0] = gi
                        cur_group[1] = load_group(gi)
                    slot = cur_group[1]
                    j = n % 4
                    cg = seq * NCs + n
                    t0 = cg * 128
                    cs = seq % 2
                    chunk(seq, d, n, pos, order, slot, j, cg, t0, cs)
                    yield

            def chunk(seq, d, n, pos, order, slot, j, cg, t0, cs):
                S = S_all[:, seq]; Sp = Sp_all[:, seq]; t1 = t1_all[:, seq]; t2 = t2_all[:, seq]
                osum = osum_a[:, cs]; sqt = sqt_a[:, cs]; yn = yn_a[:, cs]
                ss8 = ss8_a[:, cs]; ln8 = ln8_a[:, cs]; rs8 = rs8_a[:, cs]
                bS, bSp, bt1, bt2 = b("S%d" % seq), b("Sp%d" % seq), b("t1%d" % seq), b("t2%d" % seq)
                bos, bsq, byn, bss, bln, brs = (b("%s%d" % (nm, cs)) for nm in ("osum", "sqt", "yn", "ss8", "ln8", "rs8"))
                qd_c = qd[:, slot, :, j * 128:(j + 1) * 128]
                ki_c = ki[:, slot, :, j * 128:(j + 1) * 128]
                v_c = vt[:, slot, j, :]
                rq, rk, rv = b("qd%d" % slot), b("ki%d" % slot), b("vt%d" % slot)
                if d == 1:
                    oft, ofb = ring_of.get()
                    gtt, gtb = ring_g.get()
                    st.dma(lambda h: h.dma_start(out=oft, in_=of_s[t0:t0 + 128, :]), reads=[b("ofs%d" % cg)], writes=[ofb])
                    st.dma(lambda h: h.dma_start(out=gtt, in_=g_s[t0:t0 + 128, :]), writes=[gtb])
                for bk in range(4):
                    st.op("pe", lambda h, bk=bk: h.transpose(out=pT[:, cs, bk * 128:(bk + 1) * 128], in_=ki_c[:, bk, :], identity=ident[:]),
                          reads=[rk, b("ident")], writes=[b("pT%d" % cs)])
                st.op("act", lambda h: h.activation(out=kitok[:, cs, :], in_=pT[:, cs, 0:512], func=AF.Copy), reads=[b("pT%d" % cs)], writes=[b("kitok%d" % cs)])
                for hh in range(8):
                    bk, hp = hh // 2, hh % 2
                    st.op("pe", lambda h, hh=hh, bk=bk, hp=hp: h.matmul(pSc[:, hp, bk, :], lhsT=ki_c[hp * 64:(hp + 1) * 64, bk, :],
                                                                        rhs=qd_c[hp * 64:(hp + 1) * 64, bk, :], start=True, stop=True),
                          reads=[rk, rq], writes=[b("pSc%d" % hp)])
                scv = scT[:, cs].rearrange("p (k two) t -> p k two t", two=2)
                for half in range(2):
                    st.op("dve", lambda h, half=half: h.tensor_tensor(out=scv[:, :, half, :], in0=pSc[:, half],
                                                                      in1=bc(masks[:, d:d + 1, :], [128, 4, 128]), op=ALU.mult),
                          reads=[b("pSc%d" % half), b("masks")], writes=[b("scT%d_%d" % (cs, half))])
                for hh in range(8):
                    bk, hp = hh // 2, hh % 2
                    st.op("pe", lambda h, hh=hh: h.matmul(pOo[:, cs, hh * 64:(hh + 1) * 64], lhsT=scT[:, cs, hh, :], rhs=v_c[:, hh * 64:(hh + 1) * 64],
                                                          start=True, stop=False),
                          reads=[b("scT%d_%d" % (cs, hh % 2)), rv], writes=[b("pOo%d" % cs)])
                    st.op("pe", lambda h, hh=hh, bk=bk, hp=hp: h.matmul(pOo[:, cs, hh * 64:(hh + 1) * 64], lhsT=qd_c[hp * 64:(hp + 1) * 64, bk, :],
                                                                        rhs=Sp[hp * 64:(hp + 1) * 64, bk, hp * 64:(hp + 1) * 64], start=False, stop=True),
                          reads=[rq, bSp], writes=[b("pOo%d" % cs)])
                for bk in range(4):
                    st.op("pe", lambda h, bk=bk: h.matmul(pU[:, 0, bk, :], lhsT=kitok[:, cs, bk * 128:(bk + 1) * 128], rhs=v_c[:, bk * 128:(bk + 1) * 128],
                                                          start=True, stop=True),
                          reads=[b("kitok%d" % cs), rv], writes=[b("pU0")])
                if pos + 1 < len(order):
                    cgn = seq * NCs + order[pos + 1]
                    Abc = bc(scal[:, d, 0, :, cg:cg + 1], [128, 4, 128])
                    Bbc = bc(scal[:, d, 1, :, cg:cg + 1], [128, 4, 128])
                    Cbc = bc(scal[:, d, 2, :, cgn:cgn + 1], [128, 4, 128])
                    st.op("pool", lambda h: h.tensor_tensor(out=t1, in0=S, in1=Abc, op=ALU.mult), reads=[bS, b("scal")], writes=[bt1])
                    st.op("dve", lambda h: h.tensor_tensor(out=t2, in0=pU[:, 0], in1=Bbc, op=ALU.mult), reads=[b("pU0"), b("scal")], writes=[bt2])
                    st.op("dve", lambda h: h.tensor_tensor(out=S, in0=t1, in1=t2, op=ALU.add), reads=[bt1, bt2], writes=[bS])
                    st.op("dve", lambda h: h.tensor_tensor(out=Sp, in0=S, in1=Cbc, op=ALU.mult), reads=[bS, b("scal")], writes=[bSp])
                if d == 0:
                    oft, ofb = ring_of.get()
                    st.op("act", lambda h: h.activation(out=oft, in_=pOo[:, cs, :], func=AF.Copy), reads=[b("pOo%d" % cs)], writes=[ofb])
                    st.dma(lambda h: h.dma_start(out=of_s[t0:t0 + 128, :], in_=oft), reads=[ofb], writes=[b("ofs%d" % cg)])
                else:
                    st.op("dve", lambda h: h.tensor_tensor(out=osum, in0=pOo[:, cs, :], in1=oft, op=ALU.add), reads=[b("pOo%d" % cs), ofb], writes=[bos])
                    st.op("act", lambda h: h.activation(out=sqt, in_=osum, func=AF.Square), reads=[bos], writes=[bsq])
                    st.op("dve", lambda h: h.tensor_reduce(out=ss8, in_=sqt.rearrange("p (h e) -> p h e", e=64), axis=AX.X, op=ALU.add),
                          reads=[bsq], writes=[bss])
                    st.op("act", lambda h: h.activation(out=ln8, in_=ss8, func=AF.Ln, scale=1.0 / 64, bias=EPS), reads=[bss], writes=[bln])
                    st.op("act", lambda h: h.activation(out=rs8, in_=ln8, func=AF.Exp, scale=-0.5), reads=[bln], writes=[brs])
                    st.op("dve", lambda h: h.tensor_tensor(out=yn.rearrange("p (h e) -> p h e", e=64), in0=osum.rearrange("p (h e) -> p h e", e=64),
                                                           in1=bc(rs8.unsqueeze(2), [128, 8, 64]), op=ALU.mult),
                          reads=[bos, brs], writes=[byn])
                    st.op("pool", lambda h: h.tensor_tensor(out=ya[:, cs, :], in0=yn, in1=gtt, op=ALU.mult), reads=[byn, gtb], writes=[b("ya%d" % cs)])
                    for bk in range(4):
                        st.op("pe", lambda h, bk=bk: h.transpose(out=pY[:, bk, :], in_=ya[:, cs, bk * 128:(bk + 1) * 128], identity=ident[:]),
                              reads=[b("ya%d" % cs), b("ident")], writes=[b("pY")])
                    st.op("act", lambda h: h.activation(out=yaT[:, cs], in_=pY[:, 0:4, :], func=AF.Copy), reads=[b("pY")], writes=[b("yaT%d" % cs)])
                    st.dma(lambda h: h.dma_start(out=yT_s[0:4, :, t0:t0 + 128].rearrange("k p t -> p k t"), in_=yaT[:, cs]), reads=[b("yaT%d" % cs)])

            for d in range(2):
                gens = [run_dir(seq, d) for seq in range(n_seq)]
                live = list(gens)
                while live:
                    for g in list(live):
                        try:
                            next(g)
                        except StopIteration:
                            live.remove(g)
            st.finish()
            st.emit()

    def stage4():
        TT = 256
        NTL = T // TT
        with contextlib.ExitStack() as es:
            def sb(name, shape, dt=F32):
                return es.enter_context(nc.sbuf_tensor("s4_" + name, list(shape), dt))

            def ps(name, shape, dt=F32):
                return es.enter_context(nc.psum_tensor("s4_" + name, list(shape), dt))

            st = Stage(nc, "s4")
            wo = sb("wo", [128, 8, D], BF16)
            wg = sb("wg", [128, 8, DFF], BF16)
            wu = sb("wu", [128, 8, DFF], BF16)
            wd = sb("wd", [128, NFB, D], BF16)
            goutt = sb("goutt", [128, 8]); g2t = sb("g2t", [128, 8])
            gfb = sb("gfb", [128, D])
            ident = sb("ident", [128, 128], BF16)
            xt = sb("xt", [128, 2, 2, D])
            yT = sb("yT", [128, 2, 8, TT], BF16)
            h2n = sb("h2n", [128, 2, D], BF16)
            h2T = sb("h2T", [128, 8, TT], BF16)
            aT = sb("aT", [128, NFB, TT], BF16)
            sg = sb("sg", [128, 2, TT])
            junk = sb("junk", [128, D], BF16)
            ss = sb("ss", [128, 4]); lnt = sb("lnt", [128, 4]); rs = sb("rs", [128, 4])
            pxt = ps("pxt", [128, 8, 128], BF16)
            pG = ps("pG", [128, 2, 512])
            pUu = ps("pUu", [128, 2, 512])
            pA = ps("pA", [128, 2, 512])
            B = {}

            def b(name):
                if name not in B:
                    B[name] = Buf(name)
                return B[name]
            ring_A = Ring(pA, 2, "pA")
            for (dst, src, nm) in ((goutt, gout, "goutt"), (g2t, g2, "g2t"), (ident, c_ident, "ident")):
                st.dma(lambda h, dst=dst, src=src: h.dma_start(out=dst[:], in_=src), writes=[b(nm)])
            if os.environ.get("S4_NOBC"):
                for pp in range(0, 128, 32):
                    pass
                st.op("pool", lambda h: h.memset(gfb[:], 1.0), writes=[b("gfb")])
            else:
                st.dma(lambda h: h.dma_start(out=gfb[:], in_=gfin.partition_broadcast(128)), writes=[b("gfb")])
            stg = xt[:].rearrange("p a b d -> p (a b) d")
            pcnt = [0]

            def prep(dst3, src3, nchunk, ncols, gain, name, gname):
                for c in range(nchunk):
                    for c0 in range(0, ncols, 1024):
                        c1 = min(ncols, c0 + 1024)
                        slot = pcnt[0] % 4
                        eng = (("dve", "pool") if os.environ.get("S4_NOACT") else ("dve", "pool", "act"))[pcnt[0] % (2 if os.environ.get("S4_NOACT") else 3)]
                        pcnt[0] += 1
                        sbuf = b("xt%d" % slot)
                        st.dma(lambda h, c=c, slot=slot, c0=c0, c1=c1: h.dma_start(out=stg[:, slot, 0:c1 - c0], in_=src3[:, c, c0:c1]), writes=[sbuf])
                        rd = [sbuf] + ([b(gname)] if gain is not None else [])
                        if eng == "act":
                            if gain is None:
                                st.op("act", lambda h, c=c, slot=slot, c0=c0, c1=c1: h.activation(out=dst3[:, c, c0:c1], in_=stg[:, slot, 0:c1 - c0], func=AF.Copy),
                                      reads=rd, writes=[b(name)])
                            else:
                                st.op("act", lambda h, c=c, slot=slot, c0=c0, c1=c1: h.activation(out=dst3[:, c, c0:c1], in_=stg[:, slot, 0:c1 - c0], func=AF.Copy,
                                                                                               scale=gain[:, c:c + 1]),
                                      reads=rd, writes=[b(name)])
                        else:
                            if gain is None:
                                st.op(eng, lambda h, c=c, slot=slot, c0=c0, c1=c1: h.tensor_copy(out=dst3[:, c, c0:c1], in_=stg[:, slot, 0:c1 - c0]),
                                      reads=rd, writes=[b(name)])
                            else:
                                st.op(eng, lambda h, c=c, slot=slot, c0=c0, c1=c1: h.tensor_scalar(out=dst3[:, c, c0:c1], in0=stg[:, slot, 0:c1 - c0],
                                                                                                scalar1=gain[:, c:c + 1], scalar2=None, op0=ALU.mult),
                                      reads=rd, writes=[b(name)])
            prep(wo, w_out, 8, D, goutt, "wo", "goutt")
            prep(wg, w_gate, 8, DFF, g2t, "wg", "g2t")
            prep(wu, w_up, 8, DFF, g2t, "wu", "g2t")
            prep(wd, w_down, NFB, D, None, "wd", None)

            def loads(i):
                slot = i % 2
                t0 = i * TT
                st.dma(lambda h: h.dma_start(out=xt[:, slot], in_=x[t0:t0 + TT, :].rearrange("(s p) d -> p s d", p=128)),
                       writes=[b("xt%d" % (slot * 2)), b("xt%d" % (slot * 2 + 1))])
                st.dma(lambda h: h.dma_start(out=yT[:, slot], in_=yT_s[:, :, t0:t0 + TT].rearrange("c p t -> p c t")), writes=[b("yT%d" % slot)])

            def rms_rstd(i, sub, which):
                slot = i % 2
                col = which * 2 + sub
                xb = b("xt%d" % (slot * 2 + sub))
                st.op("act", lambda h: h.activation(out=junk[:], in_=xt[:, slot, sub, :], func=AF.Square, accum_out=ss[:, col:col + 1]),
                      reads=[xb], writes=[b("junk"), b("ss%d" % col)])
                st.op("act", lambda h: h.activation(out=lnt[:, col:col + 1], in_=ss[:, col:col + 1], func=AF.Ln, scale=1.0 / D, bias=EPS),
                      reads=[b("ss%d" % col)], writes=[b("ln%d" % col)])
                st.op("act", lambda h: h.activation(out=rs[:, col:col + 1], in_=lnt[:, col:col + 1], func=AF.Exp, scale=-0.5),
                      reads=[b("ln%d" % col)], writes=[b("rs%d" % col)])
                return col

            def tile(i):
                slot = i % 2
                t0 = i * TT
                if i + 1 < NTL:
                    loads(i + 1)
                for sub in range(2):
                    xb = b("xt%d" % (slot * 2 + sub))
                    for half in range(2):
                        pt, pb = ring_A.get()
                        for c in range(8):
                            st.op("pe", lambda h, c=c, pt=pt, sub=sub, half=half: h.matmul(pt, lhsT=yT[:, slot, c, sub * 128:(sub + 1) * 128],
                                                                                          rhs=wo[:, c, half * 512:(half + 1) * 512], start=(c == 0), stop=(c == 7)),
                                  reads=[b("yT%d" % slot), b("wo")], writes=[pb])
                        st.op("dve", lambda h, pt=pt, sub=sub, half=half: h.tensor_tensor(out=xt[:, slot, sub, half * 512:(half + 1) * 512],
                                                                                         in0=xt[:, slot, sub, half * 512:(half + 1) * 512], in1=pt, op=ALU.add),
                              reads=[pb, xb], writes=[xb])
                for sub in range(2):
                    xb = b("xt%d" % (slot * 2 + sub))
                    col = rms_rstd(i, sub, 0)
                    st.op("dve", lambda h, sub=sub, col=col: h.tensor_scalar(out=h2n[:, sub, :], in0=xt[:, slot, sub, :], scalar1=rs[:, col:col + 1],
                                                                            scalar2=None, op0=ALU.mult),
                          reads=[xb, b("rs%d" % col)], writes=[b("h2n%d" % sub)])
                    for c in range(8):
                        st.op("pe", lambda h, sub=sub, c=c: h.transpose(out=pxt[:, c, :], in_=h2n[:, sub, c * 128:(c + 1) * 128], identity=ident[:]),
                              reads=[b("h2n%d" % sub), b("ident")], writes=[b("pxt")])
                    st.op("dve", lambda h, sub=sub: h.tensor_copy(out=h2T[:, :, sub * 128:(sub + 1) * 128], in_=pxt[:]), reads=[b("pxt")], writes=[b("h2T")])
                for fb in range(NFB):
                    gs = fb % 2
                    for c in range(8):
                        st.op("pe", lambda h, c=c, fb=fb, gs=gs: h.matmul(pG[:, gs, 0:TT], lhsT=wg[:, c, fb * 128:(fb + 1) * 128], rhs=h2T[:, c, :],
                                                                          start=(c == 0), stop=(c == 7)),
                              reads=[b("wg"), b("h2T")], writes=[b("pG%d" % gs)])
                    for c in range(8):
                        st.op("pe", lambda h, c=c, fb=fb, gs=gs: h.matmul(pUu[:, gs, 0:TT], lhsT=wu[:, c, fb * 128:(fb + 1) * 128], rhs=h2T[:, c, :],
                                                                          start=(c == 0), stop=(c == 7)),
                              reads=[b("wu"), b("h2T")], writes=[b("pU%d" % gs)])
                    st.op("act", lambda h, gs=gs: h.activation(out=sg[:, gs, :], in_=pG[:, gs, 0:TT], func=AF.Silu), reads=[b("pG%d" % gs)], writes=[b("sg%d" % gs)])
                    st.op("dve", lambda h, gs=gs, fb=fb: h.tensor_tensor(out=aT[:, fb, :], in0=sg[:, gs, :], in1=pUu[:, gs, 0:TT], op=ALU.mult),
                          reads=[b("sg%d" % gs), b("pU%d" % gs)], writes=[b("aT")])
                for sub in range(2):
                    xb = b("xt%d" % (slot * 2 + sub))
                    for half in range(2):
                        pt, pb = ring_A.get()
                        for fb in range(NFB):
                            st.op("pe", lambda h, fb=fb, pt=pt, sub=sub, half=half: h.matmul(pt, lhsT=aT[:, fb, sub * 128:(sub + 1) * 128],
                                                                                            rhs=wd[:, fb, half * 512:(half + 1) * 512], start=(fb == 0), stop=(fb == NFB - 1)),
                                  reads=[b("aT"), b("wd")], writes=[pb])
                        st.op("dve", lambda h, pt=pt, sub=sub, half=half: h.tensor_tensor(out=xt[:, slot, sub, half * 512:(half + 1) * 512],
                                                                                         in0=xt[:, slot, sub, half * 512:(half + 1) * 512], in1=pt, op=ALU.add),
                              reads=[pb, xb], writes=[xb])
                for sub in range(2):
                    xb = b("xt%d" % (slot * 2 + sub))
                    col = rms_rstd(i, sub, 1)
                    st.op("dve", lambda h, sub=sub, col=col: h.scalar_tensor_tensor(out=xt[:, slot, sub, :], in0=xt[:, slot, sub, :], scalar=rs[:, col:col + 1],
                                                                                   in1=gfb[:], op0=ALU.mult, op1=ALU.mult),
                          reads=[xb, b("rs%d" % col), b("gfb")], writes=[xb])
                    st.dma(lambda h, sub=sub: h.dma_start(out=out[t0 + sub * 128:t0 + (sub + 1) * 128, :], in_=xt[:, slot, sub, :]), reads=[xb])

            loads(0)
            for i in range(NTL):
                tile(i)
            st.finish()
            st.emit()

    if 1 in stages:
        stage1()
    if 2 in stages:
        stage2()
    if 3 in stages:
        stage3()
    if 4 in stages:
        stage4()
    return nc


def _pcn(w, rows):
    n = w.shape[1]
    return np.ascontiguousarray(w.reshape(rows // 128, 128, n).transpose(1, 0, 2))


def _pc(g):
    return np.ascontiguousarray(g.reshape(-1, 128).T)


def layout_inputs(inp, L):
    f32 = np.float32
    w_in = np.asarray(inp["w_in"][0], f32)
    hq, hi, hff, hfb, hg, cq, ckv, kr = np.split(w_in, np.cumsum([512, 512, 512, 512, 512, 384, 256])[:], axis=1)
    krot = np.concatenate([kr[:, 32:64], kr[:, 0:32]], axis=1)
    w1 = np.concatenate([hq, hff, hfb, hi, hg, cq, ckv, kr, kr, krot, krot], axis=1)
    assert w1.shape[1] == W1C
    wqb = np.asarray(inp["w_q_b"][0], f32).reshape(384, 4, 192)
    nope = wqb[:, :, 0:128].reshape(384, 512)
    rp = wqb[:, :, 128:192]
    rope = rp.reshape(384, 256)
    rot = np.concatenate([rp[:, :, 32:64], rp[:, :, 0:32]], axis=2).reshape(384, 256)
    wq = np.concatenate([nope, rope, rot], axis=1)
    wkvb = np.asarray(inp["w_kv_b"][0], f32).reshape(256, 4, 256)
    wkv = np.concatenate([wkvb[:, :, 0:128].reshape(256, 512), wkvb[:, :, 128:256].reshape(256, 512)], axis=1)
    lbl = np.asarray(inp["lb_logits"], f32)
    lbl_l = np.ascontiguousarray(lbl.reshape(2, 2, 4, 128).transpose(3, 0, 1, 2))
    gout = np.concatenate([np.asarray(inp["hgrn_norm_g"][0], f32), np.asarray(inp["mla_norm_g"][0], f32)])
    inv = 1.0 / (10000.0 ** (np.arange(0, 64, 2, dtype=np.float32) / 64.0))
    ang = np.arange(L, dtype=np.float32)[None, :] * inv[:, None].astype(np.float32)
    cos = np.cos(ang).astype(f32)
    sin = np.sin(ang).astype(f32)
    c_cos = np.ascontiguousarray(np.tile(cos, (4, 1)))
    c_sin = np.ascontiguousarray(np.tile(sin, (4, 1)))
    rmask = np.ones((128, 512), f32)
    rmask[:, 0::128] = 0.0
    jj = np.arange(128)[:, None]
    ii = np.arange(128)[None, :]
    d = {
        "w_in": _pcn(w1, 1024), "g1": _pc(np.asarray(inp["norm1_g"][0], f32)), "lbl": lbl_l,
        "w_qb": _pcn(wq, 384), "gqa": _pc(np.asarray(inp["q_a_norm_g"][0], f32)),
        "w_kvb": _pcn(wkv, 256), "gkva": _pc(np.asarray(inp["kv_a_norm_g"][0], f32)),
        "w_out": _pcn(np.asarray(inp["w_out"][0], f32), 1024), "gout": _pc(gout),
        "w_gate": _pcn(np.asarray(inp["w_gate"][0], f32), 1024), "w_up": _pcn(np.asarray(inp["w_up"][0], f32), 1024),
        "g2": _pc(np.asarray(inp["norm2_g"][0], f32)),
        "w_down": _pcn(np.asarray(inp["w_down"][0], f32), DFF),
        "gfin": np.asarray(inp["final_norm_g"], f32).reshape(1, D),
        "c_ident": np.eye(128).astype(ml_dtypes.bfloat16), "c_ones": np.ones((128, 128), ml_dtypes.bfloat16),
        "c_cos": c_cos, "c_sin": c_sin, "c_rmask": rmask,
        "c_maskf": (jj <= ii).astype(f32), "c_maskb": (jj >= ii).astype(f32),
    }
    return d


_NC_CACHE = {}


def kernel(**inputs):
    x = np.asarray(inputs["x"], np.float32)
    Bt, L, _ = x.shape
    n_seq = Bt // NCORES
    key = (n_seq, L)
    if key not in _NC_CACHE:
        _NC_CACHE[key] = build_nc(n_seq, L)
    nc = _NC_CACHE[key]
    shared = layout_inputs(inputs, L)
    in_maps = []
    for c in range(NCORES):
        m = dict(shared)
        m["x"] = np.ascontiguousarray(x[c * n_seq:(c + 1) * n_seq].reshape(n_seq * L, D))
        in_maps.append(m)
    res = run_bass_kernel_spmd(nc, in_maps, core_ids=list(range(NCORES)))
    out = np.stack([r["out"].reshape(n_seq, L, D) for r in res.results], axis=0)
    return out.reshape(Bt, L, D).astype(np.float32)
```

```python
import contextlib
import os
import numpy as np
import ml_dtypes
import concourse.bass as bass
import concourse.mybir as mybir
from concourse.bass_utils import run_bass_kernel_spmd

F32 = mybir.dt.float32
BF16 = mybir.dt.bfloat16
AF = mybir.ActivationFunctionType
ALU = mybir.AluOpType
AX = mybir.AxisListType

D = 1024
DFF = 2816
NFB = DFF // 128
EPS = 1e-6
NCORES = 8
W1C = 3456
SCALE = 192 ** -0.5


class Buf:
    __slots__ = ("name", "w", "r", "x")

    def __init__(self, name=""):
        self.name = name
        self.w = None
        self.r = {}
        self.x = len(name) > 1 and name[0] == "p" and (name[1].isupper() or name.startswith(("pmm", "pxt")))


class Stage:
    ENGS = ("pe", "act", "dve", "pool", "sp")

    def __init__(self, nc, name, n_dma_sems=16):
        self.nc = nc
        self.name = name
        self.ops = {e: [] for e in self.ENGS}
        self.cnt = {e: 0 for e in ("pe", "act", "dve", "pool")}
        self.waited = {e: {} for e in self.ENGS}
        self.n_dma = n_dma_sems
        self.dma_cnt = [0] * n_dma_sems
        self.dma_rr = 0
        self.sems = {}

    def _need(self, eng, ev, waits):
        if ev is None:
            return
        key, val = ev
        if key == "pe" and eng == "pe":
            return
        if self.waited[eng].get(key, 0) >= val:
            return
        self.waited[eng][key] = val
        waits.append((key, val))

    def _deps(self, eng, reads, writes):
        waits = []
        for b in reads:
            self._need(eng, b.w, waits)
            if b.x:
                for k, v in b.r.items():
                    if k != eng:
                        self._need(eng, (k, v), waits)
        for b in writes:
            self._need(eng, b.w, waits)
            for k, v in b.r.items():
                self._need(eng, (k, v), waits)
        return waits

    def _commit(self, ev, reads, writes):
        k, v = ev
        for b in reads:
            if b.r.get(k, 0) < v:
                b.r[k] = v
        for b in writes:
            b.w = ev
            b.r = {}

    def op(self, eng, fn, reads=(), writes=()):
        waits = self._deps(eng, reads, writes)
        self.cnt[eng] += 1
        ev = (eng, self.cnt[eng])
        self.ops[eng].append((waits, fn, (eng, 1)))
        self._commit(ev, reads, writes)
        return ev

    def dma(self, fn, reads=(), writes=(), queue="sp"):
        waits = self._deps(queue, reads, writes)
        k = self.dma_rr
        self.dma_rr = (self.dma_rr + 1) % self.n_dma
        key = "dma%d" % k
        if self.dma_cnt[k] > 0:
            self._need(queue, (key, self.dma_cnt[k]), waits)
        self.dma_cnt[k] += 16
        ev = (key, self.dma_cnt[k])
        self.ops[queue].append((waits, fn, (key, 16)))
        self._commit(ev, reads, writes)
        return ev

    def finish(self, eng="sp"):
        waits = []
        for k in range(self.n_dma):
            if self.dma_cnt[k] > 0:
                self._need(eng, ("dma%d" % k, self.dma_cnt[k]), waits)
        if waits:
            self.ops[eng].append((waits, None, None))

    def emit(self):
        nc = self.nc
        with contextlib.ExitStack() as st:
            for e in ("pe", "act", "dve", "pool"):
                self.sems[e] = st.enter_context(nc.semaphore("%s_%s" % (self.name, e)))
            for k in range(self.n_dma):
                self.sems["dma%d" % k] = st.enter_context(nc.semaphore("%s_d%d" % (self.name, k)))
            block = st.enter_context(nc.Block())
            sems = self.sems

            def run(h, lst):
                for waits, fn, inc in lst:
                    for key, val in waits:
                        h.wait_ge(sems[key], val)
                    if fn is not None:
                        fn(h).then_inc(sems[inc[0]], inc[1])

            if self.ops["sp"]:
                @block.sync
                def _(h):
                    run(h, self.ops["sp"])
            if self.ops["pe"]:
                @block.tensor
                def _(h):
                    run(h, self.ops["pe"])
            if self.ops["act"]:
                @block.scalar
                def _(h):
                    run(h, self.ops["act"])
            if self.ops["dve"]:
                @block.vector
                def _(h):
                    run(h, self.ops["dve"])
            if self.ops["pool"]:
                @block.gpsimd
                def _(h):
                    run(h, self.ops["pool"])


class Ring:
    def __init__(self, tens, n, name):
        self.t = tens
        self.n = n
        self.i = 0
        self.bufs = [Buf("%s%d" % (name, k)) for k in range(n)]

    def get(self):
        k = self.i
        self.i = (self.i + 1) % self.n
        return self.t[:, k], self.bufs[k]


def bc(ap, shape):
    return ap.to_broadcast(shape)


def build_nc(n_seq, L, debug=False, stages=(1, 2, 3, 4)):
    T = n_seq * L
    NT = T // 128
    NS = T // 512
    NCH = T // 128
    nc = bass.Bass("TRN2", target_bir_lowering=False)

    def din(name, shape, dt=F32):
        return nc.dram_tensor(name, list(shape), dt, kind="ExternalInput").ap()

    skind = "ExternalOutput" if debug else "Internal"

    def dscr(name, shape, dt):
        return nc.dram_tensor(name, list(shape), dt, kind=skind).ap()

    x = din("x", [T, D])
    out = nc.dram_tensor("out", [T, D], F32, kind="ExternalOutput").ap()
    w_in = din("w_in", [128, 8, W1C])
    g1 = din("g1", [128, 8])
    lbl = din("lbl", [128, 2, 2, 4])
    w_qb = din("w_qb", [128, 3, 1024])
    gqa = din("gqa", [128, 3])
    w_kvb = din("w_kvb", [128, 2, 1024])
    gkva = din("gkva", [128, 2])
    w_out = din("w_out", [128, 8, D])
    gout = din("gout", [128, 8])
    w_gate = din("w_gate", [128, 8, DFF])
    w_up = din("w_up", [128, 8, DFF])
    g2 = din("g2", [128, 8])
    w_down = din("w_down", [128, NFB, D])
    gfin = din("gfin", [1, D])
    c_ident = din("c_ident", [128, 128], BF16)
    c_ones = din("c_ones", [128, 128], BF16)
    c_cos = din("c_cos", [128, L])
    c_sin = din("c_sin", [128, L])
    c_rmask = din("c_rmask", [128, 512])
    c_maskf = din("c_maskf", [128, 128])
    c_maskb = din("c_maskb", [128, 128])

    qdT_s = dscr("qdT_s", [2, 4, 128, T], BF16)
    kiT_s = dscr("kiT_s", [2, 4, 128, T], BF16)
    scal_s = dscr("scal_s", [128, 2, 3, 4, NCH], F32)
    v_s = dscr("v_s", [T, 512], BF16)
    g_s = dscr("g_s", [T, 512], F32)
    qnT_s = dscr("qnT_s", [4, 128, T], BF16)
    qrT_s = dscr("qrT_s", [2, 128, T], BF16)
    knT_s = dscr("knT_s", [4, 128, T], BF16)
    krT_s = dscr("krT_s", [128, T], BF16)
    vm_s = dscr("vm_s", [T, 512], BF16)
    yT_s = dscr("yT_s", [8, 128, T], BF16)
    of_s = dscr("of_s", [T, 512], F32)

    def stage1():
        with contextlib.ExitStack() as es:
            def sb(name, shape, dt=F32):
                return es.enter_context(nc.sbuf_tensor("s1_" + name, list(shape), dt))

            def ps(name, shape, dt=F32):
                return es.enter_context(nc.psum_tensor("s1_" + name, list(shape), dt))

            st = Stage(nc, "s1")
            w1 = sb("w1", [128, 8, W1C], BF16)
            wq = sb("wq", [128, 3, 1024], BF16)
            wkv = sb("wkv", [128, 2, 1024], BF16)
            stg = sb("stg", [128, 2, 1024], F32)
            g1t = sb("g1t", [128, 8]); gqat = sb("gqat", [128, 3]); gkvat = sb("gkvat", [128, 2])
            lblt = sb("lblt", [128, 2, 2, 4])
            lbt = sb("lbt", [128, 2, 4]); omlt = sb("omlt", [128, 2, 4])
            fa = sb("fa", [128, 2, 4]); fb_ = sb("fb", [128, 2, 4]); nfb = sb("nfb", [128, 2, 4])
            ident = sb("ident", [128, 128], BF16)
            rmask = sb("rmask", [128, 512])
            xt = sb("xt", [128, 4, D]); xjunk = sb("xjunk", [128, 2, D], BF16)
            xn = sb("xn", [128, 2, D], BF16)
            hT = sb("hT", [128, 2, 8, 512], BF16)
            ssx = sb("ssx", [128, 2, 4]); lnx = sb("lnx", [128, 2, 4]); rsx = sb("rsx", [128, 2, 4])
            cst = sb("cst", [128, 2, 512]); snt = sb("snt", [128, 2, 512])
            qT = sb("qT", [128, 4, 512])
            th = sb("th", [128, 8, 512])
            tmpf = sb("tmpf", [128, 8, 512])
            tmpb = sb("tmpb", [128, 8, 512], BF16)
            gt = sb("gt", [128, 2, 512])
            cqf = sb("cqf", [128, 4, 640]); cqn = sb("cqn", [128, 4, 640], BF16)
            ssq = sb("ssq", [128, 8]); lnq = sb("lnq", [128, 8]); rsq = sb("rsq", [128, 8])
            cqT = sb("cqT", [128, 3, 512], BF16); ckvT = sb("ckvT", [128, 2, 512], BF16)
            scal = sb("scal", [128, 2, 3, 4, NCH])
            pxt = ps("pxt", [128, 2, 8, 128], BF16)
            pmm = ps("pmm", [128, 6, 512])

            B = {}
            def b(name):
                if name not in B:
                    B[name] = Buf(name)
                return B[name]

            ring_f = Ring(tmpf, 8, "tmpf")
            ring_b = Ring(tmpb, 8, "tmpb")
            ring_p = Ring(pmm, 6, "pmm")
            ring_g = Ring(gt, 2, "gt")
            pxb = [Buf("pxt0"), Buf("pxt1")]
            hTb = [Buf("hT0"), Buf("hT1")]
            xtb = [Buf("xt%d" % i) for i in range(4)]
            xnb = [Buf("xn0"), Buf("xn1")]

            for (dst, src, nm) in ((g1t, g1, "g1t"), (gqat, gqa, "gqat"), (gkvat, gkva, "gkvat"),
                                   (lblt, lbl, "lblt"), (ident, c_ident, "ident"), (rmask, c_rmask, "rmask")):
                st.dma(lambda h, dst=dst, src=src: h.dma_start(out=dst[:], in_=src), writes=[b(nm)])
            st.op("dve", lambda h: h.tensor_tensor(out=lbt[:], in0=lblt[:, :, 1, :], in1=lblt[:, :, 0, :], op=ALU.subtract),
                  reads=[b("lblt")], writes=[b("lbt")])
            st.op("act", lambda h: h.activation(out=lbt[:], in_=lbt[:], func=AF.Exp), reads=[b("lbt")], writes=[b("lbt")])
            st.op("dve", lambda h: h.tensor_scalar(out=lbt[:], in0=lbt[:], scalar1=1.0, scalar2=None, op0=ALU.add),
                  reads=[b("lbt")], writes=[b("lbt")])
            st.op("dve", lambda h: h.reciprocal(out=lbt[:], in_=lbt[:]), reads=[b("lbt")], writes=[b("lbt")])
            st.op("dve", lambda h: h.tensor_scalar(out=omlt[:], in0=lbt[:], scalar1=-1.0, scalar2=1.0, op0=ALU.mult, op1=ALU.add),
                  reads=[b("lbt")], writes=[b("omlt")])
            st.op("dve", lambda h: h.tensor_scalar(out=fb_[:], in0=omlt[:], scalar1=0.5, scalar2=None, op0=ALU.mult),
                  reads=[b("omlt")], writes=[b("fb")])
            st.op("dve", lambda h: h.tensor_tensor(out=fa[:], in0=lbt[:], in1=fb_[:], op=ALU.add),
                  reads=[b("lbt"), b("fb")], writes=[b("fa")])
            st.op("dve", lambda h: h.tensor_scalar(out=nfb[:], in0=fb_[:], scalar1=-1.0, scalar2=None, op0=ALU.mult),
                  reads=[b("fb")], writes=[b("nfb")])

            pcnt = [0]

            def prep(dst3, src3, nchunk, ncols, gain, eng_cycle, name):
                for c in range(nchunk):
                    for c0 in range(0, ncols, 1024):
                        c1 = min(ncols, c0 + 1024)
                        slot = pcnt[0] % 2
                        eng = eng_cycle[pcnt[0] % len(eng_cycle)]
                        pcnt[0] += 1
                        sbuf = b("stg%d" % slot)
                        st.dma(lambda h, c=c, slot=slot, c0=c0, c1=c1: h.dma_start(out=stg[:, slot, 0:c1 - c0], in_=src3[:, c, c0:c1]),
                               writes=[sbuf])
                        st.op(eng, lambda h, c=c, slot=slot, c0=c0, c1=c1: h.tensor_scalar(
                            out=dst3[:, c, c0:c1], in0=stg[:, slot, 0:c1 - c0], scalar1=gain[:, c:c + 1], scalar2=None, op0=ALU.mult),
                            reads=[sbuf, b(name + "_g")], writes=[b(name)])
            B["w1_g"] = b("g1t"); B["wq_g"] = b("gqat"); B["wkv_g"] = b("gkvat")
            prep(w1, w_in, 8, W1C, g1t, ["dve", "pool"], "w1")
            prep(wq, w_qb, 3, 1024, gqat, ["dve", "pool"], "wq")
            prep(wkv, w_kvb, 2, 1024, gkvat, ["dve", "pool"], "wkv")
            v1 = w1[:, :, 3328:3456].rearrange("p c (g r) -> p c g r", r=64)[:, :, :, 0:32]
            st.op("dve", lambda h: h.tensor_scalar(out=v1, in0=v1, scalar1=-1.0, scalar2=None, op0=ALU.mult),
                  reads=[b("w1")], writes=[b("w1")])
            v2 = wq[:, :, 768:1024].rearrange("p c (g r) -> p c g r", r=64)[:, :, :, 0:32]
            st.op("dve", lambda h: h.tensor_scalar(out=v2, in0=v2, scalar1=-1.0, scalar2=None, op0=ALU.mult),
                  reads=[b("wq")], writes=[b("wq")])

            def xnorm_part1(s):
                slot = s % 2
                for j in range(4):
                    k = j
                    t0 = s * 512 + j * 128
                    st.dma(lambda h, k=k, t0=t0: h.dma_start(out=xt[:, k, :], in_=x[t0:t0 + 128, :]), writes=[xtb[k]])
                    st.op("act", lambda h, k=k, j=j, slot=slot: h.activation(out=xjunk[:, j % 2, :], in_=xt[:, k, :], func=AF.Square,
                                                                             accum_out=ssx[:, slot, j:j + 1]),
                          reads=[xtb[k]], writes=[b("ssx%d_%d" % (slot, j)), b("xjunk%d" % (j % 2))])
                return [0, 1, 2, 3]

            def xnorm_part2(s, ks):
                slot = s % 2
                st.op("act", lambda h: h.activation(out=lnx[:, slot, :], in_=ssx[:, slot, :], func=AF.Ln, scale=1.0 / D, bias=EPS),
                      reads=[b("ssx%d_%d" % (slot, j)) for j in range(4)], writes=[b("lnx%d" % slot)])
                st.op("act", lambda h: h.activation(out=rsx[:, slot, :], in_=lnx[:, slot, :], func=AF.Exp, scale=-0.5),
                      reads=[b("lnx%d" % slot)], writes=[b("rsx%d" % slot)])
                for j in range(4):
                    k = ks[j]
                    n = j % 2
                    st.op("dve", lambda h, k=k, j=j, n=n: h.tensor_scalar(out=xn[:, n, :], in0=xt[:, k, :], scalar1=rsx[:, slot, j:j + 1],
                                                                        scalar2=None, op0=ALU.mult),
                          reads=[xtb[k], b("rsx%d" % slot)], writes=[xnb[n]])
                    for c in range(8):
                        st.op("pe", lambda h, n=n, c=c: h.transpose(out=pxt[:, n, c, :], in_=xn[:, n, c * 128:(c + 1) * 128], identity=ident[:]),
                              reads=[xnb[n], b("ident")], writes=[pxb[n]])
                    st.op("act", lambda h, n=n, j=j: h.activation(out=hT[:, slot, :, j * 128:(j + 1) * 128], in_=pxt[:, n, :, :], func=AF.Copy),
                          reads=[pxb[n]], writes=[hTb[slot]])

            def mm_group(lhs_fn, rhs_fn, nk, reads, nfree=512):
                pt, pb = ring_p.get()
                pt = pt[:, 0:nfree]
                for c in range(nk):
                    st.op("pe", lambda h, c=c, pt=pt: h.matmul(pt, lhsT=lhs_fn(c), rhs=rhs_fn(c), start=(c == 0), stop=(c == nk - 1)),
                          reads=reads, writes=[pb])
                return pt, pb

            ks0 = xnorm_part1(0)
            xnorm_part2(0, ks0)
            def super_tile(s):
                slot = s % 2
                t0 = s * 512
                pos0 = t0 % L
                c0 = s * 4
                hs = hTb[slot]
                st.dma(lambda h, pos0=pos0: h.dma_start(out=cst[:, slot, :], in_=c_cos[:, pos0:pos0 + 512]), writes=[b("cst%d" % slot)])
                st.dma(lambda h, pos0=pos0: h.dma_start(out=snt[:, slot, :], in_=c_sin[:, pos0:pos0 + 512]), writes=[b("snt%d" % slot)])
                ks_next = xnorm_part1(s + 1) if s + 1 < NS else None
                for blk in range(4):
                    pt, pb = mm_group(lambda c, blk=blk: w1[:, c, blk * 128:(blk + 1) * 128], lambda c: hT[:, slot, c, :], 8, [hs, b("w1")])
                    st.op("act", lambda h, pt=pt, blk=blk: h.activation(out=qT[:, blk, :], in_=pt, func=AF.Silu),
                          reads=[pb], writes=[b("qT%d" % blk)])
                for db in range(8):
                    col = 512 + db * 128
                    pt, pb = mm_group(lambda c, col=col: w1[:, c, col:col + 128], lambda c: hT[:, slot, c, :], 8, [hs, b("w1")])
                    st.op("act", lambda h, pt=pt, db=db: h.activation(out=th[:, db, :], in_=pt, func=AF.Tanh, scale=0.5),
                          reads=[pb], writes=[b("th%d" % db)])
                for j in range(4):
                    tt = t0 + j * 128
                    pt, pb = mm_group(lambda c, j=j: hT[:, slot, c, j * 128:(j + 1) * 128], lambda c: w1[:, c, 1536:2048], 8, [hs, b("w1")])
                    ot, ob = ring_b.get()
                    st.op("dve", lambda h, pt=pt, ot=ot: h.tensor_copy(out=ot, in_=pt), reads=[pb], writes=[ob])
                    st.dma(lambda h, ot=ot, tt=tt: h.dma_start(out=v_s[tt:tt + 128, :], in_=ot), reads=[ob])
                    pt, pb = mm_group(lambda c, j=j: hT[:, slot, c, j * 128:(j + 1) * 128], lambda c: w1[:, c, 2048:2560], 8, [hs, b("w1")])
                    ot, ob = ring_g.get()
                    st.op("act", lambda h, pt=pt, ot=ot: h.activation(out=ot, in_=pt, func=AF.Silu), reads=[pb], writes=[ob])
                    st.dma(lambda h, ot=ot, tt=tt: h.dma_start(out=g_s[tt:tt + 128, :], in_=ot), reads=[ob])
                for j in range(4):
                    pt, pb = mm_group(lambda c, j=j: hT[:, slot, c, j * 128:(j + 1) * 128], lambda c: w1[:, c, 2560:2944], 8, [hs, b("w1")], nfree=384)
                    st.op("act", lambda h, pt=pt, j=j: h.activation(out=xjunk[:, 0, 0:384], in_=pt[:, 0:384], func=AF.Square, accum_out=ssq[:, j:j + 1]),
                          reads=[pb], writes=[b("ssq%d" % j), b("xjunk0")])
                    st.op("dve", lambda h, pt=pt, j=j: h.tensor_copy(out=cqf[:, j, 0:384], in_=pt[:, 0:384]), reads=[pb], writes=[b("cqf%d" % j)])
                    pt, pb = mm_group(lambda c, j=j: hT[:, slot, c, j * 128:(j + 1) * 128], lambda c: w1[:, c, 2944:3200], 8, [hs, b("w1")], nfree=256)
                    st.op("act", lambda h, pt=pt, j=j: h.activation(out=xjunk[:, 1, 0:256], in_=pt[:, 0:256], func=AF.Square, accum_out=ssq[:, 4 + j:5 + j]),
                          reads=[pb], writes=[b("ssq%d" % (4 + j)), b("xjunk1")])
                    st.op("dve", lambda h, pt=pt, j=j: h.tensor_copy(out=cqf[:, j, 384:640], in_=pt[:, 0:256]), reads=[pb], writes=[b("cqf%d" % j)])
                pt1, pb1 = mm_group(lambda c: w1[:, c, 3200:3328], lambda c: hT[:, slot, c, :], 8, [hs, b("w1")])
                pt2, pb2 = mm_group(lambda c: w1[:, c, 3328:3456], lambda c: hT[:, slot, c, :], 8, [hs, b("w1")])

                def rope(pt1, pb1, pt2, pb2, dst_fn):
                    f1, fb1 = ring_f.get()
                    f2, fb2 = ring_f.get()
                    ot, ob = ring_b.get()
                    st.op("dve", lambda h: h.tensor_tensor(out=f1, in0=pt1, in1=cst[:, slot, :], op=ALU.mult),
                          reads=[pb1, b("cst%d" % slot)], writes=[fb1])
                    st.op("dve", lambda h: h.tensor_tensor(out=f2, in0=pt2, in1=snt[:, slot, :], op=ALU.mult),
                          reads=[pb2, b("snt%d" % slot)], writes=[fb2])
                    st.op("pool", lambda h: h.tensor_tensor(out=ot, in0=f1, in1=f2, op=ALU.add), reads=[fb1, fb2], writes=[ob])
                    st.dma(lambda h: h.dma_start(out=dst_fn(), in_=ot), reads=[ob])
                rope(pt1, pb1, pt2, pb2, lambda: krT_s[:, t0:t0 + 512])

                for db in range(8):
                    d, blk = db // 4, db % 4
                    thb = b("th%d" % db)
                    ft, fbuf = ring_f.get()
                    k1, k1b = ring_f.get()
                    lf, lfb = ring_f.get()
                    st.op("pool", lambda h, ft=ft, db=db, d=d, blk=blk: h.tensor_scalar(
                        out=ft, in0=th[:, db, :], scalar1=fb_[:, d, blk:blk + 1], scalar2=fa[:, d, blk:blk + 1], op0=ALU.mult, op1=ALU.add),
                        reads=[thb, b("fb"), b("fa")], writes=[fbuf])
                    st.op("pool", lambda h, k1=k1, db=db, d=d, blk=blk: h.tensor_scalar(
                        out=k1, in0=th[:, db, :], scalar1=nfb[:, d, blk:blk + 1], scalar2=fb_[:, d, blk:blk + 1], op0=ALU.mult, op1=ALU.add),
                        reads=[thb, b("fb"), b("nfb")], writes=[k1b])
                    st.op("act", lambda h, lf=lf, ft=ft: h.activation(out=lf, in_=ft, func=AF.Ln), reads=[fbuf], writes=[lfb])
                    cumt, cumb = ring_f.get()
                    cct, ccb = ring_f.get()
                    st.op("dve", lambda h, lf=lf, cumt=cumt: h.tensor_tensor_scan(out=cumt, data0=rmask[:], data1=lf, initial=0.0,
                                                                                  op0=ALU.mult, op1=ALU.add),
                          reads=[lfb, b("rmask")], writes=[cumb])
                    cumv = cumt.rearrange("p (c t) -> p c t", t=128)
                    ccv = cct.rearrange("p (c t) -> p c t", t=128)
                    st.op("dve", lambda h, cumv=cumv, ccv=ccv: h.tensor_tensor(out=ccv, in0=cumv, in1=bc(cumv[:, :, 63:64], [128, 4, 128]), op=ALU.subtract),
                          reads=[cumb], writes=[ccb])
                    Xi, Yi = (1, 2) if d == 0 else (2, 1)
                    st.op("act", lambda h, d=d, blk=blk, cumv=cumv: h.activation(out=scal[:, d, 0, blk, c0:c0 + 4], in_=cumv[:, :, 127], func=AF.Exp),
                          reads=[cumb], writes=[b("scal")])
                    st.op("act", lambda h, d=d, blk=blk, ccv=ccv, Xi=Xi: h.activation(out=scal[:, d, Xi, blk, c0:c0 + 4], in_=ccv[:, :, 127], func=AF.Exp),
                          reads=[ccb], writes=[b("scal")])
                    st.op("act", lambda h, d=d, blk=blk, cumv=cumv, Yi=Yi: h.activation(out=scal[:, d, Yi, blk, c0:c0 + 4], in_=cumv[:, :, 63], func=AF.Exp),
                          reads=[cumb], writes=[b("scal")])
                    if d == 0:
                        src = cct
                        srcb = ccb
                    else:
                        t2, t2b = ring_f.get()
                        st.op("pool", lambda h, t2=t2, lf=lf, cct=cct: h.tensor_tensor(out=t2, in0=lf, in1=cct, op=ALU.subtract),
                              reads=[lfb, ccb], writes=[t2b])
                        src, srcb = t2, t2b
                    ea, eab = ring_f.get()
                    eb, ebb = ring_f.get()
                    st.op("act", lambda h, ea=ea, src=src: h.activation(out=ea, in_=src, func=AF.Exp), reads=[srcb], writes=[eab])
                    st.op("act", lambda h, eb=eb, src=src: h.activation(out=eb, in_=src, func=AF.Exp, scale=-1.0), reads=[srcb], writes=[ebb])
                    o1, o1b = ring_b.get()
                    o2, o2b = ring_b.get()
                    st.op("dve", lambda h, o1=o1, ea=ea, blk=blk: h.tensor_tensor(out=o1, in0=qT[:, blk, :], in1=ea, op=ALU.mult),
                          reads=[b("qT%d" % blk), eab], writes=[o1b])
                    st.op("dve", lambda h, o2=o2, eb=eb, k1=k1: h.tensor_tensor(out=o2, in0=k1, in1=eb, op=ALU.mult),
                          reads=[k1b, ebb], writes=[o2b])
                    st.dma(lambda h, o1=o1, d=d, blk=blk: h.dma_start(out=qdT_s[d, blk, :, t0:t0 + 512], in_=o1), reads=[o1b])
                    st.dma(lambda h, o2=o2, d=d, blk=blk: h.dma_start(out=kiT_s[d, blk, :, t0:t0 + 512], in_=o2), reads=[o2b])
                st.op("act", lambda h: h.activation(out=lnq[:, 0:4], in_=ssq[:, 0:4], func=AF.Ln, scale=1.0 / 384, bias=EPS),
                      reads=[b("ssq%d" % i) for i in range(8)], writes=[b("lnq")])
                st.op("act", lambda h: h.activation(out=lnq[:, 4:8], in_=ssq[:, 4:8], func=AF.Ln, scale=1.0 / 256, bias=EPS),
                      reads=[b("ssq%d" % i) for i in range(8)], writes=[b("lnq")])
                st.op("act", lambda h: h.activation(out=rsq[:], in_=lnq[:], func=AF.Exp, scale=-0.5), reads=[b("lnq")], writes=[b("rsq")])
                for j in range(4):
                    st.op("dve", lambda h, j=j: h.tensor_scalar(out=cqn[:, j, 0:384], in0=cqf[:, j, 0:384], scalar1=rsq[:, j:j + 1], scalar2=None, op0=ALU.mult),
                          reads=[b("cqf%d" % j), b("rsq")], writes=[b("cqn%d" % j)])
                    st.op("dve", lambda h, j=j: h.tensor_scalar(out=cqn[:, j, 384:640], in0=cqf[:, j, 384:640], scalar1=rsq[:, 4 + j:5 + j], scalar2=None, op0=ALU.mult),
                          reads=[b("cqf%d" % j), b("rsq")], writes=[b("cqn%d" % j)])
                if ks_next is not None:
                    xnorm_part2(s + 1, ks_next)
                for j in range(4):
                    n = j % 2
                    for c in range(5):
                        st.op("pe", lambda h, n=n, c=c, j=j: h.transpose(out=pxt[:, n, c, :], in_=cqn[:, j, c * 128:(c + 1) * 128], identity=ident[:]),
                              reads=[b("cqn%d" % j), b("ident")], writes=[pxb[n]])
                    st.op("act", lambda h, n=n, j=j: h.activation(out=cqT[:, :, j * 128:(j + 1) * 128], in_=pxt[:, n, 0:3, :], func=AF.Copy),
                          reads=[pxb[n]], writes=[b("cqT")])
                    st.op("act", lambda h, n=n, j=j: h.activation(out=ckvT[:, :, j * 128:(j + 1) * 128], in_=pxt[:, n, 3:5, :], func=AF.Copy),
                          reads=[pxb[n]], writes=[b("ckvT")])
                for hh in range(4):
                    pt, pb = mm_group(lambda c, hh=hh: wq[:, c, hh * 128:(hh + 1) * 128], lambda c: cqT[:, c, :], 3, [b("cqT"), b("wq")])
                    ot, ob = ring_b.get()
                    st.op("act", lambda h, pt=pt, ot=ot: h.activation(out=ot, in_=pt, func=AF.Copy), reads=[pb], writes=[ob])
                    st.dma(lambda h, ot=ot, hh=hh: h.dma_start(out=qnT_s[hh, :, t0:t0 + 512], in_=ot), reads=[ob])
                for pr in range(2):
                    pt1, pb1 = mm_group(lambda c, pr=pr: wq[:, c, 512 + pr * 128:640 + pr * 128], lambda c: cqT[:, c, :], 3, [b("cqT"), b("wq")])
                    pt2, pb2 = mm_group(lambda c, pr=pr: wq[:, c, 768 + pr * 128:896 + pr * 128], lambda c: cqT[:, c, :], 3, [b("cqT"), b("wq")])
                    rope(pt1, pb1, pt2, pb2, lambda pr=pr: qrT_s[pr, :, t0:t0 + 512])
                for hh in range(4):
                    pt, pb = mm_group(lambda c, hh=hh: wkv[:, c, hh * 128:(hh + 1) * 128], lambda c: ckvT[:, c, :], 2, [b("ckvT"), b("wkv")])
                    ot, ob = ring_b.get()
                    st.op("act", lambda h, pt=pt, ot=ot: h.activation(out=ot, in_=pt, func=AF.Copy), reads=[pb], writes=[ob])
                    st.dma(lambda h, ot=ot, hh=hh: h.dma_start(out=knT_s[hh, :, t0:t0 + 512], in_=ot), reads=[ob])
                for j in range(4):
                    tt = t0 + j * 128
                    pt, pb = mm_group(lambda c, j=j: ckvT[:, c, j * 128:(j + 1) * 128], lambda c: wkv[:, c, 512:1024], 2, [b("ckvT"), b("wkv")])
                    ot, ob = ring_b.get()
                    st.op("dve", lambda h, pt=pt, ot=ot: h.tensor_copy(out=ot, in_=pt), reads=[pb], writes=[ob])
                    st.dma(lambda h, ot=ot, tt=tt: h.dma_start(out=vm_s[tt:tt + 128, :], in_=ot), reads=[ob])
            for s in range(NS):
                super_tile(s)
            st.dma(lambda h: h.dma_start(out=scal_s, in_=scal[:]), reads=[b("scal")])
            st.finish()
            st.emit()

    def stage2():
        NK = L // 128
        NQ = L // 512
        DEN = os.environ.get("S2_DEN", "pe")
        ROPE128 = os.environ.get("S2_ROPE", "k128") == "k128"
        NPS = int(os.environ.get("S2_NPS", "4"))
        NP = int(os.environ.get("S2_NP", "6"))
        with contextlib.ExitStack() as es:
            def sb(name, shape, dt=F32):
                return es.enter_context(nc.sbuf_tensor("s2_" + name, list(shape), dt))

            def ps(name, shape, dt=F32):
                return es.enter_context(nc.psum_tensor("s2_" + name, list(shape), dt))

            st = Stage(nc, "s2")
            knT = sb("knT", [128, 4, L], BF16)
            vm = sb("vm", [128, NK, 512], BF16)
            krT = sb("krT", [128, L], BF16)
            qn = sb("qn", [128, 2, 4, 512], BF16)
            qr = sb("qr", [128, 2, 2, 512], BF16)
            ones = sb("ones", [128, 128], BF16)
            Pt = sb("Pt", [128, NP, 512], BF16)
            qrz = sb("qrz", [128, 2, 4, 512], BF16)
            hm = sb("hm", [128, 2])
            accP = sb("accP", [128, 2, 512])
            oT = sb("oT", [128, 2, 4, 512])
            rden = sb("rden", [128, 2, 512])
            sq = sb("sq", [128, 4, 512], BF16)
            lnv = sb("lnv", [128, 512]); rstd = sb("rstd", [128, 512])
            yb = sb("yb", [128, 4, 512], BF16)
            pS = ps("pS", [128, NPS, 512])
            pO = ps("pO", [128, 2, 512])
            pD = ps("pD", [128, 1, 512])
            pSS = ps("pSS", [128, 512])
            acc = sb("acc", [128, 2, 512])
            onesf = sb("onesf", [128, 128])
            B = {}

            def b(name):
                if name not in B:
                    B[name] = Buf(name)
                return B[name]
            ring_P = Ring(Pt, NP, "Pt")
            ring_S = Ring(pS, NPS, "pS")
            ring_y = Ring(yb, 4, "yb")
            st.dma(lambda h: h.dma_start(out=ones[:], in_=c_ones), writes=[b("ones")])
            st.op("pool", lambda h: h.memset(onesf[:], 1.0), writes=[b("onesf")])
            st.op("pool", lambda h: h.memset(hm[:], 0.0), writes=[b("hm")])
            st.op("pool", lambda h: h.memset(hm[0:64, 0:1], 1.0), writes=[b("hm")])
            st.op("pool", lambda h: h.memset(hm[64:128, 1:2], 1.0), writes=[b("hm")])

            items = []
            for seq in range(n_seq):
                for qb in range(NQ):
                    for hh in range(4):
                        for kc in range(NK):
                            items.append((seq, qb, hh, kc))

            def loads_seq(seq):
                s0 = seq * L
                for hh in range(4):
                    st.dma(lambda h, hh=hh: h.dma_start(out=knT[:, hh, :], in_=knT_s[hh, :, s0:s0 + L]), writes=[b("knT")])
                st.dma(lambda h: h.dma_start(out=krT[:], in_=krT_s[:, s0:s0 + L]), writes=[b("krT")])
                st.dma(lambda h: h.dma_start(out=vm[:], in_=vm_s[s0:s0 + L, :].rearrange("(k p) f -> p k f", p=128)), writes=[b("vm")])

            def loads_q(seq, qb):
                slot = (seq * NQ + qb) % 2
                tq = seq * L + qb * 512
                st.dma(lambda h: h.dma_start(out=qn[:, slot], in_=qnT_s[:, :, tq:tq + 512].rearrange("h p t -> p h t")), writes=[b("qn%d" % slot)])
                st.dma(lambda h: h.dma_start(out=qr[:, slot], in_=qrT_s[:, :, tq:tq + 512].rearrange("h p t -> p h t")), writes=[b("qr%d" % slot)])
                if ROPE128:
                    for hh in range(4):
                        st.op("pool", lambda h, hh=hh: h.tensor_scalar(out=qrz[:, slot, hh, :], in0=qr[:, slot, hh // 2, :], scalar1=hm[:, hh % 2:hh % 2 + 1],
                                                                        scalar2=None, op0=ALU.mult),
                              reads=[b("qr%d" % slot), b("hm")], writes=[b("qrz%d" % slot)])

            def qk(item):
                seq, qb, hh, kc = item
                slot = (seq * NQ + qb) % 2
                hp, pr = hh % 2, hh // 2
                pt, pb = ring_S.get()
                st.op("pe", lambda h: h.matmul(pt, lhsT=knT[:, hh, kc * 128:(kc + 1) * 128], rhs=qn[:, slot, hh, :], start=True, stop=False),
                      reads=[b("knT"), b("qn%d" % slot)], writes=[pb])
                if ROPE128:
                    st.op("pe", lambda h: h.matmul(pt, lhsT=krT[:, kc * 128:(kc + 1) * 128], rhs=qrz[:, slot, hh, :], start=False, stop=True),
                          reads=[b("krT"), b("qrz%d" % slot)], writes=[pb])
                else:
                    st.op("pe", lambda h: h.matmul(pt, lhsT=krT[hp * 64:(hp + 1) * 64, kc * 128:(kc + 1) * 128],
                                                   rhs=qr[hp * 64:(hp + 1) * 64, slot, pr, :], start=False, stop=True),
                          reads=[b("krT"), b("qr%d" % slot)], writes=[pb])
                return pt, pb

            def finish_head(seq, qb, hh, oslot):
                qslot = (seq * NQ + qb) % 2
                if DEN == "dve":
                    st.op("pe", lambda h: h.matmul(pD[:, 0, :], lhsT=onesf[:], rhs=acc[:, oslot, :], start=True, stop=True),
                          reads=[b("onesf"), b("acc%d" % oslot)], writes=[b("pD0")])
                elif DEN == "split":
                    st.op("pe", lambda h: h.matmul(pD[:, 0, :], lhsT=onesf[:], rhs=acc[:, oslot, :], start=True, stop=False),
                          reads=[b("onesf"), b("acc%d" % oslot)], writes=[b("pD0")])
                    st.op("pe", lambda h: h.matmul(pD[:, 0, :], lhsT=onesf[:], rhs=accP[:, oslot, :], start=False, stop=True),
                          reads=[b("onesf"), b("accP%d" % oslot)], writes=[b("pD0")])
                st.op("dve", lambda h: h.reciprocal(out=rden[:, oslot, :], in_=pD[:, 0, :]), reads=[b("pD0")], writes=[b("rden%d" % oslot)])
                st.op("dve", lambda h: h.tensor_tensor(out=oT[:, qslot, hh, :], in0=pO[:, oslot, :], in1=rden[:, oslot, :], op=ALU.mult),
                      reads=[b("pO%d" % oslot), b("rden%d" % oslot)], writes=[b("oT%d_%d" % (qslot, hh))])

            def finish_q(seq, qb):
                qslot = (seq * NQ + qb) % 2
                tq = seq * L + qb * 512
                for hh in range(4):
                    st.op("act", lambda h, hh=hh: h.activation(out=sq[:, hh, :], in_=oT[:, qslot, hh, :], func=AF.Square),
                          reads=[b("oT%d_%d" % (qslot, hh))], writes=[b("sq%d" % hh)])
                for hh in range(4):
                    st.op("pe", lambda h, hh=hh: h.matmul(pSS[:], lhsT=ones[:], rhs=sq[:, hh, :], start=(hh == 0), stop=(hh == 3)),
                          reads=[b("ones"), b("sq%d" % hh)], writes=[b("pSS")])
                st.op("act", lambda h: h.activation(out=lnv[:], in_=pSS[:], func=AF.Ln, scale=1.0 / 512, bias=EPS), reads=[b("pSS")], writes=[b("lnv")])
                st.op("act", lambda h: h.activation(out=rstd[:], in_=lnv[:], func=AF.Exp, scale=-0.5), reads=[b("lnv")], writes=[b("rstd")])
                for hh in range(4):
                    yt, ybuf = ring_y.get()
                    st.op("dve", lambda h, hh=hh, yt=yt: h.tensor_tensor(out=yt, in0=oT[:, qslot, hh, :], in1=rstd[:], op=ALU.mult),
                          reads=[b("oT%d_%d" % (qslot, hh)), b("rstd")], writes=[ybuf])
                    st.dma(lambda h, hh=hh, yt=yt: h.dma_start(out=yT_s[4 + hh, :, tq:tq + 512], in_=yt), reads=[ybuf])

            import collections
            LA = int(os.environ.get("S2_LA", "2"))
            pending = collections.deque()
            nq = [0]

            def ensure(upto, seq):
                while nq[0] < min(upto, len(items)) and items[nq[0]][0] == seq:
                    pending.append(qk(items[nq[0]]))
                    nq[0] += 1

            ocnt = 0
            for idx, item in enumerate(items):
                seq, qb, hh, kc = item
                if qb == 0 and hh == 0 and kc == 0:
                    loads_seq(seq)
                    loads_q(seq, 0)
                if hh == 0 and kc == 0 and qb + 1 < NQ:
                    loads_q(seq, qb + 1)
                ensure(idx + 1 + LA, seq)
                pt, pb = pending.popleft()
                oslot = ocnt % 2
                Pa, Pb = ring_P.get()
                st.op("act", lambda h, Pa=Pa, pt=pt: h.activation(out=Pa, in_=pt, func=AF.Exp, scale=SCALE), reads=[pb], writes=[Pb])
                st.op("pe", lambda h, Pa=Pa, oslot=oslot, hh=hh, kc=kc: h.matmul(pO[:, oslot, :], lhsT=vm[:, kc, hh * 128:(hh + 1) * 128], rhs=Pa,
                                                                           start=(kc == 0), stop=(kc == NK - 1)),
                      reads=[b("vm"), Pb], writes=[b("pO%d" % oslot)])
                if DEN == "pe":
                    st.op("pe", lambda h, Pa=Pa, kc=kc: h.matmul(pD[:, 0, :], lhsT=ones[:], rhs=Pa, start=(kc == 0), stop=(kc == NK - 1)),
                          reads=[b("ones"), Pb], writes=[b("pD0")])
                else:
                    on_pool = (DEN == "split" and kc % 4 == 3)
                    eng, at, an, first = ("pool", accP, "accP", kc == 3) if on_pool else ("dve", acc, "acc", kc == 0)
                    if first:
                        st.op(eng, lambda h, Pa=Pa, oslot=oslot, at=at: h.tensor_copy(out=at[:, oslot, :], in_=Pa), reads=[Pb], writes=[b("%s%d" % (an, oslot))])
                    else:
                        st.op(eng, lambda h, Pa=Pa, oslot=oslot, at=at: h.tensor_tensor(out=at[:, oslot, :], in0=at[:, oslot, :], in1=Pa, op=ALU.add),
                              reads=[Pb, b("%s%d" % (an, oslot))], writes=[b("%s%d" % (an, oslot))])
                if kc == NK - 1:
                    finish_head(seq, qb, hh, oslot)
                    ocnt += 1
                    if hh == 3:
                        finish_q(seq, qb)
            st.finish()
            st.emit()

    def stage3():
        NCs = L // 128
        with contextlib.ExitStack() as es:
            def sb(name, shape, dt=F32):
                return es.enter_context(nc.sbuf_tensor("s3_" + name, list(shape), dt))

            def ps(name, shape, dt=F32):
                return es.enter_context(nc.psum_tensor("s3_" + name, list(shape), dt))

            st = Stage(nc, "s3")
            scal = sb("scal", [128, 2, 3, 4, NCH])
            masks = sb("masks", [128, 2, 128])
            ident = sb("ident", [128, 128], BF16)
            qd = sb("qd", [128, 2 * n_seq, 4, 512], BF16)
            ki = sb("ki", [128, 2 * n_seq, 4, 512], BF16)
            vt = sb("vt", [128, 2 * n_seq, 4, 512], BF16)
            kitok = sb("kitok", [128, 2, 512], BF16)
            scT = sb("scT", [128, 2, 8, 128], BF16)
            S_all = sb("S", [128, n_seq, 4, 128]); Sp_all = sb("Sp", [128, n_seq, 4, 128], BF16)
            t1_all = sb("t1", [128, n_seq, 4, 128]); t2_all = sb("t2", [128, n_seq, 4, 128])
            ofw = sb("ofw", [128, 4, 512])
            gtl = sb("gtl", [128, 4, 512])
            osum_a = sb("osum", [128, 2, 512]); sqt_a = sb("sqt", [128, 2, 512]); yn_a = sb("yn", [128, 2, 512])
            ss8_a = sb("ss8", [128, 2, 8]); ln8_a = sb("ln8", [128, 2, 8]); rs8_a = sb("rs8", [128, 2, 8])
            ya = sb("ya", [128, 2, 512], BF16)
            yaT = sb("yaT", [128, 2, 4, 128], BF16)
            pT = ps("pT", [128, 2, 1024], BF16)
            pSc = ps("pSc", [128, 2, 4, 128])
            pOo = ps("pOo", [128, 2, 512])
            pU = ps("pU", [128, 1, 4, 128])
            pY = ps("pY", [128, 8, 128], BF16)
            B = {}

            def b(name):
                if name not in B:
                    B[name] = Buf(name)
                return B[name]
            BD = sb("BD", [128, 4, 128])
            Bm_all = sb("Bm", [128, n_seq, 4, 128])
            st.op("pool", lambda h: h.memset(BD[:], 0.0), writes=[b("BD")])
            st.op("pool", lambda h: h.memset(BD[0:64, :, 0:64], 1.0), writes=[b("BD")])
            st.op("pool", lambda h: h.memset(BD[64:128, :, 64:128], 1.0), writes=[b("BD")])
            ring_of = Ring(ofw, 4, "ofw")
            ring_g = Ring(gtl, 4, "gtl")
            st.dma(lambda h: h.dma_start(out=scal[:], in_=scal_s), writes=[b("scal")])
            st.dma(lambda h: h.dma_start(out=masks[:, 0, :], in_=c_maskf), writes=[b("masks")])
            st.dma(lambda h: h.dma_start(out=masks[:, 1, :], in_=c_maskb), writes=[b("masks")])
            st.dma(lambda h: h.dma_start(out=ident[:], in_=c_ident), writes=[b("ident")])

            def run_dir(seq, d):
                order = list(range(NCs)) if d == 0 else list(range(NCs - 1, -1, -1))
                S = S_all[:, seq]; Sp = Sp_all[:, seq]
                st.op("pool", lambda h: h.memset(S, 0.0), writes=[b("S%d" % seq)])
                st.op("pool", lambda h: h.memset(Sp, 0.0), writes=[b("Sp%d" % seq)])
                cur_group = [None, None]
                gcount = [0]

                def load_group(gi):
                    slot = seq * 2 + gcount[0] % 2
                    gcount[0] += 1
                    tg = seq * L + gi * 512
                    st.dma(lambda h: h.dma_start(out=qd[:, slot], in_=qdT_s[d, :, :, tg:tg + 512].rearrange("k p t -> p k t")), writes=[b("qd%d" % slot)])
                    st.dma(lambda h: h.dma_start(out=ki[:, slot], in_=kiT_s[d, :, :, tg:tg + 512].rearrange("k p t -> p k t")), writes=[b("ki%d" % slot)])
                    st.dma(lambda h: h.dma_start(out=vt[:, slot], in_=v_s[tg:tg + 512, :].rearrange("(j p) f -> p j f", p=128)), writes=[b("vt%d" % slot)])
                    return slot

                for pos, n in enumerate(order):
                    gi = n // 4
                    if cur_group[0] != gi:
                        cur_group[0] = gi
                        cur_group[1] = load_group(gi)
                    slot = cur_group[1]
                    j = n % 4
                    cg = seq * NCs + n
                    t0 = cg * 128
                    cs = seq % 2
                    yield from chunk(seq, d, n, pos, order, slot, j, cg, t0, cs)

            def chunk(seq, d, n, pos, order, slot, j, cg, t0, cs):
                S = S_all[:, seq]; Sp = Sp_all[:, seq]; t1 = t1_all[:, seq]; t2 = t2_all[:, seq]
                osum = osum_a[:, cs]; sqt = sqt_a[:, cs]; yn = yn_a[:, cs]
                ss8 = ss8_a[:, cs]; ln8 = ln8_a[:, cs]; rs8 = rs8_a[:, cs]
                bS, bSp, bt1, bt2 = b("S%d" % seq), b("Sp%d" % seq), b("t1%d" % seq), b("t2%d" % seq)
                bos, bsq, byn, bss, bln, brs = (b("%s%d" % (nm, cs)) for nm in ("osum", "sqt", "yn", "ss8", "ln8", "rs8"))
                qd_c = qd[:, slot, :, j * 128:(j + 1) * 128]
                ki_c = ki[:, slot, :, j * 128:(j + 1) * 128]
                v_c = vt[:, slot, j, :]
                rq, rk, rv = b("qd%d" % slot), b("ki%d" % slot), b("vt%d" % slot)
                if d == 1:
                    oft, ofb = ring_of.get()
                    gtt, gtb = ring_g.get()
                    st.dma(lambda h: h.dma_start(out=oft, in_=of_s[t0:t0 + 128, :]), reads=[b("ofs%d" % cg)], writes=[ofb])
                    st.dma(lambda h: h.dma_start(out=gtt, in_=g_s[t0:t0 + 128, :]), writes=[gtb])
                for bk in range(4):
                    st.op("pe", lambda h, bk=bk: h.transpose(out=pT[:, cs, bk * 128:(bk + 1) * 128], in_=ki_c[:, bk, :], identity=ident[:]),
                          reads=[rk, b("ident")], writes=[b("pT%d" % cs)])
                st.op("act", lambda h: h.activation(out=kitok[:, cs, :], in_=pT[:, cs, 0:512], func=AF.Copy), reads=[b("pT%d" % cs)], writes=[b("kitok%d" % cs)])
                for hh in range(8):
                    bk, hp = hh // 2, hh % 2
                    st.op("pe", lambda h, hh=hh, bk=bk, hp=hp: h.matmul(pSc[:, hp, bk, :], lhsT=ki_c[hp * 64:(hp + 1) * 64, bk, :],
                                                                        rhs=qd_c[hp * 64:(hp + 1) * 64, bk, :], start=True, stop=True),
                          reads=[rk, rq], writes=[b("pSc%d" % hp)])
                scv = scT[:, cs].rearrange("p (k two) t -> p k two t", two=2)
                for half in range(2):
                    st.op("dve", lambda h, half=half: h.tensor_tensor(out=scv[:, :, half, :], in0=pSc[:, half],
                                                                      in1=bc(masks[:, d:d + 1, :], [128, 4, 128]), op=ALU.mult),
                          reads=[b("pSc%d" % half), b("masks")], writes=[b("scT%d_%d" % (cs, half))])
                has_next = pos + 1 < len(order)
                if has_next:
                    Bm = Bm_all[:, seq]
                    st.op("pool", lambda h: h.tensor_tensor(out=Bm, in0=BD[:], in1=bc(scal[:, d, 1, :, cg:cg + 1], [128, 4, 128]), op=ALU.mult),
                          reads=[b("BD"), b("scal")], writes=[b("Bm%d" % seq)])
                yield
                for bk in range(4):
                    st.op("pe", lambda h, bk=bk: h.matmul(pOo[:, cs, bk * 128:(bk + 1) * 128], lhsT=qd_c[:, bk, :], rhs=Sp[:, bk, :], start=True, stop=False),
                          reads=[rq, bSp], writes=[b("pOo%d" % cs)])
                    for hp in range(2):
                        hh = 2 * bk + hp
                        st.op("pe", lambda h, hh=hh, hp=hp: h.matmul(pOo[:, cs, hh * 64:(hh + 1) * 64], lhsT=scT[:, cs, hh, :], rhs=v_c[:, hh * 64:(hh + 1) * 64],
                                                                     start=False, stop=(hp == 1)),
                              reads=[b("scT%d_%d" % (cs, hh % 2)), rv], writes=[b("pOo%d" % cs)])
                yield
                if has_next:
                    for bk in range(4):
                        st.op("pe", lambda h, bk=bk: h.matmul(pU[:, 0, bk, :], lhsT=kitok[:, cs, bk * 128:(bk + 1) * 128], rhs=v_c[:, bk * 128:(bk + 1) * 128],
                                                              start=True, stop=True),
                              reads=[b("kitok%d" % cs), rv], writes=[b("pU0")])
                    cgn = seq * NCs + order[pos + 1]
                    Abc = bc(scal[:, d, 0, :, cg:cg + 1], [128, 4, 128])
                    Cbc = bc(scal[:, d, 2, :, cgn:cgn + 1], [128, 4, 128])
                    st.op("pool", lambda h: h.tensor_tensor(out=t1, in0=S, in1=Abc, op=ALU.mult), reads=[bS, b("scal")], writes=[bt1])
                    st.op("dve", lambda h: h.tensor_tensor(out=t2, in0=pU[:, 0], in1=Bm, op=ALU.mult), reads=[b("pU0"), b("Bm%d" % seq)], writes=[bt2])
                    st.op("dve", lambda h: h.tensor_tensor(out=S, in0=t1, in1=t2, op=ALU.add), reads=[bt1, bt2], writes=[bS])
                    st.op("dve", lambda h: h.tensor_tensor(out=Sp, in0=S, in1=Cbc, op=ALU.mult), reads=[bS, b("scal")], writes=[bSp])
                yield
                if d == 0:
                    oft, ofb = ring_of.get()
                    st.op("act", lambda h: h.activation(out=oft, in_=pOo[:, cs, :], func=AF.Copy), reads=[b("pOo%d" % cs)], writes=[ofb])
                    st.dma(lambda h: h.dma_start(out=of_s[t0:t0 + 128, :], in_=oft), reads=[ofb], writes=[b("ofs%d" % cg)])
                else:
                    st.op("dve", lambda h: h.tensor_tensor(out=osum, in0=pOo[:, cs, :], in1=oft, op=ALU.add), reads=[b("pOo%d" % cs), ofb], writes=[bos])
                    st.op("act", lambda h: h.activation(out=sqt, in_=osum, func=AF.Square), reads=[bos], writes=[bsq])
                    st.op("dve", lambda h: h.tensor_reduce(out=ss8, in_=sqt.rearrange("p (h e) -> p h e", e=64), axis=AX.X, op=ALU.add),
                          reads=[bsq], writes=[bss])
                    st.op("act", lambda h: h.activation(out=ln8, in_=ss8, func=AF.Ln, scale=1.0 / 64, bias=EPS), reads=[bss], writes=[bln])
                    st.op("act", lambda h: h.activation(out=rs8, in_=ln8, func=AF.Exp, scale=-0.5), reads=[bln], writes=[brs])
                    st.op("dve", lambda h: h.tensor_tensor(out=yn.rearrange("p (h e) -> p h e", e=64), in0=osum.rearrange("p (h e) -> p h e", e=64),
                                                           in1=bc(rs8.unsqueeze(2), [128, 8, 64]), op=ALU.mult),
                          reads=[bos, brs], writes=[byn])
                    st.op("pool", lambda h: h.tensor_tensor(out=ya[:, cs, :], in0=yn, in1=gtt, op=ALU.mult), reads=[byn, gtb], writes=[b("ya%d" % cs)])
                    for bk in range(4):
                        st.op("pe", lambda h, bk=bk: h.transpose(out=pY[:, bk, :], in_=ya[:, cs, bk * 128:(bk + 1) * 128], identity=ident[:]),
                              reads=[b("ya%d" % cs), b("ident")], writes=[b("pY")])
                    st.op("act", lambda h: h.activation(out=yaT[:, cs], in_=pY[:, 0:4, :], func=AF.Copy), reads=[b("pY")], writes=[b("yaT%d" % cs)])
                    st.dma(lambda h: h.dma_start(out=yT_s[0:4, :, t0:t0 + 128].rearrange("k p t -> p k t"), in_=yaT[:, cs]), reads=[b("yaT%d" % cs)])
                yield

            for d in range(2):
                gens = [run_dir(seq, d) for seq in range(n_seq)]
                live = list(gens)
                while live:
                    for g in list(live):
                        try:
                            next(g)
                        except StopIteration:
                            live.remove(g)
            st.finish()
            st.emit()

    def stage4():
        TT = 256
        NTL = T // TT
        with contextlib.ExitStack() as es:
            def sb(name, shape, dt=F32):
                return es.enter_context(nc.sbuf_tensor("s4_" + name, list(shape), dt))

            def ps(name, shape, dt=F32):
                return es.enter_context(nc.psum_tensor("s4_" + name, list(shape), dt))

            st = Stage(nc, "s4")
            wo = sb("wo", [128, 8, D], BF16)
            wg = sb("wg", [128, 8, DFF], BF16)
            wu = sb("wu", [128, 8, DFF], BF16)
            wd = sb("wd", [128, NFB, D], BF16)
            goutt = sb("goutt", [128, 8]); g2t = sb("g2t", [128, 8])
            gfb = sb("gfb", [128, D])
            ident = sb("ident", [128, 128], BF16)
            xt = sb("xt", [128, 2, 2, D])
            yT = sb("yT", [128, 2, 8, TT], BF16)
            h2n = sb("h2n", [128, 2, D], BF16)
            h2T = sb("h2T", [128, 8, TT], BF16)
            aT = sb("aT", [128, NFB, TT], BF16)
            sg = sb("sg", [128, 2, TT])
            junk = sb("junk", [128, D], BF16)
            ss = sb("ss", [128, 4]); lnt = sb("lnt", [128, 4]); rs = sb("rs", [128, 4])
            pxt = ps("pxt", [128, 8, 128], BF16)
            pG = ps("pG", [128, 2, 512])
            pUu = ps("pUu", [128, 2, 512])
            pA = ps("pA", [128, 2, 512])
            B = {}

            def b(name):
                if name not in B:
                    B[name] = Buf(name)
                return B[name]
            ring_A = Ring(pA, 2, "pA")
            for (dst, src, nm) in ((goutt, gout, "goutt"), (g2t, g2, "g2t"), (ident, c_ident, "ident")):
                st.dma(lambda h, dst=dst, src=src: h.dma_start(out=dst[:], in_=src), writes=[b(nm)])
            if os.environ.get("S4_NOBC"):
                for pp in range(0, 128, 32):
                    pass
                st.op("pool", lambda h: h.memset(gfb[:], 1.0), writes=[b("gfb")])
            else:
                st.dma(lambda h: h.dma_start(out=gfb[:], in_=gfin.partition_broadcast(128)), writes=[b("gfb")])
            stg = xt[:].rearrange("p a b d -> p (a b) d")
            pcnt = [0]

            def prep(dst3, src3, nchunk, ncols, gain, name, gname):
                for c in range(nchunk):
                    for c0 in range(0, ncols, 1024):
                        c1 = min(ncols, c0 + 1024)
                        slot = pcnt[0] % 4
                        eng = (("dve", "pool") if os.environ.get("S4_NOACT") else ("dve", "pool", "act"))[pcnt[0] % (2 if os.environ.get("S4_NOACT") else 3)]
                        pcnt[0] += 1
                        sbuf = b("xt%d" % slot)
                        st.dma(lambda h, c=c, slot=slot, c0=c0, c1=c1: h.dma_start(out=stg[:, slot, 0:c1 - c0], in_=src3[:, c, c0:c1]), writes=[sbuf])
                        rd = [sbuf] + ([b(gname)] if gain is not None else [])
                        if eng == "act":
                            if gain is None:
                                st.op("act", lambda h, c=c, slot=slot, c0=c0, c1=c1: h.activation(out=dst3[:, c, c0:c1], in_=stg[:, slot, 0:c1 - c0], func=AF.Copy),
                                      reads=rd, writes=[b(name)])
                            else:
                                st.op("act", lambda h, c=c, slot=slot, c0=c0, c1=c1: h.activation(out=dst3[:, c, c0:c1], in_=stg[:, slot, 0:c1 - c0], func=AF.Copy,
                                                                                               scale=gain[:, c:c + 1]),
                                      reads=rd, writes=[b(name)])
                        else:
                            if gain is None:
                                st.op(eng, lambda h, c=c, slot=slot, c0=c0, c1=c1: h.tensor_copy(out=dst3[:, c, c0:c1], in_=stg[:, slot, 0:c1 - c0]),
                                      reads=rd, writes=[b(name)])
                            else:
                                st.op(eng, lambda h, c=c, slot=slot, c0=c0, c1=c1: h.tensor_scalar(out=dst3[:, c, c0:c1], in0=stg[:, slot, 0:c1 - c0],
                                                                                                scalar1=gain[:, c:c + 1], scalar2=None, op0=ALU.mult),
                                      reads=rd, writes=[b(name)])
            prep(wo, w_out, 8, D, goutt, "wo", "goutt")
            prep(wg, w_gate, 8, DFF, g2t, "wg", "g2t")
            prep(wu, w_up, 8, DFF, g2t, "wu", "g2t")
            prep(wd, w_down, NFB, D, None, "wd", None)

            def loads(i):
                slot = i % 2
                t0 = i * TT
                st.dma(lambda h: h.dma_start(out=xt[:, slot], in_=x[t0:t0 + TT, :].rearrange("(s p) d -> p s d", p=128)),
                       writes=[b("xt%d" % (slot * 2)), b("xt%d" % (slot * 2 + 1))])
                st.dma(lambda h: h.dma_start(out=yT[:, slot], in_=yT_s[:, :, t0:t0 + TT].rearrange("c p t -> p c t")), writes=[b("yT%d" % slot)])

            def rms_rstd(i, sub, which):
                slot = i % 2
                col = which * 2 + sub
                xb = b("xt%d" % (slot * 2 + sub))
                st.op("act", lambda h: h.activation(out=junk[:], in_=xt[:, slot, sub, :], func=AF.Square, accum_out=ss[:, col:col + 1]),
                      reads=[xb], writes=[b("junk"), b("ss%d" % col)])
                st.op("act", lambda h: h.activation(out=lnt[:, col:col + 1], in_=ss[:, col:col + 1], func=AF.Ln, scale=1.0 / D, bias=EPS),
                      reads=[b("ss%d" % col)], writes=[b("ln%d" % col)])
                st.op("act", lambda h: h.activation(out=rs[:, col:col + 1], in_=lnt[:, col:col + 1], func=AF.Exp, scale=-0.5),
                      reads=[b("ln%d" % col)], writes=[b("rs%d" % col)])
                return col

            def tile(i):
                slot = i % 2
                t0 = i * TT
                if i + 1 < NTL:
                    loads(i + 1)
                for sub in range(2):
                    xb = b("xt%d" % (slot * 2 + sub))
                    for half in range(2):
                        pt, pb = ring_A.get()
                        for c in range(8):
                            st.op("pe", lambda h, c=c, pt=pt, sub=sub, half=half: h.matmul(pt, lhsT=yT[:, slot, c, sub * 128:(sub + 1) * 128],
                                                                                          rhs=wo[:, c, half * 512:(half + 1) * 512], start=(c == 0), stop=(c == 7)),
                                  reads=[b("yT%d" % slot), b("wo")], writes=[pb])
                        st.op("dve", lambda h, pt=pt, sub=sub, half=half: h.tensor_tensor(out=xt[:, slot, sub, half * 512:(half + 1) * 512],
                                                                                         in0=xt[:, slot, sub, half * 512:(half + 1) * 512], in1=pt, op=ALU.add),
                              reads=[pb, xb], writes=[xb])
                for sub in range(2):
                    xb = b("xt%d" % (slot * 2 + sub))
                    col = rms_rstd(i, sub, 0)
                    st.op("dve", lambda h, sub=sub, col=col: h.tensor_scalar(out=h2n[:, sub, :], in0=xt[:, slot, sub, :], scalar1=rs[:, col:col + 1],
                                                                            scalar2=None, op0=ALU.mult),
                          reads=[xb, b("rs%d" % col)], writes=[b("h2n%d" % sub)])
                    for c in range(8):
                        st.op("pe", lambda h, sub=sub, c=c: h.transpose(out=pxt[:, c, :], in_=h2n[:, sub, c * 128:(c + 1) * 128], identity=ident[:]),
                              reads=[b("h2n%d" % sub), b("ident")], writes=[b("pxt")])
                    st.op("dve", lambda h, sub=sub: h.tensor_copy(out=h2T[:, :, sub * 128:(sub + 1) * 128], in_=pxt[:]), reads=[b("pxt")], writes=[b("h2T")])
                for fb in range(NFB):
                    gs = fb % 2
                    for c in range(8):
                        st.op("pe", lambda h, c=c, fb=fb, gs=gs: h.matmul(pG[:, gs, 0:TT], lhsT=wg[:, c, fb * 128:(fb + 1) * 128], rhs=h2T[:, c, :],
                                                                          start=(c == 0), stop=(c == 7)),
                              reads=[b("wg"), b("h2T")], writes=[b("pG%d" % gs)])
                    for c in range(8):
                        st.op("pe", lambda h, c=c, fb=fb, gs=gs: h.matmul(pUu[:, gs, 0:TT], lhsT=wu[:, c, fb * 128:(fb + 1) * 128], rhs=h2T[:, c, :],
                                                                          start=(c == 0), stop=(c == 7)),
                              reads=[b("wu"), b("h2T")], writes=[b("pU%d" % gs)])
                    st.op("act", lambda h, gs=gs: h.activation(out=sg[:, gs, :], in_=pG[:, gs, 0:TT], func=AF.Silu), reads=[b("pG%d" % gs)], writes=[b("sg%d" % gs)])
                    st.op("dve", lambda h, gs=gs, fb=fb: h.tensor_tensor(out=aT[:, fb, :], in0=sg[:, gs, :], in1=pUu[:, gs, 0:TT], op=ALU.mult),
                          reads=[b("sg%d" % gs), b("pU%d" % gs)], writes=[b("aT")])
                for sub in range(2):
                    xb = b("xt%d" % (slot * 2 + sub))
                    for half in range(2):
                        pt, pb = ring_A.get()
                        for fb in range(NFB):
                            st.op("pe", lambda h, fb=fb, pt=pt, sub=sub, half=half: h.matmul(pt, lhsT=aT[:, fb, sub * 128:(sub + 1) * 128],
                                                                                            rhs=wd[:, fb, half * 512:(half + 1) * 512], start=(fb == 0), stop=(fb == NFB - 1)),
                                  reads=[b("aT"), b("wd")], writes=[pb])
                        st.op("dve", lambda h, pt=pt, sub=sub, half=half: h.tensor_tensor(out=xt[:, slot, sub, half * 512:(half + 1) * 512],
                                                                                         in0=xt[:, slot, sub, half * 512:(half + 1) * 512], in1=pt, op=ALU.add),
                              reads=[pb, xb], writes=[xb])
                for sub in range(2):
                    xb = b("xt%d" % (slot * 2 + sub))
                    col = rms_rstd(i, sub, 1)
                    st.op("dve", lambda h, sub=sub, col=col: h.scalar_tensor_tensor(out=xt[:, slot, sub, :], in0=xt[:, slot, sub, :], scalar=rs[:, col:col + 1],
                                                                                   in1=gfb[:], op0=ALU.mult, op1=ALU.mult),
                          reads=[xb, b("rs%d" % col), b("gfb")], writes=[xb])
                    st.dma(lambda h, sub=sub: h.dma_start(out=out[t0 + sub * 128:t0 + (sub + 1) * 128, :], in_=xt[:, slot, sub, :]), reads=[xb])

            loads(0)
            for i in range(NTL):
                tile(i)
            st.finish()
            st.emit()

    if 1 in stages:
        stage1()
    if 2 in stages:
        stage2()
    if 3 in stages:
        stage3()
    if 4 in stages:
        stage4()
    return nc


def _pcn(w, rows):
    n = w.shape[1]
    return np.ascontiguousarray(w.reshape(rows // 128, 128, n).transpose(1, 0, 2))


def _pc(g):
    return np.ascontiguousarray(g.reshape(-1, 128).T)


def layout_inputs(inp, L):
    f32 = np.float32
    w_in = np.asarray(inp["w_in"][0], f32)
    hq, hi, hff, hfb, hg, cq, ckv, kr = np.split(w_in, np.cumsum([512, 512, 512, 512, 512, 384, 256])[:], axis=1)
    krot = np.concatenate([kr[:, 32:64], kr[:, 0:32]], axis=1)
    w1 = np.concatenate([hq, hff, hfb, hi, hg, cq, ckv, kr, kr, krot, krot], axis=1)
    assert w1.shape[1] == W1C
    wqb = np.asarray(inp["w_q_b"][0], f32).reshape(384, 4, 192)
    nope = wqb[:, :, 0:128].reshape(384, 512)
    rp = wqb[:, :, 128:192]
    rope = rp.reshape(384, 256)
    rot = np.concatenate([rp[:, :, 32:64], rp[:, :, 0:32]], axis=2).reshape(384, 256)
    wq = np.concatenate([nope, rope, rot], axis=1)
    wkvb = np.asarray(inp["w_kv_b"][0], f32).reshape(256, 4, 256)
    wkv = np.concatenate([wkvb[:, :, 0:128].reshape(256, 512), wkvb[:, :, 128:256].reshape(256, 512)], axis=1)
    lbl = np.asarray(inp["lb_logits"], f32)
    lbl_l = np.ascontiguousarray(lbl.reshape(2, 2, 4, 128).transpose(3, 0, 1, 2))
    gout = np.concatenate([np.asarray(inp["hgrn_norm_g"][0], f32), np.asarray(inp["mla_norm_g"][0], f32)])
    inv = 1.0 / (10000.0 ** (np.arange(0, 64, 2, dtype=np.float32) / 64.0))
    ang = np.arange(L, dtype=np.float32)[None, :] * inv[:, None].astype(np.float32)
    cos = np.cos(ang).astype(f32)
    sin = np.sin(ang).astype(f32)
    c_cos = np.ascontiguousarray(np.tile(cos, (4, 1)))
    c_sin = np.ascontiguousarray(np.tile(sin, (4, 1)))
    rmask = np.ones((128, 512), f32)
    rmask[:, 0::128] = 0.0
    jj = np.arange(128)[:, None]
    ii = np.arange(128)[None, :]
    d = {
        "w_in": _pcn(w1, 1024), "g1": _pc(np.asarray(inp["norm1_g"][0], f32)), "lbl": lbl_l,
        "w_qb": _pcn(wq, 384), "gqa": _pc(np.asarray(inp["q_a_norm_g"][0], f32)),
        "w_kvb": _pcn(wkv, 256), "gkva": _pc(np.asarray(inp["kv_a_norm_g"][0], f32)),
        "w_out": _pcn(np.asarray(inp["w_out"][0], f32), 1024), "gout": _pc(gout),
        "w_gate": _pcn(np.asarray(inp["w_gate"][0], f32), 1024), "w_up": _pcn(np.asarray(inp["w_up"][0], f32), 1024),
        "g2": _pc(np.asarray(inp["norm2_g"][0], f32)),
        "w_down": _pcn(np.asarray(inp["w_down"][0], f32), DFF),
        "gfin": np.asarray(inp["final_norm_g"], f32).reshape(1, D),
        "c_ident": np.eye(128).astype(ml_dtypes.bfloat16), "c_ones": np.ones((128, 128), ml_dtypes.bfloat16),
        "c_cos": c_cos, "c_sin": c_sin, "c_rmask": rmask,
        "c_maskf": (jj <= ii).astype(f32), "c_maskb": (jj >= ii).astype(f32),
    }
    return d


_NC_CACHE = {}


def kernel(**inputs):
    x = np.asarray(inputs["x"], np.float32)
    Bt, L, _ = x.shape
    n_seq = Bt // NCORES
    key = (n_seq, L)
    if key not in _NC_CACHE:
        _NC_CACHE[key] = build_nc(n_seq, L)
    nc = _NC_CACHE[key]
    shared = layout_inputs(inputs, L)
    in_maps = []
    for c in range(NCORES):
        m = dict(shared)
        m["x"] = np.ascontiguousarray(x[c * n_seq:(c + 1) * n_seq].reshape(n_seq * L, D))
        in_maps.append(m)
    res = run_bass_kernel_spmd(nc, in_maps, core_ids=list(range(NCORES)))
    out = np.stack([r["out"].reshape(n_seq, L, D) for r in res.results], axis=0)
    return out.reshape(Bt, L, D).astype(np.float32)
```

```python
import contextlib
import os
import numpy as np
import ml_dtypes
import concourse.bass as bass
import concourse.mybir as mybir
from concourse.bass_utils import run_bass_kernel_spmd

F32 = mybir.dt.float32
BF16 = mybir.dt.bfloat16
AF = mybir.ActivationFunctionType
ALU = mybir.AluOpType
AX = mybir.AxisListType

D = 1024
DFF = 2816
NFB = DFF // 128
EPS = 1e-6
NCORES = 8
W1C = 3456
SCALE = 192 ** -0.5


class Buf:
    __slots__ = ("name", "w", "r", "x")

    def __init__(self, name=""):
        self.name = name
        self.w = None
        self.r = {}
        self.x = len(name) > 1 and name[0] == "p" and (name[1].isupper() or name.startswith(("pmm", "pxt")))


class Stage:
    ENGS = ("pe", "act", "dve", "pool", "sp")

    def __init__(self, nc, name, n_dma_sems=16):
        self.nc = nc
        self.name = name
        self.ops = {e: [] for e in self.ENGS}
        self.cnt = {e: 0 for e in ("pe", "act", "dve", "pool")}
        self.waited = {e: {} for e in self.ENGS}
        self.n_dma = n_dma_sems
        self.dma_cnt = [0] * n_dma_sems
        self.dma_rr = 0
        self.sems = {}

    def _need(self, eng, ev, waits):
        if ev is None:
            return
        key, val = ev
        if key == "pe" and eng == "pe":
            return
        if self.waited[eng].get(key, 0) >= val:
            return
        self.waited[eng][key] = val
        waits.append((key, val))

    def _deps(self, eng, reads, writes):
        waits = []
        for b in reads:
            self._need(eng, b.w, waits)
            if b.x:
                for k, v in b.r.items():
                    if k != eng:
                        self._need(eng, (k, v), waits)
        for b in writes:
            self._need(eng, b.w, waits)
            for k, v in b.r.items():
                self._need(eng, (k, v), waits)
        return waits

    def _commit(self, ev, reads, writes):
        k, v = ev
        for b in reads:
            if b.r.get(k, 0) < v:
                b.r[k] = v
        for b in writes:
            b.w = ev
            b.r = {}

    def op(self, eng, fn, reads=(), writes=()):
        waits = self._deps(eng, reads, writes)
        self.cnt[eng] += 1
        ev = (eng, self.cnt[eng])
        self.ops[eng].append((waits, fn, (eng, 1)))
        self._commit(ev, reads, writes)
        return ev

    def dma(self, fn, reads=(), writes=(), queue="sp"):
        waits = self._deps(queue, reads, writes)
        k = self.dma_rr
        self.dma_rr = (self.dma_rr + 1) % self.n_dma
        key = "dma%d" % k
        if self.dma_cnt[k] > 0:
            self._need(queue, (key, self.dma_cnt[k]), waits)
        self.dma_cnt[k] += 16
        ev = (key, self.dma_cnt[k])
        self.ops[queue].append((waits, fn, (key, 16)))
        self._commit(ev, reads, writes)
        return ev

    def finish(self, eng="sp"):
        waits = []
        for k in range(self.n_dma):
            if self.dma_cnt[k] > 0:
                self._need(eng, ("dma%d" % k, self.dma_cnt[k]), waits)
        if waits:
            self.ops[eng].append((waits, None, None))

    def emit(self):
        nc = self.nc
        with contextlib.ExitStack() as st:
            for e in ("pe", "act", "dve", "pool"):
                self.sems[e] = st.enter_context(nc.semaphore("%s_%s" % (self.name, e)))
            for k in range(self.n_dma):
                self.sems["dma%d" % k] = st.enter_context(nc.semaphore("%s_d%d" % (self.name, k)))
            block = st.enter_context(nc.Block())
            sems = self.sems

            def run(h, lst):
                for waits, fn, inc in lst:
                    for key, val in waits:
                        h.wait_ge(sems[key], val)
                    if fn is not None:
                        fn(h).then_inc(sems[inc[0]], inc[1])

            if self.ops["sp"]:
                @block.sync
                def _(h):
                    run(h, self.ops["sp"])
            if self.ops["pe"]:
                @block.tensor
                def _(h):
                    run(h, self.ops["pe"])
            if self.ops["act"]:
                @block.scalar
                def _(h):
                    run(h, self.ops["act"])
            if self.ops["dve"]:
                @block.vector
                def _(h):
                    run(h, self.ops["dve"])
            if self.ops["pool"]:
                @block.gpsimd
                def _(h):
                    run(h, self.ops["pool"])


class Ring:
    def __init__(self, tens, n, name):
        self.t = tens
        self.n = n
        self.i = 0
        self.bufs = [Buf("%s%d" % (name, k)) for k in range(n)]

    def get(self):
        k = self.i
        self.i = (self.i + 1) % self.n
        return self.t[:, k], self.bufs[k]


def bc(ap, shape):
    return ap.to_broadcast(shape)


def build_nc(n_seq, L, debug=False, stages=(1, 2, 3, 4)):
    T = n_seq * L
    NT = T // 128
    NS = T // 512
    NCH = T // 128
    nc = bass.Bass("TRN2", target_bir_lowering=False)

    def din(name, shape, dt=F32):
        return nc.dram_tensor(name, list(shape), dt, kind="ExternalInput").ap()

    skind = "ExternalOutput" if debug else "Internal"

    def dscr(name, shape, dt):
        return nc.dram_tensor(name, list(shape), dt, kind=skind).ap()

    x = din("x", [T, D])
    out = nc.dram_tensor("out", [T, D], F32, kind="ExternalOutput").ap()
    w_in = din("w_in", [128, 8, W1C])
    g1 = din("g1", [128, 8])
    lbl = din("lbl", [128, 2, 2, 4])
    w_qb = din("w_qb", [128, 3, 1024])
    gqa = din("gqa", [128, 3])
    w_kvb = din("w_kvb", [128, 2, 1024])
    gkva = din("gkva", [128, 2])
    w_out = din("w_out", [128, 8, D])
    gout = din("gout", [128, 8])
    w_gate = din("w_gate", [128, 8, DFF])
    w_up = din("w_up", [128, 8, DFF])
    g2 = din("g2", [128, 8])
    w_down = din("w_down", [128, NFB, D])
    gfin = din("gfin", [1, D])
    c_ident = din("c_ident", [128, 128], BF16)
    c_ones = din("c_ones", [128, 128], BF16)
    c_cos = din("c_cos", [128, L])
    c_sin = din("c_sin", [128, L])
    c_rmask = din("c_rmask", [128, 512])
    c_maskf = din("c_maskf", [128, 128])
    c_maskb = din("c_maskb", [128, 128])

    qdT_s = dscr("qdT_s", [2, 4, 128, T], BF16)
    kiT_s = dscr("kiT_s", [2, 4, 128, T], BF16)
    scal_s = dscr("scal_s", [128, 2, 3, 4, NCH], F32)
    v_s = dscr("v_s", [T, 512], BF16)
    g_s = dscr("g_s", [T, 512], F32)
    qnT_s = dscr("qnT_s", [4, 128, T], BF16)
    qrT_s = dscr("qrT_s", [2, 128, T], BF16)
    knT_s = dscr("knT_s", [4, 128, T], BF16)
    krT_s = dscr("krT_s", [128, T], BF16)
    vm_s = dscr("vm_s", [T, 512], BF16)
    yT_s = dscr("yT_s", [8, 128, T], BF16)
    of_s = dscr("of_s", [T, 512], F32)

    def stage1():
        with contextlib.ExitStack() as es:
            def sb(name, shape, dt=F32):
                return es.enter_context(nc.sbuf_tensor("s1_" + name, list(shape), dt))

            def ps(name, shape, dt=F32):
                return es.enter_context(nc.psum_tensor("s1_" + name, list(shape), dt))

            st = Stage(nc, "s1")
            w1 = sb("w1", [128, 8, W1C], BF16)
            wq = sb("wq", [128, 3, 1024], BF16)
            wkv = sb("wkv", [128, 2, 1024], BF16)
            stg = sb("stg", [128, 2, 1024], F32)
            g1t = sb("g1t", [128, 8]); gqat = sb("gqat", [128, 3]); gkvat = sb("gkvat", [128, 2])
            lblt = sb("lblt", [128, 2, 2, 4])
            lbt = sb("lbt", [128, 2, 4]); omlt = sb("omlt", [128, 2, 4])
            fa = sb("fa", [128, 2, 4]); fb_ = sb("fb", [128, 2, 4]); nfb = sb("nfb", [128, 2, 4]); lnfb = sb("lnfb", [128, 2, 4])
            ident = sb("ident", [128, 128], BF16)
            rmask = sb("rmask", [128, 512])
            xt = sb("xt", [128, 4, D]); xjunk = sb("xjunk", [128, 2, D], BF16)
            xn = sb("xn", [128, 2, D], BF16)
            hT = sb("hT", [128, 1, 8, 512], BF16)
            ssx = sb("ssx", [128, 2, 4]); lnx = sb("lnx", [128, 2, 4]); rsx = sb("rsx", [128, 2, 4])
            cst = sb("cst", [128, 2, 512]); snt = sb("snt", [128, 2, 512])
            qT = sb("qT", [128, 4, 512])
            th = sb("th", [128, 8, 512])
            tmpf = sb("tmpf", [128, 8, 512])
            tmpb = sb("tmpb", [128, 8, 512], BF16)
            gt = sb("gt", [128, 4, 512])
            cqf = sb("cqf", [128, 4, 640]); cqn = sb("cqn", [128, 4, 640], BF16)
            ssq = sb("ssq", [128, 8]); lnq = sb("lnq", [128, 8]); rsq = sb("rsq", [128, 8])
            cqT = sb("cqT", [128, 3, 512], BF16); ckvT = sb("ckvT", [128, 2, 512], BF16)
            scal = sb("scal", [128, 2, 3, 4, NCH])
            pxt = ps("pxt", [128, 2, 8, 128], BF16)
            pmm = ps("pmm", [128, 6, 512])

            B = {}
            def b(name):
                if name not in B:
                    B[name] = Buf(name)
                return B[name]

            ring_f = Ring(tmpf, 8, "tmpf")
            ring_b = Ring(tmpb, 8, "tmpb")
            ring_p = Ring(pmm, 6, "pmm")
            ring_g = Ring(gt, 2, "gt")
            pxb = [Buf("pxt0"), Buf("pxt1")]
            hTb = [Buf("hT0"), Buf("hT1")]
            xtb = [Buf("xt%d" % i) for i in range(4)]
            xnb = [Buf("xn0"), Buf("xn1")]

            for (dst, src, nm) in ((g1t, g1, "g1t"), (gqat, gqa, "gqat"), (gkvat, gkva, "gkvat"),
                                   (lblt, lbl, "lblt"), (ident, c_ident, "ident"), (rmask, c_rmask, "rmask")):
                st.dma(lambda h, dst=dst, src=src: h.dma_start(out=dst[:], in_=src), writes=[b(nm)])
            st.op("dve", lambda h: h.tensor_tensor(out=lbt[:], in0=lblt[:, :, 1, :], in1=lblt[:, :, 0, :], op=ALU.subtract),
                  reads=[b("lblt")], writes=[b("lbt")])
            st.op("act", lambda h: h.activation(out=lbt[:], in_=lbt[:], func=AF.Exp), reads=[b("lbt")], writes=[b("lbt")])
            st.op("dve", lambda h: h.tensor_scalar(out=lbt[:], in0=lbt[:], scalar1=1.0, scalar2=None, op0=ALU.add),
                  reads=[b("lbt")], writes=[b("lbt")])
            st.op("dve", lambda h: h.reciprocal(out=lbt[:], in_=lbt[:]), reads=[b("lbt")], writes=[b("lbt")])
            st.op("dve", lambda h: h.tensor_scalar(out=omlt[:], in0=lbt[:], scalar1=-1.0, scalar2=1.0, op0=ALU.mult, op1=ALU.add),
                  reads=[b("lbt")], writes=[b("omlt")])
            st.op("dve", lambda h: h.tensor_scalar(out=fb_[:], in0=omlt[:], scalar1=0.5, scalar2=None, op0=ALU.mult),
                  reads=[b("omlt")], writes=[b("fb")])
            st.op("dve", lambda h: h.tensor_tensor(out=fa[:], in0=lbt[:], in1=fb_[:], op=ALU.add),
                  reads=[b("lbt"), b("fb")], writes=[b("fa")])
            st.op("dve", lambda h: h.tensor_scalar(out=nfb[:], in0=fb_[:], scalar1=-1.0, scalar2=None, op0=ALU.mult),
                  reads=[b("fb")], writes=[b("nfb")])
            st.op("act", lambda h: h.activation(out=lnfb[:], in_=fb_[:], func=AF.Ln), reads=[b("fb")], writes=[b("lnfb")])

            pcnt = [0]

            def prep(dst3, src3, nchunk, ncols, gain, eng_cycle, name):
                for c in range(nchunk):
                    for c0 in range(0, ncols, 1024):
                        c1 = min(ncols, c0 + 1024)
                        slot = pcnt[0] % 2
                        eng = eng_cycle[pcnt[0] % len(eng_cycle)]
                        pcnt[0] += 1
                        sbuf = b("stg%d" % slot)
                        st.dma(lambda h, c=c, slot=slot, c0=c0, c1=c1: h.dma_start(out=stg[:, slot, 0:c1 - c0], in_=src3[:, c, c0:c1]),
                               writes=[sbuf])
                        st.op(eng, lambda h, c=c, slot=slot, c0=c0, c1=c1: h.tensor_scalar(
                            out=dst3[:, c, c0:c1], in0=stg[:, slot, 0:c1 - c0], scalar1=gain[:, c:c + 1], scalar2=0.0, op0=ALU.mult, op1=ALU.add),
                            reads=[sbuf, b(name + "_g")], writes=[b(name)])
            B["w1_g"] = b("g1t"); B["wq_g"] = b("gqat"); B["wkv_g"] = b("gkvat")
            prep(w1, w_in, 8, W1C, g1t, ["dve", "pool"], "w1")
            prep(wq, w_qb, 3, 1024, gqat, ["dve", "pool"], "wq")
            prep(wkv, w_kvb, 2, 1024, gkvat, ["dve", "pool"], "wkv")
            v1 = w1[:, :, 3328:3456].rearrange("p c (g r) -> p c g r", r=64)[:, :, :, 0:32]
            st.op("dve", lambda h: h.tensor_scalar(out=v1, in0=v1, scalar1=-1.0, scalar2=None, op0=ALU.mult),
                  reads=[b("w1")], writes=[b("w1")])
            v2 = wq[:, :, 768:1024].rearrange("p c (g r) -> p c g r", r=64)[:, :, :, 0:32]
            st.op("dve", lambda h: h.tensor_scalar(out=v2, in0=v2, scalar1=-1.0, scalar2=None, op0=ALU.mult),
                  reads=[b("wq")], writes=[b("wq")])

            w1b = b("w1")

            def mm_group(lhs_fn, rhs_fn, nk, reads, nfree=512):
                pt, pb = ring_p.get()
                pt = pt[:, 0:nfree]
                for c in range(nk):
                    st.op("pe", lambda h, c=c, pt=pt: h.matmul(pt, lhsT=lhs_fn(c), rhs=rhs_fn(c), start=(c == 0), stop=(c == nk - 1)),
                          reads=reads, writes=[pb])
                return pt, pb

            def xload(s):
                for j in range(4):
                    t0 = s * 512 + j * 128
                    st.dma(lambda h, j=j, t0=t0: h.dma_start(out=xt[:, j, :], in_=x[t0:t0 + 128, :]), writes=[xtb[j]])
                    st.op("act", lambda h, j=j: h.activation(out=xjunk[:, j % 2, :], in_=xt[:, j, :], func=AF.Square, accum_out=ssx[:, 0, j:j + 1]),
                          reads=[xtb[j]], writes=[b("ssx%d" % j), b("xjunk%d" % (j % 2))])

            def xnorm(s):
                st.op("act", lambda h: h.activation(out=lnx[:, 0, :], in_=ssx[:, 0, :], func=AF.Ln, scale=1.0 / D, bias=EPS),
                      reads=[b("ssx%d" % j) for j in range(4)], writes=[b("lnx")])
                st.op("act", lambda h: h.activation(out=rsx[:, 0, :], in_=lnx[:, 0, :], func=AF.Exp, scale=-0.5), reads=[b("lnx")], writes=[b("rsx")])
                for j in range(4):
                    n = j % 2
                    st.op("dve", lambda h, j=j, n=n: h.tensor_scalar(out=xn[:, n, :], in0=xt[:, j, :], scalar1=rsx[:, 0, j:j + 1], scalar2=None, op0=ALU.mult),
                          reads=[xtb[j], b("rsx")], writes=[xnb[n]])
                    for c in range(8):
                        st.op("pe", lambda h, n=n, c=c: h.transpose(out=pxt[:, n, c, :], in_=xn[:, n, c * 128:(c + 1) * 128], identity=ident[:]),
                              reads=[xnb[n], b("ident")], writes=[pxb[n]])
                    st.op("dve", lambda h, n=n, j=j: h.tensor_copy(out=hT[:, 0, :, j * 128:(j + 1) * 128], in_=pxt[:, n, :, :]),
                          reads=[pxb[n]], writes=[hTb[0]])

            def load_tables(s):
                slot = s % 2
                pos0 = (s * 512) % L
                st.dma(lambda h: h.dma_start(out=cst[:, slot, :], in_=c_cos[:, pos0:pos0 + 512]), writes=[b("cst%d" % slot)])
                st.dma(lambda h: h.dma_start(out=snt[:, slot, :], in_=c_sin[:, pos0:pos0 + 512]), writes=[b("snt%d" % slot)])

            def rope(s, pt1, pb1, pt2, pb2, dst):
                slot = s % 2
                f1, fb1 = ring_f.get()
                f2, fb2 = ring_f.get()
                ot, ob = ring_b.get()
                st.op("dve", lambda h: h.tensor_tensor(out=f1, in0=pt1, in1=cst[:, slot, :], op=ALU.mult), reads=[pb1, b("cst%d" % slot)], writes=[fb1])
                st.op("dve", lambda h: h.tensor_tensor(out=f2, in0=pt2, in1=snt[:, slot, :], op=ALU.mult), reads=[pb2, b("snt%d" % slot)], writes=[fb2])
                st.op("pool", lambda h: h.tensor_tensor(out=ot, in0=f1, in1=f2, op=ALU.add), reads=[fb1, fb2], writes=[ob])
                st.dma(lambda h: h.dma_start(out=dst, in_=ot), reads=[ob])

            def tm_piece(s, p):
                t0 = s * 512
                hs = hTb[0]
                if p == 8:
                    pt1, pb1 = mm_group(lambda c: w1[:, c, 3200:3328], lambda c: hT[:, 0, c, :], 8, [hs, w1b])
                    pt2, pb2 = mm_group(lambda c: w1[:, c, 3328:3456], lambda c: hT[:, 0, c, :], 8, [hs, w1b])
                    rope(s, pt1, pb1, pt2, pb2, krT_s[:, t0:t0 + 512])
                    return
                j = p // 2
                tt = t0 + j * 128
                lhs = lambda c: hT[:, 0, c, j * 128:(j + 1) * 128]
                if p % 2 == 0:
                    pt, pb = mm_group(lhs, lambda c: w1[:, c, 1536:2048], 8, [hs, w1b])
                    ot, ob = ring_b.get()
                    st.op("dve", lambda h: h.tensor_copy(out=ot, in_=pt), reads=[pb], writes=[ob])
                    st.dma(lambda h: h.dma_start(out=v_s[tt:tt + 128, :], in_=ot), reads=[ob])
                    pt2, pb2 = mm_group(lhs, lambda c: w1[:, c, 2048:2560], 8, [hs, w1b])
                    st.op("act", lambda h: h.activation(out=gt[:, j, :], in_=pt2, func=AF.Copy), reads=[pb2], writes=[b("gt%d" % j)])
                else:
                    pt, pb = mm_group(lhs, lambda c: w1[:, c, 2560:2944], 8, [hs, w1b], nfree=384)
                    st.op("act", lambda h: h.activation(out=xjunk[:, 0, 0:384], in_=pt[:, 0:384], func=AF.Square, accum_out=ssq[:, j:j + 1]),
                          reads=[pb], writes=[b("ssq%d" % j), b("xjunk0")])
                    st.op("dve", lambda h: h.tensor_copy(out=cqf[:, j, 0:384], in_=pt[:, 0:384]), reads=[pb], writes=[b("cqf%d" % j)])
                    pt2, pb2 = mm_group(lhs, lambda c: w1[:, c, 2944:3200], 8, [hs, w1b], nfree=256)
                    st.op("act", lambda h: h.activation(out=xjunk[:, 1, 0:256], in_=pt2[:, 0:256], func=AF.Square, accum_out=ssq[:, 4 + j:5 + j]),
                          reads=[pb2], writes=[b("ssq%d" % (4 + j)), b("xjunk1")])
                    st.op("dve", lambda h: h.tensor_copy(out=cqf[:, j, 384:640], in_=pt2[:, 0:256]), reads=[pb2], writes=[b("cqf%d" % j)])

            def fm_hf(s, db):
                col = 512 + db * 128
                pt, pb = mm_group(lambda c: w1[:, c, col:col + 128], lambda c: hT[:, 0, c, :], 8, [hTb[0], w1b])
                st.op("act", lambda h: h.activation(out=th[:, db, :], in_=pt, func=AF.Copy), reads=[pb], writes=[b("th%d" % db)])

            def fm_hq(s, blk):
                pt, pb = mm_group(lambda c: w1[:, c, blk * 128:(blk + 1) * 128], lambda c: hT[:, 0, c, :], 8, [hTb[0], w1b])
                st.op("dve", lambda h: h.tensor_copy(out=qT[:, blk, :], in_=pt), reads=[pb], writes=[b("qT%d" % blk)])

            def act18(s):
                t0 = s * 512
                for blk in range(4):
                    st.op("act", lambda h, blk=blk: h.activation(out=qT[:, blk, :], in_=qT[:, blk, :], func=AF.Silu),
                          reads=[b("qT%d" % blk)], writes=[b("qT%d" % blk)])
                for db in range(8):
                    st.op("act", lambda h, db=db: h.activation(out=th[:, db, :], in_=th[:, db, :], func=AF.Tanh, scale=-0.5),
                          reads=[b("th%d" % db)], writes=[b("th%d" % db)])
                for j in range(4):
                    tt = t0 + j * 128
                    st.op("act", lambda h, j=j: h.activation(out=gt[:, j, :], in_=gt[:, j, :], func=AF.Silu), reads=[b("gt%d" % j)], writes=[b("gt%d" % j)])
                    st.dma(lambda h, j=j, tt=tt: h.dma_start(out=g_s[tt:tt + 128, :], in_=gt[:, j, :]), reads=[b("gt%d" % j)])

            def gate_chain(s, db):
                d, blk = db // 4, db % 4
                t0 = s * 512
                c0 = s * 4
                thb = b("th%d" % db)
                lf, lfb = ring_f.get()
                st.op("pool", lambda h: h.tensor_scalar(out=lf, in0=th[:, db, :], scalar1=nfb[:, d, blk:blk + 1], scalar2=fa[:, d, blk:blk + 1],
                                                        op0=ALU.mult, op1=ALU.add),
                      reads=[thb, b("nfb"), b("fa")], writes=[lfb])
                st.op("act", lambda h: h.activation(out=lf, in_=lf, func=AF.Ln), reads=[lfb], writes=[lfb])
                cumt, cumb = ring_f.get()
                cct, ccb = ring_f.get()
                st.op("dve", lambda h: h.tensor_tensor_scan(out=cumt, data0=rmask[:], data1=lf, initial=0.0, op0=ALU.mult, op1=ALU.add),
                      reads=[lfb, b("rmask")], writes=[cumb])
                cumv = cumt.rearrange("p (c t) -> p c t", t=128)
                ccv = cct.rearrange("p (c t) -> p c t", t=128)
                st.op("dve", lambda h: h.tensor_tensor(out=ccv, in0=cumv, in1=bc(cumv[:, :, 63:64], [128, 4, 128]), op=ALU.subtract),
                      reads=[cumb], writes=[ccb])
                Xi, Yi = (1, 2) if d == 0 else (2, 1)
                st.op("act", lambda h: h.activation(out=scal[:, d, 0, blk, c0:c0 + 4], in_=cumv[:, :, 127], func=AF.Exp), reads=[cumb], writes=[b("scal")])
                st.op("act", lambda h: h.activation(out=scal[:, d, Xi, blk, c0:c0 + 4], in_=ccv[:, :, 127], func=AF.Exp), reads=[ccb], writes=[b("scal")])
                st.op("act", lambda h: h.activation(out=scal[:, d, Yi, blk, c0:c0 + 4], in_=cumv[:, :, 63], func=AF.Exp), reads=[cumb], writes=[b("scal")])
                if d == 0:
                    srct, srcb = cct, ccb
                else:
                    srct, srcb = ring_f.get()
                    st.op("pool", lambda h: h.tensor_tensor(out=srct, in0=lf, in1=cct, op=ALU.subtract), reads=[lfb, ccb], writes=[srcb])
                ea, eab = ring_f.get()
                eb, ebb = ring_f.get()
                st.op("act", lambda h: h.activation(out=ea, in_=srct, func=AF.Exp), reads=[srcb], writes=[eab])
                st.op("act", lambda h: h.activation(out=eb, in_=srct, func=AF.Exp, scale=-1.0, bias=lnfb[:, d, blk:blk + 1]),
                      reads=[srcb, b("lnfb")], writes=[ebb])
                o1, o1b = ring_b.get()
                o2, o2b = ring_b.get()
                st.op("dve", lambda h: h.tensor_tensor(out=o1, in0=qT[:, blk, :], in1=ea, op=ALU.mult), reads=[b("qT%d" % blk), eab], writes=[o1b])
                st.op("dve", lambda h: h.scalar_tensor_tensor(out=o2, in0=th[:, db, :], scalar=1.0, in1=eb, op0=ALU.add, op1=ALU.mult),
                      reads=[thb, ebb], writes=[o2b])
                st.dma(lambda h: h.dma_start(out=qdT_s[d, blk, :, t0:t0 + 512], in_=o1), reads=[o1b])
                st.dma(lambda h: h.dma_start(out=kiT_s[d, blk, :, t0:t0 + 512], in_=o2), reads=[o2b])

            def cq_rstd():
                st.op("act", lambda h: h.activation(out=lnq[:, 0:4], in_=ssq[:, 0:4], func=AF.Ln, scale=1.0 / 384, bias=EPS),
                      reads=[b("ssq%d" % i) for i in range(8)], writes=[b("lnq")])
                st.op("act", lambda h: h.activation(out=lnq[:, 4:8], in_=ssq[:, 4:8], func=AF.Ln, scale=1.0 / 256, bias=EPS),
                      reads=[b("ssq%d" % i) for i in range(8)], writes=[b("lnq")])
                st.op("act", lambda h: h.activation(out=rsq[:], in_=lnq[:], func=AF.Exp, scale=-0.5), reads=[b("lnq")], writes=[b("rsq")])

            def cq_transpose(s):
                for j in range(4):
                    st.op("dve", lambda h, j=j: h.tensor_scalar(out=cqn[:, j, 0:384], in0=cqf[:, j, 0:384], scalar1=rsq[:, j:j + 1], scalar2=None, op0=ALU.mult),
                          reads=[b("cqf%d" % j), b("rsq")], writes=[b("cqn%d" % j)])
                    st.op("dve", lambda h, j=j: h.tensor_scalar(out=cqn[:, j, 384:640], in0=cqf[:, j, 384:640], scalar1=rsq[:, 4 + j:5 + j], scalar2=None, op0=ALU.mult),
                          reads=[b("cqf%d" % j), b("rsq")], writes=[b("cqn%d" % j)])
                for j in range(4):
                    n = j % 2
                    for c in range(5):
                        st.op("pe", lambda h, n=n, c=c, j=j: h.transpose(out=pxt[:, n, c, :], in_=cqn[:, j, c * 128:(c + 1) * 128], identity=ident[:]),
                              reads=[b("cqn%d" % j), b("ident")], writes=[pxb[n]])
                    st.op("dve", lambda h, n=n, j=j: h.tensor_copy(out=cqT[:, :, j * 128:(j + 1) * 128], in_=pxt[:, n, 0:3, :]), reads=[pxb[n]], writes=[b("cqT")])
                    st.op("dve", lambda h, n=n, j=j: h.tensor_copy(out=ckvT[:, :, j * 128:(j + 1) * 128], in_=pxt[:, n, 3:5, :]), reads=[pxb[n]], writes=[b("ckvT")])

            def second_proj(s):
                t0 = s * 512
                out = []

                def qn(hh):
                    pt, pb = mm_group(lambda c: wq[:, c, hh * 128:(hh + 1) * 128], lambda c: cqT[:, c, :], 3, [b("cqT"), b("wq")])
                    ot, ob = ring_b.get()
                    st.op("dve", lambda h: h.tensor_copy(out=ot, in_=pt), reads=[pb], writes=[ob])
                    st.dma(lambda h: h.dma_start(out=qnT_s[hh, :, t0:t0 + 512], in_=ot), reads=[ob])

                def qr_(pr):
                    pt1, pb1 = mm_group(lambda c: wq[:, c, 512 + pr * 128:640 + pr * 128], lambda c: cqT[:, c, :], 3, [b("cqT"), b("wq")])
                    pt2, pb2 = mm_group(lambda c: wq[:, c, 768 + pr * 128:896 + pr * 128], lambda c: cqT[:, c, :], 3, [b("cqT"), b("wq")])
                    rope(s, pt1, pb1, pt2, pb2, qrT_s[pr, :, t0:t0 + 512])

                def kn(hh):
                    pt, pb = mm_group(lambda c: wkv[:, c, hh * 128:(hh + 1) * 128], lambda c: ckvT[:, c, :], 2, [b("ckvT"), b("wkv")])
                    ot, ob = ring_b.get()
                    st.op("act", lambda h: h.activation(out=ot, in_=pt, func=AF.Copy), reads=[pb], writes=[ob])
                    st.dma(lambda h: h.dma_start(out=knT_s[hh, :, t0:t0 + 512], in_=ot), reads=[ob])

                def vv(j):
                    tt = t0 + j * 128
                    pt, pb = mm_group(lambda c: ckvT[:, c, j * 128:(j + 1) * 128], lambda c: wkv[:, c, 512:1024], 2, [b("ckvT"), b("wkv")])
                    ot, ob = ring_b.get()
                    st.op("dve", lambda h: h.tensor_copy(out=ot, in_=pt), reads=[pb], writes=[ob])
                    st.dma(lambda h: h.dma_start(out=vm_s[tt:tt + 128, :], in_=ot), reads=[ob])
                for hh in range(4):
                    out.append(lambda hh=hh: qn(hh))
                for pr in range(2):
                    out.append(lambda pr=pr: qr_(pr))
                for hh in range(4):
                    out.append(lambda hh=hh: kn(hh))
                for j in range(4):
                    out.append(lambda j=j: vv(j))
                return out

            xload(0)
            load_tables(0)
            xnorm(0)
            for p in range(9):
                tm_piece(0, p)
            for blk in range(4):
                fm_hq(0, blk)
            for db in range(8):
                fm_hf(0, db)
            cq_rstd()
            for s in range(NS):
                nxt = s + 1 < NS
                if nxt:
                    xload(s + 1)
                    load_tables(s + 1)
                cq_transpose(s)
                act18(s)
                sp = second_proj(s)
                for f_ in sp[:6]:
                    f_()
                if nxt:
                    xnorm(s + 1)
                rest = sp[6:]
                for db in range(8):
                    gate_chain(s, db)
                    if nxt:
                        tm_piece(s + 1, db)
                        if db == 7:
                            tm_piece(s + 1, 8)
                        fm_hf(s + 1, db)
                        if db >= 4:
                            fm_hq(s + 1, db - 4)
                    if rest:
                        rest.pop(0)()
                while rest:
                    rest.pop(0)()
                if nxt:
                    cq_rstd()
            st.dma(lambda h: h.dma_start(out=scal_s, in_=scal[:]), reads=[b("scal")])
            st.finish()
            st.emit()

    def stage2():
        NK = L // 128
        NQ = L // 512
        DEN = os.environ.get("S2_DEN", "pe")
        ROPE128 = os.environ.get("S2_ROPE", "k128") == "k128"
        NPS = int(os.environ.get("S2_NPS", "4"))
        NP = int(os.environ.get("S2_NP", "6"))
        with contextlib.ExitStack() as es:
            def sb(name, shape, dt=F32):
                return es.enter_context(nc.sbuf_tensor("s2_" + name, list(shape), dt))

            def ps(name, shape, dt=F32):
                return es.enter_context(nc.psum_tensor("s2_" + name, list(shape), dt))

            st = Stage(nc, "s2")
            knT = sb("knT", [128, 4, L], BF16)
            vm = sb("vm", [128, NK, 512], BF16)
            krT = sb("krT", [128, L], BF16)
            qn = sb("qn", [128, 2, 4, 512], BF16)
            qr = sb("qr", [128, 2, 2, 512], BF16)
            ones = sb("ones", [128, 128], BF16)
            Pt = sb("Pt", [128, NP, 512], BF16)
            qrz = sb("qrz", [128, 2, 4, 512], BF16)
            hm = sb("hm", [128, 2])
            accP = sb("accP", [128, 2, 512])
            oT = sb("oT", [128, 2, 4, 512])
            rden = sb("rden", [128, 2, 512])
            sq = sb("sq", [128, 4, 512], BF16)
            lnv = sb("lnv", [128, 512]); rstd = sb("rstd", [128, 512])
            yb = sb("yb", [128, 4, 512], BF16)
            pS = ps("pS", [128, NPS, 512])
            pO = ps("pO", [128, 2, 512])
            pD = ps("pD", [128, 1, 512])
            pSS = ps("pSS", [128, 512])
            acc = sb("acc", [128, 2, 512])
            onesf = sb("onesf", [128, 128])
            B = {}

            def b(name):
                if name not in B:
                    B[name] = Buf(name)
                return B[name]
            ring_P = Ring(Pt, NP, "Pt")
            ring_S = Ring(pS, NPS, "pS")
            ring_y = Ring(yb, 4, "yb")
            st.dma(lambda h: h.dma_start(out=ones[:], in_=c_ones), writes=[b("ones")])
            st.op("pool", lambda h: h.memset(onesf[:], 1.0), writes=[b("onesf")])
            st.op("pool", lambda h: h.memset(hm[:], 0.0), writes=[b("hm")])
            st.op("pool", lambda h: h.memset(hm[0:64, 0:1], 1.0), writes=[b("hm")])
            st.op("pool", lambda h: h.memset(hm[64:128, 1:2], 1.0), writes=[b("hm")])

            items = []
            for seq in range(n_seq):
                for qb in range(NQ):
                    for hh in range(4):
                        for kc in range(NK):
                            items.append((seq, qb, hh, kc))

            def loads_seq(seq):
                s0 = seq * L
                for hh in range(4):
                    st.dma(lambda h, hh=hh: h.dma_start(out=knT[:, hh, :], in_=knT_s[hh, :, s0:s0 + L]), writes=[b("knT")])
                st.dma(lambda h: h.dma_start(out=krT[:], in_=krT_s[:, s0:s0 + L]), writes=[b("krT")])
                st.dma(lambda h: h.dma_start(out=vm[:], in_=vm_s[s0:s0 + L, :].rearrange("(k p) f -> p k f", p=128)), writes=[b("vm")])

            def loads_q(seq, qb):
                slot = (seq * NQ + qb) % 2
                tq = seq * L + qb * 512
                st.dma(lambda h: h.dma_start(out=qn[:, slot], in_=qnT_s[:, :, tq:tq + 512].rearrange("h p t -> p h t")), writes=[b("qn%d" % slot)])
                st.dma(lambda h: h.dma_start(out=qr[:, slot], in_=qrT_s[:, :, tq:tq + 512].rearrange("h p t -> p h t")), writes=[b("qr%d" % slot)])
                if ROPE128:
                    for hh in range(4):
                        st.op("pool", lambda h, hh=hh: h.tensor_scalar(out=qrz[:, slot, hh, :], in0=qr[:, slot, hh // 2, :], scalar1=hm[:, hh % 2:hh % 2 + 1],
                                                                        scalar2=None, op0=ALU.mult),
                              reads=[b("qr%d" % slot), b("hm")], writes=[b("qrz%d" % slot)])

            def qk(item):
                seq, qb, hh, kc = item
                slot = (seq * NQ + qb) % 2
                hp, pr = hh % 2, hh // 2
                pt, pb = ring_S.get()
                st.op("pe", lambda h: h.matmul(pt, lhsT=knT[:, hh, kc * 128:(kc + 1) * 128], rhs=qn[:, slot, hh, :], start=True, stop=False),
                      reads=[b("knT"), b("qn%d" % slot)], writes=[pb])
                if ROPE128:
                    st.op("pe", lambda h: h.matmul(pt, lhsT=krT[:, kc * 128:(kc + 1) * 128], rhs=qrz[:, slot, hh, :], start=False, stop=True),
                          reads=[b("krT"), b("qrz%d" % slot)], writes=[pb])
                else:
                    st.op("pe", lambda h: h.matmul(pt, lhsT=krT[hp * 64:(hp + 1) * 64, kc * 128:(kc + 1) * 128],
                                                   rhs=qr[hp * 64:(hp + 1) * 64, slot, pr, :], start=False, stop=True),
                          reads=[b("krT"), b("qr%d" % slot)], writes=[pb])
                return pt, pb

            def finish_head(seq, qb, hh, oslot):
                qslot = (seq * NQ + qb) % 2
                if DEN == "dve":
                    st.op("pe", lambda h: h.matmul(pD[:, 0, :], lhsT=onesf[:], rhs=acc[:, oslot, :], start=True, stop=True),
                          reads=[b("onesf"), b("acc%d" % oslot)], writes=[b("pD0")])
                elif DEN == "split":
                    st.op("pe", lambda h: h.matmul(pD[:, 0, :], lhsT=onesf[:], rhs=acc[:, oslot, :], start=True, stop=False),
                          reads=[b("onesf"), b("acc%d" % oslot)], writes=[b("pD0")])
                    st.op("pe", lambda h: h.matmul(pD[:, 0, :], lhsT=onesf[:], rhs=accP[:, oslot, :], start=False, stop=True),
                          reads=[b("onesf"), b("accP%d" % oslot)], writes=[b("pD0")])
                st.op("dve", lambda h: h.reciprocal(out=rden[:, oslot, :], in_=pD[:, 0, :]), reads=[b("pD0")], writes=[b("rden%d" % oslot)])
                st.op("dve", lambda h: h.tensor_tensor(out=oT[:, qslot, hh, :], in0=pO[:, oslot, :], in1=rden[:, oslot, :], op=ALU.mult),
                      reads=[b("pO%d" % oslot), b("rden%d" % oslot)], writes=[b("oT%d_%d" % (qslot, hh))])

            def finish_q(seq, qb):
                qslot = (seq * NQ + qb) % 2
                tq = seq * L + qb * 512
                for hh in range(4):
                    st.op("act", lambda h, hh=hh: h.activation(out=sq[:, hh, :], in_=oT[:, qslot, hh, :], func=AF.Square),
                          reads=[b("oT%d_%d" % (qslot, hh))], writes=[b("sq%d" % hh)])
                for hh in range(4):
                    st.op("pe", lambda h, hh=hh: h.matmul(pSS[:], lhsT=ones[:], rhs=sq[:, hh, :], start=(hh == 0), stop=(hh == 3)),
                          reads=[b("ones"), b("sq%d" % hh)], writes=[b("pSS")])
                st.op("act", lambda h: h.activation(out=lnv[:], in_=pSS[:], func=AF.Ln, scale=1.0 / 512, bias=EPS), reads=[b("pSS")], writes=[b("lnv")])
                st.op("act", lambda h: h.activation(out=rstd[:], in_=lnv[:], func=AF.Exp, scale=-0.5), reads=[b("lnv")], writes=[b("rstd")])
                for hh in range(4):
                    yt, ybuf = ring_y.get()
                    st.op("dve", lambda h, hh=hh, yt=yt: h.tensor_tensor(out=yt, in0=oT[:, qslot, hh, :], in1=rstd[:], op=ALU.mult),
                          reads=[b("oT%d_%d" % (qslot, hh)), b("rstd")], writes=[ybuf])
                    st.dma(lambda h, hh=hh, yt=yt: h.dma_start(out=yT_s[4 + hh, :, tq:tq + 512], in_=yt), reads=[ybuf])

            import collections
            LA = int(os.environ.get("S2_LA", "2"))
            pending = collections.deque()
            nq = [0]

            def ensure(upto, seq):
                while nq[0] < min(upto, len(items)) and items[nq[0]][0] == seq:
                    pending.append(qk(items[nq[0]]))
                    nq[0] += 1

            ocnt = 0
            for idx, item in enumerate(items):
                seq, qb, hh, kc = item
                if qb == 0 and hh == 0 and kc == 0:
                    loads_seq(seq)
                    loads_q(seq, 0)
                if hh == 0 and kc == 0 and qb + 1 < NQ:
                    loads_q(seq, qb + 1)
                ensure(idx + 1 + LA, seq)
                pt, pb = pending.popleft()
                oslot = ocnt % 2
                Pa, Pb = ring_P.get()
                st.op("act", lambda h, Pa=Pa, pt=pt: h.activation(out=Pa, in_=pt, func=AF.Exp, scale=SCALE), reads=[pb], writes=[Pb])
                st.op("pe", lambda h, Pa=Pa, oslot=oslot, hh=hh, kc=kc: h.matmul(pO[:, oslot, :], lhsT=vm[:, kc, hh * 128:(hh + 1) * 128], rhs=Pa,
                                                                           start=(kc == 0), stop=(kc == NK - 1)),
                      reads=[b("vm"), Pb], writes=[b("pO%d" % oslot)])
                if DEN == "pe":
                    st.op("pe", lambda h, Pa=Pa, kc=kc: h.matmul(pD[:, 0, :], lhsT=ones[:], rhs=Pa, start=(kc == 0), stop=(kc == NK - 1)),
                          reads=[b("ones"), Pb], writes=[b("pD0")])
                else:
                    on_pool = (DEN == "split" and kc % 4 == 3)
                    eng, at, an, first = ("pool", accP, "accP", kc == 3) if on_pool else ("dve", acc, "acc", kc == 0)
                    if first:
                        st.op(eng, lambda h, Pa=Pa, oslot=oslot, at=at: h.tensor_copy(out=at[:, oslot, :], in_=Pa), reads=[Pb], writes=[b("%s%d" % (an, oslot))])
                    else:
                        st.op(eng, lambda h, Pa=Pa, oslot=oslot, at=at: h.tensor_tensor(out=at[:, oslot, :], in0=at[:, oslot, :], in1=Pa, op=ALU.add),
                              reads=[Pb, b("%s%d" % (an, oslot))], writes=[b("%s%d" % (an, oslot))])
                if kc == NK - 1:
                    finish_head(seq, qb, hh, oslot)
                    ocnt += 1
                    if hh == 3:
                        finish_q(seq, qb)
            st.finish()
            st.emit()

    def stage3():
        NCs = L // 128
        with contextlib.ExitStack() as es:
            def sb(name, shape, dt=F32):
                return es.enter_context(nc.sbuf_tensor("s3_" + name, list(shape), dt))

            def ps(name, shape, dt=F32):
                return es.enter_context(nc.psum_tensor("s3_" + name, list(shape), dt))

            st = Stage(nc, "s3")
            scal = sb("scal", [128, 2, 3, 4, NCH])
            masks = sb("masks", [128, 2, 128])
            ident = sb("ident", [128, 128], BF16)
            qd = sb("qd", [128, 2 * n_seq, 4, 512], BF16)
            ki = sb("ki", [128, 2 * n_seq, 4, 512], BF16)
            vt = sb("vt", [128, 2 * n_seq, 4, 512], BF16)
            kitok = sb("kitok", [128, 2, 512], BF16)
            scT = sb("scT", [128, 2, 8, 128], BF16)
            S_all = sb("S", [128, n_seq, 4, 128]); Sp_all = sb("Sp", [128, n_seq, 4, 128], BF16)
            t1_all = sb("t1", [128, n_seq, 4, 128]); t2_all = sb("t2", [128, n_seq, 4, 128])
            ofw = sb("ofw", [128, 4, 512])
            gtl = sb("gtl", [128, 4, 512])
            osum_a = sb("osum", [128, 2, 512]); sqt_a = sb("sqt", [128, 2, 512]); yn_a = sb("yn", [128, 2, 512])
            ss8_a = sb("ss8", [128, 2, 8]); ln8_a = sb("ln8", [128, 2, 8]); rs8_a = sb("rs8", [128, 2, 8])
            ya = sb("ya", [128, 2, 512], BF16)
            yaT = sb("yaT", [128, 2, 4, 128], BF16)
            pT = ps("pT", [128, 2, 1024], BF16)
            pSc = ps("pSc", [128, 2, 4, 128])
            pOo = ps("pOo", [128, 2, 512])
            pU = ps("pU", [128, 1, 4, 128])
            pY = ps("pY", [128, 8, 128], BF16)
            B = {}

            def b(name):
                if name not in B:
                    B[name] = Buf(name)
                return B[name]
            BD = sb("BD", [128, 4, 128])
            Bm_all = sb("Bm", [128, n_seq, 4, 128])
            st.op("pool", lambda h: h.memset(BD[:], 0.0), writes=[b("BD")])
            st.op("pool", lambda h: h.memset(BD[0:64, :, 0:64], 1.0), writes=[b("BD")])
            st.op("pool", lambda h: h.memset(BD[64:128, :, 64:128], 1.0), writes=[b("BD")])
            ring_of = Ring(ofw, 4, "ofw")
            ring_g = Ring(gtl, 4, "gtl")
            st.dma(lambda h: h.dma_start(out=scal[:], in_=scal_s), writes=[b("scal")])
            st.dma(lambda h: h.dma_start(out=masks[:, 0, :], in_=c_maskf), writes=[b("masks")])
            st.dma(lambda h: h.dma_start(out=masks[:, 1, :], in_=c_maskb), writes=[b("masks")])
            st.dma(lambda h: h.dma_start(out=ident[:], in_=c_ident), writes=[b("ident")])

            def run_dir(seq, d):
                order = list(range(NCs)) if d == 0 else list(range(NCs - 1, -1, -1))
                S = S_all[:, seq]; Sp = Sp_all[:, seq]
                st.op("pool", lambda h: h.memset(S, 0.0), writes=[b("S%d" % seq)])
                st.op("pool", lambda h: h.memset(Sp, 0.0), writes=[b("Sp%d" % seq)])
                cur_group = [None, None]
                gcount = [0]

                def load_group(gi):
                    slot = seq * 2 + gcount[0] % 2
                    gcount[0] += 1
                    tg = seq * L + gi * 512
                    st.dma(lambda h: h.dma_start(out=qd[:, slot], in_=qdT_s[d, :, :, tg:tg + 512].rearrange("k p t -> p k t")), writes=[b("qd%d" % slot)])
                    st.dma(lambda h: h.dma_start(out=ki[:, slot], in_=kiT_s[d, :, :, tg:tg + 512].rearrange("k p t -> p k t")), writes=[b("ki%d" % slot)])
                    st.dma(lambda h: h.dma_start(out=vt[:, slot], in_=v_s[tg:tg + 512, :].rearrange("(j p) f -> p j f", p=128)), writes=[b("vt%d" % slot)])
                    return slot

                for pos, n in enumerate(order):
                    gi = n // 4
                    if cur_group[0] != gi:
                        cur_group[0] = gi
                        cur_group[1] = load_group(gi)
                    slot = cur_group[1]
                    j = n % 4
                    cg = seq * NCs + n
                    t0 = cg * 128
                    cs = seq % 2
                    yield from chunk(seq, d, n, pos, order, slot, j, cg, t0, cs)

            def chunk(seq, d, n, pos, order, slot, j, cg, t0, cs):
                S = S_all[:, seq]; Sp = Sp_all[:, seq]; t1 = t1_all[:, seq]; t2 = t2_all[:, seq]
                osum = osum_a[:, cs]; sqt = sqt_a[:, cs]; yn = yn_a[:, cs]
                ss8 = ss8_a[:, cs]; ln8 = ln8_a[:, cs]; rs8 = rs8_a[:, cs]
                bS, bSp, bt1, bt2 = b("S%d" % seq), b("Sp%d" % seq), b("t1%d" % seq), b("t2%d" % seq)
                bos, bsq, byn, bss, bln, brs = (b("%s%d" % (nm, cs)) for nm in ("osum", "sqt", "yn", "ss8", "ln8", "rs8"))
                qd_c = qd[:, slot, :, j * 128:(j + 1) * 128]
                ki_c = ki[:, slot, :, j * 128:(j + 1) * 128]
                v_c = vt[:, slot, j, :]
                rq, rk, rv = b("qd%d" % slot), b("ki%d" % slot), b("vt%d" % slot)
                if d == 1:
                    oft, ofb = ring_of.get()
                    gtt, gtb = ring_g.get()
                    st.dma(lambda h: h.dma_start(out=oft, in_=of_s[t0:t0 + 128, :]), reads=[b("ofs%d" % cg)], writes=[ofb])
                    st.dma(lambda h: h.dma_start(out=gtt, in_=g_s[t0:t0 + 128, :]), writes=[gtb])
                for bk in range(4):
                    st.op("pe", lambda h, bk=bk: h.transpose(out=pT[:, cs, bk * 128:(bk + 1) * 128], in_=ki_c[:, bk, :], identity=ident[:]),
                          reads=[rk, b("ident")], writes=[b("pT%d" % cs)])
                st.op("act", lambda h: h.activation(out=kitok[:, cs, :], in_=pT[:, cs, 0:512], func=AF.Copy), reads=[b("pT%d" % cs)], writes=[b("kitok%d" % cs)])
                for hh in range(8):
                    bk, hp = hh // 2, hh % 2
                    st.op("pe", lambda h, hh=hh, bk=bk, hp=hp: h.matmul(pSc[:, hp, bk, :], lhsT=ki_c[hp * 64:(hp + 1) * 64, bk, :],
                                                                        rhs=qd_c[hp * 64:(hp + 1) * 64, bk, :], start=True, stop=True),
                          reads=[rk, rq], writes=[b("pSc%d" % hp)])
                scv = scT[:, cs].rearrange("p (k two) t -> p k two t", two=2)
                for half in range(2):
                    st.op("dve", lambda h, half=half: h.tensor_tensor(out=scv[:, :, half, :], in0=pSc[:, half],
                                                                      in1=bc(masks[:, d:d + 1, :], [128, 4, 128]), op=ALU.mult),
                          reads=[b("pSc%d" % half), b("masks")], writes=[b("scT%d_%d" % (cs, half))])
                has_next = pos + 1 < len(order)
                if has_next:
                    Bm = Bm_all[:, seq]
                    st.op("pool", lambda h: h.tensor_tensor(out=Bm, in0=BD[:], in1=bc(scal[:, d, 1, :, cg:cg + 1], [128, 4, 128]), op=ALU.mult),
                          reads=[b("BD"), b("scal")], writes=[b("Bm%d" % seq)])
                yield
                for bk in range(4):
                    st.op("pe", lambda h, bk=bk: h.matmul(pOo[:, cs, bk * 128:(bk + 1) * 128], lhsT=qd_c[:, bk, :], rhs=Sp[:, bk, :], start=True, stop=False),
                          reads=[rq, bSp], writes=[b("pOo%d" % cs)])
                    for hp in range(2):
                        hh = 2 * bk + hp
                        st.op("pe", lambda h, hh=hh, hp=hp: h.matmul(pOo[:, cs, hh * 64:(hh + 1) * 64], lhsT=scT[:, cs, hh, :], rhs=v_c[:, hh * 64:(hh + 1) * 64],
                                                                     start=False, stop=(hp == 1)),
                              reads=[b("scT%d_%d" % (cs, hh % 2)), rv], writes=[b("pOo%d" % cs)])
                yield
                if has_next:
                    for bk in range(4):
                        st.op("pe", lambda h, bk=bk: h.matmul(pU[:, 0, bk, :], lhsT=kitok[:, cs, bk * 128:(bk + 1) * 128], rhs=v_c[:, bk * 128:(bk + 1) * 128],
                                                              start=True, stop=True),
                              reads=[b("kitok%d" % cs), rv], writes=[b("pU0")])
                    cgn = seq * NCs + order[pos + 1]
                    Abc = bc(scal[:, d, 0, :, cg:cg + 1], [128, 4, 128])
                    Cbc = bc(scal[:, d, 2, :, cgn:cgn + 1], [128, 4, 128])
                    st.op("pool", lambda h: h.tensor_tensor(out=t1, in0=S, in1=Abc, op=ALU.mult), reads=[bS, b("scal")], writes=[bt1])
                    st.op("dve", lambda h: h.tensor_tensor(out=t2, in0=pU[:, 0], in1=Bm, op=ALU.mult), reads=[b("pU0"), b("Bm%d" % seq)], writes=[bt2])
                    st.op("dve", lambda h: h.tensor_tensor(out=S, in0=t1, in1=t2, op=ALU.add), reads=[bt1, bt2], writes=[bS])
                    st.op("dve", lambda h: h.tensor_tensor(out=Sp, in0=S, in1=Cbc, op=ALU.mult), reads=[bS, b("scal")], writes=[bSp])
                yield
                if d == 0:
                    oft, ofb = ring_of.get()
                    st.op("act", lambda h: h.activation(out=oft, in_=pOo[:, cs, :], func=AF.Copy), reads=[b("pOo%d" % cs)], writes=[ofb])
                    st.dma(lambda h: h.dma_start(out=of_s[t0:t0 + 128, :], in_=oft), reads=[ofb], writes=[b("ofs%d" % cg)])
                else:
                    st.op("dve", lambda h: h.tensor_tensor(out=osum, in0=pOo[:, cs, :], in1=oft, op=ALU.add), reads=[b("pOo%d" % cs), ofb], writes=[bos])
                    st.op("act", lambda h: h.activation(out=sqt, in_=osum, func=AF.Square), reads=[bos], writes=[bsq])
                    st.op("dve", lambda h: h.tensor_reduce(out=ss8, in_=sqt.rearrange("p (h e) -> p h e", e=64), axis=AX.X, op=ALU.add),
                          reads=[bsq], writes=[bss])
                    st.op("act", lambda h: h.activation(out=ln8, in_=ss8, func=AF.Ln, scale=1.0 / 64, bias=EPS), reads=[bss], writes=[bln])
                    st.op("act", lambda h: h.activation(out=rs8, in_=ln8, func=AF.Exp, scale=-0.5), reads=[bln], writes=[brs])
                    st.op("dve", lambda h: h.tensor_tensor(out=yn.rearrange("p (h e) -> p h e", e=64), in0=osum.rearrange("p (h e) -> p h e", e=64),
                                                           in1=bc(rs8.unsqueeze(2), [128, 8, 64]), op=ALU.mult),
                          reads=[bos, brs], writes=[byn])
                    st.op("pool", lambda h: h.tensor_tensor(out=ya[:, cs, :], in0=yn, in1=gtt, op=ALU.mult), reads=[byn, gtb], writes=[b("ya%d" % cs)])
                    for bk in range(4):
                        st.op("pe", lambda h, bk=bk: h.transpose(out=pY[:, bk, :], in_=ya[:, cs, bk * 128:(bk + 1) * 128], identity=ident[:]),
                              reads=[b("ya%d" % cs), b("ident")], writes=[b("pY")])
                    st.op("act", lambda h: h.activation(out=yaT[:, cs], in_=pY[:, 0:4, :], func=AF.Copy), reads=[b("pY")], writes=[b("yaT%d" % cs)])
                    st.dma(lambda h: h.dma_start(out=yT_s[0:4, :, t0:t0 + 128].rearrange("k p t -> p k t"), in_=yaT[:, cs]), reads=[b("yaT%d" % cs)])
                yield

            for d in range(2):
                gens = [run_dir(seq, d) for seq in range(n_seq)]
                live = list(gens)
                while live:
                    for g in list(live):
                        try:
                            next(g)
                        except StopIteration:
                            live.remove(g)
            st.finish()
            st.emit()

    def stage4():
        TT = 256
        NTL = T // TT
        with contextlib.ExitStack() as es:
            def sb(name, shape, dt=F32):
                return es.enter_context(nc.sbuf_tensor("s4_" + name, list(shape), dt))

            def ps(name, shape, dt=F32):
                return es.enter_context(nc.psum_tensor("s4_" + name, list(shape), dt))

            st = Stage(nc, "s4")
            wo = sb("wo", [128, 8, D], BF16)
            wg = sb("wg", [128, 8, DFF], BF16)
            wu = sb("wu", [128, 8, DFF], BF16)
            wd = sb("wd", [128, NFB, D], BF16)
            goutt = sb("goutt", [128, 8]); g2t = sb("g2t", [128, 8])
            gfb = sb("gfb", [128, D])
            ident = sb("ident", [128, 128], BF16)
            xt = sb("xt", [128, 2, 2, D])
            yT = sb("yT", [128, 2, 8, TT], BF16)
            h2n = sb("h2n", [128, 2, D], BF16)
            h2T = sb("h2T", [128, 8, TT], BF16)
            aT = sb("aT", [128, NFB, TT], BF16)
            sg = sb("sg", [128, 2, TT])
            junk = sb("junk", [128, D], BF16)
            ss = sb("ss", [128, 4]); lnt = sb("lnt", [128, 4]); rs = sb("rs", [128, 4])
            pxt = ps("pxt", [128, 8, 128], BF16)
            pG = ps("pG", [128, 2, 512])
            pUu = ps("pUu", [128, 2, 512])
            pA = ps("pA", [128, 2, 512])
            B = {}

            def b(name):
                if name not in B:
                    B[name] = Buf(name)
                return B[name]
            ring_A = Ring(pA, 2, "pA")
            for (dst, src, nm) in ((goutt, gout, "goutt"), (g2t, g2, "g2t"), (ident, c_ident, "ident")):
                st.dma(lambda h, dst=dst, src=src: h.dma_start(out=dst[:], in_=src), writes=[b(nm)])
            if os.environ.get("S4_NOBC"):
                for pp in range(0, 128, 32):
                    pass
                st.op("pool", lambda h: h.memset(gfb[:], 1.0), writes=[b("gfb")])
            else:
                st.dma(lambda h: h.dma_start(out=gfb[:], in_=gfin.partition_broadcast(128)), writes=[b("gfb")])
            stg = xt[:].rearrange("p a b d -> p (a b) d")
            pcnt = [0]

            def prep(dst3, src3, nchunk, ncols, gain, name, gname):
                for c in range(nchunk):
                    for c0 in range(0, ncols, 1024):
                        c1 = min(ncols, c0 + 1024)
                        slot = pcnt[0] % 4
                        eng = (("dve", "pool") if os.environ.get("S4_NOACT") else ("dve", "pool", "act"))[pcnt[0] % (2 if os.environ.get("S4_NOACT") else 3)]
                        pcnt[0] += 1
                        sbuf = b("xt%d" % slot)
                        st.dma(lambda h, c=c, slot=slot, c0=c0, c1=c1: h.dma_start(out=stg[:, slot, 0:c1 - c0], in_=src3[:, c, c0:c1]), writes=[sbuf])
                        rd = [sbuf] + ([b(gname)] if gain is not None else [])
                        if eng == "act":
                            if gain is None:
                                st.op("act", lambda h, c=c, slot=slot, c0=c0, c1=c1: h.activation(out=dst3[:, c, c0:c1], in_=stg[:, slot, 0:c1 - c0], func=AF.Copy),
                                      reads=rd, writes=[b(name)])
                            else:
                                st.op("act", lambda h, c=c, slot=slot, c0=c0, c1=c1: h.activation(out=dst3[:, c, c0:c1], in_=stg[:, slot, 0:c1 - c0], func=AF.Copy,
                                                                                               scale=gain[:, c:c + 1]),
                                      reads=rd, writes=[b(name)])
                        else:
                            if gain is None:
                                st.op(eng, lambda h, c=c, slot=slot, c0=c0, c1=c1: h.tensor_copy(out=dst3[:, c, c0:c1], in_=stg[:, slot, 0:c1 - c0]),
                                      reads=rd, writes=[b(name)])
                            else:
                                st.op(eng, lambda h, c=c, slot=slot, c0=c0, c1=c1: h.tensor_scalar(out=dst3[:, c, c0:c1], in0=stg[:, slot, 0:c1 - c0],
                                                                                                scalar1=gain[:, c:c + 1], scalar2=None, op0=ALU.mult),
                                      reads=rd, writes=[b(name)])
            prep(wo, w_out, 8, D, goutt, "wo", "goutt")
            prep(wg, w_gate, 8, DFF, g2t, "wg", "g2t")
            prep(wu, w_up, 8, DFF, g2t, "wu", "g2t")
            prep(wd, w_down, NFB, D, None, "wd", None)

            def loads(i):
                slot = i % 2
                t0 = i * TT
                st.dma(lambda h: h.dma_start(out=xt[:, slot], in_=x[t0:t0 + TT, :].rearrange("(s p) d -> p s d", p=128)),
                       writes=[b("xt%d" % (slot * 2)), b("xt%d" % (slot * 2 + 1))])
                st.dma(lambda h: h.dma_start(out=yT[:, slot], in_=yT_s[:, :, t0:t0 + TT].rearrange("c p t -> p c t")), writes=[b("yT%d" % slot)])

            def rms_rstd(i, sub, which):
                slot = i % 2
                col = which * 2 + sub
                xb = b("xt%d" % (slot * 2 + sub))
                st.op("act", lambda h: h.activation(out=junk[:], in_=xt[:, slot, sub, :], func=AF.Square, accum_out=ss[:, col:col + 1]),
                      reads=[xb], writes=[b("junk"), b("ss%d" % col)])
                st.op("act", lambda h: h.activation(out=lnt[:, col:col + 1], in_=ss[:, col:col + 1], func=AF.Ln, scale=1.0 / D, bias=EPS),
                      reads=[b("ss%d" % col)], writes=[b("ln%d" % col)])
                st.op("act", lambda h: h.activation(out=rs[:, col:col + 1], in_=lnt[:, col:col + 1], func=AF.Exp, scale=-0.5),
                      reads=[b("ln%d" % col)], writes=[b("rs%d" % col)])
                return col

            def tile(i):
                slot = i % 2
                t0 = i * TT
                if i + 1 < NTL:
                    loads(i + 1)
                for sub in range(2):
                    xb = b("xt%d" % (slot * 2 + sub))
                    for half in range(2):
                        pt, pb = ring_A.get()
                        for c in range(8):
                            st.op("pe", lambda h, c=c, pt=pt, sub=sub, half=half: h.matmul(pt, lhsT=yT[:, slot, c, sub * 128:(sub + 1) * 128],
                                                                                          rhs=wo[:, c, half * 512:(half + 1) * 512], start=(c == 0), stop=(c == 7)),
                                  reads=[b("yT%d" % slot), b("wo")], writes=[pb])
                        st.op("dve", lambda h, pt=pt, sub=sub, half=half: h.tensor_tensor(out=xt[:, slot, sub, half * 512:(half + 1) * 512],
                                                                                         in0=xt[:, slot, sub, half * 512:(half + 1) * 512], in1=pt, op=ALU.add),
                              reads=[pb, xb], writes=[xb])
                for sub in range(2):
                    xb = b("xt%d" % (slot * 2 + sub))
                    col = rms_rstd(i, sub, 0)
                    st.op("dve", lambda h, sub=sub, col=col: h.tensor_scalar(out=h2n[:, sub, :], in0=xt[:, slot, sub, :], scalar1=rs[:, col:col + 1],
                                                                            scalar2=None, op0=ALU.mult),
                          reads=[xb, b("rs%d" % col)], writes=[b("h2n%d" % sub)])
                    for c in range(8):
                        st.op("pe", lambda h, sub=sub, c=c: h.transpose(out=pxt[:, c, :], in_=h2n[:, sub, c * 128:(c + 1) * 128], identity=ident[:]),
                              reads=[b("h2n%d" % sub), b("ident")], writes=[b("pxt")])
                    st.op("dve", lambda h, sub=sub: h.tensor_copy(out=h2T[:, :, sub * 128:(sub + 1) * 128], in_=pxt[:]), reads=[b("pxt")], writes=[b("h2T")])
                for fb in range(NFB):
                    gs = fb % 2
                    for c in range(8):
                        st.op("pe", lambda h, c=c, fb=fb, gs=gs: h.matmul(pG[:, gs, 0:TT], lhsT=wg[:, c, fb * 128:(fb + 1) * 128], rhs=h2T[:, c, :],
                                                                          start=(c == 0), stop=(c == 7)),
                              reads=[b("wg"), b("h2T")], writes=[b("pG%d" % gs)])
                    for c in range(8):
                        st.op("pe", lambda h, c=c, fb=fb, gs=gs: h.matmul(pUu[:, gs, 0:TT], lhsT=wu[:, c, fb * 128:(fb + 1) * 128], rhs=h2T[:, c, :],
                                                                          start=(c == 0), stop=(c == 7)),
                              reads=[b("wu"), b("h2T")], writes=[b("pU%d" % gs)])
                    st.op("act", lambda h, gs=gs: h.activation(out=sg[:, gs, :], in_=pG[:, gs, 0:TT], func=AF.Silu), reads=[b("pG%d" % gs)], writes=[b("sg%d" % gs)])
                    st.op("dve", lambda h, gs=gs, fb=fb: h.tensor_tensor(out=aT[:, fb, :], in0=sg[:, gs, :], in1=pUu[:, gs, 0:TT], op=ALU.mult),
                          reads=[b("sg%d" % gs), b("pU%d" % gs)], writes=[b("aT")])
                for sub in range(2):
                    xb = b("xt%d" % (slot * 2 + sub))
                    for half in range(2):
                        pt, pb = ring_A.get()
                        for fb in range(NFB):
                            st.op("pe", lambda h, fb=fb, pt=pt, sub=sub, half=half: h.matmul(pt, lhsT=aT[:, fb, sub * 128:(sub + 1) * 128],
                                                                                            rhs=wd[:, fb, half * 512:(half + 1) * 512], start=(fb == 0), stop=(fb == NFB - 1)),
                                  reads=[b("aT"), b("wd")], writes=[pb])
                        st.op("dve", lambda h, pt=pt, sub=sub, half=half: h.tensor_tensor(out=xt[:, slot, sub, half * 512:(half + 1) * 512],
                                                                                         in0=xt[:, slot, sub, half * 512:(half + 1) * 512], in1=pt, op=ALU.add),
                              reads=[pb, xb], writes=[xb])
                for sub in range(2):
                    xb = b("xt%d" % (slot * 2 + sub))
                    col = rms_rstd(i, sub, 1)
                    st.op("dve", lambda h, sub=sub, col=col: h.scalar_tensor_tensor(out=xt[:, slot, sub, :], in0=xt[:, slot, sub, :], scalar=rs[:, col:col + 1],
                                                                                   in1=gfb[:], op0=ALU.mult, op1=ALU.mult),
                          reads=[xb, b("rs%d" % col), b("gfb")], writes=[xb])
                    st.dma(lambda h, sub=sub: h.dma_start(out=out[t0 + sub * 128:t0 + (sub + 1) * 128, :], in_=xt[:, slot, sub, :]), reads=[xb])

            loads(0)
            for i in range(NTL):
                tile(i)
            st.finish()
            st.emit()

    if 1 in stages:
        stage1()
    if 2 in stages:
        stage2()
    if 3 in stages:
        stage3()
    if 4 in stages:
        stage4()
    return nc


def _pcn(w, rows):
    n = w.shape[1]
    return np.ascontiguousarray(w.reshape(rows // 128, 128, n).transpose(1, 0, 2))


def _pc(g):
    return np.ascontiguousarray(g.reshape(-1, 128).T)


def layout_inputs(inp, L):
    f32 = np.float32
    w_in = np.asarray(inp["w_in"][0], f32)
    hq, hi, hff, hfb, hg, cq, ckv, kr = np.split(w_in, np.cumsum([512, 512, 512, 512, 512, 384, 256])[:], axis=1)
    krot = np.concatenate([kr[:, 32:64], kr[:, 0:32]], axis=1)
    w1 = np.concatenate([hq, hff, hfb, hi, hg, cq, ckv, kr, kr, krot, krot], axis=1)
    assert w1.shape[1] == W1C
    wqb = np.asarray(inp["w_q_b"][0], f32).reshape(384, 4, 192)
    nope = wqb[:, :, 0:128].reshape(384, 512)
    rp = wqb[:, :, 128:192]
    rope = rp.reshape(384, 256)
    rot = np.concatenate([rp[:, :, 32:64], rp[:, :, 0:32]], axis=2).reshape(384, 256)
    wq = np.concatenate([nope, rope, rot], axis=1)
    wkvb = np.asarray(inp["w_kv_b"][0], f32).reshape(256, 4, 256)
    wkv = np.concatenate([wkvb[:, :, 0:128].reshape(256, 512), wkvb[:, :, 128:256].reshape(256, 512)], axis=1)
    lbl = np.asarray(inp["lb_logits"], f32)
    lbl_l = np.ascontiguousarray(lbl.reshape(2, 2, 4, 128).transpose(3, 0, 1, 2))
    gout = np.concatenate([np.asarray(inp["hgrn_norm_g"][0], f32), np.asarray(inp["mla_norm_g"][0], f32)])
    inv = 1.0 / (10000.0 ** (np.arange(0, 64, 2, dtype=np.float32) / 64.0))
    ang = np.arange(L, dtype=np.float32)[None, :] * inv[:, None].astype(np.float32)
    cos = np.cos(ang).astype(f32)
    sin = np.sin(ang).astype(f32)
    c_cos = np.ascontiguousarray(np.tile(cos, (4, 1)))
    c_sin = np.ascontiguousarray(np.tile(sin, (4, 1)))
    rmask = np.ones((128, 512), f32)
    rmask[:, 0::128] = 0.0
    jj = np.arange(128)[:, None]
    ii = np.arange(128)[None, :]
    d = {
        "w_in": _pcn(w1, 1024), "g1": _pc(np.asarray(inp["norm1_g"][0], f32)), "lbl": lbl_l,
        "w_qb": _pcn(wq, 384), "gqa": _pc(np.asarray(inp["q_a_norm_g"][0], f32)),
        "w_kvb": _pcn(wkv, 256), "gkva": _pc(np.asarray(inp["kv_a_norm_g"][0], f32)),
        "w_out": _pcn(np.asarray(inp["w_out"][0], f32), 1024), "gout": _pc(gout),
        "w_gate": _pcn(np.asarray(inp["w_gate"][0], f32), 1024), "w_up": _pcn(np.asarray(inp["w_up"][0], f32), 1024),
        "g2": _pc(np.asarray(inp["norm2_g"][0], f32)),
        "w_down": _pcn(np.asarray(inp["w_down"][0], f32), DFF),
        "gfin": np.asarray(inp["final_norm_g"], f32).reshape(1, D),
        "c_ident": np.eye(128).astype(ml_dtypes.bfloat16), "c_ones": np.ones((128, 128), ml_dtypes.bfloat16),
        "c_cos": c_cos, "c_sin": c_sin, "c_rmask": rmask,
        "c_maskf": (jj <= ii).astype(f32), "c_maskb": (jj >= ii).astype(f32),
    }
    return d


_NC_CACHE = {}


def kernel(**inputs):
    x = np.asarray(inputs["x"], np.float32)
    Bt, L, _ = x.shape
    n_seq = Bt // NCORES
    key = (n_seq, L)
    if key not in _NC_CACHE:
        _NC_CACHE[key] = build_nc(n_seq, L)
    nc = _NC_CACHE[key]
    shared = layout_inputs(inputs, L)
    in_maps = []
    for c in range(NCORES):
        m = dict(shared)
        m["x"] = np.ascontiguousarray(x[c * n_seq:(c + 1) * n_seq].reshape(n_seq * L, D))
        in_maps.append(m)
    res = run_bass_kernel_spmd(nc, in_maps, core_ids=list(range(NCORES)))
    out = np.stack([r["out"].reshape(n_seq, L, D) for r in res.results], axis=0)
    return out.reshape(Bt, L, D).astype(np.float32)
```

```python
import contextlib
import os
import numpy as np
import ml_dtypes
import concourse.bass as bass
import concourse.mybir as mybir
from concourse.bass_utils import run_bass_kernel_spmd

F32 = mybir.dt.float32
BF16 = mybir.dt.bfloat16
AF = mybir.ActivationFunctionType
ALU = mybir.AluOpType
AX = mybir.AxisListType

D = 1024
DFF = 2816
NFB = DFF // 128
EPS = 1e-6
NCORES = 8
W1C = 3456
SCALE = 192 ** -0.5


class Buf:
    __slots__ = ("name", "w", "r", "x")

    def __init__(self, name=""):
        self.name = name
        self.w = None
        self.r = {}
        self.x = len(name) > 1 and name[0] == "p" and (name[1].isupper() or name.startswith(("pmm", "pxt")))


class Stage:
    ENGS = ("pe", "act", "dve", "pool", "sp")

    def __init__(self, nc, name, n_dma_sems=16):
        self.nc = nc
        self.name = name
        self.ops = {e: [] for e in self.ENGS}
        self.cnt = {e: 0 for e in ("pe", "act", "dve", "pool")}
        self.waited = {e: {} for e in self.ENGS}
        self.n_dma = n_dma_sems
        self.dma_cnt = [0] * n_dma_sems
        self.dma_rr = 0
        self.sems = {}

    def _need(self, eng, ev, waits):
        if ev is None:
            return
        key, val = ev
        if key == "pe" and eng == "pe":
            return
        if self.waited[eng].get(key, 0) >= val:
            return
        self.waited[eng][key] = val
        waits.append((key, val))

    def _deps(self, eng, reads, writes):
        waits = []
        for b in reads:
            self._need(eng, b.w, waits)
            if b.x:
                for k, v in b.r.items():
                    if k != eng:
                        self._need(eng, (k, v), waits)
        for b in writes:
            self._need(eng, b.w, waits)
            for k, v in b.r.items():
                self._need(eng, (k, v), waits)
        return waits

    def _commit(self, ev, reads, writes):
        k, v = ev
        for b in reads:
            if b.r.get(k, 0) < v:
                b.r[k] = v
        for b in writes:
            b.w = ev
            b.r = {}

    def op(self, eng, fn, reads=(), writes=()):
        waits = self._deps(eng, reads, writes)
        self.cnt[eng] += 1
        ev = (eng, self.cnt[eng])
        self.ops[eng].append((waits, fn, (eng, 1)))
        self._commit(ev, reads, writes)
        return ev

    def dma(self, fn, reads=(), writes=(), queue="sp"):
        waits = self._deps(queue, reads, writes)
        k = self.dma_rr
        self.dma_rr = (self.dma_rr + 1) % self.n_dma
        key = "dma%d" % k
        if self.dma_cnt[k] > 0:
            self._need(queue, (key, self.dma_cnt[k]), waits)
        self.dma_cnt[k] += 16
        ev = (key, self.dma_cnt[k])
        self.ops[queue].append((waits, fn, (key, 16)))
        self._commit(ev, reads, writes)
        return ev

    def finish(self, eng="sp"):
        waits = []
        for k in range(self.n_dma):
            if self.dma_cnt[k] > 0:
                self._need(eng, ("dma%d" % k, self.dma_cnt[k]), waits)
        if waits:
            self.ops[eng].append((waits, None, None))

    def emit(self):
        nc = self.nc
        with contextlib.ExitStack() as st:
            for e in ("pe", "act", "dve", "pool"):
                self.sems[e] = st.enter_context(nc.semaphore("%s_%s" % (self.name, e)))
            for k in range(self.n_dma):
                self.sems["dma%d" % k] = st.enter_context(nc.semaphore("%s_d%d" % (self.name, k)))
            block = st.enter_context(nc.Block())
            sems = self.sems

            def run(h, lst):
                for waits, fn, inc in lst:
                    for key, val in waits:
                        h.wait_ge(sems[key], val)
                    if fn is not None:
                        fn(h).then_inc(sems[inc[0]], inc[1])

            if self.ops["sp"]:
                @block.sync
                def _(h):
                    run(h, self.ops["sp"])
            if self.ops["pe"]:
                @block.tensor
                def _(h):
                    run(h, self.ops["pe"])
            if self.ops["act"]:
                @block.scalar
                def _(h):
                    run(h, self.ops["act"])
            if self.ops["dve"]:
                @block.vector
                def _(h):
                    run(h, self.ops["dve"])
            if self.ops["pool"]:
                @block.gpsimd
                def _(h):
                    run(h, self.ops["pool"])


class Ring:
    def __init__(self, tens, n, name):
        self.t = tens
        self.n = n
        self.i = 0
        self.bufs = [Buf("%s%d" % (name, k)) for k in range(n)]

    def get(self):
        k = self.i
        self.i = (self.i + 1) % self.n
        return self.t[:, k], self.bufs[k]


def bc(ap, shape):
    return ap.to_broadcast(shape)


def build_nc(n_seq, L, debug=False, stages=(1, 2, 3, 4)):
    T = n_seq * L
    NT = T // 128
    NS = T // 512
    NCH = T // 128
    nc = bass.Bass("TRN2", target_bir_lowering=False)

    def din(name, shape, dt=F32):
        return nc.dram_tensor(name, list(shape), dt, kind="ExternalInput").ap()

    skind = "ExternalOutput" if debug else "Internal"

    def dscr(name, shape, dt):
        return nc.dram_tensor(name, list(shape), dt, kind=skind).ap()

    x = din("x", [T, D])
    out = nc.dram_tensor("out", [T, D], F32, kind="ExternalOutput").ap()
    w_in = din("w_in", [128, 8, W1C])
    g1 = din("g1", [128, 8])
    lbl = din("lbl", [128, 2, 2, 4])
    w_qb = din("w_qb", [128, 3, 1024])
    gqa = din("gqa", [128, 3])
    w_kvb = din("w_kvb", [128, 2, 1024])
    gkva = din("gkva", [128, 2])
    w_out = din("w_out", [128, 8, D])
    gout = din("gout", [128, 8])
    w_gate = din("w_gate", [128, 8, DFF])
    w_up = din("w_up", [128, 8, DFF])
    g2 = din("g2", [128, 8])
    w_down = din("w_down", [128, NFB, D])
    gfin = din("gfin", [1, D])
    c_ident = din("c_ident", [128, 128], BF16)
    c_ones = din("c_ones", [128, 128], BF16)
    c_cos = din("c_cos", [128, L])
    c_sin = din("c_sin", [128, L])
    c_rmask = din("c_rmask", [128, 512])
    c_maskf = din("c_maskf", [128, 128])
    c_maskb = din("c_maskb", [128, 128])

    qdT_s = dscr("qdT_s", [2, 4, 128, T], BF16)
    kiT_s = dscr("kiT_s", [2, 4, 128, T], BF16)
    scal_s = dscr("scal_s", [128, 2, 3, 4, NCH], F32)
    v_s = dscr("v_s", [T, 512], BF16)
    g_s = dscr("g_s", [T, 512], F32)
    qnT_s = dscr("qnT_s", [4, 128, T], BF16)
    qrT_s = dscr("qrT_s", [2, 128, T], BF16)
    knT_s = dscr("knT_s", [4, 128, T], BF16)
    krT_s = dscr("krT_s", [128, T], BF16)
    vm_s = dscr("vm_s", [T, 512], BF16)
    yT_s = dscr("yT_s", [8, 128, T], BF16)
    of_s = dscr("of_s", [T, 512], F32)
    ob_s = dscr("ob_s", [T, 512], F32)

    def stage1():
        with contextlib.ExitStack() as es:
            def sb(name, shape, dt=F32):
                return es.enter_context(nc.sbuf_tensor("s1_" + name, list(shape), dt))

            def ps(name, shape, dt=F32):
                return es.enter_context(nc.psum_tensor("s1_" + name, list(shape), dt))

            st = Stage(nc, "s1")
            w1 = sb("w1", [128, 8, W1C], BF16)
            wq = sb("wq", [128, 3, 1024], BF16)
            wkv = sb("wkv", [128, 2, 1024], BF16)
            stg = sb("stg", [128, 2, 1024], F32)
            g1t = sb("g1t", [128, 8]); gqat = sb("gqat", [128, 3]); gkvat = sb("gkvat", [128, 2])
            lblt = sb("lblt", [128, 2, 2, 4])
            lbt = sb("lbt", [128, 2, 4]); omlt = sb("omlt", [128, 2, 4])
            fa = sb("fa", [128, 2, 4]); fb_ = sb("fb", [128, 2, 4]); nfb = sb("nfb", [128, 2, 4]); lnfb = sb("lnfb", [128, 2, 4])
            ident = sb("ident", [128, 128], BF16)
            rmask = sb("rmask", [128, 512])
            xt = sb("xt", [128, 4, D]); xjunk = sb("xjunk", [128, 2, D], BF16)
            xn = sb("xn", [128, 2, D], BF16)
            hT = sb("hT", [128, 1, 8, 512], BF16)
            ssx = sb("ssx", [128, 2, 4]); lnx = sb("lnx", [128, 2, 4]); rsx = sb("rsx", [128, 2, 4])
            cst = sb("cst", [128, 2, 512]); snt = sb("snt", [128, 2, 512])
            qT = sb("qT", [128, 4, 512])
            th = sb("th", [128, 8, 512])
            tmpf = sb("tmpf", [128, 8, 512])
            tmpb = sb("tmpb", [128, 8, 512], BF16)
            gt = sb("gt", [128, 4, 512])
            cqf = sb("cqf", [128, 4, 640]); cqn = sb("cqn", [128, 4, 640], BF16)
            ssq = sb("ssq", [128, 8]); lnq = sb("lnq", [128, 8]); rsq = sb("rsq", [128, 8])
            cqT = sb("cqT", [128, 3, 512], BF16); ckvT = sb("ckvT", [128, 2, 512], BF16)
            scal = sb("scal", [128, 2, 3, 4, NCH])
            pxt = ps("pxt", [128, 2, 8, 128], BF16)
            pmm = ps("pmm", [128, 6, 512])

            B = {}
            def b(name):
                if name not in B:
                    B[name] = Buf(name)
                return B[name]

            ring_f = Ring(tmpf, 8, "tmpf")
            ring_b = Ring(tmpb, 8, "tmpb")
            ring_p = Ring(pmm, 6, "pmm")
            ring_g = Ring(gt, 2, "gt")
            pxb = [Buf("pxt0"), Buf("pxt1")]
            hTb = [Buf("hT0"), Buf("hT1")]
            xtb = [Buf("xt%d" % i) for i in range(4)]
            xnb = [Buf("xn0"), Buf("xn1")]

            for (dst, src, nm) in ((g1t, g1, "g1t"), (gqat, gqa, "gqat"), (gkvat, gkva, "gkvat"),
                                   (lblt, lbl, "lblt"), (ident, c_ident, "ident"), (rmask, c_rmask, "rmask")):
                st.dma(lambda h, dst=dst, src=src: h.dma_start(out=dst[:], in_=src), writes=[b(nm)])
            st.op("dve", lambda h: h.tensor_tensor(out=lbt[:], in0=lblt[:, :, 1, :], in1=lblt[:, :, 0, :], op=ALU.subtract),
                  reads=[b("lblt")], writes=[b("lbt")])
            st.op("act", lambda h: h.activation(out=lbt[:], in_=lbt[:], func=AF.Exp), reads=[b("lbt")], writes=[b("lbt")])
            st.op("dve", lambda h: h.tensor_scalar(out=lbt[:], in0=lbt[:], scalar1=1.0, scalar2=None, op0=ALU.add),
                  reads=[b("lbt")], writes=[b("lbt")])
            st.op("dve", lambda h: h.reciprocal(out=lbt[:], in_=lbt[:]), reads=[b("lbt")], writes=[b("lbt")])
            st.op("dve", lambda h: h.tensor_scalar(out=omlt[:], in0=lbt[:], scalar1=-1.0, scalar2=1.0, op0=ALU.mult, op1=ALU.add),
                  reads=[b("lbt")], writes=[b("omlt")])
            st.op("dve", lambda h: h.tensor_scalar(out=fb_[:], in0=omlt[:], scalar1=0.5, scalar2=None, op0=ALU.mult),
                  reads=[b("omlt")], writes=[b("fb")])
            st.op("dve", lambda h: h.tensor_tensor(out=fa[:], in0=lbt[:], in1=fb_[:], op=ALU.add),
                  reads=[b("lbt"), b("fb")], writes=[b("fa")])
            st.op("dve", lambda h: h.tensor_scalar(out=nfb[:], in0=fb_[:], scalar1=-1.0, scalar2=None, op0=ALU.mult),
                  reads=[b("fb")], writes=[b("nfb")])
            st.op("act", lambda h: h.activation(out=lnfb[:], in_=fb_[:], func=AF.Ln), reads=[b("fb")], writes=[b("lnfb")])

            pcnt = [0]

            def prep(dst3, src3, nchunk, ncols, gain, eng_cycle, name):
                for c in range(nchunk):
                    for c0 in range(0, ncols, 1024):
                        c1 = min(ncols, c0 + 1024)
                        slot = pcnt[0] % 2
                        eng = eng_cycle[pcnt[0] % len(eng_cycle)]
                        pcnt[0] += 1
                        sbuf = b("stg%d" % slot)
                        st.dma(lambda h, c=c, slot=slot, c0=c0, c1=c1: h.dma_start(out=stg[:, slot, 0:c1 - c0], in_=src3[:, c, c0:c1]),
                               writes=[sbuf])
                        st.op(eng, lambda h, c=c, slot=slot, c0=c0, c1=c1: h.tensor_scalar(
                            out=dst3[:, c, c0:c1], in0=stg[:, slot, 0:c1 - c0], scalar1=gain[:, c:c + 1], scalar2=0.0, op0=ALU.mult, op1=ALU.add),
                            reads=[sbuf, b(name + "_g")], writes=[b(name)])
            B["w1_g"] = b("g1t"); B["wq_g"] = b("gqat"); B["wkv_g"] = b("gkvat")
            prep(w1, w_in, 8, W1C, g1t, ["dve", "pool"], "w1")
            prep(wq, w_qb, 3, 1024, gqat, ["dve", "pool"], "wq")
            prep(wkv, w_kvb, 2, 1024, gkvat, ["dve", "pool"], "wkv")
            v1 = w1[:, :, 3328:3456].rearrange("p c (g r) -> p c g r", r=64)[:, :, :, 0:32]
            st.op("dve", lambda h: h.tensor_scalar(out=v1, in0=v1, scalar1=-1.0, scalar2=None, op0=ALU.mult),
                  reads=[b("w1")], writes=[b("w1")])
            v2 = wq[:, :, 768:1024].rearrange("p c (g r) -> p c g r", r=64)[:, :, :, 0:32]
            st.op("dve", lambda h: h.tensor_scalar(out=v2, in0=v2, scalar1=-1.0, scalar2=None, op0=ALU.mult),
                  reads=[b("wq")], writes=[b("wq")])

            w1b = b("w1")

            def mm_group(lhs_fn, rhs_fn, nk, reads, nfree=512):
                pt, pb = ring_p.get()
                pt = pt[:, 0:nfree]
                for c in range(nk):
                    st.op("pe", lambda h, c=c, pt=pt: h.matmul(pt, lhsT=lhs_fn(c), rhs=rhs_fn(c), start=(c == 0), stop=(c == nk - 1)),
                          reads=reads, writes=[pb])
                return pt, pb

            def xload(s):
                for j in range(4):
                    t0 = s * 512 + j * 128
                    st.dma(lambda h, j=j, t0=t0: h.dma_start(out=xt[:, j, :], in_=x[t0:t0 + 128, :]), writes=[xtb[j]])
                    st.op("act", lambda h, j=j: h.activation(out=xjunk[:, j % 2, :], in_=xt[:, j, :], func=AF.Square, accum_out=ssx[:, 0, j:j + 1]),
                          reads=[xtb[j]], writes=[b("ssx%d" % j), b("xjunk%d" % (j % 2))])

            def xnorm(s):
                st.op("act", lambda h: h.activation(out=lnx[:, 0, :], in_=ssx[:, 0, :], func=AF.Ln, scale=1.0 / D, bias=EPS),
                      reads=[b("ssx%d" % j) for j in range(4)], writes=[b("lnx")])
                st.op("act", lambda h: h.activation(out=rsx[:, 0, :], in_=lnx[:, 0, :], func=AF.Exp, scale=-0.5), reads=[b("lnx")], writes=[b("rsx")])
                for j in range(4):
                    n = j % 2
                    st.op("dve", lambda h, j=j, n=n: h.tensor_scalar(out=xn[:, n, :], in0=xt[:, j, :], scalar1=rsx[:, 0, j:j + 1], scalar2=None, op0=ALU.mult),
                          reads=[xtb[j], b("rsx")], writes=[xnb[n]])
                    for c in range(8):
                        st.op("pe", lambda h, n=n, c=c: h.transpose(out=pxt[:, n, c, :], in_=xn[:, n, c * 128:(c + 1) * 128], identity=ident[:]),
                              reads=[xnb[n], b("ident")], writes=[pxb[n]])
                    st.op("dve", lambda h, n=n, j=j: h.tensor_copy(out=hT[:, 0, :, j * 128:(j + 1) * 128], in_=pxt[:, n, :, :]),
                          reads=[pxb[n]], writes=[hTb[0]])

            def load_tables(s):
                slot = s % 2
                pos0 = (s * 512) % L
                st.dma(lambda h: h.dma_start(out=cst[:, slot, :], in_=c_cos[:, pos0:pos0 + 512]), writes=[b("cst%d" % slot)])
                st.dma(lambda h: h.dma_start(out=snt[:, slot, :], in_=c_sin[:, pos0:pos0 + 512]), writes=[b("snt%d" % slot)])

            def rope(s, pt1, pb1, pt2, pb2, dst):
                slot = s % 2
                f1, fb1 = ring_f.get()
                f2, fb2 = ring_f.get()
                ot, ob = ring_b.get()
                st.op("dve", lambda h: h.tensor_tensor(out=f1, in0=pt1, in1=cst[:, slot, :], op=ALU.mult), reads=[pb1, b("cst%d" % slot)], writes=[fb1])
                st.op("dve", lambda h: h.tensor_tensor(out=f2, in0=pt2, in1=snt[:, slot, :], op=ALU.mult), reads=[pb2, b("snt%d" % slot)], writes=[fb2])
                st.op("pool", lambda h: h.tensor_tensor(out=ot, in0=f1, in1=f2, op=ALU.add), reads=[fb1, fb2], writes=[ob])
                st.dma(lambda h: h.dma_start(out=dst, in_=ot), reads=[ob])

            def tm_piece(s, p):
                t0 = s * 512
                hs = hTb[0]
                if p == 8:
                    pt1, pb1 = mm_group(lambda c: w1[:, c, 3200:3328], lambda c: hT[:, 0, c, :], 8, [hs, w1b])
                    pt2, pb2 = mm_group(lambda c: w1[:, c, 3328:3456], lambda c: hT[:, 0, c, :], 8, [hs, w1b])
                    rope(s, pt1, pb1, pt2, pb2, krT_s[:, t0:t0 + 512])
                    return
                j = p // 2
                tt = t0 + j * 128
                lhs = lambda c: hT[:, 0, c, j * 128:(j + 1) * 128]
                if p % 2 == 0:
                    pt, pb = mm_group(lhs, lambda c: w1[:, c, 1536:2048], 8, [hs, w1b])
                    ot, ob = ring_b.get()
                    st.op("dve", lambda h: h.tensor_copy(out=ot, in_=pt), reads=[pb], writes=[ob])
                    st.dma(lambda h: h.dma_start(out=v_s[tt:tt + 128, :], in_=ot), reads=[ob])
                    pt2, pb2 = mm_group(lhs, lambda c: w1[:, c, 2048:2560], 8, [hs, w1b])
                    st.op("act", lambda h: h.activation(out=gt[:, j, :], in_=pt2, func=AF.Copy), reads=[pb2], writes=[b("gt%d" % j)])
                else:
                    pt, pb = mm_group(lhs, lambda c: w1[:, c, 2560:2944], 8, [hs, w1b], nfree=384)
                    st.op("act", lambda h: h.activation(out=xjunk[:, 0, 0:384], in_=pt[:, 0:384], func=AF.Square, accum_out=ssq[:, j:j + 1]),
                          reads=[pb], writes=[b("ssq%d" % j), b("xjunk0")])
                    st.op("dve", lambda h: h.tensor_copy(out=cqf[:, j, 0:384], in_=pt[:, 0:384]), reads=[pb], writes=[b("cqf%d" % j)])
                    pt2, pb2 = mm_group(lhs, lambda c: w1[:, c, 2944:3200], 8, [hs, w1b], nfree=256)
                    st.op("act", lambda h: h.activation(out=xjunk[:, 1, 0:256], in_=pt2[:, 0:256], func=AF.Square, accum_out=ssq[:, 4 + j:5 + j]),
                          reads=[pb2], writes=[b("ssq%d" % (4 + j)), b("xjunk1")])
                    st.op("dve", lambda h: h.tensor_copy(out=cqf[:, j, 384:640], in_=pt2[:, 0:256]), reads=[pb2], writes=[b("cqf%d" % j)])

            def fm_hf(s, db):
                col = 512 + db * 128
                pt, pb = mm_group(lambda c: w1[:, c, col:col + 128], lambda c: hT[:, 0, c, :], 8, [hTb[0], w1b])
                st.op("act", lambda h: h.activation(out=th[:, db, :], in_=pt, func=AF.Copy), reads=[pb], writes=[b("th%d" % db)])

            def fm_hq(s, blk):
                pt, pb = mm_group(lambda c: w1[:, c, blk * 128:(blk + 1) * 128], lambda c: hT[:, 0, c, :], 8, [hTb[0], w1b])
                st.op("dve", lambda h: h.tensor_copy(out=qT[:, blk, :], in_=pt), reads=[pb], writes=[b("qT%d" % blk)])

            def act18(s):
                t0 = s * 512
                for blk in range(4):
                    st.op("act", lambda h, blk=blk: h.activation(out=qT[:, blk, :], in_=qT[:, blk, :], func=AF.Silu),
                          reads=[b("qT%d" % blk)], writes=[b("qT%d" % blk)])
                for db in range(8):
                    st.op("act", lambda h, db=db: h.activation(out=th[:, db, :], in_=th[:, db, :], func=AF.Tanh, scale=-0.5),
                          reads=[b("th%d" % db)], writes=[b("th%d" % db)])
                for j in range(4):
                    tt = t0 + j * 128
                    st.op("act", lambda h, j=j: h.activation(out=gt[:, j, :], in_=gt[:, j, :], func=AF.Silu), reads=[b("gt%d" % j)], writes=[b("gt%d" % j)])
                    st.dma(lambda h, j=j, tt=tt: h.dma_start(out=g_s[tt:tt + 128, :], in_=gt[:, j, :]), reads=[b("gt%d" % j)])

            def gate_chain(s, db):
                d, blk = db // 4, db % 4
                t0 = s * 512
                c0 = s * 4
                thb = b("th%d" % db)
                lf, lfb = ring_f.get()
                st.op("pool", lambda h: h.tensor_scalar(out=lf, in0=th[:, db, :], scalar1=nfb[:, d, blk:blk + 1], scalar2=fa[:, d, blk:blk + 1],
                                                        op0=ALU.mult, op1=ALU.add),
                      reads=[thb, b("nfb"), b("fa")], writes=[lfb])
                st.op("act", lambda h: h.activation(out=lf, in_=lf, func=AF.Ln), reads=[lfb], writes=[lfb])
                cumt, cumb = ring_f.get()
                cct, ccb = ring_f.get()
                st.op("dve", lambda h: h.tensor_tensor_scan(out=cumt, data0=rmask[:], data1=lf, initial=0.0, op0=ALU.mult, op1=ALU.add),
                      reads=[lfb, b("rmask")], writes=[cumb])
                cumv = cumt.rearrange("p (c t) -> p c t", t=128)
                ccv = cct.rearrange("p (c t) -> p c t", t=128)
                st.op("dve", lambda h: h.tensor_tensor(out=ccv, in0=cumv, in1=bc(cumv[:, :, 63:64], [128, 4, 128]), op=ALU.subtract),
                      reads=[cumb], writes=[ccb])
                Xi, Yi = (1, 2) if d == 0 else (2, 1)
                st.op("act", lambda h: h.activation(out=scal[:, d, 0, blk, c0:c0 + 4], in_=cumv[:, :, 127], func=AF.Exp), reads=[cumb], writes=[b("scal")])
                st.op("act", lambda h: h.activation(out=scal[:, d, Xi, blk, c0:c0 + 4], in_=ccv[:, :, 127], func=AF.Exp), reads=[ccb], writes=[b("scal")])
                st.op("act", lambda h: h.activation(out=scal[:, d, Yi, blk, c0:c0 + 4], in_=cumv[:, :, 63], func=AF.Exp), reads=[cumb], writes=[b("scal")])
                if d == 0:
                    srct, srcb = cct, ccb
                else:
                    srct, srcb = ring_f.get()
                    st.op("pool", lambda h: h.tensor_tensor(out=srct, in0=lf, in1=cct, op=ALU.subtract), reads=[lfb, ccb], writes=[srcb])
                ea, eab = ring_f.get()
                eb, ebb = ring_f.get()
                st.op("act", lambda h: h.activation(out=ea, in_=srct, func=AF.Exp), reads=[srcb], writes=[eab])
                st.op("act", lambda h: h.activation(out=eb, in_=srct, func=AF.Exp, scale=-1.0, bias=lnfb[:, d, blk:blk + 1]),
                      reads=[srcb, b("lnfb")], writes=[ebb])
                o1, o1b = ring_b.get()
                o2, o2b = ring_b.get()
                st.op("dve", lambda h: h.tensor_tensor(out=o1, in0=qT[:, blk, :], in1=ea, op=ALU.mult), reads=[b("qT%d" % blk), eab], writes=[o1b])
                st.op("dve", lambda h: h.scalar_tensor_tensor(out=o2, in0=th[:, db, :], scalar=1.0, in1=eb, op0=ALU.add, op1=ALU.mult),
                      reads=[thb, ebb], writes=[o2b])
                st.dma(lambda h: h.dma_start(out=qdT_s[d, blk, :, t0:t0 + 512], in_=o1), reads=[o1b])
                st.dma(lambda h: h.dma_start(out=kiT_s[d, blk, :, t0:t0 + 512], in_=o2), reads=[o2b])

            def cq_rstd():
                st.op("act", lambda h: h.activation(out=lnq[:, 0:4], in_=ssq[:, 0:4], func=AF.Ln, scale=1.0 / 384, bias=EPS),
                      reads=[b("ssq%d" % i) for i in range(8)], writes=[b("lnq")])
                st.op("act", lambda h: h.activation(out=lnq[:, 4:8], in_=ssq[:, 4:8], func=AF.Ln, scale=1.0 / 256, bias=EPS),
                      reads=[b("ssq%d" % i) for i in range(8)], writes=[b("lnq")])
                st.op("act", lambda h: h.activation(out=rsq[:], in_=lnq[:], func=AF.Exp, scale=-0.5), reads=[b("lnq")], writes=[b("rsq")])

            def cq_transpose(s):
                for j in range(4):
                    st.op("dve", lambda h, j=j: h.tensor_scalar(out=cqn[:, j, 0:384], in0=cqf[:, j, 0:384], scalar1=rsq[:, j:j + 1], scalar2=None, op0=ALU.mult),
                          reads=[b("cqf%d" % j), b("rsq")], writes=[b("cqn%d" % j)])
                    st.op("dve", lambda h, j=j: h.tensor_scalar(out=cqn[:, j, 384:640], in0=cqf[:, j, 384:640], scalar1=rsq[:, 4 + j:5 + j], scalar2=None, op0=ALU.mult),
                          reads=[b("cqf%d" % j), b("rsq")], writes=[b("cqn%d" % j)])
                for j in range(4):
                    n = j % 2
                    for c in range(5):
                        st.op("pe", lambda h, n=n, c=c, j=j: h.transpose(out=pxt[:, n, c, :], in_=cqn[:, j, c * 128:(c + 1) * 128], identity=ident[:]),
                              reads=[b("cqn%d" % j), b("ident")], writes=[pxb[n]])
                    st.op("dve", lambda h, n=n, j=j: h.tensor_copy(out=cqT[:, :, j * 128:(j + 1) * 128], in_=pxt[:, n, 0:3, :]), reads=[pxb[n]], writes=[b("cqT")])
                    st.op("dve", lambda h, n=n, j=j: h.tensor_copy(out=ckvT[:, :, j * 128:(j + 1) * 128], in_=pxt[:, n, 3:5, :]), reads=[pxb[n]], writes=[b("ckvT")])

            def second_proj(s):
                t0 = s * 512
                out = []

                def qn(hh):
                    pt, pb = mm_group(lambda c: wq[:, c, hh * 128:(hh + 1) * 128], lambda c: cqT[:, c, :], 3, [b("cqT"), b("wq")])
                    ot, ob = ring_b.get()
                    st.op("dve", lambda h: h.tensor_copy(out=ot, in_=pt), reads=[pb], writes=[ob])
                    st.dma(lambda h: h.dma_start(out=qnT_s[hh, :, t0:t0 + 512], in_=ot), reads=[ob])

                def qr_(pr):
                    pt1, pb1 = mm_group(lambda c: wq[:, c, 512 + pr * 128:640 + pr * 128], lambda c: cqT[:, c, :], 3, [b("cqT"), b("wq")])
                    pt2, pb2 = mm_group(lambda c: wq[:, c, 768 + pr * 128:896 + pr * 128], lambda c: cqT[:, c, :], 3, [b("cqT"), b("wq")])
                    rope(s, pt1, pb1, pt2, pb2, qrT_s[pr, :, t0:t0 + 512])

                def kn(hh):
                    pt, pb = mm_group(lambda c: wkv[:, c, hh * 128:(hh + 1) * 128], lambda c: ckvT[:, c, :], 2, [b("ckvT"), b("wkv")])
                    ot, ob = ring_b.get()
                    st.op("act", lambda h: h.activation(out=ot, in_=pt, func=AF.Copy), reads=[pb], writes=[ob])
                    st.dma(lambda h: h.dma_start(out=knT_s[hh, :, t0:t0 + 512], in_=ot), reads=[ob])

                def vv(j):
                    tt = t0 + j * 128
                    pt, pb = mm_group(lambda c: ckvT[:, c, j * 128:(j + 1) * 128], lambda c: wkv[:, c, 512:1024], 2, [b("ckvT"), b("wkv")])
                    ot, ob = ring_b.get()
                    st.op("dve", lambda h: h.tensor_copy(out=ot, in_=pt), reads=[pb], writes=[ob])
                    st.dma(lambda h: h.dma_start(out=vm_s[tt:tt + 128, :], in_=ot), reads=[ob])
                for hh in range(4):
                    out.append(lambda hh=hh: qn(hh))
                for pr in range(2):
                    out.append(lambda pr=pr: qr_(pr))
                for hh in range(4):
                    out.append(lambda hh=hh: kn(hh))
                for j in range(4):
                    out.append(lambda j=j: vv(j))
                return out

            xload(0)
            load_tables(0)
            xnorm(0)
            for p in range(9):
                tm_piece(0, p)
            for blk in range(4):
                fm_hq(0, blk)
            for db in range(8):
                fm_hf(0, db)
            cq_rstd()
            for s in range(NS):
                nxt = s + 1 < NS
                if nxt:
                    xload(s + 1)
                    load_tables(s + 1)
                cq_transpose(s)
                act18(s)
                sp = second_proj(s)
                for f_ in sp[:6]:
                    f_()
                if nxt:
                    xnorm(s + 1)
                rest = sp[6:]
                for db in range(8):
                    gate_chain(s, db)
                    if nxt:
                        tm_piece(s + 1, db)
                        if db == 7:
                            tm_piece(s + 1, 8)
                        fm_hf(s + 1, db)
                        if db >= 4:
                            fm_hq(s + 1, db - 4)
                    if rest:
                        rest.pop(0)()
                while rest:
                    rest.pop(0)()
                if nxt:
                    cq_rstd()
            st.dma(lambda h: h.dma_start(out=scal_s, in_=scal[:]), reads=[b("scal")])
            st.finish()
            st.emit()

    def stage2():
        NK = L // 128
        NQ = L // 512
        DEN = os.environ.get("S2_DEN", "pe")
        ROPE128 = os.environ.get("S2_ROPE", "k128") == "k128"
        NPS = int(os.environ.get("S2_NPS", "4"))
        NP = int(os.environ.get("S2_NP", "6"))
        with contextlib.ExitStack() as es:
            def sb(name, shape, dt=F32):
                return es.enter_context(nc.sbuf_tensor("s2_" + name, list(shape), dt))

            def ps(name, shape, dt=F32):
                return es.enter_context(nc.psum_tensor("s2_" + name, list(shape), dt))

            st = Stage(nc, "s2")
            knT = sb("knT", [128, 4, L], BF16)
            vm = sb("vm", [128, NK, 512], BF16)
            krT = sb("krT", [128, L], BF16)
            qn = sb("qn", [128, 2, 4, 512], BF16)
            qr = sb("qr", [128, 2, 2, 512], BF16)
            ones = sb("ones", [128, 128], BF16)
            Pt = sb("Pt", [128, NP, 512], BF16)
            qrz = sb("qrz", [128, 2, 4, 512], BF16)
            hm = sb("hm", [128, 2])
            accP = sb("accP", [128, 2, 512])
            oT = sb("oT", [128, 2, 4, 512])
            rden = sb("rden", [128, 2, 512])
            sq = sb("sq", [128, 4, 512], BF16)
            lnv = sb("lnv", [128, 512]); rstd = sb("rstd", [128, 512])
            yb = sb("yb", [128, 4, 512], BF16)
            pS = ps("pS", [128, NPS, 512])
            pO = ps("pO", [128, 2, 512])
            pD = ps("pD", [128, 1, 512])
            pSS = ps("pSS", [128, 512])
            acc = sb("acc", [128, 2, 512])
            onesf = sb("onesf", [128, 128])
            B = {}

            def b(name):
                if name not in B:
                    B[name] = Buf(name)
                return B[name]
            ring_P = Ring(Pt, NP, "Pt")
            ring_S = Ring(pS, NPS, "pS")
            ring_y = Ring(yb, 4, "yb")
            st.dma(lambda h: h.dma_start(out=ones[:], in_=c_ones), writes=[b("ones")])
            st.op("pool", lambda h: h.memset(onesf[:], 1.0), writes=[b("onesf")])
            st.op("pool", lambda h: h.memset(hm[:], 0.0), writes=[b("hm")])
            st.op("pool", lambda h: h.memset(hm[0:64, 0:1], 1.0), writes=[b("hm")])
            st.op("pool", lambda h: h.memset(hm[64:128, 1:2], 1.0), writes=[b("hm")])

            items = []
            for seq in range(n_seq):
                for qb in range(NQ):
                    for hh in range(4):
                        for kc in range(NK):
                            items.append((seq, qb, hh, kc))

            def loads_seq(seq):
                s0 = seq * L
                for hh in range(4):
                    st.dma(lambda h, hh=hh: h.dma_start(out=knT[:, hh, :], in_=knT_s[hh, :, s0:s0 + L]), writes=[b("knT")])
                st.dma(lambda h: h.dma_start(out=krT[:], in_=krT_s[:, s0:s0 + L]), writes=[b("krT")])
                st.dma(lambda h: h.dma_start(out=vm[:], in_=vm_s[s0:s0 + L, :].rearrange("(k p) f -> p k f", p=128)), writes=[b("vm")])

            def loads_q(seq, qb):
                slot = (seq * NQ + qb) % 2
                tq = seq * L + qb * 512
                st.dma(lambda h: h.dma_start(out=qn[:, slot], in_=qnT_s[:, :, tq:tq + 512].rearrange("h p t -> p h t")), writes=[b("qn%d" % slot)])
                st.dma(lambda h: h.dma_start(out=qr[:, slot], in_=qrT_s[:, :, tq:tq + 512].rearrange("h p t -> p h t")), writes=[b("qr%d" % slot)])
                if ROPE128:
                    for hh in range(4):
                        st.op("pool", lambda h, hh=hh: h.tensor_scalar(out=qrz[:, slot, hh, :], in0=qr[:, slot, hh // 2, :], scalar1=hm[:, hh % 2:hh % 2 + 1],
                                                                        scalar2=None, op0=ALU.mult),
                              reads=[b("qr%d" % slot), b("hm")], writes=[b("qrz%d" % slot)])

            def qk(item):
                seq, qb, hh, kc = item
                slot = (seq * NQ + qb) % 2
                hp, pr = hh % 2, hh // 2
                pt, pb = ring_S.get()
                st.op("pe", lambda h: h.matmul(pt, lhsT=knT[:, hh, kc * 128:(kc + 1) * 128], rhs=qn[:, slot, hh, :], start=True, stop=False),
                      reads=[b("knT"), b("qn%d" % slot)], writes=[pb])
                if ROPE128:
                    st.op("pe", lambda h: h.matmul(pt, lhsT=krT[:, kc * 128:(kc + 1) * 128], rhs=qrz[:, slot, hh, :], start=False, stop=True),
                          reads=[b("krT"), b("qrz%d" % slot)], writes=[pb])
                else:
                    st.op("pe", lambda h: h.matmul(pt, lhsT=krT[hp * 64:(hp + 1) * 64, kc * 128:(kc + 1) * 128],
                                                   rhs=qr[hp * 64:(hp + 1) * 64, slot, pr, :], start=False, stop=True),
                          reads=[b("krT"), b("qr%d" % slot)], writes=[pb])
                return pt, pb

            def finish_head(seq, qb, hh, oslot):
                qslot = (seq * NQ + qb) % 2
                if DEN == "dve":
                    st.op("pe", lambda h: h.matmul(pD[:, 0, :], lhsT=onesf[:], rhs=acc[:, oslot, :], start=True, stop=True),
                          reads=[b("onesf"), b("acc%d" % oslot)], writes=[b("pD0")])
                elif DEN == "split":
                    st.op("pe", lambda h: h.matmul(pD[:, 0, :], lhsT=onesf[:], rhs=acc[:, oslot, :], start=True, stop=False),
                          reads=[b("onesf"), b("acc%d" % oslot)], writes=[b("pD0")])
                    st.op("pe", lambda h: h.matmul(pD[:, 0, :], lhsT=onesf[:], rhs=accP[:, oslot, :], start=False, stop=True),
                          reads=[b("onesf"), b("accP%d" % oslot)], writes=[b("pD0")])
                st.op("dve", lambda h: h.reciprocal(out=rden[:, oslot, :], in_=pD[:, 0, :]), reads=[b("pD0")], writes=[b("rden%d" % oslot)])
                st.op("dve", lambda h: h.tensor_tensor(out=oT[:, qslot, hh, :], in0=pO[:, oslot, :], in1=rden[:, oslot, :], op=ALU.mult),
                      reads=[b("pO%d" % oslot), b("rden%d" % oslot)], writes=[b("oT%d_%d" % (qslot, hh))])

            def finish_q(seq, qb):
                qslot = (seq * NQ + qb) % 2
                tq = seq * L + qb * 512
                for hh in range(4):
                    st.op("act", lambda h, hh=hh: h.activation(out=sq[:, hh, :], in_=oT[:, qslot, hh, :], func=AF.Square),
                          reads=[b("oT%d_%d" % (qslot, hh))], writes=[b("sq%d" % hh)])
                for hh in range(4):
                    st.op("pe", lambda h, hh=hh: h.matmul(pSS[:], lhsT=ones[:], rhs=sq[:, hh, :], start=(hh == 0), stop=(hh == 3)),
                          reads=[b("ones"), b("sq%d" % hh)], writes=[b("pSS")])
                st.op("act", lambda h: h.activation(out=lnv[:], in_=pSS[:], func=AF.Ln, scale=1.0 / 512, bias=EPS), reads=[b("pSS")], writes=[b("lnv")])
                st.op("act", lambda h: h.activation(out=rstd[:], in_=lnv[:], func=AF.Exp, scale=-0.5), reads=[b("lnv")], writes=[b("rstd")])
                for hh in range(4):
                    yt, ybuf = ring_y.get()
                    st.op("dve", lambda h, hh=hh, yt=yt: h.tensor_tensor(out=yt, in0=oT[:, qslot, hh, :], in1=rstd[:], op=ALU.mult),
                          reads=[b("oT%d_%d" % (qslot, hh)), b("rstd")], writes=[ybuf])
                    st.dma(lambda h, hh=hh, yt=yt: h.dma_start(out=yT_s[4 + hh, :, tq:tq + 512], in_=yt), reads=[ybuf])

            import collections
            LA = int(os.environ.get("S2_LA", "2"))
            pending = collections.deque()
            nq = [0]

            def ensure(upto, seq):
                while nq[0] < min(upto, len(items)) and items[nq[0]][0] == seq:
                    pending.append(qk(items[nq[0]]))
                    nq[0] += 1

            ocnt = 0
            for idx, item in enumerate(items):
                seq, qb, hh, kc = item
                if qb == 0 and hh == 0 and kc == 0:
                    loads_seq(seq)
                    loads_q(seq, 0)
                if hh == 0 and kc == 0 and qb + 1 < NQ:
                    loads_q(seq, qb + 1)
                ensure(idx + 1 + LA, seq)
                pt, pb = pending.popleft()
                oslot = ocnt % 2
                Pa, Pb = ring_P.get()
                st.op("act", lambda h, Pa=Pa, pt=pt: h.activation(out=Pa, in_=pt, func=AF.Exp, scale=SCALE), reads=[pb], writes=[Pb])
                st.op("pe", lambda h, Pa=Pa, oslot=oslot, hh=hh, kc=kc: h.matmul(pO[:, oslot, :], lhsT=vm[:, kc, hh * 128:(hh + 1) * 128], rhs=Pa,
                                                                           start=(kc == 0), stop=(kc == NK - 1)),
                      reads=[b("vm"), Pb], writes=[b("pO%d" % oslot)])
                if DEN == "pe":
                    st.op("pe", lambda h, Pa=Pa, kc=kc: h.matmul(pD[:, 0, :], lhsT=ones[:], rhs=Pa, start=(kc == 0), stop=(kc == NK - 1)),
                          reads=[b("ones"), Pb], writes=[b("pD0")])
                else:
                    on_pool = (DEN == "split" and kc % 4 == 3)
                    eng, at, an, first = ("pool", accP, "accP", kc == 3) if on_pool else ("dve", acc, "acc", kc == 0)
                    if first:
                        st.op(eng, lambda h, Pa=Pa, oslot=oslot, at=at: h.tensor_copy(out=at[:, oslot, :], in_=Pa), reads=[Pb], writes=[b("%s%d" % (an, oslot))])
                    else:
                        st.op(eng, lambda h, Pa=Pa, oslot=oslot, at=at: h.tensor_tensor(out=at[:, oslot, :], in0=at[:, oslot, :], in1=Pa, op=ALU.add),
                              reads=[Pb, b("%s%d" % (an, oslot))], writes=[b("%s%d" % (an, oslot))])
                if kc == NK - 1:
                    finish_head(seq, qb, hh, oslot)
                    ocnt += 1
                    if hh == 3:
                        finish_q(seq, qb)
            st.finish()
            st.emit()

    def stage3():
        NCs = L // 128
        NCHN = 2 * n_seq
        with contextlib.ExitStack() as es:
            def sb(name, shape, dt=F32):
                return es.enter_context(nc.sbuf_tensor("s3_" + name, list(shape), dt))

            def ps(name, shape, dt=F32):
                return es.enter_context(nc.psum_tensor("s3_" + name, list(shape), dt))

            st = Stage(nc, "s3")
            scal = sb("scal", [128, 2, 3, 4, NCH])
            masks = sb("masks", [128, 2, 128])
            ident = sb("ident", [128, 128], BF16)
            BD = sb("BD", [128, 4, 128])
            qd = sb("qd", [128, 2 * NCHN, 4, 256], BF16)
            ki = sb("ki", [128, 2 * NCHN, 4, 256], BF16)
            vt = sb("vt", [128, 2 * NCHN, 2, 512], BF16)
            kitok = sb("kitok", [128, NCHN, 512], BF16)
            scT = sb("scT", [128, NCHN, 8, 128], BF16)
            S_all = sb("S", [128, NCHN, 4, 128]); Sp_all = sb("Sp", [128, NCHN, 4, 128], BF16)
            t1_all = sb("t1", [128, NCHN, 4, 128]); t2_all = sb("t2", [128, NCHN, 4, 128])
            Bm_all = sb("Bm", [128, NCHN, 4, 128])
            oev = sb("oev", [128, 4, 512])
            cof = sb("cof", [128, 3, 512]); cob = sb("cob", [128, 3, 512]); cg_ = sb("cg", [128, 3, 512])
            osum_a = sb("osum", [128, 2, 512]); sqt_a = sb("sqt", [128, 2, 512]); yn_a = sb("yn", [128, 2, 512])
            ss8_a = sb("ss8", [128, 2, 8]); ln8_a = sb("ln8", [128, 2, 8]); rs8_a = sb("rs8", [128, 2, 8])
            ya = sb("ya", [128, 2, 512], BF16)
            yaT = sb("yaT", [128, 2, 4, 128], BF16)
            pT = ps("pT", [128, 1, 1024], BF16)
            pSc = ps("pSc", [128, 4, 4, 128])
            pOo = ps("pOo", [128, 2, 512])
            pU = ps("pU", [128, 1, 4, 128])
            pY = pU[:].bitcast(BF16).rearrange("p a b (c t) -> p (a b c) t", t=128)
            B = {}

            def b(name):
                if name not in B:
                    B[name] = Buf(name)
                return B[name]
            ring_oev = Ring(oev, 4, "oev")
            st.dma(lambda h: h.dma_start(out=scal[:], in_=scal_s), writes=[b("scal")])
            st.dma(lambda h: h.dma_start(out=masks[:, 0, :], in_=c_maskf), writes=[b("masks")])
            st.dma(lambda h: h.dma_start(out=masks[:, 1, :], in_=c_maskb), writes=[b("masks")])
            st.dma(lambda h: h.dma_start(out=ident[:], in_=c_ident), writes=[b("ident")])
            st.op("pool", lambda h: h.memset(BD[:], 0.0), writes=[b("BD")])
            st.op("pool", lambda h: h.memset(BD[0:64, :, 0:64], 1.0), writes=[b("BD")])
            st.op("pool", lambda h: h.memset(BD[64:128, :, 64:128], 1.0), writes=[b("BD")])
            rot = {"pT": 0, "pOo": 0, "pSc": 0}

            def chain(seq, d):
                ci = seq * 2 + d
                order = list(range(NCs)) if d == 0 else list(range(NCs - 1, -1, -1))
                S = S_all[:, ci]; Sp = Sp_all[:, ci]; t1 = t1_all[:, ci]; t2 = t2_all[:, ci]; Bm = Bm_all[:, ci]
                bS, bSp, bt1, bt2, bBm = (b("%s%d" % (nm, ci)) for nm in ("S", "Sp", "t1", "t2", "Bm"))
                st.op("pool", lambda h: h.memset(S, 0.0), writes=[bS])
                st.op("pool", lambda h: h.memset(Sp, 0.0), writes=[bSp])
                odst = of_s if d == 0 else ob_s
                def load_group(k):
                    gi = order[2 * k] // 2
                    slot = ci * 2 + k % 2
                    tg = seq * L + gi * 256
                    st.dma(lambda h: h.dma_start(out=qd[:, slot], in_=qdT_s[d, :, :, tg:tg + 256].rearrange("k p t -> p k t")), writes=[b("qd%d" % slot)])
                    st.dma(lambda h: h.dma_start(out=ki[:, slot], in_=kiT_s[d, :, :, tg:tg + 256].rearrange("k p t -> p k t")), writes=[b("ki%d" % slot)])
                    st.dma(lambda h: h.dma_start(out=vt[:, slot], in_=v_s[tg:tg + 256, :].rearrange("(j p) f -> p j f", p=128)), writes=[b("vt%d" % slot)])

                load_group(0)
                for pos, n in enumerate(order):
                    if pos % 2 == 0 and pos + 2 < len(order):
                        load_group(pos // 2 + 1)
                    slot = ci * 2 + (pos // 2) % 2
                    j = n % 2
                    cg = seq * NCs + n
                    t0 = cg * 128
                    qd_c = qd[:, slot, :, j * 128:(j + 1) * 128]
                    ki_c = ki[:, slot, :, j * 128:(j + 1) * 128]
                    v_c = vt[:, slot, j, :]
                    rq, rk, rv = b("qd%d" % slot), b("ki%d" % slot), b("vt%d" % slot)
                    has_next = pos + 1 < len(order)
                    tb = 0
                    sset = rot["pSc"] % 2
                    rot["pSc"] += 1
                    for bk in range(4):
                        st.op("pe", lambda h, bk=bk, tb=tb, ki_c=ki_c: h.transpose(out=pT[:, tb, bk * 128:(bk + 1) * 128], in_=ki_c[:, bk, :], identity=ident[:]),
                              reads=[rk, b("ident")], writes=[b("pT%d" % tb)])
                    st.op("act", lambda h, tb=tb: h.activation(out=kitok[:, ci, :], in_=pT[:, tb, 0:512], func=AF.Copy),
                          reads=[b("pT%d" % tb)], writes=[b("kitok%d" % ci)])
                    for hh in range(8):
                        bk, hp = hh // 2, hh % 2
                        st.op("pe", lambda h, bk=bk, hp=hp, ki_c=ki_c, qd_c=qd_c, sset=sset: h.matmul(pSc[:, sset * 2 + hp, bk, :], lhsT=ki_c[hp * 64:(hp + 1) * 64, bk, :],
                                                                                      rhs=qd_c[hp * 64:(hp + 1) * 64, bk, :], start=True, stop=True),
                              reads=[rk, rq], writes=[b("pSc%d" % (sset * 2 + hp))])
                    scv = scT[:, ci].rearrange("p (k two) t -> p k two t", two=2)
                    for half in range(2):
                        st.op("dve", lambda h, half=half, scv=scv, sset=sset: h.tensor_tensor(out=scv[:, :, half, :], in0=pSc[:, sset * 2 + half],
                                                                                             in1=bc(masks[:, d:d + 1, :], [128, 4, 128]), op=ALU.mult),
                              reads=[b("pSc%d" % (sset * 2 + half)), b("masks")], writes=[b("scT%d_%d" % (ci, half))])
                    if has_next:
                        st.op("pool", lambda h, cg=cg: h.tensor_tensor(out=Bm, in0=BD[:], in1=bc(scal[:, d, 1, :, cg:cg + 1], [128, 4, 128]), op=ALU.mult),
                              reads=[b("BD"), b("scal")], writes=[bBm])
                    yield
                    ob_ = rot["pOo"] % 2
                    rot["pOo"] += 1
                    for bk in range(4):
                        st.op("pe", lambda h, bk=bk, ob_=ob_, qd_c=qd_c: h.matmul(pOo[:, ob_, bk * 128:(bk + 1) * 128], lhsT=qd_c[:, bk, :], rhs=Sp[:, bk, :],
                                                                              start=True, stop=False),
                              reads=[rq, bSp], writes=[b("pOo%d" % ob_)])
                        for hp in range(2):
                            hh = 2 * bk + hp
                            st.op("pe", lambda h, hh=hh, hp=hp, ob_=ob_, v_c=v_c: h.matmul(pOo[:, ob_, hh * 64:(hh + 1) * 64], lhsT=scT[:, ci, hh, :],
                                                                                       rhs=v_c[:, hh * 64:(hh + 1) * 64], start=False, stop=(hp == 1)),
                                  reads=[b("scT%d_%d" % (ci, hh % 2)), rv], writes=[b("pOo%d" % ob_)])
                    ot, otb = ring_oev.get()
                    st.op("act", lambda h, ot=ot, ob_=ob_: h.activation(out=ot, in_=pOo[:, ob_, :], func=AF.Copy), reads=[b("pOo%d" % ob_)], writes=[otb])
                    st.dma(lambda h, ot=ot, t0=t0: h.dma_start(out=odst[t0:t0 + 128, :], in_=ot), reads=[otb], writes=[b("o%d_%d" % (d, cg))], queue="act")
                    yield
                    if has_next:
                        cgn = seq * NCs + order[pos + 1]
                        for bk in range(4):
                            st.op("pe", lambda h, bk=bk, v_c=v_c: h.matmul(pU[:, 0, bk, :], lhsT=kitok[:, ci, bk * 128:(bk + 1) * 128], rhs=v_c[:, bk * 128:(bk + 1) * 128],
                                                                          start=True, stop=True),
                                  reads=[b("kitok%d" % ci), rv], writes=[b("pU0")])
                        Abc = bc(scal[:, d, 0, :, cg:cg + 1], [128, 4, 128])
                        Cbc = bc(scal[:, d, 2, :, cgn:cgn + 1], [128, 4, 128])
                        st.op("pool", lambda h, Abc=Abc: h.tensor_tensor(out=t1, in0=S, in1=Abc, op=ALU.mult), reads=[bS, b("scal")], writes=[bt1])
                        st.op("dve", lambda h: h.tensor_tensor(out=t2, in0=pU[:, 0], in1=Bm, op=ALU.mult), reads=[b("pU0"), bBm], writes=[bt2])
                        st.op("dve", lambda h: h.tensor_tensor(out=S, in0=t1, in1=t2, op=ALU.add), reads=[bt1, bt2], writes=[bS])
                        st.op("dve", lambda h, Cbc=Cbc: h.tensor_tensor(out=Sp, in0=S, in1=Cbc, op=ALU.mult), reads=[bS, b("scal")], writes=[bSp])
                    yield

            gens = [chain(seq, d) for seq in range(n_seq) for d in range(2)]
            live = list(gens)
            while live:
                for g in list(live):
                    try:
                        next(g)
                    except StopIteration:
                        live.remove(g)

            for cg in range(NCH):
                t0 = cg * 128
                k3 = cg % 3
                k2 = cg % 2
                osum = osum_a[:, k2]; sqt = sqt_a[:, k2]; yn = yn_a[:, k2]
                ss8 = ss8_a[:, k2]; ln8 = ln8_a[:, k2]; rs8 = rs8_a[:, k2]
                bos, bsq, byn, bss, bln, brs = (b("%s%d" % (nm, k2)) for nm in ("osum", "sqt", "yn", "ss8", "ln8", "rs8"))
                st.dma(lambda h, k3=k3, t0=t0: h.dma_start(out=cof[:, k3], in_=of_s[t0:t0 + 128, :]), reads=[b("o0_%d" % cg)], writes=[b("cof%d" % k3)])
                st.dma(lambda h, k3=k3, t0=t0: h.dma_start(out=cob[:, k3], in_=ob_s[t0:t0 + 128, :]), reads=[b("o1_%d" % cg)], writes=[b("cob%d" % k3)])
                st.dma(lambda h, k3=k3, t0=t0: h.dma_start(out=cg_[:, k3], in_=g_s[t0:t0 + 128, :]), writes=[b("cg%d" % k3)])
                st.op("dve", lambda h, k3=k3, osum=osum: h.tensor_tensor(out=osum, in0=cof[:, k3], in1=cob[:, k3], op=ALU.add),
                      reads=[b("cof%d" % k3), b("cob%d" % k3)], writes=[bos])
                st.op("act", lambda h, osum=osum, sqt=sqt: h.activation(out=sqt, in_=osum, func=AF.Square), reads=[bos], writes=[bsq])
                st.op("dve", lambda h, sqt=sqt, ss8=ss8: h.tensor_reduce(out=ss8, in_=sqt.rearrange("p (h e) -> p h e", e=64), axis=AX.X, op=ALU.add),
                      reads=[bsq], writes=[bss])
                st.op("act", lambda h, ss8=ss8, ln8=ln8: h.activation(out=ln8, in_=ss8, func=AF.Ln, scale=1.0 / 64, bias=EPS), reads=[bss], writes=[bln])
                st.op("act", lambda h, ln8=ln8, rs8=rs8: h.activation(out=rs8, in_=ln8, func=AF.Exp, scale=-0.5), reads=[bln], writes=[brs])
                st.op("dve", lambda h, yn=yn, osum=osum, rs8=rs8: h.tensor_tensor(out=yn.rearrange("p (h e) -> p h e", e=64), in0=osum.rearrange("p (h e) -> p h e", e=64),
                                                                              in1=bc(rs8.unsqueeze(2), [128, 8, 64]), op=ALU.mult),
                      reads=[bos, brs], writes=[byn])
                st.op("pool", lambda h, yn=yn, k2=k2, k3=k3: h.tensor_tensor(out=ya[:, k2, :], in0=yn, in1=cg_[:, k3], op=ALU.mult),
                      reads=[byn, b("cg%d" % k3)], writes=[b("ya%d" % k2)])
                for bk in range(4):
                    st.op("pe", lambda h, bk=bk, k2=k2: h.transpose(out=pY[:, bk, :], in_=ya[:, k2, bk * 128:(bk + 1) * 128], identity=ident[:]),
                          reads=[b("ya%d" % k2), b("ident")], writes=[b("pU0")])
                st.op("act", lambda h, k2=k2: h.activation(out=yaT[:, k2], in_=pY[:, 0:4, :], func=AF.Copy), reads=[b("pU0")], writes=[b("yaT%d" % k2)])
                st.dma(lambda h, k2=k2, t0=t0: h.dma_start(out=yT_s[0:4, :, t0:t0 + 128].rearrange("k p t -> p k t"), in_=yaT[:, k2]), reads=[b("yaT%d" % k2)], queue="act")
            st.finish()
            st.emit()

    def stage4():
        TT = 256
        NTL = T // TT
        with contextlib.ExitStack() as es:
            def sb(name, shape, dt=F32):
                return es.enter_context(nc.sbuf_tensor("s4_" + name, list(shape), dt))

            def ps(name, shape, dt=F32):
                return es.enter_context(nc.psum_tensor("s4_" + name, list(shape), dt))

            st = Stage(nc, "s4")
            wo = sb("wo", [128, 8, D], BF16)
            wg = sb("wg", [128, 8, DFF], BF16)
            wu = sb("wu", [128, 8, DFF], BF16)
            wd = sb("wd", [128, NFB, D], BF16)
            goutt = sb("goutt", [128, 8]); g2t = sb("g2t", [128, 8])
            gfb = sb("gfb", [128, D])
            ident = sb("ident", [128, 128], BF16)
            xt = sb("xt", [128, 2, 2, D])
            yT = sb("yT", [128, 2, 8, TT], BF16)
            h2n = sb("h2n", [128, 2, D], BF16)
            h2T = sb("h2T", [128, 8, TT], BF16)
            aT = sb("aT", [128, NFB, TT], BF16)
            sg = sb("sg", [128, 2, TT])
            junk = sb("junk", [128, D], BF16)
            ss = sb("ss", [128, 4]); lnt = sb("lnt", [128, 4]); rs = sb("rs", [128, 4])
            pxt = ps("pxt", [128, 8, 128], BF16)
            pG = ps("pG", [128, 2, 512])
            pUu = ps("pUu", [128, 2, 512])
            pA = ps("pA", [128, 2, 512])
            B = {}

            def b(name):
                if name not in B:
                    B[name] = Buf(name)
                return B[name]
            ring_A = Ring(pA, 2, "pA")
            for (dst, src, nm) in ((goutt, gout, "goutt"), (g2t, g2, "g2t"), (ident, c_ident, "ident")):
                st.dma(lambda h, dst=dst, src=src: h.dma_start(out=dst[:], in_=src), writes=[b(nm)])
            if os.environ.get("S4_NOBC"):
                for pp in range(0, 128, 32):
                    pass
                st.op("pool", lambda h: h.memset(gfb[:], 1.0), writes=[b("gfb")])
            else:
                st.dma(lambda h: h.dma_start(out=gfb[:], in_=gfin.partition_broadcast(128)), writes=[b("gfb")])
            stg = xt[:].rearrange("p a b d -> p (a b) d")
            pcnt = [0]

            def prep(dst3, src3, nchunk, ncols, gain, name, gname):
                for c in range(nchunk):
                    for c0 in range(0, ncols, 1024):
                        c1 = min(ncols, c0 + 1024)
                        slot = pcnt[0] % 4
                        eng = ("dve", "act")[pcnt[0] % 2]
                        pcnt[0] += 1
                        sbuf = b("xt%d" % slot)
                        st.dma(lambda h, c=c, slot=slot, c0=c0, c1=c1: h.dma_start(out=stg[:, slot, 0:c1 - c0], in_=src3[:, c, c0:c1]), writes=[sbuf])
                        rd = [sbuf] + ([b(gname)] if gain is not None else [])
                        if eng == "act":
                            if gain is None:
                                st.op("act", lambda h, c=c, slot=slot, c0=c0, c1=c1: h.activation(out=dst3[:, c, c0:c1], in_=stg[:, slot, 0:c1 - c0], func=AF.Copy),
                                      reads=rd, writes=[b(name)])
                            else:
                                st.op("act", lambda h, c=c, slot=slot, c0=c0, c1=c1: h.activation(out=dst3[:, c, c0:c1], in_=stg[:, slot, 0:c1 - c0], func=AF.Copy,
                                                                                               scale=gain[:, c:c + 1]),
                                      reads=rd, writes=[b(name)])
                        else:
                            if gain is None:
                                st.op(eng, lambda h, c=c, slot=slot, c0=c0, c1=c1: h.tensor_copy(out=dst3[:, c, c0:c1], in_=stg[:, slot, 0:c1 - c0]),
                                      reads=rd, writes=[b(name)])
                            else:
                                st.op(eng, lambda h, c=c, slot=slot, c0=c0, c1=c1: h.tensor_scalar(out=dst3[:, c, c0:c1], in0=stg[:, slot, 0:c1 - c0],
                                                                                                scalar1=gain[:, c:c + 1], scalar2=None, op0=ALU.mult),
                                      reads=rd, writes=[b(name)])
            prep(wo, w_out, 8, D, goutt, "wo", "goutt")
            prep(wg, w_gate, 8, DFF, g2t, "wg", "g2t")
            prep(wu, w_up, 8, DFF, g2t, "wu", "g2t")
            prep(wd, w_down, NFB, D, None, "wd", None)

            def loads(i):
                slot = i % 2
                t0 = i * TT
                st.dma(lambda h: h.dma_start(out=xt[:, slot], in_=x[t0:t0 + TT, :].rearrange("(s p) d -> p s d", p=128)),
                       writes=[b("xt%d" % (slot * 2)), b("xt%d" % (slot * 2 + 1))])
                st.dma(lambda h: h.dma_start(out=yT[:, slot], in_=yT_s[:, :, t0:t0 + TT].rearrange("c p t -> p c t")), writes=[b("yT%d" % slot)])

            def rms_rstd(i, sub, which):
                slot = i % 2
                col = which * 2 + sub
                xb = b("xt%d" % (slot * 2 + sub))
                st.op("act", lambda h: h.activation(out=junk[:], in_=xt[:, slot, sub, :], func=AF.Square, accum_out=ss[:, col:col + 1]),
                      reads=[xb], writes=[b("junk"), b("ss%d" % col)])
                st.op("act", lambda h: h.activation(out=lnt[:, col:col + 1], in_=ss[:, col:col + 1], func=AF.Ln, scale=1.0 / D, bias=EPS),
                      reads=[b("ss%d" % col)], writes=[b("ln%d" % col)])
                st.op("act", lambda h: h.activation(out=rs[:, col:col + 1], in_=lnt[:, col:col + 1], func=AF.Exp, scale=-0.5),
                      reads=[b("ln%d" % col)], writes=[b("rs%d" % col)])
                return col

            def tile(i):
                slot = i % 2
                t0 = i * TT
                if i + 1 < NTL:
                    loads(i + 1)
                for sub in range(2):
                    xb = b("xt%d" % (slot * 2 + sub))
                    for half in range(2):
                        pt, pb = ring_A.get()
                        for c in range(8):
                            st.op("pe", lambda h, c=c, pt=pt, sub=sub, half=half: h.matmul(pt, lhsT=yT[:, slot, c, sub * 128:(sub + 1) * 128],
                                                                                          rhs=wo[:, c, half * 512:(half + 1) * 512], start=(c == 0), stop=(c == 7)),
                                  reads=[b("yT%d" % slot), b("wo")], writes=[pb])
                        st.op("dve", lambda h, pt=pt, sub=sub, half=half: h.tensor_tensor(out=xt[:, slot, sub, half * 512:(half + 1) * 512],
                                                                                         in0=xt[:, slot, sub, half * 512:(half + 1) * 512], in1=pt, op=ALU.add),
                              reads=[pb, xb], writes=[xb])
                for sub in range(2):
                    xb = b("xt%d" % (slot * 2 + sub))
                    col = rms_rstd(i, sub, 0)
                    st.op("dve", lambda h, sub=sub, col=col: h.tensor_scalar(out=h2n[:, sub, :], in0=xt[:, slot, sub, :], scalar1=rs[:, col:col + 1],
                                                                            scalar2=None, op0=ALU.mult),
                          reads=[xb, b("rs%d" % col)], writes=[b("h2n%d" % sub)])
                    for c in range(8):
                        st.op("pe", lambda h, sub=sub, c=c: h.transpose(out=pxt[:, c, :], in_=h2n[:, sub, c * 128:(c + 1) * 128], identity=ident[:]),
                              reads=[b("h2n%d" % sub), b("ident")], writes=[b("pxt")])
                    st.op("dve", lambda h, sub=sub: h.tensor_copy(out=h2T[:, :, sub * 128:(sub + 1) * 128], in_=pxt[:]), reads=[b("pxt")], writes=[b("h2T")])
                for fb in range(NFB):
                    gs = fb % 2
                    for c in range(8):
                        st.op("pe", lambda h, c=c, fb=fb, gs=gs: h.matmul(pG[:, gs, 0:TT], lhsT=wg[:, c, fb * 128:(fb + 1) * 128], rhs=h2T[:, c, :],
                                                                          start=(c == 0), stop=(c == 7)),
                              reads=[b("wg"), b("h2T")], writes=[b("pG%d" % gs)])
                    for c in range(8):
                        st.op("pe", lambda h, c=c, fb=fb, gs=gs: h.matmul(pUu[:, gs, 0:TT], lhsT=wu[:, c, fb * 128:(fb + 1) * 128], rhs=h2T[:, c, :],
                                                                          start=(c == 0), stop=(c == 7)),
                              reads=[b("wu"), b("h2T")], writes=[b("pU%d" % gs)])
                    st.op("act", lambda h, gs=gs: h.activation(out=sg[:, gs, :], in_=pG[:, gs, 0:TT], func=AF.Silu), reads=[b("pG%d" % gs)], writes=[b("sg%d" % gs)])
                    st.op("dve", lambda h, gs=gs, fb=fb: h.tensor_tensor(out=aT[:, fb, :], in0=sg[:, gs, :], in1=pUu[:, gs, 0:TT], op=ALU.mult),
                          reads=[b("sg%d" % gs), b("pU%d" % gs)], writes=[b("aT")])
                for sub in range(2):
                    xb = b("xt%d" % (slot * 2 + sub))
                    for half in range(2):
                        pt, pb = ring_A.get()
                        for fb in range(NFB):
                            st.op("pe", lambda h, fb=fb, pt=pt, sub=sub, half=half: h.matmul(pt, lhsT=aT[:, fb, sub * 128:(sub + 1) * 128],
                                                                                            rhs=wd[:, fb, half * 512:(half + 1) * 512], start=(fb == 0), stop=(fb == NFB - 1)),
                                  reads=[b("aT"), b("wd")], writes=[pb])
                        st.op("dve", lambda h, pt=pt, sub=sub, half=half: h.tensor_tensor(out=xt[:, slot, sub, half * 512:(half + 1) * 512],
                                                                                         in0=xt[:, slot, sub, half * 512:(half + 1) * 512], in1=pt, op=ALU.add),
                              reads=[pb, xb], writes=[xb])
                for sub in range(2):
                    xb = b("xt%d" % (slot * 2 + sub))
                    col = rms_rstd(i, sub, 1)
                    st.op("dve", lambda h, sub=sub, col=col: h.scalar_tensor_tensor(out=xt[:, slot, sub, :], in0=xt[:, slot, sub, :], scalar=rs[:, col:col + 1],
                                                                                   in1=gfb[:], op0=ALU.mult, op1=ALU.mult),
                          reads=[xb, b("rs%d" % col), b("gfb")], writes=[xb])
                    st.dma(lambda h, sub=sub: h.dma_start(out=out[t0 + sub * 128:t0 + (sub + 1) * 128, :], in_=xt[:, slot, sub, :]), reads=[xb])

            loads(0)
            for i in range(NTL):
                tile(i)
            st.finish()
            st.emit()

    if 1 in stages:
        stage1()
    if 2 in stages:
        stage2()
    if 3 in stages:
        stage3()
    if 4 in stages:
        stage4()
    return nc


def _pcn(w, rows):
    n = w.shape[1]
    return np.ascontiguousarray(w.reshape(rows // 128, 128, n).transpose(1, 0, 2))


def _pc(g):
    return np.ascontiguousarray(g.reshape(-1, 128).T)


def layout_inputs(inp, L):
    f32 = np.float32
    w_in = np.asarray(inp["w_in"][0], f32)
    hq, hi, hff, hfb, hg, cq, ckv, kr = np.split(w_in, np.cumsum([512, 512, 512, 512, 512, 384, 256])[:], axis=1)
    krot = np.concatenate([kr[:, 32:64], kr[:, 0:32]], axis=1)
    w1 = np.concatenate([hq, hff, hfb, hi, hg, cq, ckv, kr, kr, krot, krot], axis=1)
    assert w1.shape[1] == W1C
    wqb = np.asarray(inp["w_q_b"][0], f32).reshape(384, 4, 192)
    nope = wqb[:, :, 0:128].reshape(384, 512)
    rp = wqb[:, :, 128:192]
    rope = rp.reshape(384, 256)
    rot = np.concatenate([rp[:, :, 32:64], rp[:, :, 0:32]], axis=2).reshape(384, 256)
    wq = np.concatenate([nope, rope, rot], axis=1)
    wkvb = np.asarray(inp["w_kv_b"][0], f32).reshape(256, 4, 256)
    wkv = np.concatenate([wkvb[:, :, 0:128].reshape(256, 512), wkvb[:, :, 128:256].reshape(256, 512)], axis=1)
    lbl = np.asarray(inp["lb_logits"], f32)
    lbl_l = np.ascontiguousarray(lbl.reshape(2, 2, 4, 128).transpose(3, 0, 1, 2))
    gout = np.concatenate([np.asarray(inp["hgrn_norm_g"][0], f32), np.asarray(inp["mla_norm_g"][0], f32)])
    inv = 1.0 / (10000.0 ** (np.arange(0, 64, 2, dtype=np.float32) / 64.0))
    ang = np.arange(L, dtype=np.float32)[None, :] * inv[:, None].astype(np.float32)
    cos = np.cos(ang).astype(f32)
    sin = np.sin(ang).astype(f32)
    c_cos = np.ascontiguousarray(np.tile(cos, (4, 1)))
    c_sin = np.ascontiguousarray(np.tile(sin, (4, 1)))
    rmask = np.ones((128, 512), f32)
    rmask[:, 0::128] = 0.0
    jj = np.arange(128)[:, None]
    ii = np.arange(128)[None, :]
    d = {
        "w_in": _pcn(w1, 1024), "g1": _pc(np.asarray(inp["norm1_g"][0], f32)), "lbl": lbl_l,
        "w_qb": _pcn(wq, 384), "gqa": _pc(np.asarray(inp["q_a_norm_g"][0], f32)),
        "w_kvb": _pcn(wkv, 256), "gkva": _pc(np.asarray(inp["kv_a_norm_g"][0], f32)),
        "w_out": _pcn(np.asarray(inp["w_out"][0], f32), 1024), "gout": _pc(gout),
        "w_gate": _pcn(np.asarray(inp["w_gate"][0], f32), 1024), "w_up": _pcn(np.asarray(inp["w_up"][0], f32), 1024),
        "g2": _pc(np.asarray(inp["norm2_g"][0], f32)),
        "w_down": _pcn(np.asarray(inp["w_down"][0], f32), DFF),
        "gfin": np.asarray(inp["final_norm_g"], f32).reshape(1, D),
        "c_ident": np.eye(128).astype(ml_dtypes.bfloat16), "c_ones": np.ones((128, 128), ml_dtypes.bfloat16),
        "c_cos": c_cos, "c_sin": c_sin, "c_rmask": rmask,
        "c_maskf": (jj <= ii).astype(f32), "c_maskb": (jj >= ii).astype(f32),
    }
    return d


_NC_CACHE = {}


def kernel(**inputs):
    x = np.asarray(inputs["x"], np.float32)
    Bt, L, _ = x.shape
    n_seq = Bt // NCORES
    key = (n_seq, L)
    if key not in _NC_CACHE:
        _NC_CACHE[key] = build_nc(n_seq, L)
    nc = _NC_CACHE[key]
    shared = layout_inputs(inputs, L)
    in_maps = []
    for c in range(NCORES):
        m = dict(shared)
        m["x"] = np.ascontiguousarray(x[c * n_seq:(c + 1) * n_seq].reshape(n_seq * L, D))
        in_maps.append(m)
    res = run_bass_kernel_spmd(nc, in_maps, core_ids=list(range(NCORES)))
    out = np.stack([r["out"].reshape(n_seq, L, D) for r in res.results], axis=0)
    return out.reshape(Bt, L, D).astype(np.float32)
```

```python
import contextlib
import os
import numpy as np
import ml_dtypes
import concourse.bass as bass
import concourse.mybir as mybir
from concourse.bass_utils import run_bass_kernel_spmd

F32 = mybir.dt.float32
BF16 = mybir.dt.bfloat16
AF = mybir.ActivationFunctionType
ALU = mybir.AluOpType
AX = mybir.AxisListType

D = 1024
DFF = 2816
NFB = DFF // 128
EPS = 1e-6
NCORES = 8
W1C = 3456
SCALE = 192 ** -0.5


class Buf:
    __slots__ = ("name", "w", "r", "x")

    def __init__(self, name=""):
        self.name = name
        self.w = None
        self.r = {}
        self.x = len(name) > 1 and name[0] == "p" and (name[1].isupper() or name.startswith(("pmm", "pxt")))


class Stage:
    ENGS = ("pe", "act", "dve", "pool", "sp")

    def __init__(self, nc, name, n_dma_sems=16):
        self.nc = nc
        self.name = name
        self.ops = {e: [] for e in self.ENGS}
        self.cnt = {e: 0 for e in ("pe", "act", "dve", "pool")}
        self.waited = {e: {} for e in self.ENGS}
        self.n_dma = n_dma_sems
        self.dma_cnt = [0] * n_dma_sems
        self.dma_rr = 0
        self.sems = {}

    def _need(self, eng, ev, waits):
        if ev is None:
            return
        key, val = ev
        if key == "pe" and eng == "pe":
            return
        if self.waited[eng].get(key, 0) >= val:
            return
        self.waited[eng][key] = val
        waits.append((key, val))

    def _deps(self, eng, reads, writes):
        waits = []
        for b in reads:
            self._need(eng, b.w, waits)
            if b.x:
                for k, v in b.r.items():
                    if k != eng:
                        self._need(eng, (k, v), waits)
        for b in writes:
            self._need(eng, b.w, waits)
            for k, v in b.r.items():
                self._need(eng, (k, v), waits)
        return waits

    def _commit(self, ev, reads, writes):
        k, v = ev
        for b in reads:
            if b.r.get(k, 0) < v:
                b.r[k] = v
        for b in writes:
            b.w = ev
            b.r = {}

    def op(self, eng, fn, reads=(), writes=()):
        waits = self._deps(eng, reads, writes)
        self.cnt[eng] += 1
        ev = (eng, self.cnt[eng])
        self.ops[eng].append((waits, fn, (eng, 1)))
        self._commit(ev, reads, writes)
        return ev

    def dma(self, fn, reads=(), writes=(), queue="sp"):
        waits = self._deps(queue, reads, writes)
        k = self.dma_rr
        self.dma_rr = (self.dma_rr + 1) % self.n_dma
        key = "dma%d" % k
        if self.dma_cnt[k] > 0:
            self._need(queue, (key, self.dma_cnt[k]), waits)
        self.dma_cnt[k] += 16
        ev = (key, self.dma_cnt[k])
        self.ops[queue].append((waits, fn, (key, 16)))
        self._commit(ev, reads, writes)
        return ev

    def finish(self, eng="sp"):
        waits = []
        for k in range(self.n_dma):
            if self.dma_cnt[k] > 0:
                self._need(eng, ("dma%d" % k, self.dma_cnt[k]), waits)
        if waits:
            self.ops[eng].append((waits, None, None))

    def emit(self):
        nc = self.nc
        with contextlib.ExitStack() as st:
            for e in ("pe", "act", "dve", "pool"):
                self.sems[e] = st.enter_context(nc.semaphore("%s_%s" % (self.name, e)))
            for k in range(self.n_dma):
                self.sems["dma%d" % k] = st.enter_context(nc.semaphore("%s_d%d" % (self.name, k)))
            block = st.enter_context(nc.Block())
            sems = self.sems

            def run(h, lst):
                for waits, fn, inc in lst:
                    for key, val in waits:
                        h.wait_ge(sems[key], val)
                    if fn is not None:
                        fn(h).then_inc(sems[inc[0]], inc[1])

            if self.ops["sp"]:
                @block.sync
                def _(h):
                    run(h, self.ops["sp"])
            if self.ops["pe"]:
                @block.tensor
                def _(h):
                    run(h, self.ops["pe"])
            if self.ops["act"]:
                @block.scalar
                def _(h):
                    run(h, self.ops["act"])
            if self.ops["dve"]:
                @block.vector
                def _(h):
                    run(h, self.ops["dve"])
            if self.ops["pool"]:
                @block.gpsimd
                def _(h):
                    run(h, self.ops["pool"])


class Ring:
    def __init__(self, tens, n, name):
        self.t = tens
        self.n = n
        self.i = 0
        self.bufs = [Buf("%s%d" % (name, k)) for k in range(n)]

    def get(self):
        k = self.i
        self.i = (self.i + 1) % self.n
        return self.t[:, k], self.bufs[k]


def bc(ap, shape):
    return ap.to_broadcast(shape)


def build_nc(n_seq, L, debug=False, stages=(1, 2, 3, 4)):
    T = n_seq * L
    NT = T // 128
    NS = T // 512
    NCH = T // 128
    nc = bass.Bass("TRN2", target_bir_lowering=False)

    def din(name, shape, dt=F32):
        return nc.dram_tensor(name, list(shape), dt, kind="ExternalInput").ap()

    skind = "ExternalOutput" if debug else "Internal"

    def dscr(name, shape, dt):
        return nc.dram_tensor(name, list(shape), dt, kind=skind).ap()

    x = din("x", [T, D])
    out = nc.dram_tensor("out", [T, D], F32, kind="ExternalOutput").ap()
    w_in = din("w_in", [128, 8, W1C])
    g1 = din("g1", [128, 8])
    lbl = din("lbl", [128, 2, 2, 4])
    w_qb = din("w_qb", [128, 3, 1024])
    gqa = din("gqa", [128, 3])
    w_kvb = din("w_kvb", [128, 2, 1024])
    gkva = din("gkva", [128, 2])
    w_out = din("w_out", [128, 8, D])
    gout = din("gout", [128, 8])
    w_gate = din("w_gate", [128, 8, DFF])
    w_up = din("w_up", [128, 8, DFF])
    g2 = din("g2", [128, 8])
    w_down = din("w_down", [128, NFB, D])
    gfin = din("gfin", [1, D])
    c_ident = din("c_ident", [128, 128], BF16)
    c_ones = din("c_ones", [128, 128], BF16)
    c_cos = din("c_cos", [128, L])
    c_sin = din("c_sin", [128, L])
    c_rmask = din("c_rmask", [128, 512])
    c_maskf = din("c_maskf", [128, 128])
    c_maskb = din("c_maskb", [128, 128])

    qdT_s = dscr("qdT_s", [2, 4, 128, T], BF16)
    kiT_s = dscr("kiT_s", [2, 4, 128, T], BF16)
    scal_s = dscr("scal_s", [128, 2, 3, 4, NCH], F32)
    v_s = dscr("v_s", [T, 512], BF16)
    g_s = dscr("g_s", [T, 512], F32)
    qnT_s = dscr("qnT_s", [4, 128, T], BF16)
    qrT_s = dscr("qrT_s", [2, 128, T], BF16)
    knT_s = dscr("knT_s", [4, 128, T], BF16)
    krT_s = dscr("krT_s", [128, T], BF16)
    vm_s = dscr("vm_s", [T, 512], BF16)
    yT_s = dscr("yT_s", [8, 128, T], BF16)
    of_s = dscr("of_s", [T, 512], F32)
    ob_s = dscr("ob_s", [T, 512], F32)

    def stage1():
        with contextlib.ExitStack() as es:
            def sb(name, shape, dt=F32):
                return es.enter_context(nc.sbuf_tensor("s1_" + name, list(shape), dt))

            def ps(name, shape, dt=F32):
                return es.enter_context(nc.psum_tensor("s1_" + name, list(shape), dt))

            st = Stage(nc, "s1")
            w1 = sb("w1", [128, 8, W1C], BF16)
            wq = sb("wq", [128, 3, 1024], BF16)
            wkv = sb("wkv", [128, 2, 1024], BF16)
            g1t = sb("g1t", [128, 8]); gqat = sb("gqat", [128, 3]); gkvat = sb("gkvat", [128, 2])
            lblt = sb("lblt", [128, 2, 2, 4])
            lbt = sb("lbt", [128, 2, 4]); omlt = sb("omlt", [128, 2, 4])
            fa = sb("fa", [128, 2, 4]); fb_ = sb("fb", [128, 2, 4]); nfb = sb("nfb", [128, 2, 4]); lnfb = sb("lnfb", [128, 2, 4])
            ident = sb("ident", [128, 128], BF16)
            rmask = sb("rmask", [128, 512])
            xt = sb("xt", [128, 4, D]); xjunk = sb("xjunk", [128, 2, D], BF16)
            xn = sb("xn", [128, 2, D], BF16)
            hT = sb("hT", [128, 1, 8, 512], BF16)
            ssx = sb("ssx", [128, 2, 4]); lnx = sb("lnx", [128, 2, 4]); rsx = sb("rsx", [128, 2, 4])
            cst = sb("cst", [128, 2, 512]); snt = sb("snt", [128, 2, 512])
            qT = sb("qT", [128, 4, 512])
            th = sb("th", [128, 8, 512])
            tmpf = sb("tmpf", [128, 12, 512])
            stg = tmpf[:, 0:4, :].rearrange("p (a k) f -> p a (k f)", a=2)
            tmpb = sb("tmpb", [128, 8, 512], BF16)
            gt = sb("gt", [128, 4, 512])
            cqf = sb("cqf", [128, 4, 640]); cqn = sb("cqn", [128, 4, 640], BF16)
            ssq = sb("ssq", [128, 8]); lnq = sb("lnq", [128, 8]); rsq = sb("rsq", [128, 8])
            cqT = sb("cqT", [128, 3, 512], BF16); ckvT = sb("ckvT", [128, 2, 512], BF16)
            scal = sb("scal", [128, 2, 3, 4, NCH])
            pxt = ps("pxt", [128, 2, 8, 128], BF16)
            pmm = ps("pmm", [128, 6, 512])

            B = {}
            def b(name):
                if name not in B:
                    B[name] = Buf(name)
                return B[name]

            ring_f = Ring(tmpf, 12, "tmpf")
            ring_b = Ring(tmpb, 8, "tmpb")
            ring_p = Ring(pmm, 6, "pmm")
            ring_g = Ring(gt, 2, "gt")
            pxb = [Buf("pxt0"), Buf("pxt1")]
            hTb = [Buf("hT0"), Buf("hT1")]
            xtb = [Buf("xt%d" % i) for i in range(4)]
            xnb = [Buf("xn0"), Buf("xn1")]

            for (dst, src, nm) in ((g1t, g1, "g1t"), (gqat, gqa, "gqat"), (gkvat, gkva, "gkvat"),
                                   (lblt, lbl, "lblt"), (ident, c_ident, "ident"), (rmask, c_rmask, "rmask")):
                st.dma(lambda h, dst=dst, src=src: h.dma_start(out=dst[:], in_=src), writes=[b(nm)])
            st.op("dve", lambda h: h.tensor_tensor(out=lbt[:], in0=lblt[:, :, 1, :], in1=lblt[:, :, 0, :], op=ALU.subtract),
                  reads=[b("lblt")], writes=[b("lbt")])
            st.op("act", lambda h: h.activation(out=lbt[:], in_=lbt[:], func=AF.Exp), reads=[b("lbt")], writes=[b("lbt")])
            st.op("dve", lambda h: h.tensor_scalar(out=lbt[:], in0=lbt[:], scalar1=1.0, scalar2=None, op0=ALU.add),
                  reads=[b("lbt")], writes=[b("lbt")])
            st.op("dve", lambda h: h.reciprocal(out=lbt[:], in_=lbt[:]), reads=[b("lbt")], writes=[b("lbt")])
            st.op("dve", lambda h: h.tensor_scalar(out=omlt[:], in0=lbt[:], scalar1=-1.0, scalar2=1.0, op0=ALU.mult, op1=ALU.add),
                  reads=[b("lbt")], writes=[b("omlt")])
            st.op("dve", lambda h: h.tensor_scalar(out=fb_[:], in0=omlt[:], scalar1=0.5, scalar2=None, op0=ALU.mult),
                  reads=[b("omlt")], writes=[b("fb")])
            st.op("dve", lambda h: h.tensor_tensor(out=fa[:], in0=lbt[:], in1=fb_[:], op=ALU.add),
                  reads=[b("lbt"), b("fb")], writes=[b("fa")])
            st.op("dve", lambda h: h.tensor_scalar(out=nfb[:], in0=fb_[:], scalar1=-1.0, scalar2=None, op0=ALU.mult),
                  reads=[b("fb")], writes=[b("nfb")])
            st.op("act", lambda h: h.activation(out=lnfb[:], in_=fb_[:], func=AF.Ln), reads=[b("fb")], writes=[b("lnfb")])

            pcnt = [0]

            def prep(dst3, src3, nchunk, ncols, gain, eng_cycle, name):
                for c in range(nchunk):
                    for c0 in range(0, ncols, 1024):
                        c1 = min(ncols, c0 + 1024)
                        slot = pcnt[0] % 2
                        eng = eng_cycle[pcnt[0] % len(eng_cycle)]
                        pcnt[0] += 1
                        sbufs = [ring_f.bufs[2 * slot], ring_f.bufs[2 * slot + 1]]
                        st.dma(lambda h, c=c, slot=slot, c0=c0, c1=c1: h.dma_start(out=stg[:, slot, 0:c1 - c0], in_=src3[:, c, c0:c1]),
                               writes=sbufs)
                        st.op(eng, lambda h, c=c, slot=slot, c0=c0, c1=c1: h.tensor_scalar(
                            out=dst3[:, c, c0:c1], in0=stg[:, slot, 0:c1 - c0], scalar1=gain[:, c:c + 1], scalar2=0.0, op0=ALU.mult, op1=ALU.add),
                            reads=sbufs + [b(name + "_g")], writes=[b(name)])
            B["w1_g"] = b("g1t"); B["wq_g"] = b("gqat"); B["wkv_g"] = b("gkvat")
            prep(w1, w_in, 8, W1C, g1t, ["dve", "pool"], "w1")
            prep(wq, w_qb, 3, 1024, gqat, ["dve", "pool"], "wq")
            prep(wkv, w_kvb, 2, 1024, gkvat, ["dve", "pool"], "wkv")
            v1 = w1[:, :, 3328:3456].rearrange("p c (g r) -> p c g r", r=64)[:, :, :, 0:32]
            st.op("dve", lambda h: h.tensor_scalar(out=v1, in0=v1, scalar1=-1.0, scalar2=None, op0=ALU.mult),
                  reads=[b("w1")], writes=[b("w1")])
            v2 = wq[:, :, 768:1024].rearrange("p c (g r) -> p c g r", r=64)[:, :, :, 0:32]
            st.op("dve", lambda h: h.tensor_scalar(out=v2, in0=v2, scalar1=-1.0, scalar2=None, op0=ALU.mult),
                  reads=[b("wq")], writes=[b("wq")])

            w1b = b("w1")

            def mm_group(lhs_fn, rhs_fn, nk, reads, nfree=512):
                pt, pb = ring_p.get()
                pt = pt[:, 0:nfree]
                for c in range(nk):
                    st.op("pe", lambda h, c=c, pt=pt: h.matmul(pt, lhsT=lhs_fn(c), rhs=rhs_fn(c), start=(c == 0), stop=(c == nk - 1)),
                          reads=reads, writes=[pb])
                return pt, pb

            def xload(s):
                for j in range(4):
                    t0 = s * 512 + j * 128
                    st.dma(lambda h, j=j, t0=t0: h.dma_start(out=xt[:, j, :], in_=x[t0:t0 + 128, :]), writes=[xtb[j]])
                    st.op("act", lambda h, j=j: h.activation(out=xjunk[:, j % 2, :], in_=xt[:, j, :], func=AF.Square, accum_out=ssx[:, 0, j:j + 1]),
                          reads=[xtb[j]], writes=[b("ssx%d" % j), b("xjunk%d" % (j % 2))])

            def xnorm(s):
                st.op("act", lambda h: h.activation(out=lnx[:, 0, :], in_=ssx[:, 0, :], func=AF.Ln, scale=1.0 / D, bias=EPS),
                      reads=[b("ssx%d" % j) for j in range(4)], writes=[b("lnx")])
                st.op("act", lambda h: h.activation(out=rsx[:, 0, :], in_=lnx[:, 0, :], func=AF.Exp, scale=-0.5), reads=[b("lnx")], writes=[b("rsx")])
                for j in range(4):
                    n = j % 2
                    st.op("dve", lambda h, j=j, n=n: h.tensor_scalar(out=xn[:, n, :], in0=xt[:, j, :], scalar1=rsx[:, 0, j:j + 1], scalar2=None, op0=ALU.mult),
                          reads=[xtb[j], b("rsx")], writes=[xnb[n]])
                    for c in range(8):
                        st.op("pe", lambda h, n=n, c=c: h.transpose(out=pxt[:, n, c, :], in_=xn[:, n, c * 128:(c + 1) * 128], identity=ident[:]),
                              reads=[xnb[n], b("ident")], writes=[pxb[n]])
                    st.op("dve", lambda h, n=n, j=j: h.tensor_copy(out=hT[:, 0, :, j * 128:(j + 1) * 128], in_=pxt[:, n, :, :]),
                          reads=[pxb[n]], writes=[hTb[0]])

            def load_tables(s):
                slot = s % 2
                pos0 = (s * 512) % L
                st.dma(lambda h: h.dma_start(out=cst[:, slot, :], in_=c_cos[:, pos0:pos0 + 512]), writes=[b("cst%d" % slot)])
                st.dma(lambda h: h.dma_start(out=snt[:, slot, :], in_=c_sin[:, pos0:pos0 + 512]), writes=[b("snt%d" % slot)])

            def rope(s, pt1, pb1, pt2, pb2, dst):
                slot = s % 2
                f1, fb1 = ring_f.get()
                f2, fb2 = ring_f.get()
                ot, ob = ring_b.get()
                st.op("dve", lambda h: h.tensor_tensor(out=f1, in0=pt1, in1=cst[:, slot, :], op=ALU.mult), reads=[pb1, b("cst%d" % slot)], writes=[fb1])
                st.op("dve", lambda h: h.tensor_tensor(out=f2, in0=pt2, in1=snt[:, slot, :], op=ALU.mult), reads=[pb2, b("snt%d" % slot)], writes=[fb2])
                st.op("pool", lambda h: h.tensor_tensor(out=ot, in0=f1, in1=f2, op=ALU.add), reads=[fb1, fb2], writes=[ob])
                st.dma(lambda h: h.dma_start(out=dst, in_=ot), reads=[ob])

            def tm_piece(s, p):
                t0 = s * 512
                hs = hTb[0]
                if p == 8:
                    pt1, pb1 = mm_group(lambda c: w1[:, c, 3200:3328], lambda c: hT[:, 0, c, :], 8, [hs, w1b])
                    pt2, pb2 = mm_group(lambda c: w1[:, c, 3328:3456], lambda c: hT[:, 0, c, :], 8, [hs, w1b])
                    rope(s, pt1, pb1, pt2, pb2, krT_s[:, t0:t0 + 512])
                    return
                j = p // 2
                tt = t0 + j * 128
                lhs = lambda c: hT[:, 0, c, j * 128:(j + 1) * 128]
                if p % 2 == 0:
                    pt, pb = mm_group(lhs, lambda c: w1[:, c, 1536:2048], 8, [hs, w1b])
                    ot, ob = ring_b.get()
                    st.op("dve", lambda h: h.tensor_copy(out=ot, in_=pt), reads=[pb], writes=[ob])
                    st.dma(lambda h: h.dma_start(out=v_s[tt:tt + 128, :], in_=ot), reads=[ob])
                    pt2, pb2 = mm_group(lhs, lambda c: w1[:, c, 2048:2560], 8, [hs, w1b])
                    st.op("act", lambda h: h.activation(out=gt[:, j, :], in_=pt2, func=AF.Copy), reads=[pb2], writes=[b("gt%d" % j)])
                else:
                    pt, pb = mm_group(lhs, lambda c: w1[:, c, 2560:2944], 8, [hs, w1b], nfree=384)
                    st.op("act", lambda h: h.activation(out=xjunk[:, 0, 0:384], in_=pt[:, 0:384], func=AF.Square, accum_out=ssq[:, j:j + 1]),
                          reads=[pb], writes=[b("ssq%d" % j), b("xjunk0")])
                    st.op("dve", lambda h: h.tensor_copy(out=cqf[:, j, 0:384], in_=pt[:, 0:384]), reads=[pb], writes=[b("cqf%d" % j)])
                    pt2, pb2 = mm_group(lhs, lambda c: w1[:, c, 2944:3200], 8, [hs, w1b], nfree=256)
                    st.op("act", lambda h: h.activation(out=xjunk[:, 1, 0:256], in_=pt2[:, 0:256], func=AF.Square, accum_out=ssq[:, 4 + j:5 + j]),
                          reads=[pb2], writes=[b("ssq%d" % (4 + j)), b("xjunk1")])
                    st.op("dve", lambda h: h.tensor_copy(out=cqf[:, j, 384:640], in_=pt2[:, 0:256]), reads=[pb2], writes=[b("cqf%d" % j)])

            def fm_hf(s, db):
                col = 512 + db * 128
                pt, pb = mm_group(lambda c: w1[:, c, col:col + 128], lambda c: hT[:, 0, c, :], 8, [hTb[0], w1b])
                st.op("act", lambda h: h.activation(out=th[:, db, :], in_=pt, func=AF.Copy), reads=[pb], writes=[b("th%d" % db)])

            def fm_hq(s, blk):
                pt, pb = mm_group(lambda c: w1[:, c, blk * 128:(blk + 1) * 128], lambda c: hT[:, 0, c, :], 8, [hTb[0], w1b])
                st.op("dve", lambda h: h.tensor_copy(out=qT[:, blk, :], in_=pt), reads=[pb], writes=[b("qT%d" % blk)])

            def act18(s):
                t0 = s * 512
                for blk in range(4):
                    st.op("act", lambda h, blk=blk: h.activation(out=qT[:, blk, :], in_=qT[:, blk, :], func=AF.Silu),
                          reads=[b("qT%d" % blk)], writes=[b("qT%d" % blk)])
                for db in range(8):
                    st.op("act", lambda h, db=db: h.activation(out=th[:, db, :], in_=th[:, db, :], func=AF.Tanh, scale=-0.5),
                          reads=[b("th%d" % db)], writes=[b("th%d" % db)])
                for j in range(4):
                    tt = t0 + j * 128
                    st.op("act", lambda h, j=j: h.activation(out=gt[:, j, :], in_=gt[:, j, :], func=AF.Silu), reads=[b("gt%d" % j)], writes=[b("gt%d" % j)])
                    st.dma(lambda h, j=j, tt=tt: h.dma_start(out=g_s[tt:tt + 128, :], in_=gt[:, j, :]), reads=[b("gt%d" % j)])

            def gate_chain(s, db):
                d, blk = db // 4, db % 4
                t0 = s * 512
                c0 = s * 4
                thb = b("th%d" % db)
                lf, lfb = ring_f.get()
                st.op("pool", lambda h: h.tensor_scalar(out=lf, in0=th[:, db, :], scalar1=nfb[:, d, blk:blk + 1], scalar2=fa[:, d, blk:blk + 1],
                                                        op0=ALU.mult, op1=ALU.add),
                      reads=[thb, b("nfb"), b("fa")], writes=[lfb])
                st.op("act", lambda h: h.activation(out=lf, in_=lf, func=AF.Ln), reads=[lfb], writes=[lfb])
                yield
                cumt, cumb = ring_f.get()
                cct, ccb = ring_f.get()
                st.op("dve", lambda h: h.tensor_tensor_scan(out=cumt, data0=rmask[:], data1=lf, initial=0.0, op0=ALU.mult, op1=ALU.add),
                      reads=[lfb, b("rmask")], writes=[cumb])
                cumv = cumt.rearrange("p (c t) -> p c t", t=128)
                ccv = cct.rearrange("p (c t) -> p c t", t=128)
                st.op("dve", lambda h: h.tensor_tensor(out=ccv, in0=cumv, in1=bc(cumv[:, :, 63:64], [128, 4, 128]), op=ALU.subtract),
                      reads=[cumb], writes=[ccb])
                Xi, Yi = (1, 2) if d == 0 else (2, 1)
                st.op("act", lambda h: h.activation(out=scal[:, d, 0, blk, c0:c0 + 4], in_=cumv[:, :, 127], func=AF.Exp), reads=[cumb], writes=[b("scal")])
                st.op("act", lambda h: h.activation(out=scal[:, d, Xi, blk, c0:c0 + 4], in_=ccv[:, :, 127], func=AF.Exp), reads=[ccb], writes=[b("scal")])
                st.op("act", lambda h: h.activation(out=scal[:, d, Yi, blk, c0:c0 + 4], in_=cumv[:, :, 63], func=AF.Exp), reads=[cumb], writes=[b("scal")])
                if d == 0:
                    srct, srcb = cct, ccb
                else:
                    srct, srcb = ring_f.get()
                    st.op("pool", lambda h: h.tensor_tensor(out=srct, in0=lf, in1=cct, op=ALU.subtract), reads=[lfb, ccb], writes=[srcb])
                yield
                ea, eab = ring_f.get()
                eb, ebb = ring_f.get()
                st.op("act", lambda h: h.activation(out=ea, in_=srct, func=AF.Exp), reads=[srcb], writes=[eab])
                st.op("act", lambda h: h.activation(out=eb, in_=srct, func=AF.Exp, scale=-1.0, bias=lnfb[:, d, blk:blk + 1]),
                      reads=[srcb, b("lnfb")], writes=[ebb])
                o1, o1b = ring_b.get()
                o2, o2b = ring_b.get()
                st.op("dve", lambda h: h.tensor_tensor(out=o1, in0=qT[:, blk, :], in1=ea, op=ALU.mult), reads=[b("qT%d" % blk), eab], writes=[o1b])
                st.op("dve", lambda h: h.scalar_tensor_tensor(out=o2, in0=th[:, db, :], scalar=1.0, in1=eb, op0=ALU.add, op1=ALU.mult),
                      reads=[thb, ebb], writes=[o2b])
                st.dma(lambda h: h.dma_start(out=qdT_s[d, blk, :, t0:t0 + 512], in_=o1), reads=[o1b])
                st.dma(lambda h: h.dma_start(out=kiT_s[d, blk, :, t0:t0 + 512], in_=o2), reads=[o2b])
                yield

            def cq_rstd():
                st.op("act", lambda h: h.activation(out=lnq[:, 0:4], in_=ssq[:, 0:4], func=AF.Ln, scale=1.0 / 384, bias=EPS),
                      reads=[b("ssq%d" % i) for i in range(8)], writes=[b("lnq")])
                st.op("act", lambda h: h.activation(out=lnq[:, 4:8], in_=ssq[:, 4:8], func=AF.Ln, scale=1.0 / 256, bias=EPS),
                      reads=[b("ssq%d" % i) for i in range(8)], writes=[b("lnq")])
                st.op("act", lambda h: h.activation(out=rsq[:], in_=lnq[:], func=AF.Exp, scale=-0.5), reads=[b("lnq")], writes=[b("rsq")])

            def cq_transpose(s):
                for j in range(4):
                    st.op("dve", lambda h, j=j: h.tensor_scalar(out=cqn[:, j, 0:384], in0=cqf[:, j, 0:384], scalar1=rsq[:, j:j + 1], scalar2=None, op0=ALU.mult),
                          reads=[b("cqf%d" % j), b("rsq")], writes=[b("cqn%d" % j)])
                    st.op("dve", lambda h, j=j: h.tensor_scalar(out=cqn[:, j, 384:640], in0=cqf[:, j, 384:640], scalar1=rsq[:, 4 + j:5 + j], scalar2=None, op0=ALU.mult),
                          reads=[b("cqf%d" % j), b("rsq")], writes=[b("cqn%d" % j)])
                for j in range(4):
                    n = j % 2
                    for c in range(5):
                        st.op("pe", lambda h, n=n, c=c, j=j: h.transpose(out=pxt[:, n, c, :], in_=cqn[:, j, c * 128:(c + 1) * 128], identity=ident[:]),
                              reads=[b("cqn%d" % j), b("ident")], writes=[pxb[n]])
                    st.op("dve", lambda h, n=n, j=j: h.tensor_copy(out=cqT[:, :, j * 128:(j + 1) * 128], in_=pxt[:, n, 0:3, :]), reads=[pxb[n]], writes=[b("cqT")])
                    st.op("dve", lambda h, n=n, j=j: h.tensor_copy(out=ckvT[:, :, j * 128:(j + 1) * 128], in_=pxt[:, n, 3:5, :]), reads=[pxb[n]], writes=[b("ckvT")])

            def second_proj(s):
                t0 = s * 512
                out = []

                def qn(hh):
                    pt, pb = mm_group(lambda c: wq[:, c, hh * 128:(hh + 1) * 128], lambda c: cqT[:, c, :], 3, [b("cqT"), b("wq")])
                    ot, ob = ring_b.get()
                    st.op("dve", lambda h: h.tensor_copy(out=ot, in_=pt), reads=[pb], writes=[ob])
                    st.dma(lambda h: h.dma_start(out=qnT_s[hh, :, t0:t0 + 512], in_=ot), reads=[ob])

                def qr_(pr):
                    pt1, pb1 = mm_group(lambda c: wq[:, c, 512 + pr * 128:640 + pr * 128], lambda c: cqT[:, c, :], 3, [b("cqT"), b("wq")])
                    pt2, pb2 = mm_group(lambda c: wq[:, c, 768 + pr * 128:896 + pr * 128], lambda c: cqT[:, c, :], 3, [b("cqT"), b("wq")])
                    rope(s, pt1, pb1, pt2, pb2, qrT_s[pr, :, t0:t0 + 512])

                def kn(hh):
                    pt, pb = mm_group(lambda c: wkv[:, c, hh * 128:(hh + 1) * 128], lambda c: ckvT[:, c, :], 2, [b("ckvT"), b("wkv")])
                    ot, ob = ring_b.get()
                    st.op("dve", lambda h: h.tensor_copy(out=ot, in_=pt), reads=[pb], writes=[ob])
                    st.dma(lambda h: h.dma_start(out=knT_s[hh, :, t0:t0 + 512], in_=ot), reads=[ob])

                def vv(j):
                    tt = t0 + j * 128
                    pt, pb = mm_group(lambda c: ckvT[:, c, j * 128:(j + 1) * 128], lambda c: wkv[:, c, 512:1024], 2, [b("ckvT"), b("wkv")])
                    ot, ob = ring_b.get()
                    st.op("dve", lambda h: h.tensor_copy(out=ot, in_=pt), reads=[pb], writes=[ob])
                    st.dma(lambda h: h.dma_start(out=vm_s[tt:tt + 128, :], in_=ot), reads=[ob])
                for hh in range(4):
                    out.append(lambda hh=hh: qn(hh))
                for pr in range(2):
                    out.append(lambda pr=pr: qr_(pr))
                for hh in range(4):
                    out.append(lambda hh=hh: kn(hh))
                for j in range(4):
                    out.append(lambda j=j: vv(j))
                return out

            xload(0)
            load_tables(0)
            xnorm(0)
            if NS > 1:
                xload(1)
            for p in range(9):
                tm_piece(0, p)
            for blk in range(4):
                fm_hq(0, blk)
            for db in range(8):
                fm_hf(0, db)
            cq_rstd()
            for s in range(NS):
                nxt = s + 1 < NS
                if nxt:
                    load_tables(s + 1)
                    xnorm(s + 1)
                    if s + 2 < NS:
                        xload(s + 2)
                cq_transpose(s)
                act18(s)
                for f_ in second_proj(s):
                    f_()
                rest = []
                for db0 in range(0, 8, 2):
                    live = [gate_chain(s, db0), gate_chain(s, db0 + 1)]
                    while live:
                        for g_ in list(live):
                            try:
                                next(g_)
                            except StopIteration:
                                live.remove(g_)
                    for db in (db0, db0 + 1):
                        if nxt:
                            tm_piece(s + 1, db)
                            if db == 7:
                                tm_piece(s + 1, 8)
                            fm_hf(s + 1, db)
                            if db >= 4:
                                fm_hq(s + 1, db - 4)
                        if rest:
                            rest.pop(0)()
                while rest:
                    rest.pop(0)()
                if nxt:
                    cq_rstd()
            st.dma(lambda h: h.dma_start(out=scal_s, in_=scal[:]), reads=[b("scal")])
            st.finish()
            st.emit()

    def stage2():
        NK = L // 128
        NQ = L // 512
        DEN = os.environ.get("S2_DEN", "pe")
        ROPE128 = os.environ.get("S2_ROPE", "k128") == "k128"
        NPS = int(os.environ.get("S2_NPS", "4"))
        NP = int(os.environ.get("S2_NP", "6"))
        with contextlib.ExitStack() as es:
            def sb(name, shape, dt=F32):
                return es.enter_context(nc.sbuf_tensor("s2_" + name, list(shape), dt))

            def ps(name, shape, dt=F32):
                return es.enter_context(nc.psum_tensor("s2_" + name, list(shape), dt))

            st = Stage(nc, "s2")
            knT = sb("knT", [128, 4, L], BF16)
            vm = sb("vm", [128, NK, 512], BF16)
            krT = sb("krT", [128, L], BF16)
            qn = sb("qn", [128, 2, 4, 512], BF16)
            qr = sb("qr", [128, 2, 2, 512], BF16)
            ones = sb("ones", [128, 128], BF16)
            Pt = sb("Pt", [128, NP, 512], BF16)
            qrz = sb("qrz", [128, 2, 4, 512], BF16)
            hm = sb("hm", [128, 2])
            accP = sb("accP", [128, 2, 512])
            oT = sb("oT", [128, 2, 4, 512])
            rden = sb("rden", [128, 2, 512])
            sq = sb("sq", [128, 4, 512], BF16)
            lnv = sb("lnv", [128, 512]); rstd = sb("rstd", [128, 512])
            yb = sb("yb", [128, 4, 512], BF16)
            pS = ps("pS", [128, NPS, 512])
            pO = ps("pO", [128, 2, 512])
            pD = ps("pD", [128, 1, 512])
            pSS = ps("pSS", [128, 512])
            acc = sb("acc", [128, 2, 512])
            onesf = sb("onesf", [128, 128])
            B = {}

            def b(name):
                if name not in B:
                    B[name] = Buf(name)
                return B[name]
            ring_P = Ring(Pt, NP, "Pt")
            ring_S = Ring(pS, NPS, "pS")
            ring_y = Ring(yb, 4, "yb")
            st.dma(lambda h: h.dma_start(out=ones[:], in_=c_ones), writes=[b("ones")])
            st.op("pool", lambda h: h.memset(onesf[:], 1.0), writes=[b("onesf")])
            st.op("pool", lambda h: h.memset(hm[:], 0.0), writes=[b("hm")])
            st.op("pool", lambda h: h.memset(hm[0:64, 0:1], 1.0), writes=[b("hm")])
            st.op("pool", lambda h: h.memset(hm[64:128, 1:2], 1.0), writes=[b("hm")])

            items = []
            for seq in range(n_seq):
                for qb in range(NQ):
                    for hh in range(4):
                        for kc in range(NK):
                            items.append((seq, qb, hh, kc))

            def loads_seq(seq):
                s0 = seq * L
                for hh in range(4):
                    st.dma(lambda h, hh=hh: h.dma_start(out=knT[:, hh, :], in_=knT_s[hh, :, s0:s0 + L]), writes=[b("knT")])
                st.dma(lambda h: h.dma_start(out=krT[:], in_=krT_s[:, s0:s0 + L]), writes=[b("krT")])
                st.dma(lambda h: h.dma_start(out=vm[:], in_=vm_s[s0:s0 + L, :].rearrange("(k p) f -> p k f", p=128)), writes=[b("vm")])

            def loads_q(seq, qb):
                slot = (seq * NQ + qb) % 2
                tq = seq * L + qb * 512
                st.dma(lambda h: h.dma_start(out=qn[:, slot], in_=qnT_s[:, :, tq:tq + 512].rearrange("h p t -> p h t")), writes=[b("qn%d" % slot)])
                st.dma(lambda h: h.dma_start(out=qr[:, slot], in_=qrT_s[:, :, tq:tq + 512].rearrange("h p t -> p h t")), writes=[b("qr%d" % slot)])
                if ROPE128:
                    for hh in range(4):
                        st.op("pool", lambda h, hh=hh: h.tensor_scalar(out=qrz[:, slot, hh, :], in0=qr[:, slot, hh // 2, :], scalar1=hm[:, hh % 2:hh % 2 + 1],
                                                                        scalar2=None, op0=ALU.mult),
                              reads=[b("qr%d" % slot), b("hm")], writes=[b("qrz%d" % slot)])

            def qk(item):
                seq, qb, hh, kc = item
                slot = (seq * NQ + qb) % 2
                hp, pr = hh % 2, hh // 2
                pt, pb = ring_S.get()
                st.op("pe", lambda h: h.matmul(pt, lhsT=knT[:, hh, kc * 128:(kc + 1) * 128], rhs=qn[:, slot, hh, :], start=True, stop=False),
                      reads=[b("knT"), b("qn%d" % slot)], writes=[pb])
                if ROPE128:
                    st.op("pe", lambda h: h.matmul(pt, lhsT=krT[:, kc * 128:(kc + 1) * 128], rhs=qrz[:, slot, hh, :], start=False, stop=True),
                          reads=[b("krT"), b("qrz%d" % slot)], writes=[pb])
                else:
                    st.op("pe", lambda h: h.matmul(pt, lhsT=krT[hp * 64:(hp + 1) * 64, kc * 128:(kc + 1) * 128],
                                                   rhs=qr[hp * 64:(hp + 1) * 64, slot, pr, :], start=False, stop=True),
                          reads=[b("krT"), b("qr%d" % slot)], writes=[pb])
                return pt, pb

            def finish_head(seq, qb, hh, oslot):
                qslot = (seq * NQ + qb) % 2
                if DEN == "dve":
                    st.op("pe", lambda h: h.matmul(pD[:, 0, :], lhsT=onesf[:], rhs=acc[:, oslot, :], start=True, stop=True),
                          reads=[b("onesf"), b("acc%d" % oslot)], writes=[b("pD0")])
                elif DEN == "split":
                    st.op("pe", lambda h: h.matmul(pD[:, 0, :], lhsT=onesf[:], rhs=acc[:, oslot, :], start=True, stop=False),
                          reads=[b("onesf"), b("acc%d" % oslot)], writes=[b("pD0")])
                    st.op("pe", lambda h: h.matmul(pD[:, 0, :], lhsT=onesf[:], rhs=accP[:, oslot, :], start=False, stop=True),
                          reads=[b("onesf"), b("accP%d" % oslot)], writes=[b("pD0")])
                st.op("dve", lambda h: h.reciprocal(out=rden[:, oslot, :], in_=pD[:, 0, :]), reads=[b("pD0")], writes=[b("rden%d" % oslot)])
                st.op("dve", lambda h: h.tensor_tensor(out=oT[:, qslot, hh, :], in0=pO[:, oslot, :], in1=rden[:, oslot, :], op=ALU.mult),
                      reads=[b("pO%d" % oslot), b("rden%d" % oslot)], writes=[b("oT%d_%d" % (qslot, hh))])

            def finish_q(seq, qb):
                qslot = (seq * NQ + qb) % 2
                tq = seq * L + qb * 512
                for hh in range(4):
                    st.op("act", lambda h, hh=hh: h.activation(out=sq[:, hh, :], in_=oT[:, qslot, hh, :], func=AF.Square),
                          reads=[b("oT%d_%d" % (qslot, hh))], writes=[b("sq%d" % hh)])
                for hh in range(4):
                    st.op("pe", lambda h, hh=hh: h.matmul(pSS[:], lhsT=ones[:], rhs=sq[:, hh, :], start=(hh == 0), stop=(hh == 3)),
                          reads=[b("ones"), b("sq%d" % hh)], writes=[b("pSS")])
                st.op("act", lambda h: h.activation(out=lnv[:], in_=pSS[:], func=AF.Ln, scale=1.0 / 512, bias=EPS), reads=[b("pSS")], writes=[b("lnv")])
                st.op("act", lambda h: h.activation(out=rstd[:], in_=lnv[:], func=AF.Exp, scale=-0.5), reads=[b("lnv")], writes=[b("rstd")])
                for hh in range(4):
                    yt, ybuf = ring_y.get()
                    st.op("dve", lambda h, hh=hh, yt=yt: h.tensor_tensor(out=yt, in0=oT[:, qslot, hh, :], in1=rstd[:], op=ALU.mult),
                          reads=[b("oT%d_%d" % (qslot, hh)), b("rstd")], writes=[ybuf])
                    st.dma(lambda h, hh=hh, yt=yt: h.dma_start(out=yT_s[4 + hh, :, tq:tq + 512], in_=yt), reads=[ybuf])

            import collections
            LA = int(os.environ.get("S2_LA", "2"))
            pending = collections.deque()
            nq = [0]

            def ensure(upto, seq):
                while nq[0] < min(upto, len(items)) and items[nq[0]][0] == seq:
                    pending.append(qk(items[nq[0]]))
                    nq[0] += 1

            ocnt = 0
            for idx, item in enumerate(items):
                seq, qb, hh, kc = item
                if qb == 0 and hh == 0 and kc == 0:
                    loads_seq(seq)
                    loads_q(seq, 0)
                if hh == 0 and kc == 0 and qb + 1 < NQ:
                    loads_q(seq, qb + 1)
                ensure(idx + 1 + LA, seq)
                pt, pb = pending.popleft()
                oslot = ocnt % 2
                Pa, Pb = ring_P.get()
                st.op("act", lambda h, Pa=Pa, pt=pt: h.activation(out=Pa, in_=pt, func=AF.Exp, scale=SCALE), reads=[pb], writes=[Pb])
                st.op("pe", lambda h, Pa=Pa, oslot=oslot, hh=hh, kc=kc: h.matmul(pO[:, oslot, :], lhsT=vm[:, kc, hh * 128:(hh + 1) * 128], rhs=Pa,
                                                                           start=(kc == 0), stop=(kc == NK - 1)),
                      reads=[b("vm"), Pb], writes=[b("pO%d" % oslot)])
                if DEN == "pe":
                    st.op("pe", lambda h, Pa=Pa, kc=kc: h.matmul(pD[:, 0, :], lhsT=ones[:], rhs=Pa, start=(kc == 0), stop=(kc == NK - 1)),
                          reads=[b("ones"), Pb], writes=[b("pD0")])
                else:
                    on_pool = (DEN == "split" and kc % 4 == 3)
                    eng, at, an, first = ("pool", accP, "accP", kc == 3) if on_pool else ("dve", acc, "acc", kc == 0)
                    if first:
                        st.op(eng, lambda h, Pa=Pa, oslot=oslot, at=at: h.tensor_copy(out=at[:, oslot, :], in_=Pa), reads=[Pb], writes=[b("%s%d" % (an, oslot))])
                    else:
                        st.op(eng, lambda h, Pa=Pa, oslot=oslot, at=at: h.tensor_tensor(out=at[:, oslot, :], in0=at[:, oslot, :], in1=Pa, op=ALU.add),
                              reads=[Pb, b("%s%d" % (an, oslot))], writes=[b("%s%d" % (an, oslot))])
                if kc == NK - 1:
                    finish_head(seq, qb, hh, oslot)
                    ocnt += 1
                    if hh == 3:
                        finish_q(seq, qb)
            st.finish()
            st.emit()

    def stage3():
        NCs = L // 128
        NCHN = 2 * n_seq
        with contextlib.ExitStack() as es:
            def sb(name, shape, dt=F32):
                return es.enter_context(nc.sbuf_tensor("s3_" + name, list(shape), dt))

            def ps(name, shape, dt=F32):
                return es.enter_context(nc.psum_tensor("s3_" + name, list(shape), dt))

            st = Stage(nc, "s3")
            scal = sb("scal", [128, 2, 3, 4, NCH])
            masks = sb("masks", [128, 2, 128])
            ident = sb("ident", [128, 128], BF16)
            BD = sb("BD", [128, 4, 128])
            qd = sb("qd", [128, 2 * NCHN, 4, 256], BF16)
            ki = sb("ki", [128, 2 * NCHN, 4, 256], BF16)
            vt = sb("vt", [128, 2 * NCHN, 2, 512], BF16)
            kitok = sb("kitok", [128, NCHN, 512], BF16)
            scT = sb("scT", [128, NCHN, 8, 128], BF16)
            S_all = sb("S", [128, NCHN, 4, 128]); Sp_all = sb("Sp", [128, NCHN, 4, 128], BF16)
            t1_all = sb("t1", [128, NCHN, 4, 128]); t2_all = sb("t2", [128, NCHN, 4, 128])
            Bm_all = sb("Bm", [128, NCHN, 4, 128])
            oev = sb("oev", [128, 4, 512])
            cof = sb("cof", [128, 3, 512]); cob = sb("cob", [128, 3, 512]); cg_ = sb("cg", [128, 3, 512])
            osum_a = sb("osum", [128, 2, 512]); sqt_a = sb("sqt", [128, 2, 512]); yn_a = sb("yn", [128, 2, 512])
            ss8_a = sb("ss8", [128, 2, 8]); ln8_a = sb("ln8", [128, 2, 8]); rs8_a = sb("rs8", [128, 2, 8])
            ya = sb("ya", [128, 2, 512], BF16)
            yaT = sb("yaT", [128, 2, 4, 128], BF16)
            pT = ps("pT", [128, 1, 1024], BF16)
            pSc = ps("pSc", [128, 4, 4, 128])
            pOo = ps("pOo", [128, 2, 512])
            pU = ps("pU", [128, 1, 4, 128])
            pY = pU[:].bitcast(BF16).rearrange("p a b (c t) -> p (a b c) t", t=128)
            B = {}

            def b(name):
                if name not in B:
                    B[name] = Buf(name)
                return B[name]
            ring_oev = Ring(oev, 4, "oev")
            st.dma(lambda h: h.dma_start(out=scal[:], in_=scal_s), writes=[b("scal")])
            st.dma(lambda h: h.dma_start(out=masks[:, 0, :], in_=c_maskf), writes=[b("masks")])
            st.dma(lambda h: h.dma_start(out=masks[:, 1, :], in_=c_maskb), writes=[b("masks")])
            st.dma(lambda h: h.dma_start(out=ident[:], in_=c_ident), writes=[b("ident")])
            st.op("pool", lambda h: h.memset(BD[:], 0.0), writes=[b("BD")])
            st.op("pool", lambda h: h.memset(BD[0:64, :, 0:64], 1.0), writes=[b("BD")])
            st.op("pool", lambda h: h.memset(BD[64:128, :, 64:128], 1.0), writes=[b("BD")])
            rot = {"pT": 0, "pOo": 0, "pSc": 0}

            def chain(seq, d):
                ci = seq * 2 + d
                order = list(range(NCs)) if d == 0 else list(range(NCs - 1, -1, -1))
                S = S_all[:, ci]; Sp = Sp_all[:, ci]; t1 = t1_all[:, ci]; t2 = t2_all[:, ci]; Bm = Bm_all[:, ci]
                bS, bSp, bt1, bt2, bBm = (b("%s%d" % (nm, ci)) for nm in ("S", "Sp", "t1", "t2", "Bm"))
                st.op("pool", lambda h: h.memset(S, 0.0), writes=[bS])
                st.op("pool", lambda h: h.memset(Sp, 0.0), writes=[bSp])
                odst = of_s if d == 0 else ob_s
                def load_group(k):
                    gi = order[2 * k] // 2
                    slot = ci * 2 + k % 2
                    tg = seq * L + gi * 256
                    st.dma(lambda h: h.dma_start(out=qd[:, slot], in_=qdT_s[d, :, :, tg:tg + 256].rearrange("k p t -> p k t")), writes=[b("qd%d" % slot)])
                    st.dma(lambda h: h.dma_start(out=ki[:, slot], in_=kiT_s[d, :, :, tg:tg + 256].rearrange("k p t -> p k t")), writes=[b("ki%d" % slot)])
                    st.dma(lambda h: h.dma_start(out=vt[:, slot], in_=v_s[tg:tg + 256, :].rearrange("(j p) f -> p j f", p=128)), writes=[b("vt%d" % slot)])

                load_group(0)
                for pos, n in enumerate(order):
                    if pos % 2 == 0 and pos + 2 < len(order):
                        load_group(pos // 2 + 1)
                    slot = ci * 2 + (pos // 2) % 2
                    j = n % 2
                    cg = seq * NCs + n
                    t0 = cg * 128
                    qd_c = qd[:, slot, :, j * 128:(j + 1) * 128]
                    ki_c = ki[:, slot, :, j * 128:(j + 1) * 128]
                    v_c = vt[:, slot, j, :]
                    rq, rk, rv = b("qd%d" % slot), b("ki%d" % slot), b("vt%d" % slot)
                    has_next = pos + 1 < len(order)
                    tb = 0
                    sset = rot["pSc"] % 2
                    rot["pSc"] += 1
                    for bk in range(4):
                        st.op("pe", lambda h, bk=bk, tb=tb, ki_c=ki_c: h.transpose(out=pT[:, tb, bk * 128:(bk + 1) * 128], in_=ki_c[:, bk, :], identity=ident[:]),
                              reads=[rk, b("ident")], writes=[b("pT%d" % tb)])
                    st.op("act", lambda h, tb=tb: h.activation(out=kitok[:, ci, :], in_=pT[:, tb, 0:512], func=AF.Copy),
                          reads=[b("pT%d" % tb)], writes=[b("kitok%d" % ci)])
                    for hh in range(8):
                        bk, hp = hh // 2, hh % 2
                        st.op("pe", lambda h, bk=bk, hp=hp, ki_c=ki_c, qd_c=qd_c, sset=sset: h.matmul(pSc[:, sset * 2 + hp, bk, :], lhsT=ki_c[hp * 64:(hp + 1) * 64, bk, :],
                                                                                      rhs=qd_c[hp * 64:(hp + 1) * 64, bk, :], start=True, stop=True),
                              reads=[rk, rq], writes=[b("pSc%d" % (sset * 2 + hp))])
                    scv = scT[:, ci].rearrange("p (k two) t -> p k two t", two=2)
                    for half in range(2):
                        st.op("dve", lambda h, half=half, scv=scv, sset=sset: h.tensor_tensor(out=scv[:, :, half, :], in0=pSc[:, sset * 2 + half],
                                                                                             in1=bc(masks[:, d:d + 1, :], [128, 4, 128]), op=ALU.mult),
                              reads=[b("pSc%d" % (sset * 2 + half)), b("masks")], writes=[b("scT%d_%d" % (ci, half))])
                    if has_next:
                        st.op("pool", lambda h, cg=cg: h.tensor_tensor(out=Bm, in0=BD[:], in1=bc(scal[:, d, 1, :, cg:cg + 1], [128, 4, 128]), op=ALU.mult),
                              reads=[b("BD"), b("scal")], writes=[bBm])
                    yield
                    ob_ = rot["pOo"] % 2
                    rot["pOo"] += 1
                    for bk in range(4):
                        st.op("pe", lambda h, bk=bk, ob_=ob_, qd_c=qd_c: h.matmul(pOo[:, ob_, bk * 128:(bk + 1) * 128], lhsT=qd_c[:, bk, :], rhs=Sp[:, bk, :],
                                                                              start=True, stop=False),
                              reads=[rq, bSp], writes=[b("pOo%d" % ob_)])
                        for hp in range(2):
                            hh = 2 * bk + hp
                            st.op("pe", lambda h, hh=hh, hp=hp, ob_=ob_, v_c=v_c: h.matmul(pOo[:, ob_, hh * 64:(hh + 1) * 64], lhsT=scT[:, ci, hh, :],
                                                                                       rhs=v_c[:, hh * 64:(hh + 1) * 64], start=False, stop=(hp == 1)),
                                  reads=[b("scT%d_%d" % (ci, hh % 2)), rv], writes=[b("pOo%d" % ob_)])
                    ot, otb = ring_oev.get()
                    st.op("act", lambda h, ot=ot, ob_=ob_: h.activation(out=ot, in_=pOo[:, ob_, :], func=AF.Copy), reads=[b("pOo%d" % ob_)], writes=[otb])
                    st.dma(lambda h, ot=ot, t0=t0: h.dma_start(out=odst[t0:t0 + 128, :], in_=ot), reads=[otb], writes=[b("o%d_%d" % (d, cg))], queue="act")
                    yield
                    if has_next:
                        cgn = seq * NCs + order[pos + 1]
                        for bk in range(4):
                            st.op("pe", lambda h, bk=bk, v_c=v_c: h.matmul(pU[:, 0, bk, :], lhsT=kitok[:, ci, bk * 128:(bk + 1) * 128], rhs=v_c[:, bk * 128:(bk + 1) * 128],
                                                                          start=True, stop=True),
                                  reads=[b("kitok%d" % ci), rv], writes=[b("pU0")])
                        Abc = bc(scal[:, d, 0, :, cg:cg + 1], [128, 4, 128])
                        Cbc = bc(scal[:, d, 2, :, cgn:cgn + 1], [128, 4, 128])
                        st.op("pool", lambda h, Abc=Abc: h.tensor_tensor(out=t1, in0=S, in1=Abc, op=ALU.mult), reads=[bS, b("scal")], writes=[bt1])
                        st.op("dve", lambda h: h.tensor_tensor(out=t2, in0=pU[:, 0], in1=Bm, op=ALU.mult), reads=[b("pU0"), bBm], writes=[bt2])
                        st.op("dve", lambda h: h.tensor_tensor(out=S, in0=t1, in1=t2, op=ALU.add), reads=[bt1, bt2], writes=[bS])
                        st.op("dve", lambda h, Cbc=Cbc: h.tensor_tensor(out=Sp, in0=S, in1=Cbc, op=ALU.mult), reads=[bS, b("scal")], writes=[bSp])
                    yield

            gens = [chain(seq, d) for seq in range(n_seq) for d in range(2)]
            live = list(gens)
            while live:
                for g in list(live):
                    try:
                        next(g)
                    except StopIteration:
                        live.remove(g)

            for cg in range(NCH):
                t0 = cg * 128
                k3 = cg % 3
                k2 = cg % 2
                osum = osum_a[:, k2]; sqt = sqt_a[:, k2]; yn = yn_a[:, k2]
                ss8 = ss8_a[:, k2]; ln8 = ln8_a[:, k2]; rs8 = rs8_a[:, k2]
                bos, bsq, byn, bss, bln, brs = (b("%s%d" % (nm, k2)) for nm in ("osum", "sqt", "yn", "ss8", "ln8", "rs8"))
                st.dma(lambda h, k3=k3, t0=t0: h.dma_start(out=cof[:, k3], in_=of_s[t0:t0 + 128, :]), reads=[b("o0_%d" % cg)], writes=[b("cof%d" % k3)])
                st.dma(lambda h, k3=k3, t0=t0: h.dma_start(out=cob[:, k3], in_=ob_s[t0:t0 + 128, :]), reads=[b("o1_%d" % cg)], writes=[b("cob%d" % k3)])
                st.dma(lambda h, k3=k3, t0=t0: h.dma_start(out=cg_[:, k3], in_=g_s[t0:t0 + 128, :]), writes=[b("cg%d" % k3)])
                st.op("dve", lambda h, k3=k3, osum=osum: h.tensor_tensor(out=osum, in0=cof[:, k3], in1=cob[:, k3], op=ALU.add),
                      reads=[b("cof%d" % k3), b("cob%d" % k3)], writes=[bos])
                st.op("act", lambda h, osum=osum, sqt=sqt: h.activation(out=sqt, in_=osum, func=AF.Square), reads=[bos], writes=[bsq])
                st.op("dve", lambda h, sqt=sqt, ss8=ss8: h.tensor_reduce(out=ss8, in_=sqt.rearrange("p (h e) -> p h e", e=64), axis=AX.X, op=ALU.add),
                      reads=[bsq], writes=[bss])
                st.op("act", lambda h, ss8=ss8, ln8=ln8: h.activation(out=ln8, in_=ss8, func=AF.Ln, scale=1.0 / 64, bias=EPS), reads=[bss], writes=[bln])
                st.op("act", lambda h, ln8=ln8, rs8=rs8: h.activation(out=rs8, in_=ln8, func=AF.Exp, scale=-0.5), reads=[bln], writes=[brs])
                st.op("dve", lambda h, yn=yn, osum=osum, rs8=rs8: h.tensor_tensor(out=yn.rearrange("p (h e) -> p h e", e=64), in0=osum.rearrange("p (h e) -> p h e", e=64),
                                                                              in1=bc(rs8.unsqueeze(2), [128, 8, 64]), op=ALU.mult),
                      reads=[bos, brs], writes=[byn])
                st.op("pool", lambda h, yn=yn, k2=k2, k3=k3: h.tensor_tensor(out=ya[:, k2, :], in0=yn, in1=cg_[:, k3], op=ALU.mult),
                      reads=[byn, b("cg%d" % k3)], writes=[b("ya%d" % k2)])
                for bk in range(4):
                    st.op("pe", lambda h, bk=bk, k2=k2: h.transpose(out=pY[:, bk, :], in_=ya[:, k2, bk * 128:(bk + 1) * 128], identity=ident[:]),
                          reads=[b("ya%d" % k2), b("ident")], writes=[b("pU0")])
                st.op("act", lambda h, k2=k2: h.activation(out=yaT[:, k2], in_=pY[:, 0:4, :], func=AF.Copy), reads=[b("pU0")], writes=[b("yaT%d" % k2)])
                st.dma(lambda h, k2=k2, t0=t0: h.dma_start(out=yT_s[0:4, :, t0:t0 + 128].rearrange("k p t -> p k t"), in_=yaT[:, k2]), reads=[b("yaT%d" % k2)], queue="act")
            st.finish()
            st.emit()

    def stage4():
        TT = 256
        NTL = T // TT
        with contextlib.ExitStack() as es:
            def sb(name, shape, dt=F32):
                return es.enter_context(nc.sbuf_tensor("s4_" + name, list(shape), dt))

            def ps(name, shape, dt=F32):
                return es.enter_context(nc.psum_tensor("s4_" + name, list(shape), dt))

            st = Stage(nc, "s4")
            wo = sb("wo", [128, 8, D], BF16)
            wg = sb("wg", [128, 8, DFF], BF16)
            wu = sb("wu", [128, 8, DFF], BF16)
            wd = sb("wd", [128, NFB, D], BF16)
            goutt = sb("goutt", [128, 8]); g2t = sb("g2t", [128, 8])
            gfb = sb("gfb", [128, D])
            ident = sb("ident", [128, 128], BF16)
            xt = sb("xt", [128, 2, 2, D])
            yT = sb("yT", [128, 2, 8, TT], BF16)
            h2n = sb("h2n", [128, 2, D], BF16)
            h2T = sb("h2T", [128, 8, TT], BF16)
            aT = sb("aT", [128, NFB, TT], BF16)
            sg = sb("sg", [128, 2, TT])
            junk = sb("junk", [128, D], BF16)
            ss = sb("ss", [128, 4]); lnt = sb("lnt", [128, 4]); rs = sb("rs", [128, 4])
            pxt = ps("pxt", [128, 8, 128], BF16)
            pG = ps("pG", [128, 2, 512])
            pUu = ps("pUu", [128, 2, 512])
            pA = ps("pA", [128, 2, 512])
            B = {}

            def b(name):
                if name not in B:
                    B[name] = Buf(name)
                return B[name]
            ring_A = Ring(pA, 2, "pA")
            for (dst, src, nm) in ((goutt, gout, "goutt"), (g2t, g2, "g2t"), (ident, c_ident, "ident")):
                st.dma(lambda h, dst=dst, src=src: h.dma_start(out=dst[:], in_=src), writes=[b(nm)])
            if os.environ.get("S4_NOBC"):
                for pp in range(0, 128, 32):
                    pass
                st.op("pool", lambda h: h.memset(gfb[:], 1.0), writes=[b("gfb")])
            else:
                st.dma(lambda h: h.dma_start(out=gfb[:], in_=gfin.partition_broadcast(128)), writes=[b("gfb")])
            stg = xt[:].rearrange("p a b d -> p (a b) d")
            pcnt = [0]

            def prep(dst3, src3, nchunk, ncols, gain, name, gname):
                for c in range(nchunk):
                    for c0 in range(0, ncols, 1024):
                        c1 = min(ncols, c0 + 1024)
                        slot = pcnt[0] % 4
                        eng = ("dve", "act")[pcnt[0] % 2]
                        pcnt[0] += 1
                        sbuf = b("xt%d" % slot)
                        st.dma(lambda h, c=c, slot=slot, c0=c0, c1=c1: h.dma_start(out=stg[:, slot, 0:c1 - c0], in_=src3[:, c, c0:c1]), writes=[sbuf])
                        rd = [sbuf] + ([b(gname)] if gain is not None else [])
                        if eng == "act":
                            if gain is None:
                                st.op("act", lambda h, c=c, slot=slot, c0=c0, c1=c1: h.activation(out=dst3[:, c, c0:c1], in_=stg[:, slot, 0:c1 - c0], func=AF.Copy),
                                      reads=rd, writes=[b(name)])
                            else:
                                st.op("act", lambda h, c=c, slot=slot, c0=c0, c1=c1: h.activation(out=dst3[:, c, c0:c1], in_=stg[:, slot, 0:c1 - c0], func=AF.Copy,
                                                                                               scale=gain[:, c:c + 1]),
                                      reads=rd, writes=[b(name)])
                        else:
                            if gain is None:
                                st.op(eng, lambda h, c=c, slot=slot, c0=c0, c1=c1: h.tensor_copy(out=dst3[:, c, c0:c1], in_=stg[:, slot, 0:c1 - c0]),
                                      reads=rd, writes=[b(name)])
                            else:
                                st.op(eng, lambda h, c=c, slot=slot, c0=c0, c1=c1: h.tensor_scalar(out=dst3[:, c, c0:c1], in0=stg[:, slot, 0:c1 - c0],
                                                                                                scalar1=gain[:, c:c + 1], scalar2=None, op0=ALU.mult),
                                      reads=rd, writes=[b(name)])
            prep(wo, w_out, 8, D, goutt, "wo", "goutt")
            prep(wg, w_gate, 8, DFF, g2t, "wg", "g2t")
            prep(wu, w_up, 8, DFF, g2t, "wu", "g2t")
            prep(wd, w_down, NFB, D, None, "wd", None)

            def loads(i):
                slot = i % 2
                t0 = i * TT
                st.dma(lambda h: h.dma_start(out=xt[:, slot], in_=x[t0:t0 + TT, :].rearrange("(s p) d -> p s d", p=128)),
                       writes=[b("xt%d" % (slot * 2)), b("xt%d" % (slot * 2 + 1))])
                st.dma(lambda h: h.dma_start(out=yT[:, slot], in_=yT_s[:, :, t0:t0 + TT].rearrange("c p t -> p c t")), writes=[b("yT%d" % slot)])

            def rms_rstd(i, sub, which):
                slot = i % 2
                col = which * 2 + sub
                xb = b("xt%d" % (slot * 2 + sub))
                st.op("act", lambda h: h.activation(out=junk[:], in_=xt[:, slot, sub, :], func=AF.Square, accum_out=ss[:, col:col + 1]),
                      reads=[xb], writes=[b("junk"), b("ss%d" % col)])
                st.op("act", lambda h: h.activation(out=lnt[:, col:col + 1], in_=ss[:, col:col + 1], func=AF.Ln, scale=1.0 / D, bias=EPS),
                      reads=[b("ss%d" % col)], writes=[b("ln%d" % col)])
                st.op("act", lambda h: h.activation(out=rs[:, col:col + 1], in_=lnt[:, col:col + 1], func=AF.Exp, scale=-0.5),
                      reads=[b("ln%d" % col)], writes=[b("rs%d" % col)])
                return col

            def tile(i):
                slot = i % 2
                t0 = i * TT
                if i + 1 < NTL:
                    loads(i + 1)
                for sub in range(2):
                    xb = b("xt%d" % (slot * 2 + sub))
                    for half in range(2):
                        pt, pb = ring_A.get()
                        for c in range(8):
                            st.op("pe", lambda h, c=c, pt=pt, sub=sub, half=half: h.matmul(pt, lhsT=yT[:, slot, c, sub * 128:(sub + 1) * 128],
                                                                                          rhs=wo[:, c, half * 512:(half + 1) * 512], start=(c == 0), stop=(c == 7)),
                                  reads=[b("yT%d" % slot), b("wo")], writes=[pb])
                        st.op("dve", lambda h, pt=pt, sub=sub, half=half: h.tensor_tensor(out=xt[:, slot, sub, half * 512:(half + 1) * 512],
                                                                                         in0=xt[:, slot, sub, half * 512:(half + 1) * 512], in1=pt, op=ALU.add),
                              reads=[pb, xb], writes=[xb])
                for sub in range(2):
                    xb = b("xt%d" % (slot * 2 + sub))
                    col = rms_rstd(i, sub, 0)
                    st.op("dve", lambda h, sub=sub, col=col: h.tensor_scalar(out=h2n[:, sub, :], in0=xt[:, slot, sub, :], scalar1=rs[:, col:col + 1],
                                                                            scalar2=None, op0=ALU.mult),
                          reads=[xb, b("rs%d" % col)], writes=[b("h2n%d" % sub)])
                    for c in range(8):
                        st.op("pe", lambda h, sub=sub, c=c: h.transpose(out=pxt[:, c, :], in_=h2n[:, sub, c * 128:(c + 1) * 128], identity=ident[:]),
                              reads=[b("h2n%d" % sub), b("ident")], writes=[b("pxt")])
                    st.op("dve", lambda h, sub=sub: h.tensor_copy(out=h2T[:, :, sub * 128:(sub + 1) * 128], in_=pxt[:]), reads=[b("pxt")], writes=[b("h2T")])
                for fb in range(NFB):
                    gs = fb % 2
                    for c in range(8):
                        st.op("pe", lambda h, c=c, fb=fb, gs=gs: h.matmul(pG[:, gs, 0:TT], lhsT=wg[:, c, fb * 128:(fb + 1) * 128], rhs=h2T[:, c, :],
                                                                          start=(c == 0), stop=(c == 7)),
                              reads=[b("wg"), b("h2T")], writes=[b("pG%d" % gs)])
                    for c in range(8):
                        st.op("pe", lambda h, c=c, fb=fb, gs=gs: h.matmul(pUu[:, gs, 0:TT], lhsT=wu[:, c, fb * 128:(fb + 1) * 128], rhs=h2T[:, c, :],
                                                                          start=(c == 0), stop=(c == 7)),
                              reads=[b("wu"), b("h2T")], writes=[b("pU%d" % gs)])
                    st.op("act", lambda h, gs=gs: h.activation(out=sg[:, gs, :], in_=pG[:, gs, 0:TT], func=AF.Silu), reads=[b("pG%d" % gs)], writes=[b("sg%d" % gs)])
                    st.op("dve", lambda h, gs=gs, fb=fb: h.tensor_tensor(out=aT[:, fb, :], in0=sg[:, gs, :], in1=pUu[:, gs, 0:TT], op=ALU.mult),
                          reads=[b("sg%d" % gs), b("pU%d" % gs)], writes=[b("aT")])
                for sub in range(2):
                    xb = b("xt%d" % (slot * 2 + sub))
                    for half in range(2):
                        pt, pb = ring_A.get()
                        for fb in range(NFB):
                            st.op("pe", lambda h, fb=fb, pt=pt, sub=sub, half=half: h.matmul(pt, lhsT=aT[:, fb, sub * 128:(sub + 1) * 128],
                                                                                            rhs=wd[:, fb, half * 512:(half + 1) * 512], start=(fb == 0), stop=(fb == NFB - 1)),
                                  reads=[b("aT"), b("wd")], writes=[pb])
                        st.op("dve", lambda h, pt=pt, sub=sub, half=half: h.tensor_tensor(out=xt[:, slot, sub, half * 512:(half + 1) * 512],
                                                                                         in0=xt[:, slot, sub, half * 512:(half + 1) * 512], in1=pt, op=ALU.add),
                              reads=[pb, xb], writes=[xb])
                for sub in range(2):
                    xb = b("xt%d" % (slot * 2 + sub))
                    col = rms_rstd(i, sub, 1)
                    st.op("dve", lambda h, sub=sub, col=col: h.scalar_tensor_tensor(out=xt[:, slot, sub, :], in0=xt[:, slot, sub, :], scalar=rs[:, col:col + 1],
                                                                                   in1=gfb[:], op0=ALU.mult, op1=ALU.mult),
                          reads=[xb, b("rs%d" % col), b("gfb")], writes=[xb])
                    st.dma(lambda h, sub=sub: h.dma_start(out=out[t0 + sub * 128:t0 + (sub + 1) * 128, :], in_=xt[:, slot, sub, :]), reads=[xb])

            loads(0)
            for i in range(NTL):
                tile(i)
            st.finish()
            st.emit()

    if 1 in stages:
        stage1()
    if 2 in stages:
        stage2()
    if 3 in stages:
        stage3()
    if 4 in stages:
        stage4()
    return nc


def _pcn(w, rows):
    n = w.shape[1]
    return np.ascontiguousarray(w.reshape(rows // 128, 128, n).transpose(1, 0, 2))


def _pc(g):
    return np.ascontiguousarray(g.reshape(-1, 128).T)


def layout_inputs(inp, L):
    f32 = np.float32
    w_in = np.asarray(inp["w_in"][0], f32)
    hq, hi, hff, hfb, hg, cq, ckv, kr = np.split(w_in, np.cumsum([512, 512, 512, 512, 512, 384, 256])[:], axis=1)
    krot = np.concatenate([kr[:, 32:64], kr[:, 0:32]], axis=1)
    w1 = np.concatenate([hq, hff, hfb, hi, hg, cq, ckv, kr, kr, krot, krot], axis=1)
    assert w1.shape[1] == W1C
    wqb = np.asarray(inp["w_q_b"][0], f32).reshape(384, 4, 192)
    nope = wqb[:, :, 0:128].reshape(384, 512)
    rp = wqb[:, :, 128:192]
    rope = rp.reshape(384, 256)
    rot = np.concatenate([rp[:, :, 32:64], rp[:, :, 0:32]], axis=2).reshape(384, 256)
    wq = np.concatenate([nope, rope, rot], axis=1)
    wkvb = np.asarray(inp["w_kv_b"][0], f32).reshape(256, 4, 256)
    wkv = np.concatenate([wkvb[:, :, 0:128].reshape(256, 512), wkvb[:, :, 128:256].reshape(256, 512)], axis=1)
    lbl = np.asarray(inp["lb_logits"], f32)
    lbl_l = np.ascontiguousarray(lbl.reshape(2, 2, 4, 128).transpose(3, 0, 1, 2))
    gout = np.concatenate([np.asarray(inp["hgrn_norm_g"][0], f32), np.asarray(inp["mla_norm_g"][0], f32)])
    inv = 1.0 / (10000.0 ** (np.arange(0, 64, 2, dtype=np.float32) / 64.0))
    ang = np.arange(L, dtype=np.float32)[None, :] * inv[:, None].astype(np.float32)
    cos = np.cos(ang).astype(f32)
    sin = np.sin(ang).astype(f32)
    c_cos = np.ascontiguousarray(np.tile(cos, (4, 1)))
    c_sin = np.ascontiguousarray(np.tile(sin, (4, 1)))
    rmask = np.ones((128, 512), f32)
    rmask[:, 0::128] = 0.0
    jj = np.arange(128)[:, None]
    ii = np.arange(128)[None, :]
    d = {
        "w_in": _pcn(w1, 1024), "g1": _pc(np.asarray(inp["norm1_g"][0], f32)), "lbl": lbl_l,
        "w_qb": _pcn(wq, 384), "gqa": _pc(np.asarray(inp["q_a_norm_g"][0], f32)),
        "w_kvb": _pcn(wkv, 256), "gkva": _pc(np.asarray(inp["kv_a_norm_g"][0], f32)),
        "w_out": _pcn(np.asarray(inp["w_out"][0], f32), 1024), "gout": _pc(gout),
        "w_gate": _pcn(np.asarray(inp["w_gate"][0], f32), 1024), "w_up": _pcn(np.asarray(inp["w_up"][0], f32), 1024),
        "g2": _pc(np.asarray(inp["norm2_g"][0], f32)),
        "w_down": _pcn(np.asarray(inp["w_down"][0], f32), DFF),
        "gfin": np.asarray(inp["final_norm_g"], f32).reshape(1, D),
        "c_ident": np.eye(128).astype(ml_dtypes.bfloat16), "c_ones": np.ones((128, 128), ml_dtypes.bfloat16),
        "c_cos": c_cos, "c_sin": c_sin, "c_rmask": rmask,
        "c_maskf": (jj <= ii).astype(f32), "c_maskb": (jj >= ii).astype(f32),
    }
    return d


_NC_CACHE = {}


def kernel(**inputs):
    x = np.asarray(inputs["x"], np.float32)
    Bt, L, _ = x.shape
    n_seq = Bt // NCORES
    key = (n_seq, L)
    if key not in _NC_CACHE:
        _NC_CACHE[key] = build_nc(n_seq, L)
    nc = _NC_CACHE[key]
    shared = layout_inputs(inputs, L)
    in_maps = []
    for c in range(NCORES):
        m = dict(shared)
        m["x"] = np.ascontiguousarray(x[c * n_seq:(c + 1) * n_seq].reshape(n_seq * L, D))
        in_maps.append(m)
    res = run_bass_kernel_spmd(nc, in_maps, core_ids=list(range(NCORES)))
    out = np.stack([r["out"].reshape(n_seq, L, D) for r in res.results], axis=0)
    return out.reshape(Bt, L, D).astype(np.float32)
```

```python
import contextlib
import os
import numpy as np
import ml_dtypes
import concourse.bass as bass
import concourse.mybir as mybir
from concourse.bass_utils import run_bass_kernel_spmd

F32 = mybir.dt.float32
BF16 = mybir.dt.bfloat16
AF = mybir.ActivationFunctionType
ALU = mybir.AluOpType
AX = mybir.AxisListType

D = 1024
DFF = 2816
NFB = DFF // 128
EPS = 1e-6
NCORES = 8
W1C = 3456
SCALE = 192 ** -0.5


class Buf:
    __slots__ = ("name", "w", "r", "x")

    def __init__(self, name=""):
        self.name = name
        self.w = None
        self.r = {}
        self.x = len(name) > 1 and name[0] == "p" and (name[1].isupper() or name.startswith(("pmm", "pxt")))


class Stage:
    ENGS = ("pe", "act", "dve", "pool", "sp")

    def __init__(self, nc, name, n_dma_sems=16):
        self.nc = nc
        self.name = name
        self.ops = {e: [] for e in self.ENGS}
        self.cnt = {e: 0 for e in ("pe", "act", "dve", "pool")}
        self.waited = {e: {} for e in self.ENGS}
        self.n_dma = n_dma_sems
        self.dma_cnt = [0] * n_dma_sems
        self.dma_rr = 0
        self.sems = {}

    def _need(self, eng, ev, waits):
        if ev is None:
            return
        key, val = ev
        if key == "pe" and eng == "pe":
            return
        if self.waited[eng].get(key, 0) >= val:
            return
        self.waited[eng][key] = val
        waits.append((key, val))

    def _deps(self, eng, reads, writes):
        waits = []
        for b in reads:
            self._need(eng, b.w, waits)
            if b.x:
                for k, v in b.r.items():
                    if k != eng:
                        self._need(eng, (k, v), waits)
        for b in writes:
            self._need(eng, b.w, waits)
            for k, v in b.r.items():
                self._need(eng, (k, v), waits)
        return waits

    def _commit(self, ev, reads, writes):
        k, v = ev
        for b in reads:
            if b.r.get(k, 0) < v:
                b.r[k] = v
        for b in writes:
            b.w = ev
            b.r = {}

    def op(self, eng, fn, reads=(), writes=()):
        waits = self._deps(eng, reads, writes)
        self.cnt[eng] += 1
        ev = (eng, self.cnt[eng])
        self.ops[eng].append((waits, fn, (eng, 1)))
        self._commit(ev, reads, writes)
        return ev

    def dma(self, fn, reads=(), writes=(), queue="sp"):
        waits = self._deps(queue, reads, writes)
        k = self.dma_rr
        self.dma_rr = (self.dma_rr + 1) % self.n_dma
        key = "dma%d" % k
        if self.dma_cnt[k] > 0:
            self._need(queue, (key, self.dma_cnt[k]), waits)
        self.dma_cnt[k] += 16
        ev = (key, self.dma_cnt[k])
        self.ops[queue].append((waits, fn, (key, 16)))
        self._commit(ev, reads, writes)
        return ev

    def finish(self, eng="sp"):
        waits = []
        for k in range(self.n_dma):
            if self.dma_cnt[k] > 0:
                self._need(eng, ("dma%d" % k, self.dma_cnt[k]), waits)
        if waits:
            self.ops[eng].append((waits, None, None))

    def emit(self):
        nc = self.nc
        with contextlib.ExitStack() as st:
            for e in ("pe", "act", "dve", "pool"):
                self.sems[e] = st.enter_context(nc.semaphore("%s_%s" % (self.name, e)))
            for k in range(self.n_dma):
                self.sems["dma%d" % k] = st.enter_context(nc.semaphore("%s_d%d" % (self.name, k)))
            block = st.enter_context(nc.Block())
            sems = self.sems

            def run(h, lst):
                for waits, fn, inc in lst:
                    for key, val in waits:
                        h.wait_ge(sems[key], val)
                    if fn is not None:
                        fn(h).then_inc(sems[inc[0]], inc[1])

            if self.ops["sp"]:
                @block.sync
                def _(h):
                    run(h, self.ops["sp"])
            if self.ops["pe"]:
                @block.tensor
                def _(h):
                    run(h, self.ops["pe"])
            if self.ops["act"]:
                @block.scalar
                def _(h):
                    run(h, self.ops["act"])
            if self.ops["dve"]:
                @block.vector
                def _(h):
                    run(h, self.ops["dve"])
            if self.ops["pool"]:
                @block.gpsimd
                def _(h):
                    run(h, self.ops["pool"])


class Ring:
    def __init__(self, tens, n, name):
        self.t = tens
        self.n = n
        self.i = 0
        self.bufs = [Buf("%s%d" % (name, k)) for k in range(n)]

    def get(self):
        k = self.i
        self.i = (self.i + 1) % self.n
        return self.t[:, k], self.bufs[k]


def bc(ap, shape):
    return ap.to_broadcast(shape)


def build_nc(n_seq, L, debug=False, stages=(1, 2, 3, 4)):
    T = n_seq * L
    NT = T // 128
    NS = T // 512
    NCH = T // 128
    nc = bass.Bass("TRN2", target_bir_lowering=False)

    def din(name, shape, dt=F32):
        return nc.dram_tensor(name, list(shape), dt, kind="ExternalInput").ap()

    skind = "ExternalOutput" if debug else "Internal"

    def dscr(name, shape, dt):
        return nc.dram_tensor(name, list(shape), dt, kind=skind).ap()

    x = din("x", [T, D])
    out = nc.dram_tensor("out", [T, D], F32, kind="ExternalOutput").ap()
    w_in = din("w_in", [128, 8, W1C])
    g1 = din("g1", [128, 8])
    lbl = din("lbl", [128, 2, 2, 4])
    w_qb = din("w_qb", [128, 3, 1024])
    gqa = din("gqa", [128, 3])
    w_kvb = din("w_kvb", [128, 2, 1024])
    gkva = din("gkva", [128, 2])
    w_out = din("w_out", [128, 8, D])
    gout = din("gout", [128, 8])
    w_gate = din("w_gate", [128, 8, DFF])
    w_up = din("w_up", [128, 8, DFF])
    g2 = din("g2", [128, 8])
    w_down = din("w_down", [128, NFB, D])
    gfin = din("gfin", [1, D])
    c_ident = din("c_ident", [128, 128], BF16)
    c_ones = din("c_ones", [128, 128], BF16)
    c_cos = din("c_cos", [128, L])
    c_sin = din("c_sin", [128, L])
    c_rmask = din("c_rmask", [128, 512])
    c_maskf = din("c_maskf", [128, 128])
    c_maskb = din("c_maskb", [128, 128])

    qdT_s = dscr("qdT_s", [2, 4, 128, T], BF16)
    kiT_s = dscr("kiT_s", [2, 4, 128, T], BF16)
    scal_s = dscr("scal_s", [128, 2, 3, 4, NCH], F32)
    v_s = dscr("v_s", [T, 512], BF16)
    g_s = dscr("g_s", [T, 512], F32)
    qnT_s = dscr("qnT_s", [4, 128, T], BF16)
    qrT_s = dscr("qrT_s", [2, 128, T], BF16)
    knT_s = dscr("knT_s", [4, 128, T], BF16)
    krT_s = dscr("krT_s", [128, T], BF16)
    vm_s = dscr("vm_s", [T, 512], BF16)
    yT_s = dscr("yT_s", [8, 128, T], BF16)
    of_s = dscr("of_s", [T, 512], F32)
    ob_s = dscr("ob_s", [T, 512], F32)

    def stage1():
        with contextlib.ExitStack() as es:
            def sb(name, shape, dt=F32):
                return es.enter_context(nc.sbuf_tensor("s1_" + name, list(shape), dt))

            def ps(name, shape, dt=F32):
                return es.enter_context(nc.psum_tensor("s1_" + name, list(shape), dt))

            st = Stage(nc, "s1")
            w1 = sb("w1", [128, 8, W1C], BF16)
            wq = sb("wq", [128, 3, 1024], BF16)
            wkv = sb("wkv", [128, 2, 1024], BF16)
            g1t = sb("g1t", [128, 8]); gqat = sb("gqat", [128, 3]); gkvat = sb("gkvat", [128, 2])
            lblt = sb("lblt", [128, 2, 2, 4])
            lbt = sb("lbt", [128, 2, 4]); omlt = sb("omlt", [128, 2, 4])
            fa = sb("fa", [128, 2, 4]); fb_ = sb("fb", [128, 2, 4]); nfb = sb("nfb", [128, 2, 4]); lnfb = sb("lnfb", [128, 2, 4])
            ident = sb("ident", [128, 128], BF16)
            rmask = sb("rmask", [128, 512])
            xt = sb("xt", [128, 4, D]); xjunk = sb("xjunk", [128, 2, D], BF16)
            xn = sb("xn", [128, 2, D], BF16)
            hT = sb("hT", [128, 1, 8, 512], BF16)
            ssx = sb("ssx", [128, 2, 4]); lnx = sb("lnx", [128, 2, 4]); rsx = sb("rsx", [128, 2, 4])
            cst = sb("cst", [128, 2, 512]); snt = sb("snt", [128, 2, 512])
            qT = sb("qT", [128, 4, 512])
            th = sb("th", [128, 8, 512])
            tmpf = sb("tmpf", [128, 12, 512])
            stg = tmpf[:, 0:4, :].rearrange("p (a k) f -> p a (k f)", a=2)
            tmpb = sb("tmpb", [128, 8, 512], BF16)
            gt = sb("gt", [128, 4, 512])
            cqf = sb("cqf", [128, 4, 640]); cqn = sb("cqn", [128, 4, 640], BF16)
            ssq = sb("ssq", [128, 8]); lnq = sb("lnq", [128, 8]); rsq = sb("rsq", [128, 8])
            cqT = sb("cqT", [128, 3, 512], BF16); ckvT = sb("ckvT", [128, 2, 512], BF16)
            scal = sb("scal", [128, 2, 3, 4, NCH])
            pxt = ps("pxt", [128, 2, 8, 128], BF16)
            pmm = ps("pmm", [128, 6, 512])

            B = {}
            def b(name):
                if name not in B:
                    B[name] = Buf(name)
                return B[name]

            ring_f = Ring(tmpf, 12, "tmpf")
            ring_b = Ring(tmpb, 8, "tmpb")
            ring_p = Ring(pmm, 6, "pmm")
            ring_g = Ring(gt, 2, "gt")
            pxb = [Buf("pxt0"), Buf("pxt1")]
            hTb = [Buf("hT0"), Buf("hT1")]
            xtb = [Buf("xt%d" % i) for i in range(4)]
            xnb = [Buf("xn0"), Buf("xn1")]

            for (dst, src, nm) in ((g1t, g1, "g1t"), (gqat, gqa, "gqat"), (gkvat, gkva, "gkvat"),
                                   (lblt, lbl, "lblt"), (ident, c_ident, "ident"), (rmask, c_rmask, "rmask")):
                st.dma(lambda h, dst=dst, src=src: h.dma_start(out=dst[:], in_=src), writes=[b(nm)])
            st.op("dve", lambda h: h.tensor_tensor(out=lbt[:], in0=lblt[:, :, 1, :], in1=lblt[:, :, 0, :], op=ALU.subtract),
                  reads=[b("lblt")], writes=[b("lbt")])
            st.op("act", lambda h: h.activation(out=lbt[:], in_=lbt[:], func=AF.Exp), reads=[b("lbt")], writes=[b("lbt")])
            st.op("dve", lambda h: h.tensor_scalar(out=lbt[:], in0=lbt[:], scalar1=1.0, scalar2=None, op0=ALU.add),
                  reads=[b("lbt")], writes=[b("lbt")])
            st.op("dve", lambda h: h.reciprocal(out=lbt[:], in_=lbt[:]), reads=[b("lbt")], writes=[b("lbt")])
            st.op("dve", lambda h: h.tensor_scalar(out=omlt[:], in0=lbt[:], scalar1=-1.0, scalar2=1.0, op0=ALU.mult, op1=ALU.add),
                  reads=[b("lbt")], writes=[b("omlt")])
            st.op("dve", lambda h: h.tensor_scalar(out=fb_[:], in0=omlt[:], scalar1=0.5, scalar2=None, op0=ALU.mult),
                  reads=[b("omlt")], writes=[b("fb")])
            st.op("dve", lambda h: h.tensor_tensor(out=fa[:], in0=lbt[:], in1=fb_[:], op=ALU.add),
                  reads=[b("lbt"), b("fb")], writes=[b("fa")])
            st.op("dve", lambda h: h.tensor_scalar(out=nfb[:], in0=fb_[:], scalar1=-1.0, scalar2=None, op0=ALU.mult),
                  reads=[b("fb")], writes=[b("nfb")])
            st.op("act", lambda h: h.activation(out=lnfb[:], in_=fb_[:], func=AF.Ln), reads=[b("fb")], writes=[b("lnfb")])

            pcnt = [0]

            def prep(dst3, src3, nchunk, ncols, gain, eng_cycle, name):
                for c in range(nchunk):
                    for c0 in range(0, ncols, 1024):
                        c1 = min(ncols, c0 + 1024)
                        slot = pcnt[0] % 2
                        eng = eng_cycle[pcnt[0] % len(eng_cycle)]
                        pcnt[0] += 1
                        sbufs = [ring_f.bufs[2 * slot], ring_f.bufs[2 * slot + 1]]
                        st.dma(lambda h, c=c, slot=slot, c0=c0, c1=c1: h.dma_start(out=stg[:, slot, 0:c1 - c0], in_=src3[:, c, c0:c1]),
                               writes=sbufs)
                        st.op(eng, lambda h, c=c, slot=slot, c0=c0, c1=c1: h.tensor_scalar(
                            out=dst3[:, c, c0:c1], in0=stg[:, slot, 0:c1 - c0], scalar1=gain[:, c:c + 1], scalar2=0.0, op0=ALU.mult, op1=ALU.add),
                            reads=sbufs + [b(name + "_g")], writes=[b(name)])
            B["w1_g"] = b("g1t"); B["wq_g"] = b("gqat"); B["wkv_g"] = b("gkvat")
            prep(w1, w_in, 8, W1C, g1t, ["dve", "pool"], "w1")
            prep(wq, w_qb, 3, 1024, gqat, ["dve", "pool"], "wq")
            prep(wkv, w_kvb, 2, 1024, gkvat, ["dve", "pool"], "wkv")
            v1 = w1[:, :, 3328:3456].rearrange("p c (g r) -> p c g r", r=64)[:, :, :, 0:32]
            st.op("dve", lambda h: h.tensor_scalar(out=v1, in0=v1, scalar1=-1.0, scalar2=None, op0=ALU.mult),
                  reads=[b("w1")], writes=[b("w1")])
            v2 = wq[:, :, 768:1024].rearrange("p c (g r) -> p c g r", r=64)[:, :, :, 0:32]
            st.op("dve", lambda h: h.tensor_scalar(out=v2, in0=v2, scalar1=-1.0, scalar2=None, op0=ALU.mult),
                  reads=[b("wq")], writes=[b("wq")])

            w1b = b("w1")

            def mm_group(lhs_fn, rhs_fn, nk, reads, nfree=512):
                pt, pb = ring_p.get()
                pt = pt[:, 0:nfree]
                for c in range(nk):
                    st.op("pe", lambda h, c=c, pt=pt: h.matmul(pt, lhsT=lhs_fn(c), rhs=rhs_fn(c), start=(c == 0), stop=(c == nk - 1)),
                          reads=reads, writes=[pb])
                return pt, pb

            def xload(s):
                for j in range(4):
                    t0 = s * 512 + j * 128
                    st.dma(lambda h, j=j, t0=t0: h.dma_start(out=xt[:, j, :], in_=x[t0:t0 + 128, :]), writes=[xtb[j]])
                    st.op("act", lambda h, j=j: h.activation(out=xjunk[:, j % 2, :], in_=xt[:, j, :], func=AF.Square, accum_out=ssx[:, 0, j:j + 1]),
                          reads=[xtb[j]], writes=[b("ssx%d" % j), b("xjunk%d" % (j % 2))])

            def xnorm(s):
                st.op("act", lambda h: h.activation(out=lnx[:, 0, :], in_=ssx[:, 0, :], func=AF.Ln, scale=1.0 / D, bias=EPS),
                      reads=[b("ssx%d" % j) for j in range(4)], writes=[b("lnx")])
                st.op("act", lambda h: h.activation(out=rsx[:, 0, :], in_=lnx[:, 0, :], func=AF.Exp, scale=-0.5), reads=[b("lnx")], writes=[b("rsx")])
                for j in range(4):
                    n = j % 2
                    st.op("dve", lambda h, j=j, n=n: h.tensor_scalar(out=xn[:, n, :], in0=xt[:, j, :], scalar1=rsx[:, 0, j:j + 1], scalar2=None, op0=ALU.mult),
                          reads=[xtb[j], b("rsx")], writes=[xnb[n]])
                    for c in range(8):
                        st.op("pe", lambda h, n=n, c=c: h.transpose(out=pxt[:, n, c, :], in_=xn[:, n, c * 128:(c + 1) * 128], identity=ident[:]),
                              reads=[xnb[n], b("ident")], writes=[pxb[n]])
                    st.op("dve", lambda h, n=n, j=j: h.tensor_copy(out=hT[:, 0, :, j * 128:(j + 1) * 128], in_=pxt[:, n, :, :]),
                          reads=[pxb[n]], writes=[hTb[0]])

            def load_tables(s):
                slot = s % 2
                pos0 = (s * 512) % L
                st.dma(lambda h: h.dma_start(out=cst[:, slot, :], in_=c_cos[:, pos0:pos0 + 512]), writes=[b("cst%d" % slot)])
                st.dma(lambda h: h.dma_start(out=snt[:, slot, :], in_=c_sin[:, pos0:pos0 + 512]), writes=[b("snt%d" % slot)])

            def rope(s, pt1, pb1, pt2, pb2, dst):
                slot = s % 2
                f1, fb1 = ring_f.get()
                f2, fb2 = ring_f.get()
                ot, ob = ring_b.get()
                st.op("dve", lambda h: h.tensor_tensor(out=f1, in0=pt1, in1=cst[:, slot, :], op=ALU.mult), reads=[pb1, b("cst%d" % slot)], writes=[fb1])
                st.op("dve", lambda h: h.tensor_tensor(out=f2, in0=pt2, in1=snt[:, slot, :], op=ALU.mult), reads=[pb2, b("snt%d" % slot)], writes=[fb2])
                st.op("pool", lambda h: h.tensor_tensor(out=ot, in0=f1, in1=f2, op=ALU.add), reads=[fb1, fb2], writes=[ob])
                st.dma(lambda h: h.dma_start(out=dst, in_=ot), reads=[ob])

            def tm_piece(s, p):
                t0 = s * 512
                hs = hTb[0]
                if p == 8:
                    pt1, pb1 = mm_group(lambda c: w1[:, c, 3200:3328], lambda c: hT[:, 0, c, :], 8, [hs, w1b])
                    pt2, pb2 = mm_group(lambda c: w1[:, c, 3328:3456], lambda c: hT[:, 0, c, :], 8, [hs, w1b])
                    rope(s, pt1, pb1, pt2, pb2, krT_s[:, t0:t0 + 512])
                    return
                j = p // 2
                tt = t0 + j * 128
                lhs = lambda c: hT[:, 0, c, j * 128:(j + 1) * 128]
                if p % 2 == 0:
                    pt, pb = mm_group(lhs, lambda c: w1[:, c, 1536:2048], 8, [hs, w1b])
                    ot, ob = ring_b.get()
                    st.op("dve", lambda h: h.tensor_copy(out=ot, in_=pt), reads=[pb], writes=[ob])
                    st.dma(lambda h: h.dma_start(out=v_s[tt:tt + 128, :], in_=ot), reads=[ob])
                    pt2, pb2 = mm_group(lhs, lambda c: w1[:, c, 2048:2560], 8, [hs, w1b])
                    st.op("act", lambda h: h.activation(out=gt[:, j, :], in_=pt2, func=AF.Copy), reads=[pb2], writes=[b("gt%d" % j)])
                else:
                    pt, pb = mm_group(lhs, lambda c: w1[:, c, 2560:2944], 8, [hs, w1b], nfree=384)
                    st.op("act", lambda h: h.activation(out=xjunk[:, 0, 0:384], in_=pt[:, 0:384], func=AF.Square, accum_out=ssq[:, j:j + 1]),
                          reads=[pb], writes=[b("ssq%d" % j), b("xjunk0")])
                    st.op("dve", lambda h: h.tensor_copy(out=cqf[:, j, 0:384], in_=pt[:, 0:384]), reads=[pb], writes=[b("cqf%d" % j)])
                    pt2, pb2 = mm_group(lhs, lambda c: w1[:, c, 2944:3200], 8, [hs, w1b], nfree=256)
                    st.op("act", lambda h: h.activation(out=xjunk[:, 1, 0:256], in_=pt2[:, 0:256], func=AF.Square, accum_out=ssq[:, 4 + j:5 + j]),
                          reads=[pb2], writes=[b("ssq%d" % (4 + j)), b("xjunk1")])
                    st.op("dve", lambda h: h.tensor_copy(out=cqf[:, j, 384:640], in_=pt2[:, 0:256]), reads=[pb2], writes=[b("cqf%d" % j)])

            def fm_hf(s, db):
                col = 512 + db * 128
                pt, pb = mm_group(lambda c: w1[:, c, col:col + 128], lambda c: hT[:, 0, c, :], 8, [hTb[0], w1b])
                st.op("act", lambda h: h.activation(out=th[:, db, :], in_=pt, func=AF.Copy), reads=[pb], writes=[b("th%d" % db)])

            def fm_hq(s, blk):
                pt, pb = mm_group(lambda c: w1[:, c, blk * 128:(blk + 1) * 128], lambda c: hT[:, 0, c, :], 8, [hTb[0], w1b])
                st.op("dve", lambda h: h.tensor_copy(out=qT[:, blk, :], in_=pt), reads=[pb], writes=[b("qT%d" % blk)])

            def act18(s):
                t0 = s * 512
                for blk in range(4):
                    st.op("act", lambda h, blk=blk: h.activation(out=qT[:, blk, :], in_=qT[:, blk, :], func=AF.Silu),
                          reads=[b("qT%d" % blk)], writes=[b("qT%d" % blk)])
                for db in range(8):
                    st.op("act", lambda h, db=db: h.activation(out=th[:, db, :], in_=th[:, db, :], func=AF.Tanh, scale=-0.5),
                          reads=[b("th%d" % db)], writes=[b("th%d" % db)])
                for j in range(4):
                    tt = t0 + j * 128
                    st.op("act", lambda h, j=j: h.activation(out=gt[:, j, :], in_=gt[:, j, :], func=AF.Silu), reads=[b("gt%d" % j)], writes=[b("gt%d" % j)])
                    st.dma(lambda h, j=j, tt=tt: h.dma_start(out=g_s[tt:tt + 128, :], in_=gt[:, j, :]), reads=[b("gt%d" % j)])

            def gate_chain(s, db):
                d, blk = db // 4, db % 4
                t0 = s * 512
                c0 = s * 4
                thb = b("th%d" % db)
                lf, lfb = ring_f.get()
                st.op("pool", lambda h: h.tensor_scalar(out=lf, in0=th[:, db, :], scalar1=nfb[:, d, blk:blk + 1], scalar2=fa[:, d, blk:blk + 1],
                                                        op0=ALU.mult, op1=ALU.add),
                      reads=[thb, b("nfb"), b("fa")], writes=[lfb])
                st.op("act", lambda h: h.activation(out=lf, in_=lf, func=AF.Ln), reads=[lfb], writes=[lfb])
                yield
                cumt, cumb = ring_f.get()
                cct, ccb = ring_f.get()
                st.op("dve", lambda h: h.tensor_tensor_scan(out=cumt, data0=rmask[:], data1=lf, initial=0.0, op0=ALU.mult, op1=ALU.add),
                      reads=[lfb, b("rmask")], writes=[cumb])
                cumv = cumt.rearrange("p (c t) -> p c t", t=128)
                ccv = cct.rearrange("p (c t) -> p c t", t=128)
                st.op("dve", lambda h: h.tensor_tensor(out=ccv, in0=cumv, in1=bc(cumv[:, :, 63:64], [128, 4, 128]), op=ALU.subtract),
                      reads=[cumb], writes=[ccb])
                Xi, Yi = (1, 2) if d == 0 else (2, 1)
                st.op("act", lambda h: h.activation(out=scal[:, d, 0, blk, c0:c0 + 4], in_=cumv[:, :, 127], func=AF.Exp), reads=[cumb], writes=[b("scal")])
                st.op("act", lambda h: h.activation(out=scal[:, d, Xi, blk, c0:c0 + 4], in_=ccv[:, :, 127], func=AF.Exp), reads=[ccb], writes=[b("scal")])
                st.op("act", lambda h: h.activation(out=scal[:, d, Yi, blk, c0:c0 + 4], in_=cumv[:, :, 63], func=AF.Exp), reads=[cumb], writes=[b("scal")])
                if d == 0:
                    srct, srcb = cct, ccb
                else:
                    srct, srcb = ring_f.get()
                    st.op("pool", lambda h: h.tensor_tensor(out=srct, in0=lf, in1=cct, op=ALU.subtract), reads=[lfb, ccb], writes=[srcb])
                yield
                ea, eab = ring_f.get()
                eb, ebb = ring_f.get()
                st.op("act", lambda h: h.activation(out=ea, in_=srct, func=AF.Exp), reads=[srcb], writes=[eab])
                st.op("act", lambda h: h.activation(out=eb, in_=srct, func=AF.Exp, scale=-1.0, bias=lnfb[:, d, blk:blk + 1]),
                      reads=[srcb, b("lnfb")], writes=[ebb])
                o1, o1b = ring_b.get()
                o2, o2b = ring_b.get()
                st.op("dve", lambda h: h.tensor_tensor(out=o1, in0=qT[:, blk, :], in1=ea, op=ALU.mult), reads=[b("qT%d" % blk), eab], writes=[o1b])
                st.op("dve", lambda h: h.scalar_tensor_tensor(out=o2, in0=th[:, db, :], scalar=1.0, in1=eb, op0=ALU.add, op1=ALU.mult),
                      reads=[thb, ebb], writes=[o2b])
                st.dma(lambda h: h.dma_start(out=qdT_s[d, blk, :, t0:t0 + 512], in_=o1), reads=[o1b])
                st.dma(lambda h: h.dma_start(out=kiT_s[d, blk, :, t0:t0 + 512], in_=o2), reads=[o2b])
                yield

            def cq_rstd():
                st.op("act", lambda h: h.activation(out=lnq[:, 0:4], in_=ssq[:, 0:4], func=AF.Ln, scale=1.0 / 384, bias=EPS),
                      reads=[b("ssq%d" % i) for i in range(8)], writes=[b("lnq")])
                st.op("act", lambda h: h.activation(out=lnq[:, 4:8], in_=ssq[:, 4:8], func=AF.Ln, scale=1.0 / 256, bias=EPS),
                      reads=[b("ssq%d" % i) for i in range(8)], writes=[b("lnq")])
                st.op("act", lambda h: h.activation(out=rsq[:], in_=lnq[:], func=AF.Exp, scale=-0.5), reads=[b("lnq")], writes=[b("rsq")])

            def cq_transpose(s):
                for j in range(4):
                    st.op("dve", lambda h, j=j: h.tensor_scalar(out=cqn[:, j, 0:384], in0=cqf[:, j, 0:384], scalar1=rsq[:, j:j + 1], scalar2=None, op0=ALU.mult),
                          reads=[b("cqf%d" % j), b("rsq")], writes=[b("cqn%d" % j)])
                    st.op("dve", lambda h, j=j: h.tensor_scalar(out=cqn[:, j, 384:640], in0=cqf[:, j, 384:640], scalar1=rsq[:, 4 + j:5 + j], scalar2=None, op0=ALU.mult),
                          reads=[b("cqf%d" % j), b("rsq")], writes=[b("cqn%d" % j)])
                for j in range(4):
                    n = j % 2
                    for c in range(5):
                        st.op("pe", lambda h, n=n, c=c, j=j: h.transpose(out=pxt[:, n, c, :], in_=cqn[:, j, c * 128:(c + 1) * 128], identity=ident[:]),
                              reads=[b("cqn%d" % j), b("ident")], writes=[pxb[n]])
                    st.op("dve", lambda h, n=n, j=j: h.tensor_copy(out=cqT[:, :, j * 128:(j + 1) * 128], in_=pxt[:, n, 0:3, :]), reads=[pxb[n]], writes=[b("cqT")])
                    st.op("dve", lambda h, n=n, j=j: h.tensor_copy(out=ckvT[:, :, j * 128:(j + 1) * 128], in_=pxt[:, n, 3:5, :]), reads=[pxb[n]], writes=[b("ckvT")])

            def second_proj(s):
                t0 = s * 512
                out = []

                def qn(hh):
                    pt, pb = mm_group(lambda c: wq[:, c, hh * 128:(hh + 1) * 128], lambda c: cqT[:, c, :], 3, [b("cqT"), b("wq")])
                    ot, ob = ring_b.get()
                    st.op("dve", lambda h: h.tensor_copy(out=ot, in_=pt), reads=[pb], writes=[ob])
                    st.dma(lambda h: h.dma_start(out=qnT_s[hh, :, t0:t0 + 512], in_=ot), reads=[ob])

                def qr_(pr):
                    pt1, pb1 = mm_group(lambda c: wq[:, c, 512 + pr * 128:640 + pr * 128], lambda c: cqT[:, c, :], 3, [b("cqT"), b("wq")])
                    pt2, pb2 = mm_group(lambda c: wq[:, c, 768 + pr * 128:896 + pr * 128], lambda c: cqT[:, c, :], 3, [b("cqT"), b("wq")])
                    rope(s, pt1, pb1, pt2, pb2, qrT_s[pr, :, t0:t0 + 512])

                def kn(hh):
                    pt, pb = mm_group(lambda c: wkv[:, c, hh * 128:(hh + 1) * 128], lambda c: ckvT[:, c, :], 2, [b("ckvT"), b("wkv")])
                    ot, ob = ring_b.get()
                    st.op("dve", lambda h: h.tensor_copy(out=ot, in_=pt), reads=[pb], writes=[ob])
                    st.dma(lambda h: h.dma_start(out=knT_s[hh, :, t0:t0 + 512], in_=ot), reads=[ob])

                def vv(j):
                    tt = t0 + j * 128
                    pt, pb = mm_group(lambda c: ckvT[:, c, j * 128:(j + 1) * 128], lambda c: wkv[:, c, 512:1024], 2, [b("ckvT"), b("wkv")])
                    ot, ob = ring_b.get()
                    st.op("dve", lambda h: h.tensor_copy(out=ot, in_=pt), reads=[pb], writes=[ob])
                    st.dma(lambda h: h.dma_start(out=vm_s[tt:tt + 128, :], in_=ot), reads=[ob])
                for hh in range(4):
                    out.append(lambda hh=hh: qn(hh))
                for pr in range(2):
                    out.append(lambda pr=pr: qr_(pr))
                for hh in range(4):
                    out.append(lambda hh=hh: kn(hh))
                for j in range(4):
                    out.append(lambda j=j: vv(j))
                return out

            xload(0)
            load_tables(0)
            xnorm(0)
            if NS > 1:
                xload(1)
            for p in range(9):
                tm_piece(0, p)
            for blk in range(4):
                fm_hq(0, blk)
            for db in range(8):
                fm_hf(0, db)
            cq_rstd()
            for s in range(NS):
                nxt = s + 1 < NS
                if nxt:
                    load_tables(s + 1)
                    xnorm(s + 1)
                    if s + 2 < NS:
                        xload(s + 2)
                cq_transpose(s)
                act18(s)
                for f_ in second_proj(s):
                    f_()
                rest = []
                for db0 in range(0, 8, 2):
                    live = [gate_chain(s, db0), gate_chain(s, db0 + 1)]
                    while live:
                        for g_ in list(live):
                            try:
                                next(g_)
                            except StopIteration:
                                live.remove(g_)
                    for db in (db0, db0 + 1):
                        if nxt:
                            tm_piece(s + 1, db)
                            if db == 7:
                                tm_piece(s + 1, 8)
                            fm_hf(s + 1, db)
                            if db >= 4:
                                fm_hq(s + 1, db - 4)
                        if rest:
                            rest.pop(0)()
                while rest:
                    rest.pop(0)()
                if nxt:
                    cq_rstd()
            st.dma(lambda h: h.dma_start(out=scal_s, in_=scal[:]), reads=[b("scal")])
            st.finish()
            st.emit()

    def stage2():
        NK = L // 128
        NQ = L // 512
        DEN = os.environ.get("S2_DEN", "mix")
        ROPE128 = os.environ.get("S2_ROPE", "k128") == "k128"
        NPS = int(os.environ.get("S2_NPS", "4"))
        NP = int(os.environ.get("S2_NP", "6"))
        with contextlib.ExitStack() as es:
            def sb(name, shape, dt=F32):
                return es.enter_context(nc.sbuf_tensor("s2_" + name, list(shape), dt))

            def ps(name, shape, dt=F32):
                return es.enter_context(nc.psum_tensor("s2_" + name, list(shape), dt))

            st = Stage(nc, "s2")
            knT = sb("knT", [128, 4, L], BF16)
            vm = sb("vm", [128, NK, 512], BF16)
            krT = sb("krT", [128, L], BF16)
            qn = sb("qn", [128, 2, 4, 512], BF16)
            qr = sb("qr", [128, 2, 2, 512], BF16)
            ones = sb("ones", [128, 128], BF16)
            Pt = sb("Pt", [128, NP, 512], BF16)
            qrz = sb("qrz", [128, 2, 4, 512], BF16)
            hm = sb("hm", [128, 2])
            accP = sb("accP", [128, 2, 512])
            oT = sb("oT", [128, 2, 4, 512])
            rden = sb("rden", [128, 2, 512])
            sq = sb("sq", [128, 4, 512], BF16)
            lnv = sb("lnv", [128, 512]); rstd = sb("rstd", [128, 512])
            yb = sb("yb", [128, 4, 512], BF16)
            pS = ps("pS", [128, NPS, 512])
            pO = ps("pO", [128, 2, 512])
            pD = ps("pD", [128, 1, 512])
            pSS = ps("pSS", [128, 512])
            acc = sb("acc", [128, 2, 512])
            acc4 = sb("acc4", [128, 2, 4, 512])
            onesf = sb("onesf", [128, 128])
            B = {}

            def b(name):
                if name not in B:
                    B[name] = Buf(name)
                return B[name]
            ring_P = Ring(Pt, NP, "Pt")
            ring_S = Ring(pS, NPS, "pS")
            ring_y = Ring(yb, 4, "yb")
            st.dma(lambda h: h.dma_start(out=ones[:], in_=c_ones), writes=[b("ones")])
            st.op("pool", lambda h: h.memset(onesf[:], 1.0), writes=[b("onesf")])
            st.op("pool", lambda h: h.memset(hm[:], 0.0), writes=[b("hm")])
            st.op("pool", lambda h: h.memset(hm[0:64, 0:1], 1.0), writes=[b("hm")])
            st.op("pool", lambda h: h.memset(hm[64:128, 1:2], 1.0), writes=[b("hm")])

            items = []
            for seq in range(n_seq):
                for qb in range(NQ):
                    for hh in range(4):
                        for kc in range(NK):
                            items.append((seq, qb, hh, kc))

            def loads_seq(seq):
                s0 = seq * L
                for hh in range(4):
                    st.dma(lambda h, hh=hh: h.dma_start(out=knT[:, hh, :], in_=knT_s[hh, :, s0:s0 + L]), writes=[b("knT")])
                st.dma(lambda h: h.dma_start(out=krT[:], in_=krT_s[:, s0:s0 + L]), writes=[b("krT")])
                st.dma(lambda h: h.dma_start(out=vm[:], in_=vm_s[s0:s0 + L, :].rearrange("(k p) f -> p k f", p=128)), writes=[b("vm")])

            def loads_q(seq, qb):
                slot = (seq * NQ + qb) % 2
                tq = seq * L + qb * 512
                st.dma(lambda h: h.dma_start(out=qn[:, slot], in_=qnT_s[:, :, tq:tq + 512].rearrange("h p t -> p h t")), writes=[b("qn%d" % slot)])
                st.dma(lambda h: h.dma_start(out=qr[:, slot], in_=qrT_s[:, :, tq:tq + 512].rearrange("h p t -> p h t")), writes=[b("qr%d" % slot)])
                if ROPE128:
                    for hh in range(4):
                        st.op("pool", lambda h, hh=hh: h.tensor_scalar(out=qrz[:, slot, hh, :], in0=qr[:, slot, hh // 2, :], scalar1=hm[:, hh % 2:hh % 2 + 1],
                                                                        scalar2=0.0, op0=ALU.mult, op1=ALU.add),
                              reads=[b("qr%d" % slot), b("hm")], writes=[b("qrz%d" % slot)])

            def qk(item):
                seq, qb, hh, kc = item
                slot = (seq * NQ + qb) % 2
                hp, pr = hh % 2, hh // 2
                pt, pb = ring_S.get()
                st.op("pe", lambda h: h.matmul(pt, lhsT=knT[:, hh, kc * 128:(kc + 1) * 128], rhs=qn[:, slot, hh, :], start=True, stop=False),
                      reads=[b("knT"), b("qn%d" % slot)], writes=[pb])
                if ROPE128:
                    st.op("pe", lambda h: h.matmul(pt, lhsT=krT[:, kc * 128:(kc + 1) * 128], rhs=qrz[:, slot, hh, :], start=False, stop=True),
                          reads=[b("krT"), b("qrz%d" % slot)], writes=[pb])
                else:
                    st.op("pe", lambda h: h.matmul(pt, lhsT=krT[hp * 64:(hp + 1) * 64, kc * 128:(kc + 1) * 128],
                                                   rhs=qr[hp * 64:(hp + 1) * 64, slot, pr, :], start=False, stop=True),
                          reads=[b("krT"), b("qr%d" % slot)], writes=[pb])
                return pt, pb

            def finish_head(seq, qb, hh, oslot):
                qslot = (seq * NQ + qb) % 2
                if DEN == "mix":
                    a4 = acc4[:, oslot]
                    rd = [b("a4_%d_%d" % (oslot, j)) for j in range(4)]
                    st.op("dve", lambda h: h.tensor_tensor(out=a4[:, 0, :], in0=a4[:, 0, :], in1=a4[:, 1, :], op=ALU.add), reads=rd[0:2], writes=[rd[0]])
                    st.op("pool", lambda h: h.tensor_tensor(out=a4[:, 2, :], in0=a4[:, 2, :], in1=a4[:, 3, :], op=ALU.add), reads=rd[2:4], writes=[rd[2]])
                    st.op("pe", lambda h: h.matmul(pD[:, 0, :], lhsT=onesf[:], rhs=a4[:, 0, :], start=True, stop=False),
                          reads=[b("onesf"), rd[0]], writes=[b("pD0")])
                    st.op("pe", lambda h: h.matmul(pD[:, 0, :], lhsT=onesf[:], rhs=a4[:, 2, :], start=False, stop=True),
                          reads=[b("onesf"), rd[2]], writes=[b("pD0")])
                    st.op("act", lambda h: h.activation(out=rden[:, oslot, :], in_=pD[:, 0, :], func=AF.Ln), reads=[b("pD0")], writes=[b("rden%d" % oslot)])
                    st.op("act", lambda h: h.activation(out=rden[:, oslot, :], in_=rden[:, oslot, :], func=AF.Exp, scale=-1.0),
                          reads=[b("rden%d" % oslot)], writes=[b("rden%d" % oslot)])
                    st.op("dve", lambda h: h.tensor_tensor(out=oT[:, qslot, hh, :], in0=pO[:, oslot, :], in1=rden[:, oslot, :], op=ALU.mult),
                          reads=[b("pO%d" % oslot), b("rden%d" % oslot)], writes=[b("oT%d_%d" % (qslot, hh))])
                    return
                if DEN == "dve4":
                    a4 = acc4[:, oslot]
                    rd = [b("a4_%d_%d" % (oslot, j)) for j in range(4)]
                    st.op("dve", lambda h: h.tensor_tensor(out=a4[:, 0:2, :], in0=a4[:, 0:2, :], in1=a4[:, 2:4, :], op=ALU.add), reads=rd, writes=rd[0:2])
                    st.op("dve", lambda h: h.tensor_tensor(out=acc[:, oslot, :], in0=a4[:, 0, :], in1=a4[:, 1, :], op=ALU.add), reads=rd[0:2], writes=[b("acc%d" % oslot)])
                    st.op("pe", lambda h: h.matmul(pD[:, 0, :], lhsT=onesf[:], rhs=acc[:, oslot, :], start=True, stop=True),
                          reads=[b("onesf"), b("acc%d" % oslot)], writes=[b("pD0")])
                elif DEN == "dve":
                    st.op("pe", lambda h: h.matmul(pD[:, 0, :], lhsT=onesf[:], rhs=acc[:, oslot, :], start=True, stop=True),
                          reads=[b("onesf"), b("acc%d" % oslot)], writes=[b("pD0")])
                elif DEN == "split":
                    st.op("pe", lambda h: h.matmul(pD[:, 0, :], lhsT=onesf[:], rhs=acc[:, oslot, :], start=True, stop=False),
                          reads=[b("onesf"), b("acc%d" % oslot)], writes=[b("pD0")])
                    st.op("pe", lambda h: h.matmul(pD[:, 0, :], lhsT=onesf[:], rhs=accP[:, oslot, :], start=False, stop=True),
                          reads=[b("onesf"), b("accP%d" % oslot)], writes=[b("pD0")])
                st.op("dve", lambda h: h.reciprocal(out=rden[:, oslot, :], in_=pD[:, 0, :]), reads=[b("pD0")], writes=[b("rden%d" % oslot)])
                st.op("dve", lambda h: h.tensor_tensor(out=oT[:, qslot, hh, :], in0=pO[:, oslot, :], in1=rden[:, oslot, :], op=ALU.mult),
                      reads=[b("pO%d" % oslot), b("rden%d" % oslot)], writes=[b("oT%d_%d" % (qslot, hh))])

            def finish_q(seq, qb):
                qslot = (seq * NQ + qb) % 2
                tq = seq * L + qb * 512
                for hh in range(4):
                    st.op("act", lambda h, hh=hh: h.activation(out=sq[:, hh, :], in_=oT[:, qslot, hh, :], func=AF.Square),
                          reads=[b("oT%d_%d" % (qslot, hh))], writes=[b("sq%d" % hh)])
                for hh in range(4):
                    st.op("pe", lambda h, hh=hh: h.matmul(pSS[:], lhsT=ones[:], rhs=sq[:, hh, :], start=(hh == 0), stop=(hh == 3)),
                          reads=[b("ones"), b("sq%d" % hh)], writes=[b("pSS")])
                st.op("act", lambda h: h.activation(out=lnv[:], in_=pSS[:], func=AF.Ln, scale=1.0 / 512, bias=EPS), reads=[b("pSS")], writes=[b("lnv")])
                st.op("act", lambda h: h.activation(out=rstd[:], in_=lnv[:], func=AF.Exp, scale=-0.5), reads=[b("lnv")], writes=[b("rstd")])
                for hh in range(4):
                    yt, ybuf = ring_y.get()
                    st.op("dve", lambda h, hh=hh, yt=yt: h.tensor_tensor(out=yt, in0=oT[:, qslot, hh, :], in1=rstd[:], op=ALU.mult),
                          reads=[b("oT%d_%d" % (qslot, hh)), b("rstd")], writes=[ybuf])
                    st.dma(lambda h, hh=hh, yt=yt: h.dma_start(out=yT_s[4 + hh, :, tq:tq + 512], in_=yt), reads=[ybuf])

            import collections
            LA = int(os.environ.get("S2_LA", "2"))
            pending = collections.deque()
            nq = [0]

            def ensure(upto, seq):
                while nq[0] < min(upto, len(items)) and items[nq[0]][0] == seq:
                    pending.append(qk(items[nq[0]]))
                    nq[0] += 1

            ocnt = 0
            deferred = []
            DEFER = int(os.environ.get("S2_DEFER", "4"))
            for idx, item in enumerate(items):
                seq, qb, hh, kc = item
                if qb == 0 and hh == 0 and kc == 0:
                    loads_seq(seq)
                    loads_q(seq, 0)
                if hh == 0 and kc == 0 and qb + 1 < NQ:
                    loads_q(seq, qb + 1)
                ensure(idx + 1 + LA, seq)
                pt, pb = pending.popleft()
                oslot = ocnt % 2
                Pa, Pb = ring_P.get()
                st.op("act", lambda h, Pa=Pa, pt=pt: h.activation(out=Pa, in_=pt, func=AF.Exp, scale=SCALE), reads=[pb], writes=[Pb])
                st.op("pe", lambda h, Pa=Pa, oslot=oslot, hh=hh, kc=kc: h.matmul(pO[:, oslot, :], lhsT=vm[:, kc, hh * 128:(hh + 1) * 128], rhs=Pa,
                                                                           start=(kc == 0), stop=(kc == NK - 1)),
                      reads=[b("vm"), Pb], writes=[b("pO%d" % oslot)])
                if DEN == "pe":
                    st.op("pe", lambda h, Pa=Pa, kc=kc: h.matmul(pD[:, 0, :], lhsT=ones[:], rhs=Pa, start=(kc == 0), stop=(kc == NK - 1)),
                          reads=[b("ones"), Pb], writes=[b("pD0")])
                elif DEN in ("dve4", "mix"):
                    j4 = kc % 4
                    ab = b("a4_%d_%d" % (oslot, j4))
                    eng4 = "pool" if (DEN == "mix" and j4 == 3) else "dve"
                    if kc < 4:
                        st.op(eng4, lambda h, Pa=Pa, oslot=oslot, j4=j4: h.tensor_copy(out=acc4[:, oslot, j4, :], in_=Pa), reads=[Pb], writes=[ab])
                    else:
                        st.op(eng4, lambda h, Pa=Pa, oslot=oslot, j4=j4: h.tensor_tensor(out=acc4[:, oslot, j4, :], in0=acc4[:, oslot, j4, :], in1=Pa, op=ALU.add),
                              reads=[Pb, ab], writes=[ab])
                else:
                    on_pool = (DEN == "split" and kc % 4 == 3)
                    eng, at, an, first = ("pool", accP, "accP", kc == 3) if on_pool else ("dve", acc, "acc", kc == 0)
                    if first:
                        st.op(eng, lambda h, Pa=Pa, oslot=oslot, at=at: h.tensor_copy(out=at[:, oslot, :], in_=Pa), reads=[Pb], writes=[b("%s%d" % (an, oslot))])
                    else:
                        st.op(eng, lambda h, Pa=Pa, oslot=oslot, at=at: h.tensor_tensor(out=at[:, oslot, :], in0=at[:, oslot, :], in1=Pa, op=ALU.add),
                              reads=[Pb, b("%s%d" % (an, oslot))], writes=[b("%s%d" % (an, oslot))])
                if kc == NK - 1:
                    deferred.append((idx + DEFER, lambda seq=seq, qb=qb, hh=hh, oslot=oslot: finish_head(seq, qb, hh, oslot)))
                    ocnt += 1
                    if hh == 3:
                        deferred.append((idx + DEFER, lambda seq=seq, qb=qb: finish_q(seq, qb)))
                while deferred and deferred[0][0] <= idx:
                    deferred.pop(0)[1]()
            while deferred:
                deferred.pop(0)[1]()
            st.finish()
            st.emit()

    def stage3():
        NCs = L // 128
        NCHN = 2 * n_seq
        with contextlib.ExitStack() as es:
            def sb(name, shape, dt=F32):
                return es.enter_context(nc.sbuf_tensor("s3_" + name, list(shape), dt))

            def ps(name, shape, dt=F32):
                return es.enter_context(nc.psum_tensor("s3_" + name, list(shape), dt))

            st = Stage(nc, "s3")
            scal = sb("scal", [128, 2, 3, 4, NCH])
            masks = sb("masks", [128, 2, 128])
            ident = sb("ident", [128, 128], BF16)
            BD = sb("BD", [128, 4, 128])
            qd = sb("qd", [128, 2 * NCHN, 4, 256], BF16)
            ki = sb("ki", [128, 2 * NCHN, 4, 256], BF16)
            vt = sb("vt", [128, 2 * NCHN, 2, 512], BF16)
            kitok = sb("kitok", [128, NCHN, 512], BF16)
            scT = sb("scT", [128, NCHN, 8, 128], BF16)
            S_all = sb("S", [128, NCHN, 4, 128]); Sp_all = sb("Sp", [128, NCHN, 4, 128], BF16)
            t1_all = sb("t1", [128, NCHN, 4, 128]); t2_all = sb("t2", [128, NCHN, 4, 128])
            Bm_all = sb("Bm", [128, NCHN, 4, 128])
            oev = sb("oev", [128, 4, 512])
            cof = sb("cof", [128, 3, 512]); cob = sb("cob", [128, 3, 512]); cg_ = sb("cg", [128, 3, 512])
            osum_a = sb("osum", [128, 2, 512]); sqt_a = sb("sqt", [128, 2, 512]); yn_a = sb("yn", [128, 2, 512])
            ss8_a = sb("ss8", [128, 2, 8]); ln8_a = sb("ln8", [128, 2, 8]); rs8_a = sb("rs8", [128, 2, 8])
            ya = sb("ya", [128, 2, 512], BF16)
            yaT = sb("yaT", [128, 2, 4, 128], BF16)
            pT = ps("pT", [128, 1, 1024], BF16)
            pSc = ps("pSc", [128, 4, 4, 128])
            pOo = ps("pOo", [128, 2, 512])
            pU = ps("pU", [128, 1, 4, 128])
            pY = pU[:].bitcast(BF16).rearrange("p a b (c t) -> p (a b c) t", t=128)
            B = {}

            def b(name):
                if name not in B:
                    B[name] = Buf(name)
                return B[name]
            ring_oev = Ring(oev, 4, "oev")
            st.dma(lambda h: h.dma_start(out=scal[:], in_=scal_s), writes=[b("scal")])
            st.dma(lambda h: h.dma_start(out=masks[:, 0, :], in_=c_maskf), writes=[b("masks")])
            st.dma(lambda h: h.dma_start(out=masks[:, 1, :], in_=c_maskb), writes=[b("masks")])
            st.dma(lambda h: h.dma_start(out=ident[:], in_=c_ident), writes=[b("ident")])
            st.op("pool", lambda h: h.memset(BD[:], 0.0), writes=[b("BD")])
            st.op("pool", lambda h: h.memset(BD[0:64, :, 0:64], 1.0), writes=[b("BD")])
            st.op("pool", lambda h: h.memset(BD[64:128, :, 64:128], 1.0), writes=[b("BD")])
            rot = {"pT": 0, "pOo": 0, "pSc": 0}

            def chain(seq, d):
                ci = seq * 2 + d
                order = list(range(NCs)) if d == 0 else list(range(NCs - 1, -1, -1))
                S = S_all[:, ci]; Sp = Sp_all[:, ci]; t1 = t1_all[:, ci]; t2 = t2_all[:, ci]; Bm = Bm_all[:, ci]
                bS, bSp, bt1, bt2, bBm = (b("%s%d" % (nm, ci)) for nm in ("S", "Sp", "t1", "t2", "Bm"))
                st.op("pool", lambda h: h.memset(S, 0.0), writes=[bS])
                st.op("pool", lambda h: h.memset(Sp, 0.0), writes=[bSp])
                odst = of_s if d == 0 else ob_s
                def load_group(k):
                    gi = order[2 * k] // 2
                    slot = ci * 2 + k % 2
                    tg = seq * L + gi * 256
                    st.dma(lambda h: h.dma_start(out=qd[:, slot], in_=qdT_s[d, :, :, tg:tg + 256].rearrange("k p t -> p k t")), writes=[b("qd%d" % slot)])
                    st.dma(lambda h: h.dma_start(out=ki[:, slot], in_=kiT_s[d, :, :, tg:tg + 256].rearrange("k p t -> p k t")), writes=[b("ki%d" % slot)])
                    st.dma(lambda h: h.dma_start(out=vt[:, slot], in_=v_s[tg:tg + 256, :].rearrange("(j p) f -> p j f", p=128)), writes=[b("vt%d" % slot)])

                load_group(0)
                for pos, n in enumerate(order):
                    if pos % 2 == 0 and pos + 2 < len(order):
                        load_group(pos // 2 + 1)
                    slot = ci * 2 + (pos // 2) % 2
                    j = n % 2
                    cg = seq * NCs + n
                    t0 = cg * 128
                    qd_c = qd[:, slot, :, j * 128:(j + 1) * 128]
                    ki_c = ki[:, slot, :, j * 128:(j + 1) * 128]
                    v_c = vt[:, slot, j, :]
                    rq, rk, rv = b("qd%d" % slot), b("ki%d" % slot), b("vt%d" % slot)
                    has_next = pos + 1 < len(order)
                    tb = 0
                    sset = rot["pSc"] % 2
                    rot["pSc"] += 1
                    for bk in range(4):
                        st.op("pe", lambda h, bk=bk, tb=tb, ki_c=ki_c: h.transpose(out=pT[:, tb, bk * 128:(bk + 1) * 128], in_=ki_c[:, bk, :], identity=ident[:]),
                              reads=[rk, b("ident")], writes=[b("pT%d" % tb)])
                    st.op("act", lambda h, tb=tb: h.activation(out=kitok[:, ci, :], in_=pT[:, tb, 0:512], func=AF.Copy),
                          reads=[b("pT%d" % tb)], writes=[b("kitok%d" % ci)])
                    for hh in range(8):
                        bk, hp = hh // 2, hh % 2
                        st.op("pe", lambda h, bk=bk, hp=hp, ki_c=ki_c, qd_c=qd_c, sset=sset: h.matmul(pSc[:, sset * 2 + hp, bk, :], lhsT=ki_c[hp * 64:(hp + 1) * 64, bk, :],
                                                                                      rhs=qd_c[hp * 64:(hp + 1) * 64, bk, :], start=True, stop=True),
                              reads=[rk, rq], writes=[b("pSc%d" % (sset * 2 + hp))])
                    scv = scT[:, ci].rearrange("p (k two) t -> p k two t", two=2)
                    for half in range(2):
                        st.op("dve", lambda h, half=half, scv=scv, sset=sset: h.tensor_tensor(out=scv[:, :, half, :], in0=pSc[:, sset * 2 + half],
                                                                                             in1=bc(masks[:, d:d + 1, :], [128, 4, 128]), op=ALU.mult),
                              reads=[b("pSc%d" % (sset * 2 + half)), b("masks")], writes=[b("scT%d_%d" % (ci, half))])
                    if has_next:
                        st.op("pool", lambda h, cg=cg: h.tensor_tensor(out=Bm, in0=BD[:], in1=bc(scal[:, d, 1, :, cg:cg + 1], [128, 4, 128]), op=ALU.mult),
                              reads=[b("BD"), b("scal")], writes=[bBm])
                    yield
                    ob_ = rot["pOo"] % 2
                    rot["pOo"] += 1
                    for bk in range(4):
                        st.op("pe", lambda h, bk=bk, ob_=ob_, qd_c=qd_c: h.matmul(pOo[:, ob_, bk * 128:(bk + 1) * 128], lhsT=qd_c[:, bk, :], rhs=Sp[:, bk, :],
                                                                              start=True, stop=False),
                              reads=[rq, bSp], writes=[b("pOo%d" % ob_)])
                        for hp in range(2):
                            hh = 2 * bk + hp
                            st.op("pe", lambda h, hh=hh, hp=hp, ob_=ob_, v_c=v_c: h.matmul(pOo[:, ob_, hh * 64:(hh + 1) * 64], lhsT=scT[:, ci, hh, :],
                                                                                       rhs=v_c[:, hh * 64:(hh + 1) * 64], start=False, stop=(hp == 1)),
                                  reads=[b("scT%d_%d" % (ci, hh % 2)), rv], writes=[b("pOo%d" % ob_)])
                    ot, otb = ring_oev.get()
                    st.op("act", lambda h, ot=ot, ob_=ob_: h.activation(out=ot, in_=pOo[:, ob_, :], func=AF.Copy), reads=[b("pOo%d" % ob_)], writes=[otb])
                    st.dma(lambda h, ot=ot, t0=t0: h.dma_start(out=odst[t0:t0 + 128, :], in_=ot), reads=[otb], writes=[b("o%d_%d" % (d, cg))], queue="act")
                    yield
                    if has_next:
                        cgn = seq * NCs + order[pos + 1]
                        for bk in range(4):
                            st.op("pe", lambda h, bk=bk, v_c=v_c: h.matmul(pU[:, 0, bk, :], lhsT=kitok[:, ci, bk * 128:(bk + 1) * 128], rhs=v_c[:, bk * 128:(bk + 1) * 128],
                                                                          start=True, stop=True),
                                  reads=[b("kitok%d" % ci), rv], writes=[b("pU0")])
                        Abc = bc(scal[:, d, 0, :, cg:cg + 1], [128, 4, 128])
                        Cbc = bc(scal[:, d, 2, :, cgn:cgn + 1], [128, 4, 128])
                        st.op("pool", lambda h, Abc=Abc: h.tensor_tensor(out=t1, in0=S, in1=Abc, op=ALU.mult), reads=[bS, b("scal")], writes=[bt1])
                        st.op("dve", lambda h: h.tensor_tensor(out=t2, in0=pU[:, 0], in1=Bm, op=ALU.mult), reads=[b("pU0"), bBm], writes=[bt2])
                        st.op("dve", lambda h: h.tensor_tensor(out=S, in0=t1, in1=t2, op=ALU.add), reads=[bt1, bt2], writes=[bS])
                        st.op("dve", lambda h, Cbc=Cbc: h.tensor_tensor(out=Sp, in0=S, in1=Cbc, op=ALU.mult), reads=[bS, b("scal")], writes=[bSp])
                    yield

            gens = [chain(seq, d) for seq in range(n_seq) for d in range(2)]
            live = list(gens)
            while live:
                for g in list(live):
                    try:
                        next(g)
                    except StopIteration:
                        live.remove(g)

            for cg in range(NCH):
                t0 = cg * 128
                k3 = cg % 3
                k2 = cg % 2
                osum = osum_a[:, k2]; sqt = sqt_a[:, k2]; yn = yn_a[:, k2]
                ss8 = ss8_a[:, k2]; ln8 = ln8_a[:, k2]; rs8 = rs8_a[:, k2]
                bos, bsq, byn, bss, bln, brs = (b("%s%d" % (nm, k2)) for nm in ("osum", "sqt", "yn", "ss8", "ln8", "rs8"))
                st.dma(lambda h, k3=k3, t0=t0: h.dma_start(out=cof[:, k3], in_=of_s[t0:t0 + 128, :]), reads=[b("o0_%d" % cg)], writes=[b("cof%d" % k3)])
                st.dma(lambda h, k3=k3, t0=t0: h.dma_start(out=cob[:, k3], in_=ob_s[t0:t0 + 128, :]), reads=[b("o1_%d" % cg)], writes=[b("cob%d" % k3)])
                st.dma(lambda h, k3=k3, t0=t0: h.dma_start(out=cg_[:, k3], in_=g_s[t0:t0 + 128, :]), writes=[b("cg%d" % k3)])
                st.op("dve", lambda h, k3=k3, osum=osum: h.tensor_tensor(out=osum, in0=cof[:, k3], in1=cob[:, k3], op=ALU.add),
                      reads=[b("cof%d" % k3), b("cob%d" % k3)], writes=[bos])
                st.op("act", lambda h, osum=osum, sqt=sqt: h.activation(out=sqt, in_=osum, func=AF.Square), reads=[bos], writes=[bsq])
                st.op("dve", lambda h, sqt=sqt, ss8=ss8: h.tensor_reduce(out=ss8, in_=sqt.rearrange("p (h e) -> p h e", e=64), axis=AX.X, op=ALU.add),
                      reads=[bsq], writes=[bss])
                st.op("act", lambda h, ss8=ss8, ln8=ln8: h.activation(out=ln8, in_=ss8, func=AF.Ln, scale=1.0 / 64, bias=EPS), reads=[bss], writes=[bln])
                st.op("act", lambda h, ln8=ln8, rs8=rs8: h.activation(out=rs8, in_=ln8, func=AF.Exp, scale=-0.5), reads=[bln], writes=[brs])
                st.op("dve", lambda h, yn=yn, osum=osum, rs8=rs8: h.tensor_tensor(out=yn.rearrange("p (h e) -> p h e", e=64), in0=osum.rearrange("p (h e) -> p h e", e=64),
                                                                              in1=bc(rs8.unsqueeze(2), [128, 8, 64]), op=ALU.mult),
                      reads=[bos, brs], writes=[byn])
                st.op("pool", lambda h, yn=yn, k2=k2, k3=k3: h.tensor_tensor(out=ya[:, k2, :], in0=yn, in1=cg_[:, k3], op=ALU.mult),
                      reads=[byn, b("cg%d" % k3)], writes=[b("ya%d" % k2)])
                for bk in range(4):
                    st.op("pe", lambda h, bk=bk, k2=k2: h.transpose(out=pY[:, bk, :], in_=ya[:, k2, bk * 128:(bk + 1) * 128], identity=ident[:]),
                          reads=[b("ya%d" % k2), b("ident")], writes=[b("pU0")])
                st.op("act", lambda h, k2=k2: h.activation(out=yaT[:, k2], in_=pY[:, 0:4, :], func=AF.Copy), reads=[b("pU0")], writes=[b("yaT%d" % k2)])
                st.dma(lambda h, k2=k2, t0=t0: h.dma_start(out=yT_s[0:4, :, t0:t0 + 128].rearrange("k p t -> p k t"), in_=yaT[:, k2]), reads=[b("yaT%d" % k2)], queue="act")
            st.finish()
            st.emit()

    def stage4():
        TT = 256
        NTL = T // TT
        with contextlib.ExitStack() as es:
            def sb(name, shape, dt=F32):
                return es.enter_context(nc.sbuf_tensor("s4_" + name, list(shape), dt))

            def ps(name, shape, dt=F32):
                return es.enter_context(nc.psum_tensor("s4_" + name, list(shape), dt))

            st = Stage(nc, "s4")
            wo = sb("wo", [128, 8, D], BF16)
            wg = sb("wg", [128, 8, DFF], BF16)
            wu = sb("wu", [128, 8, DFF], BF16)
            wd = sb("wd", [128, NFB, D], BF16)
            goutt = sb("goutt", [128, 8]); g2t = sb("g2t", [128, 8])
            gfb = sb("gfb", [128, D])
            ident = sb("ident", [128, 128], BF16)
            xt = sb("xt", [128, 2, 2, D])
            yT = sb("yT", [128, 2, 8, TT], BF16)
            h2n = sb("h2n", [128, 2, D], BF16)
            h2T = sb("h2T", [128, 8, TT], BF16)
            aT = sb("aT", [128, NFB, TT], BF16)
            sg = sb("sg", [128, 2, TT])
            junk = sb("junk", [128, D], BF16)
            ss = sb("ss", [128, 4]); lnt = sb("lnt", [128, 4]); rs = sb("rs", [128, 4])
            pxt = ps("pxt", [128, 8, 128], BF16)
            pG = ps("pG", [128, 2, 512])
            pUu = ps("pUu", [128, 2, 512])
            pA = ps("pA", [128, 2, 512])
            B = {}

            def b(name):
                if name not in B:
                    B[name] = Buf(name)
                return B[name]
            ring_A = Ring(pA, 2, "pA")
            for (dst, src, nm) in ((goutt, gout, "goutt"), (g2t, g2, "g2t"), (ident, c_ident, "ident")):
                st.dma(lambda h, dst=dst, src=src: h.dma_start(out=dst[:], in_=src), writes=[b(nm)])
            if os.environ.get("S4_NOBC"):
                for pp in range(0, 128, 32):
                    pass
                st.op("pool", lambda h: h.memset(gfb[:], 1.0), writes=[b("gfb")])
            else:
                st.dma(lambda h: h.dma_start(out=gfb[:], in_=gfin.partition_broadcast(128)), writes=[b("gfb")])
            stg = xt[:].rearrange("p a b d -> p (a b) d")
            pcnt = [0]

            def prep(dst3, src3, nchunk, ncols, gain, name, gname):
                for c in range(nchunk):
                    for c0 in range(0, ncols, 1024):
                        c1 = min(ncols, c0 + 1024)
                        slot = pcnt[0] % 4
                        eng = ("dve", "act")[pcnt[0] % 2]
                        pcnt[0] += 1
                        sbuf = b("xt%d" % slot)
                        st.dma(lambda h, c=c, slot=slot, c0=c0, c1=c1: h.dma_start(out=stg[:, slot, 0:c1 - c0], in_=src3[:, c, c0:c1]), writes=[sbuf])
                        rd = [sbuf] + ([b(gname)] if gain is not None else [])
                        if eng == "act":
                            if gain is None:
                                st.op("act", lambda h, c=c, slot=slot, c0=c0, c1=c1: h.activation(out=dst3[:, c, c0:c1], in_=stg[:, slot, 0:c1 - c0], func=AF.Copy),
                                      reads=rd, writes=[b(name)])
                            else:
                                st.op("act", lambda h, c=c, slot=slot, c0=c0, c1=c1: h.activation(out=dst3[:, c, c0:c1], in_=stg[:, slot, 0:c1 - c0], func=AF.Copy,
                                                                                               scale=gain[:, c:c + 1]),
                                      reads=rd, writes=[b(name)])
                        else:
                            if gain is None:
                                st.op(eng, lambda h, c=c, slot=slot, c0=c0, c1=c1: h.tensor_copy(out=dst3[:, c, c0:c1], in_=stg[:, slot, 0:c1 - c0]),
                                      reads=rd, writes=[b(name)])
                            else:
                                st.op(eng, lambda h, c=c, slot=slot, c0=c0, c1=c1: h.tensor_scalar(out=dst3[:, c, c0:c1], in0=stg[:, slot, 0:c1 - c0],
                                                                                                scalar1=gain[:, c:c + 1], scalar2=None, op0=ALU.mult),
                                      reads=rd, writes=[b(name)])
            prep(wo, w_out, 8, D, goutt, "wo", "goutt")
            prep(wg, w_gate, 8, DFF, g2t, "wg", "g2t")
            prep(wu, w_up, 8, DFF, g2t, "wu", "g2t")
            prep(wd, w_down, NFB, D, None, "wd", None)

            def loads(i):
                slot = i % 2
                t0 = i * TT
                st.dma(lambda h: h.dma_start(out=xt[:, slot], in_=x[t0:t0 + TT, :].rearrange("(s p) d -> p s d", p=128)),
                       writes=[b("xt%d" % (slot * 2)), b("xt%d" % (slot * 2 + 1))])
                st.dma(lambda h: h.dma_start(out=yT[:, slot], in_=yT_s[:, :, t0:t0 + TT].rearrange("c p t -> p c t")), writes=[b("yT%d" % slot)])

            def rms_rstd(i, sub, which):
                slot = i % 2
                col = which * 2 + sub
                xb = b("xt%d" % (slot * 2 + sub))
                st.op("act", lambda h: h.activation(out=junk[:], in_=xt[:, slot, sub, :], func=AF.Square, accum_out=ss[:, col:col + 1]),
                      reads=[xb], writes=[b("junk"), b("ss%d" % col)])
                st.op("act", lambda h: h.activation(out=lnt[:, col:col + 1], in_=ss[:, col:col + 1], func=AF.Ln, scale=1.0 / D, bias=EPS),
                      reads=[b("ss%d" % col)], writes=[b("ln%d" % col)])
                st.op("act", lambda h: h.activation(out=rs[:, col:col + 1], in_=lnt[:, col:col + 1], func=AF.Exp, scale=-0.5),
                      reads=[b("ln%d" % col)], writes=[b("rs%d" % col)])
                return col

            def tile(i):
                slot = i % 2
                t0 = i * TT
                if i + 1 < NTL:
                    loads(i + 1)
                for sub in range(2):
                    xb = b("xt%d" % (slot * 2 + sub))
                    for half in range(2):
                        pt, pb = ring_A.get()
                        for c in range(8):
                            st.op("pe", lambda h, c=c, pt=pt, sub=sub, half=half: h.matmul(pt, lhsT=yT[:, slot, c, sub * 128:(sub + 1) * 128],
                                                                                          rhs=wo[:, c, half * 512:(half + 1) * 512], start=(c == 0), stop=(c == 7)),
                                  reads=[b("yT%d" % slot), b("wo")], writes=[pb])
                        st.op("dve", lambda h, pt=pt, sub=sub, half=half: h.tensor_tensor(out=xt[:, slot, sub, half * 512:(half + 1) * 512],
                                                                                         in0=xt[:, slot, sub, half * 512:(half + 1) * 512], in1=pt, op=ALU.add),
                              reads=[pb, xb], writes=[xb])
                for sub in range(2):
                    xb = b("xt%d" % (slot * 2 + sub))
                    col = rms_rstd(i, sub, 0)
                    st.op("dve", lambda h, sub=sub, col=col: h.tensor_scalar(out=h2n[:, sub, :], in0=xt[:, slot, sub, :], scalar1=rs[:, col:col + 1],
                                                                            scalar2=None, op0=ALU.mult),
                          reads=[xb, b("rs%d" % col)], writes=[b("h2n%d" % sub)])
                    for c in range(8):
                        st.op("pe", lambda h, sub=sub, c=c: h.transpose(out=pxt[:, c, :], in_=h2n[:, sub, c * 128:(c + 1) * 128], identity=ident[:]),
                              reads=[b("h2n%d" % sub), b("ident")], writes=[b("pxt")])
                    st.op("dve", lambda h, sub=sub: h.tensor_copy(out=h2T[:, :, sub * 128:(sub + 1) * 128], in_=pxt[:]), reads=[b("pxt")], writes=[b("h2T")])
                for fb in range(NFB):
                    gs = fb % 2
                    for c in range(8):
                        st.op("pe", lambda h, c=c, fb=fb, gs=gs: h.matmul(pG[:, gs, 0:TT], lhsT=wg[:, c, fb * 128:(fb + 1) * 128], rhs=h2T[:, c, :],
                                                                          start=(c == 0), stop=(c == 7)),
                              reads=[b("wg"), b("h2T")], writes=[b("pG%d" % gs)])
                    for c in range(8):
                        st.op("pe", lambda h, c=c, fb=fb, gs=gs: h.matmul(pUu[:, gs, 0:TT], lhsT=wu[:, c, fb * 128:(fb + 1) * 128], rhs=h2T[:, c, :],
                                                                          start=(c == 0), stop=(c == 7)),
                              reads=[b("wu"), b("h2T")], writes=[b("pU%d" % gs)])
                    st.op("act", lambda h, gs=gs: h.activation(out=sg[:, gs, :], in_=pG[:, gs, 0:TT], func=AF.Silu), reads=[b("pG%d" % gs)], writes=[b("sg%d" % gs)])
                    st.op("dve", lambda h, gs=gs, fb=fb: h.tensor_tensor(out=aT[:, fb, :], in0=sg[:, gs, :], in1=pUu[:, gs, 0:TT], op=ALU.mult),
                          reads=[b("sg%d" % gs), b("pU%d" % gs)], writes=[b("aT")])
                for sub in range(2):
                    xb = b("xt%d" % (slot * 2 + sub))
                    for half in range(2):
                        pt, pb = ring_A.get()
                        for fb in range(NFB):
                            st.op("pe", lambda h, fb=fb, pt=pt, sub=sub, half=half: h.matmul(pt, lhsT=aT[:, fb, sub * 128:(sub + 1) * 128],
                                                                                            rhs=wd[:, fb, half * 512:(half + 1) * 512], start=(fb == 0), stop=(fb == NFB - 1)),
                                  reads=[b("aT"), b("wd")], writes=[pb])
                        st.op("dve", lambda h, pt=pt, sub=sub, half=half: h.tensor_tensor(out=xt[:, slot, sub, half * 512:(half + 1) * 512],
                                                                                         in0=xt[:, slot, sub, half * 512:(half + 1) * 512], in1=pt, op=ALU.add),
                              reads=[pb, xb], writes=[xb])
                for sub in range(2):
                    xb = b("xt%d" % (slot * 2 + sub))
                    col = rms_rstd(i, sub, 1)
                    st.op("dve", lambda h, sub=sub, col=col: h.scalar_tensor_tensor(out=xt[:, slot, sub, :], in0=xt[:, slot, sub, :], scalar=rs[:, col:col + 1],
                                                                                   in1=gfb[:], op0=ALU.mult, op1=ALU.mult),
                          reads=[xb, b("rs%d" % col), b("gfb")], writes=[xb])
                    st.dma(lambda h, sub=sub: h.dma_start(out=out[t0 + sub * 128:t0 + (sub + 1) * 128, :], in_=xt[:, slot, sub, :]), reads=[xb])

            loads(0)
            for i in range(NTL):
                tile(i)
            st.finish()
            st.emit()

    if 1 in stages:
        stage1()
    if 2 in stages:
        stage2()
    if 3 in stages:
        stage3()
    if 4 in stages:
        stage4()
    return nc


def _pcn(w, rows):
    n = w.shape[1]
    return np.ascontiguousarray(w.reshape(rows // 128, 128, n).transpose(1, 0, 2))


def _pc(g):
    return np.ascontiguousarray(g.reshape(-1, 128).T)


def layout_inputs(inp, L):
    f32 = np.float32
    w_in = np.asarray(inp["w_in"][0], f32)
    hq, hi, hff, hfb, hg, cq, ckv, kr = np.split(w_in, np.cumsum([512, 512, 512, 512, 512, 384, 256])[:], axis=1)
    krot = np.concatenate([kr[:, 32:64], kr[:, 0:32]], axis=1)
    w1 = np.concatenate([hq, hff, hfb, hi, hg, cq, ckv, kr, kr, krot, krot], axis=1)
    assert w1.shape[1] == W1C
    wqb = np.asarray(inp["w_q_b"][0], f32).reshape(384, 4, 192)
    nope = wqb[:, :, 0:128].reshape(384, 512)
    rp = wqb[:, :, 128:192]
    rope = rp.reshape(384, 256)
    rot = np.concatenate([rp[:, :, 32:64], rp[:, :, 0:32]], axis=2).reshape(384, 256)
    wq = np.concatenate([nope, rope, rot], axis=1)
    wkvb = np.asarray(inp["w_kv_b"][0], f32).reshape(256, 4, 256)
    wkv = np.concatenate([wkvb[:, :, 0:128].reshape(256, 512), wkvb[:, :, 128:256].reshape(256, 512)], axis=1)
    lbl = np.asarray(inp["lb_logits"], f32)
    lbl_l = np.ascontiguousarray(lbl.reshape(2, 2, 4, 128).transpose(3, 0, 1, 2))
    gout = np.concatenate([np.asarray(inp["hgrn_norm_g"][0], f32), np.asarray(inp["mla_norm_g"][0], f32)])
    inv = 1.0 / (10000.0 ** (np.arange(0, 64, 2, dtype=np.float32) / 64.0))
    ang = np.arange(L, dtype=np.float32)[None, :] * inv[:, None].astype(np.float32)
    cos = np.cos(ang).astype(f32)
    sin = np.sin(ang).astype(f32)
    c_cos = np.ascontiguousarray(np.tile(cos, (4, 1)))
    c_sin = np.ascontiguousarray(np.tile(sin, (4, 1)))
    rmask = np.ones((128, 512), f32)
    rmask[:, 0::128] = 0.0
    jj = np.arange(128)[:, None]
    ii = np.arange(128)[None, :]
    d = {
        "w_in": _pcn(w1, 1024), "g1": _pc(np.asarray(inp["norm1_g"][0], f32)), "lbl": lbl_l,
        "w_qb": _pcn(wq, 384), "gqa": _pc(np.asarray(inp["q_a_norm_g"][0], f32)),
        "w_kvb": _pcn(wkv, 256), "gkva": _pc(np.asarray(inp["kv_a_norm_g"][0], f32)),
        "w_out": _pcn(np.asarray(inp["w_out"][0], f32), 1024), "gout": _pc(gout),
        "w_gate": _pcn(np.asarray(inp["w_gate"][0], f32), 1024), "w_up": _pcn(np.asarray(inp["w_up"][0], f32), 1024),
        "g2": _pc(np.asarray(inp["norm2_g"][0], f32)),
        "w_down": _pcn(np.asarray(inp["w_down"][0], f32), DFF),
        "gfin": np.asarray(inp["final_norm_g"], f32).reshape(1, D),
        "c_ident": np.eye(128).astype(ml_dtypes.bfloat16), "c_ones": np.ones((128, 128), ml_dtypes.bfloat16),
        "c_cos": c_cos, "c_sin": c_sin, "c_rmask": rmask,
        "c_maskf": (jj <= ii).astype(f32), "c_maskb": (jj >= ii).astype(f32),
    }
    return d


_NC_CACHE = {}


def kernel(**inputs):
    x = np.asarray(inputs["x"], np.float32)
    Bt, L, _ = x.shape
    n_seq = Bt // NCORES
    key = (n_seq, L)
    if key not in _NC_CACHE:
        _NC_CACHE[key] = build_nc(n_seq, L)
    nc = _NC_CACHE[key]
    shared = layout_inputs(inputs, L)
    in_maps = []
    for c in range(NCORES):
        m = dict(shared)
        m["x"] = np.ascontiguousarray(x[c * n_seq:(c + 1) * n_seq].reshape(n_seq * L, D))
        in_maps.append(m)
    res = run_bass_kernel_spmd(nc, in_maps, core_ids=list(range(NCORES)))
    out = np.stack([r["out"].reshape(n_seq, L, D) for r in res.results], axis=0)
    return out.reshape(Bt, L, D).astype(np.float32)
```

```python
import contextlib
import os
import numpy as np
import ml_dtypes
import concourse.bass as bass
import concourse.mybir as mybir
from concourse.bass_utils import run_bass_kernel_spmd

F32 = mybir.dt.float32
BF16 = mybir.dt.bfloat16
AF = mybir.ActivationFunctionType
ALU = mybir.AluOpType
AX = mybir.AxisListType

D = 1024
DFF = 2816
NFB = DFF // 128
EPS = 1e-6
NCORES = 8
W1C = 3456
SCALE = 192 ** -0.5


class Buf:
    __slots__ = ("name", "w", "r", "x")

    def __init__(self, name=""):
        self.name = name
        self.w = None
        self.r = {}
        self.x = len(name) > 1 and name[0] == "p" and (name[1].isupper() or name.startswith(("pmm", "pxt")))


class Stage:
    ENGS = ("pe", "act", "dve", "pool", "sp")

    def __init__(self, nc, name, n_dma_sems=16):
        self.nc = nc
        self.name = name
        self.ops = {e: [] for e in self.ENGS}
        self.cnt = {e: 0 for e in ("pe", "act", "dve", "pool")}
        self.waited = {e: {} for e in self.ENGS}
        self.n_dma = n_dma_sems
        self.dma_cnt = [0] * n_dma_sems
        self.dma_rr = 0
        self.sems = {}

    def _need(self, eng, ev, waits):
        if ev is None:
            return
        key, val = ev
        if key == "pe" and eng == "pe":
            return
        if self.waited[eng].get(key, 0) >= val:
            return
        self.waited[eng][key] = val
        waits.append((key, val))

    def _deps(self, eng, reads, writes):
        waits = []
        for b in reads:
            self._need(eng, b.w, waits)
            if b.x:
                for k, v in b.r.items():
                    if k != eng:
                        self._need(eng, (k, v), waits)
        for b in writes:
            self._need(eng, b.w, waits)
            for k, v in b.r.items():
                self._need(eng, (k, v), waits)
        return waits

    def _commit(self, ev, reads, writes):
        k, v = ev
        for b in reads:
            if b.r.get(k, 0) < v:
                b.r[k] = v
        for b in writes:
            b.w = ev
            b.r = {}

    def op(self, eng, fn, reads=(), writes=()):
        waits = self._deps(eng, reads, writes)
        self.cnt[eng] += 1
        ev = (eng, self.cnt[eng])
        self.ops[eng].append((waits, fn, (eng, 1)))
        self._commit(ev, reads, writes)
        return ev

    def dma(self, fn, reads=(), writes=(), queue="sp"):
        waits = self._deps(queue, reads, writes)
        k = self.dma_rr
        self.dma_rr = (self.dma_rr + 1) % self.n_dma
        key = "dma%d" % k
        if self.dma_cnt[k] > 0:
            self._need(queue, (key, self.dma_cnt[k]), waits)
        self.dma_cnt[k] += 16
        ev = (key, self.dma_cnt[k])
        self.ops[queue].append((waits, fn, (key, 16)))
        self._commit(ev, reads, writes)
        return ev

    def finish(self, eng="sp"):
        waits = []
        for k in range(self.n_dma):
            if self.dma_cnt[k] > 0:
                self._need(eng, ("dma%d" % k, self.dma_cnt[k]), waits)
        if waits:
            self.ops[eng].append((waits, None, None))

    def emit(self):
        nc = self.nc
        with contextlib.ExitStack() as st:
            for e in ("pe", "act", "dve", "pool"):
                self.sems[e] = st.enter_context(nc.semaphore("%s_%s" % (self.name, e)))
            for k in range(self.n_dma):
                self.sems["dma%d" % k] = st.enter_context(nc.semaphore("%s_d%d" % (self.name, k)))
            block = st.enter_context(nc.Block())
            sems = self.sems

            def run(h, lst):
                for waits, fn, inc in lst:
                    for key, val in waits:
                        h.wait_ge(sems[key], val)
                    if fn is not None:
                        fn(h).then_inc(sems[inc[0]], inc[1])

            if self.ops["sp"]:
                @block.sync
                def _(h):
                    run(h, self.ops["sp"])
            if self.ops["pe"]:
                @block.tensor
                def _(h):
                    run(h, self.ops["pe"])
            if self.ops["act"]:
                @block.scalar
                def _(h):
                    run(h, self.ops["act"])
            if self.ops["dve"]:
                @block.vector
                def _(h):
                    run(h, self.ops["dve"])
            if self.ops["pool"]:
                @block.gpsimd
                def _(h):
                    run(h, self.ops["pool"])


class Ring:
    def __init__(self, tens, n, name):
        self.t = tens
        self.n = n
        self.i = 0
        self.bufs = [Buf("%s%d" % (name, k)) for k in range(n)]

    def get(self):
        k = self.i
        self.i = (self.i + 1) % self.n
        return self.t[:, k], self.bufs[k]


def bc(ap, shape):
    return ap.to_broadcast(shape)


def build_nc(n_seq, L, debug=False, stages=(1, 2, 3, 4)):
    T = n_seq * L
    NT = T // 128
    NS = T // 512
    NCH = T // 128
    nc = bass.Bass("TRN2", target_bir_lowering=False)

    def din(name, shape, dt=F32):
        return nc.dram_tensor(name, list(shape), dt, kind="ExternalInput").ap()

    skind = "ExternalOutput" if debug else "Internal"

    def dscr(name, shape, dt):
        return nc.dram_tensor(name, list(shape), dt, kind=skind).ap()

    x = din("x", [T, D])
    out = nc.dram_tensor("out", [T, D], F32, kind="ExternalOutput").ap()
    w_in = din("w_in", [128, 8, W1C])
    g1 = din("g1", [128, 8])
    lbl = din("lbl", [128, 2, 2, 4])
    w_qb = din("w_qb", [128, 3, 1024])
    gqa = din("gqa", [128, 3])
    w_kvb = din("w_kvb", [128, 2, 1024])
    gkva = din("gkva", [128, 2])
    w_out = din("w_out", [128, 8, D])
    gout = din("gout", [128, 8])
    w_gate = din("w_gate", [128, 8, DFF])
    w_up = din("w_up", [128, 8, DFF])
    g2 = din("g2", [128, 8])
    w_down = din("w_down", [128, NFB, D])
    gfin = din("gfin", [1, D])
    c_ident = din("c_ident", [128, 128], BF16)
    c_ones = din("c_ones", [128, 128], BF16)
    c_cos = din("c_cos", [128, L])
    c_sin = din("c_sin", [128, L])
    c_rmask = din("c_rmask", [128, 512])
    c_maskf = din("c_maskf", [128, 128])
    c_maskb = din("c_maskb", [128, 128])

    qdT_s = dscr("qdT_s", [2, 4, 128, T], BF16)
    kiT_s = dscr("kiT_s", [2, 4, 128, T], BF16)
    scal_s = dscr("scal_s", [128, 2, 3, 4, NCH], F32)
    v_s = dscr("v_s", [T, 512], BF16)
    g_s = dscr("g_s", [T, 512], F32)
    qnT_s = dscr("qnT_s", [4, 128, T], BF16)
    qrT_s = dscr("qrT_s", [2, 128, T], BF16)
    knT_s = dscr("knT_s", [4, 128, T], BF16)
    krT_s = dscr("krT_s", [128, T], BF16)
    vm_s = dscr("vm_s", [T, 512], BF16)
    yT_s = dscr("yT_s", [8, 128, T], BF16)
    of_s = dscr("of_s", [T, 512], F32)
    ob_s = dscr("ob_s", [T, 512], F32)

    def stage1():
        with contextlib.ExitStack() as es:
            def sb(name, shape, dt=F32):
                return es.enter_context(nc.sbuf_tensor("s1_" + name, list(shape), dt))

            def ps(name, shape, dt=F32):
                return es.enter_context(nc.psum_tensor("s1_" + name, list(shape), dt))

            st = Stage(nc, "s1")
            w1 = sb("w1", [128, 8, W1C], BF16)
            wq = sb("wq", [128, 3, 1024], BF16)
            wkv = sb("wkv", [128, 2, 1024], BF16)
            g1t = sb("g1t", [128, 8]); gqat = sb("gqat", [128, 3]); gkvat = sb("gkvat", [128, 2])
            lblt = sb("lblt", [128, 2, 2, 4])
            lbt = sb("lbt", [128, 2, 4]); omlt = sb("omlt", [128, 2, 4])
            fa = sb("fa", [128, 2, 4]); fb_ = sb("fb", [128, 2, 4]); nfb = sb("nfb", [128, 2, 4]); lnfb = sb("lnfb", [128, 2, 4])
            ident = sb("ident", [128, 128], BF16)
            rmask = sb("rmask", [128, 512])
            xt = sb("xt", [128, 4, D]); xjunk = sb("xjunk", [128, 2, D], BF16)
            xn = sb("xn", [128, 2, D], BF16)
            hT = sb("hT", [128, 1, 8, 512], BF16)
            ssx = sb("ssx", [128, 2, 4]); lnx = sb("lnx", [128, 2, 4]); rsx = sb("rsx", [128, 2, 4])
            cst = sb("cst", [128, 2, 512]); snt = sb("snt", [128, 2, 512])
            qT = sb("qT", [128, 4, 512])
            th = sb("th", [128, 8, 512])
            tmpf = sb("tmpf", [128, 12, 512])
            stg = tmpf[:, 0:4, :].rearrange("p (a k) f -> p a (k f)", a=2)
            tmpb = sb("tmpb", [128, 8, 512], BF16)
            gt = sb("gt", [128, 4, 512])
            cqf = sb("cqf", [128, 4, 640]); cqn = sb("cqn", [128, 4, 640], BF16)
            ssq = sb("ssq", [128, 8]); lnq = sb("lnq", [128, 8]); rsq = sb("rsq", [128, 8])
            cqT = sb("cqT", [128, 3, 512], BF16); ckvT = sb("ckvT", [128, 2, 512], BF16)
            scal = sb("scal", [128, 2, 3, 4, NCH])
            pxt = ps("pxt", [128, 2, 8, 128], BF16)
            pmm = ps("pmm", [128, 6, 512])

            B = {}
            def b(name):
                if name not in B:
                    B[name] = Buf(name)
                return B[name]

            ring_f = Ring(tmpf, 12, "tmpf")
            ring_b = Ring(tmpb, 8, "tmpb")
            ring_p = Ring(pmm, 6, "pmm")
            ring_g = Ring(gt, 2, "gt")
            pxb = [Buf("pxt0"), Buf("pxt1")]
            hTb = [Buf("hT0"), Buf("hT1")]
            xtb = [Buf("xt%d" % i) for i in range(4)]
            xnb = [Buf("xn0"), Buf("xn1")]

            for (dst, src, nm) in ((g1t, g1, "g1t"), (gqat, gqa, "gqat"), (gkvat, gkva, "gkvat"),
                                   (lblt, lbl, "lblt"), (ident, c_ident, "ident"), (rmask, c_rmask, "rmask")):
                st.dma(lambda h, dst=dst, src=src: h.dma_start(out=dst[:], in_=src), writes=[b(nm)])
            st.op("dve", lambda h: h.tensor_tensor(out=lbt[:], in0=lblt[:, :, 1, :], in1=lblt[:, :, 0, :], op=ALU.subtract),
                  reads=[b("lblt")], writes=[b("lbt")])
            st.op("act", lambda h: h.activation(out=lbt[:], in_=lbt[:], func=AF.Exp), reads=[b("lbt")], writes=[b("lbt")])
            st.op("dve", lambda h: h.tensor_scalar(out=lbt[:], in0=lbt[:], scalar1=1.0, scalar2=None, op0=ALU.add),
                  reads=[b("lbt")], writes=[b("lbt")])
            st.op("dve", lambda h: h.reciprocal(out=lbt[:], in_=lbt[:]), reads=[b("lbt")], writes=[b("lbt")])
            st.op("dve", lambda h: h.tensor_scalar(out=omlt[:], in0=lbt[:], scalar1=-1.0, scalar2=1.0, op0=ALU.mult, op1=ALU.add),
                  reads=[b("lbt")], writes=[b("omlt")])
            st.op("dve", lambda h: h.tensor_scalar(out=fb_[:], in0=omlt[:], scalar1=0.5, scalar2=None, op0=ALU.mult),
                  reads=[b("omlt")], writes=[b("fb")])
            st.op("dve", lambda h: h.tensor_tensor(out=fa[:], in0=lbt[:], in1=fb_[:], op=ALU.add),
                  reads=[b("lbt"), b("fb")], writes=[b("fa")])
            st.op("dve", lambda h: h.tensor_scalar(out=nfb[:], in0=fb_[:], scalar1=-1.0, scalar2=None, op0=ALU.mult),
                  reads=[b("fb")], writes=[b("nfb")])
            st.op("act", lambda h: h.activation(out=lnfb[:], in_=fb_[:], func=AF.Ln), reads=[b("fb")], writes=[b("lnfb")])

            pcnt = [0]

            def prep(dst3, src3, nchunk, ncols, gain, eng_cycle, name):
                for c in range(nchunk):
                    for c0 in range(0, ncols, 1024):
                        c1 = min(ncols, c0 + 1024)
                        slot = pcnt[0] % 2
                        eng = eng_cycle[pcnt[0] % len(eng_cycle)]
                        pcnt[0] += 1
                        sbufs = [ring_f.bufs[2 * slot], ring_f.bufs[2 * slot + 1]]
                        st.dma(lambda h, c=c, slot=slot, c0=c0, c1=c1: h.dma_start(out=stg[:, slot, 0:c1 - c0], in_=src3[:, c, c0:c1]),
                               writes=sbufs)
                        st.op(eng, lambda h, c=c, slot=slot, c0=c0, c1=c1: h.tensor_scalar(
                            out=dst3[:, c, c0:c1], in0=stg[:, slot, 0:c1 - c0], scalar1=gain[:, c:c + 1], scalar2=0.0, op0=ALU.mult, op1=ALU.add),
                            reads=sbufs + [b(name + "_g")], writes=[b(name)])
            B["w1_g"] = b("g1t"); B["wq_g"] = b("gqat"); B["wkv_g"] = b("gkvat")
            prep(w1, w_in, 8, W1C, g1t, ["dve", "pool"], "w1")
            prep(wq, w_qb, 3, 1024, gqat, ["dve", "pool"], "wq")
            prep(wkv, w_kvb, 2, 1024, gkvat, ["dve", "pool"], "wkv")
            v1 = w1[:, :, 3328:3456].rearrange("p c (g r) -> p c g r", r=64)[:, :, :, 0:32]
            st.op("dve", lambda h: h.tensor_scalar(out=v1, in0=v1, scalar1=-1.0, scalar2=None, op0=ALU.mult),
                  reads=[b("w1")], writes=[b("w1")])
            v2 = wq[:, :, 768:1024].rearrange("p c (g r) -> p c g r", r=64)[:, :, :, 0:32]
            st.op("dve", lambda h: h.tensor_scalar(out=v2, in0=v2, scalar1=-1.0, scalar2=None, op0=ALU.mult),
                  reads=[b("wq")], writes=[b("wq")])

            w1b = b("w1")

            def mm_group(lhs_fn, rhs_fn, nk, reads, nfree=512):
                pt, pb = ring_p.get()
                pt = pt[:, 0:nfree]
                for c in range(nk):
                    st.op("pe", lambda h, c=c, pt=pt: h.matmul(pt, lhsT=lhs_fn(c), rhs=rhs_fn(c), start=(c == 0), stop=(c == nk - 1)),
                          reads=reads, writes=[pb])
                return pt, pb

            def xload(s):
                for j in range(4):
                    t0 = s * 512 + j * 128
                    st.dma(lambda h, j=j, t0=t0: h.dma_start(out=xt[:, j, :], in_=x[t0:t0 + 128, :]), writes=[xtb[j]])
                    st.op("act", lambda h, j=j: h.activation(out=xjunk[:, j % 2, :], in_=xt[:, j, :], func=AF.Square, accum_out=ssx[:, 0, j:j + 1]),
                          reads=[xtb[j]], writes=[b("ssx%d" % j), b("xjunk%d" % (j % 2))])

            def xnorm(s):
                st.op("act", lambda h: h.activation(out=lnx[:, 0, :], in_=ssx[:, 0, :], func=AF.Ln, scale=1.0 / D, bias=EPS),
                      reads=[b("ssx%d" % j) for j in range(4)], writes=[b("lnx")])
                st.op("act", lambda h: h.activation(out=rsx[:, 0, :], in_=lnx[:, 0, :], func=AF.Exp, scale=-0.5), reads=[b("lnx")], writes=[b("rsx")])
                for j in range(4):
                    n = j % 2
                    st.op("dve", lambda h, j=j, n=n: h.tensor_scalar(out=xn[:, n, :], in0=xt[:, j, :], scalar1=rsx[:, 0, j:j + 1], scalar2=None, op0=ALU.mult),
                          reads=[xtb[j], b("rsx")], writes=[xnb[n]])
                    for c in range(8):
                        st.op("pe", lambda h, n=n, c=c: h.transpose(out=pxt[:, n, c, :], in_=xn[:, n, c * 128:(c + 1) * 128], identity=ident[:]),
                              reads=[xnb[n], b("ident")], writes=[pxb[n]])
                    st.op("dve", lambda h, n=n, j=j: h.tensor_copy(out=hT[:, 0, :, j * 128:(j + 1) * 128], in_=pxt[:, n, :, :]),
                          reads=[pxb[n]], writes=[hTb[0]])

            def load_tables(s):
                slot = s % 2
                pos0 = (s * 512) % L
                st.dma(lambda h: h.dma_start(out=cst[:, slot, :], in_=c_cos[:, pos0:pos0 + 512]), writes=[b("cst%d" % slot)])
                st.dma(lambda h: h.dma_start(out=snt[:, slot, :], in_=c_sin[:, pos0:pos0 + 512]), writes=[b("snt%d" % slot)])

            def rope(s, pt1, pb1, pt2, pb2, dst):
                slot = s % 2
                f1, fb1 = ring_f.get()
                f2, fb2 = ring_f.get()
                ot, ob = ring_b.get()
                st.op("dve", lambda h: h.tensor_tensor(out=f1, in0=pt1, in1=cst[:, slot, :], op=ALU.mult), reads=[pb1, b("cst%d" % slot)], writes=[fb1])
                st.op("dve", lambda h: h.tensor_tensor(out=f2, in0=pt2, in1=snt[:, slot, :], op=ALU.mult), reads=[pb2, b("snt%d" % slot)], writes=[fb2])
                st.op("pool", lambda h: h.tensor_tensor(out=ot, in0=f1, in1=f2, op=ALU.add), reads=[fb1, fb2], writes=[ob])
                st.dma(lambda h: h.dma_start(out=dst, in_=ot), reads=[ob])

            def tm_piece(s, p):
                t0 = s * 512
                hs = hTb[0]
                if p == 8:
                    pt1, pb1 = mm_group(lambda c: w1[:, c, 3200:3328], lambda c: hT[:, 0, c, :], 8, [hs, w1b])
                    pt2, pb2 = mm_group(lambda c: w1[:, c, 3328:3456], lambda c: hT[:, 0, c, :], 8, [hs, w1b])
                    rope(s, pt1, pb1, pt2, pb2, krT_s[:, t0:t0 + 512])
                    return
                j = p // 2
                tt = t0 + j * 128
                lhs = lambda c: hT[:, 0, c, j * 128:(j + 1) * 128]
                if p % 2 == 0:
                    pt, pb = mm_group(lhs, lambda c: w1[:, c, 1536:2048], 8, [hs, w1b])
                    ot, ob = ring_b.get()
                    st.op("dve", lambda h: h.tensor_copy(out=ot, in_=pt), reads=[pb], writes=[ob])
                    st.dma(lambda h: h.dma_start(out=v_s[tt:tt + 128, :], in_=ot), reads=[ob])
                    pt2, pb2 = mm_group(lhs, lambda c: w1[:, c, 2048:2560], 8, [hs, w1b])
                    st.op("act", lambda h: h.activation(out=gt[:, j, :], in_=pt2, func=AF.Copy), reads=[pb2], writes=[b("gt%d" % j)])
                else:
                    pt, pb = mm_group(lhs, lambda c: w1[:, c, 2560:2944], 8, [hs, w1b], nfree=384)
                    st.op("act", lambda h: h.activation(out=xjunk[:, 0, 0:384], in_=pt[:, 0:384], func=AF.Square, accum_out=ssq[:, j:j + 1]),
                          reads=[pb], writes=[b("ssq%d" % j), b("xjunk0")])
                    st.op("dve", lambda h: h.tensor_copy(out=cqf[:, j, 0:384], in_=pt[:, 0:384]), reads=[pb], writes=[b("cqf%d" % j)])
                    pt2, pb2 = mm_group(lhs, lambda c: w1[:, c, 2944:3200], 8, [hs, w1b], nfree=256)
                    st.op("act", lambda h: h.activation(out=xjunk[:, 1, 0:256], in_=pt2[:, 0:256], func=AF.Square, accum_out=ssq[:, 4 + j:5 + j]),
                          reads=[pb2], writes=[b("ssq%d" % (4 + j)), b("xjunk1")])
                    st.op("dve", lambda h: h.tensor_copy(out=cqf[:, j, 384:640], in_=pt2[:, 0:256]), reads=[pb2], writes=[b("cqf%d" % j)])

            def fm_hf(s, db):
                col = 512 + db * 128
                pt, pb = mm_group(lambda c: w1[:, c, col:col + 128], lambda c: hT[:, 0, c, :], 8, [hTb[0], w1b])
                st.op("act", lambda h: h.activation(out=th[:, db, :], in_=pt, func=AF.Copy), reads=[pb], writes=[b("th%d" % db)])

            def fm_hq(s, blk):
                pt, pb = mm_group(lambda c: w1[:, c, blk * 128:(blk + 1) * 128], lambda c: hT[:, 0, c, :], 8, [hTb[0], w1b])
                st.op("dve", lambda h: h.tensor_copy(out=qT[:, blk, :], in_=pt), reads=[pb], writes=[b("qT%d" % blk)])

            def act18(s):
                t0 = s * 512
                for blk in range(4):
                    st.op("act", lambda h, blk=blk: h.activation(out=qT[:, blk, :], in_=qT[:, blk, :], func=AF.Silu),
                          reads=[b("qT%d" % blk)], writes=[b("qT%d" % blk)])
                for db in range(8):
                    st.op("act", lambda h, db=db: h.activation(out=th[:, db, :], in_=th[:, db, :], func=AF.Tanh, scale=-0.5),
                          reads=[b("th%d" % db)], writes=[b("th%d" % db)])
                for j in range(4):
                    tt = t0 + j * 128
                    st.op("act", lambda h, j=j: h.activation(out=gt[:, j, :], in_=gt[:, j, :], func=AF.Silu), reads=[b("gt%d" % j)], writes=[b("gt%d" % j)])
                    st.dma(lambda h, j=j, tt=tt: h.dma_start(out=g_s[tt:tt + 128, :], in_=gt[:, j, :]), reads=[b("gt%d" % j)])

            def gate_chain(s, db):
                d, blk = db // 4, db % 4
                t0 = s * 512
                c0 = s * 4
                thb = b("th%d" % db)
                lf, lfb = ring_f.get()
                st.op("pool", lambda h: h.tensor_scalar(out=lf, in0=th[:, db, :], scalar1=nfb[:, d, blk:blk + 1], scalar2=fa[:, d, blk:blk + 1],
                                                        op0=ALU.mult, op1=ALU.add),
                      reads=[thb, b("nfb"), b("fa")], writes=[lfb])
                st.op("act", lambda h: h.activation(out=lf, in_=lf, func=AF.Ln), reads=[lfb], writes=[lfb])
                yield
                cumt, cumb = ring_f.get()
                cct, ccb = ring_f.get()
                st.op("dve", lambda h: h.tensor_tensor_scan(out=cumt, data0=rmask[:], data1=lf, initial=0.0, op0=ALU.mult, op1=ALU.add),
                      reads=[lfb, b("rmask")], writes=[cumb])
                cumv = cumt.rearrange("p (c t) -> p c t", t=128)
                ccv = cct.rearrange("p (c t) -> p c t", t=128)
                st.op("dve", lambda h: h.tensor_tensor(out=ccv, in0=cumv, in1=bc(cumv[:, :, 63:64], [128, 4, 128]), op=ALU.subtract),
                      reads=[cumb], writes=[ccb])
                Xi, Yi = (1, 2) if d == 0 else (2, 1)
                st.op("act", lambda h: h.activation(out=scal[:, d, 0, blk, c0:c0 + 4], in_=cumv[:, :, 127], func=AF.Exp), reads=[cumb], writes=[b("scal")])
                st.op("act", lambda h: h.activation(out=scal[:, d, Xi, blk, c0:c0 + 4], in_=ccv[:, :, 127], func=AF.Exp), reads=[ccb], writes=[b("scal")])
                st.op("act", lambda h: h.activation(out=scal[:, d, Yi, blk, c0:c0 + 4], in_=cumv[:, :, 63], func=AF.Exp), reads=[cumb], writes=[b("scal")])
                if d == 0:
                    srct, srcb = cct, ccb
                else:
                    srct, srcb = ring_f.get()
                    st.op("pool", lambda h: h.tensor_tensor(out=srct, in0=lf, in1=cct, op=ALU.subtract), reads=[lfb, ccb], writes=[srcb])
                yield
                ea, eab = ring_f.get()
                eb, ebb = ring_f.get()
                st.op("act", lambda h: h.activation(out=ea, in_=srct, func=AF.Exp), reads=[srcb], writes=[eab])
                st.op("act", lambda h: h.activation(out=eb, in_=srct, func=AF.Exp, scale=-1.0, bias=lnfb[:, d, blk:blk + 1]),
                      reads=[srcb, b("lnfb")], writes=[ebb])
                o1, o1b = ring_b.get()
                o2, o2b = ring_b.get()
                st.op("dve", lambda h: h.tensor_tensor(out=o1, in0=qT[:, blk, :], in1=ea, op=ALU.mult), reads=[b("qT%d" % blk), eab], writes=[o1b])
                st.op("dve", lambda h: h.scalar_tensor_tensor(out=o2, in0=th[:, db, :], scalar=1.0, in1=eb, op0=ALU.add, op1=ALU.mult),
                      reads=[thb, ebb], writes=[o2b])
                st.dma(lambda h: h.dma_start(out=qdT_s[d, blk, :, t0:t0 + 512], in_=o1), reads=[o1b])
                st.dma(lambda h: h.dma_start(out=kiT_s[d, blk, :, t0:t0 + 512], in_=o2), reads=[o2b])
                yield

            def cq_rstd():
                st.op("act", lambda h: h.activation(out=lnq[:, 0:4], in_=ssq[:, 0:4], func=AF.Ln, scale=1.0 / 384, bias=EPS),
                      reads=[b("ssq%d" % i) for i in range(8)], writes=[b("lnq")])
                st.op("act", lambda h: h.activation(out=lnq[:, 4:8], in_=ssq[:, 4:8], func=AF.Ln, scale=1.0 / 256, bias=EPS),
                      reads=[b("ssq%d" % i) for i in range(8)], writes=[b("lnq")])
                st.op("act", lambda h: h.activation(out=rsq[:], in_=lnq[:], func=AF.Exp, scale=-0.5), reads=[b("lnq")], writes=[b("rsq")])

            def cq_transpose(s):
                for j in range(4):
                    st.op("dve", lambda h, j=j: h.tensor_scalar(out=cqn[:, j, 0:384], in0=cqf[:, j, 0:384], scalar1=rsq[:, j:j + 1], scalar2=None, op0=ALU.mult),
                          reads=[b("cqf%d" % j), b("rsq")], writes=[b("cqn%d" % j)])
                    st.op("dve", lambda h, j=j: h.tensor_scalar(out=cqn[:, j, 384:640], in0=cqf[:, j, 384:640], scalar1=rsq[:, 4 + j:5 + j], scalar2=None, op0=ALU.mult),
                          reads=[b("cqf%d" % j), b("rsq")], writes=[b("cqn%d" % j)])
                for j in range(4):
                    n = j % 2
                    for c in range(5):
                        st.op("pe", lambda h, n=n, c=c, j=j: h.transpose(out=pxt[:, n, c, :], in_=cqn[:, j, c * 128:(c + 1) * 128], identity=ident[:]),
                              reads=[b("cqn%d" % j), b("ident")], writes=[pxb[n]])
                    st.op("dve", lambda h, n=n, j=j: h.tensor_copy(out=cqT[:, :, j * 128:(j + 1) * 128], in_=pxt[:, n, 0:3, :]), reads=[pxb[n]], writes=[b("cqT")])
                    st.op("dve", lambda h, n=n, j=j: h.tensor_copy(out=ckvT[:, :, j * 128:(j + 1) * 128], in_=pxt[:, n, 3:5, :]), reads=[pxb[n]], writes=[b("ckvT")])

            def second_proj(s):
                t0 = s * 512
                out = []

                def qn(hh):
                    pt, pb = mm_group(lambda c: wq[:, c, hh * 128:(hh + 1) * 128], lambda c: cqT[:, c, :], 3, [b("cqT"), b("wq")])
                    ot, ob = ring_b.get()
                    st.op("dve", lambda h: h.tensor_copy(out=ot, in_=pt), reads=[pb], writes=[ob])
                    st.dma(lambda h: h.dma_start(out=qnT_s[hh, :, t0:t0 + 512], in_=ot), reads=[ob])

                def qr_(pr):
                    pt1, pb1 = mm_group(lambda c: wq[:, c, 512 + pr * 128:640 + pr * 128], lambda c: cqT[:, c, :], 3, [b("cqT"), b("wq")])
                    pt2, pb2 = mm_group(lambda c: wq[:, c, 768 + pr * 128:896 + pr * 128], lambda c: cqT[:, c, :], 3, [b("cqT"), b("wq")])
                    rope(s, pt1, pb1, pt2, pb2, qrT_s[pr, :, t0:t0 + 512])

                def kn(hh):
                    pt, pb = mm_group(lambda c: wkv[:, c, hh * 128:(hh + 1) * 128], lambda c: ckvT[:, c, :], 2, [b("ckvT"), b("wkv")])
                    ot, ob = ring_b.get()
                    st.op("dve", lambda h: h.tensor_copy(out=ot, in_=pt), reads=[pb], writes=[ob])
                    st.dma(lambda h: h.dma_start(out=knT_s[hh, :, t0:t0 + 512], in_=ot), reads=[ob])

                def vv(j):
                    tt = t0 + j * 128
                    pt, pb = mm_group(lambda c: ckvT[:, c, j * 128:(j + 1) * 128], lambda c: wkv[:, c, 512:1024], 2, [b("ckvT"), b("wkv")])
                    ot, ob = ring_b.get()
                    st.op("dve", lambda h: h.tensor_copy(out=ot, in_=pt), reads=[pb], writes=[ob])
                    st.dma(lambda h: h.dma_start(out=vm_s[tt:tt + 128, :], in_=ot), reads=[ob])
                for hh in range(4):
                    out.append(lambda hh=hh: qn(hh))
                for pr in range(2):
                    out.append(lambda pr=pr: qr_(pr))
                for hh in range(4):
                    out.append(lambda hh=hh: kn(hh))
                for j in range(4):
                    out.append(lambda j=j: vv(j))
                return out

            xload(0)
            load_tables(0)
            xnorm(0)
            if NS > 1:
                xload(1)
            for p in range(9):
                tm_piece(0, p)
            for blk in range(4):
                fm_hq(0, blk)
            for db in range(8):
                fm_hf(0, db)
            cq_rstd()
            for s in range(NS):
                nxt = s + 1 < NS
                if nxt:
                    load_tables(s + 1)
                    xnorm(s + 1)
                    if s + 2 < NS:
                        xload(s + 2)
                cq_transpose(s)
                act18(s)
                for f_ in second_proj(s):
                    f_()
                rest = []
                for db0 in range(0, 8, 2):
                    live = [gate_chain(s, db0), gate_chain(s, db0 + 1)]
                    while live:
                        for g_ in list(live):
                            try:
                                next(g_)
                            except StopIteration:
                                live.remove(g_)
                    for db in (db0, db0 + 1):
                        if nxt:
                            tm_piece(s + 1, db)
                            if db == 7:
                                tm_piece(s + 1, 8)
                            fm_hf(s + 1, db)
                            if db >= 4:
                                fm_hq(s + 1, db - 4)
                        if rest:
                            rest.pop(0)()
                while rest:
                    rest.pop(0)()
                if nxt:
                    cq_rstd()
            st.dma(lambda h: h.dma_start(out=scal_s, in_=scal[:]), reads=[b("scal")])
            st.finish()
            st.emit()

    def stage2():
        NK = L // 128
        NQ = L // 512
        DEN = os.environ.get("S2_DEN", "mix")
        ROPE128 = os.environ.get("S2_ROPE", "k128") == "k128"
        NPS = int(os.environ.get("S2_NPS", "4"))
        NP = int(os.environ.get("S2_NP", "6"))
        with contextlib.ExitStack() as es:
            def sb(name, shape, dt=F32):
                return es.enter_context(nc.sbuf_tensor("s2_" + name, list(shape), dt))

            def ps(name, shape, dt=F32):
                return es.enter_context(nc.psum_tensor("s2_" + name, list(shape), dt))

            st = Stage(nc, "s2")
            knT = sb("knT", [128, 4, L], BF16)
            vm = sb("vm", [128, NK, 512], BF16)
            krT = sb("krT", [128, L], BF16)
            qn = sb("qn", [128, 2, 4, 512], BF16)
            qr = sb("qr", [128, 2, 2, 512], BF16)
            ones = sb("ones", [128, 128], BF16)
            Pt = sb("Pt", [128, NP, 512], BF16)
            qrz = sb("qrz", [128, 2, 4, 512], BF16)
            hm = sb("hm", [128, 2])
            accP = sb("accP", [128, 2, 512])
            oT = sb("oT", [128, 2, 4, 512])
            rden = sb("rden", [128, 2, 512])
            sq = sb("sq", [128, 4, 512], BF16)
            lnv = sb("lnv", [128, 512]); rstd = sb("rstd", [128, 512])
            yb = sb("yb", [128, 4, 512], BF16)
            pS = ps("pS", [128, NPS, 512])
            pO = ps("pO", [128, 2, 512])
            pD = ps("pD", [128, 1, 512])
            pSS = ps("pSS", [128, 512])
            acc = sb("acc", [128, 2, 512])
            acc4 = sb("acc4", [128, 2, 4, 512])
            onesf = sb("onesf", [128, 128])
            B = {}

            def b(name):
                if name not in B:
                    B[name] = Buf(name)
                return B[name]
            ring_P = Ring(Pt, NP, "Pt")
            ring_S = Ring(pS, NPS, "pS")
            ring_y = Ring(yb, 4, "yb")
            st.dma(lambda h: h.dma_start(out=ones[:], in_=c_ones), writes=[b("ones")])
            st.op("pool", lambda h: h.memset(onesf[:], 1.0), writes=[b("onesf")])
            st.op("pool", lambda h: h.memset(hm[:], 0.0), writes=[b("hm")])
            st.op("pool", lambda h: h.memset(hm[0:64, 0:1], 1.0), writes=[b("hm")])
            st.op("pool", lambda h: h.memset(hm[64:128, 1:2], 1.0), writes=[b("hm")])

            items = []
            for seq in range(n_seq):
                for qb in range(NQ):
                    for hh in range(4):
                        for kc in range(NK):
                            items.append((seq, qb, hh, kc))

            def loads_seq(seq):
                s0 = seq * L
                for hh in range(4):
                    st.dma(lambda h, hh=hh: h.dma_start(out=knT[:, hh, :], in_=knT_s[hh, :, s0:s0 + L]), writes=[b("knT")])
                st.dma(lambda h: h.dma_start(out=krT[:], in_=krT_s[:, s0:s0 + L]), writes=[b("krT")])
                st.dma(lambda h: h.dma_start(out=vm[:], in_=vm_s[s0:s0 + L, :].rearrange("(k p) f -> p k f", p=128)), writes=[b("vm")])

            def loads_q(seq, qb):
                slot = (seq * NQ + qb) % 2
                tq = seq * L + qb * 512
                st.dma(lambda h: h.dma_start(out=qn[:, slot], in_=qnT_s[:, :, tq:tq + 512].rearrange("h p t -> p h t")), writes=[b("qn%d" % slot)])
                st.dma(lambda h: h.dma_start(out=qr[:, slot], in_=qrT_s[:, :, tq:tq + 512].rearrange("h p t -> p h t")), writes=[b("qr%d" % slot)])
                if ROPE128:
                    for hh in range(4):
                        st.op("pool", lambda h, hh=hh: h.tensor_scalar(out=qrz[:, slot, hh, :], in0=qr[:, slot, hh // 2, :], scalar1=hm[:, hh % 2:hh % 2 + 1],
                                                                        scalar2=0.0, op0=ALU.mult, op1=ALU.add),
                              reads=[b("qr%d" % slot), b("hm")], writes=[b("qrz%d" % slot)])

            def qk(item):
                seq, qb, hh, kc = item
                slot = (seq * NQ + qb) % 2
                hp, pr = hh % 2, hh // 2
                pt, pb = ring_S.get()
                st.op("pe", lambda h: h.matmul(pt, lhsT=knT[:, hh, kc * 128:(kc + 1) * 128], rhs=qn[:, slot, hh, :], start=True, stop=False),
                      reads=[b("knT"), b("qn%d" % slot)], writes=[pb])
                if ROPE128:
                    st.op("pe", lambda h: h.matmul(pt, lhsT=krT[:, kc * 128:(kc + 1) * 128], rhs=qrz[:, slot, hh, :], start=False, stop=True),
                          reads=[b("krT"), b("qrz%d" % slot)], writes=[pb])
                else:
                    st.op("pe", lambda h: h.matmul(pt, lhsT=krT[hp * 64:(hp + 1) * 64, kc * 128:(kc + 1) * 128],
                                                   rhs=qr[hp * 64:(hp + 1) * 64, slot, pr, :], start=False, stop=True),
                          reads=[b("krT"), b("qr%d" % slot)], writes=[pb])
                return pt, pb

            def finish_head(seq, qb, hh, oslot):
                qslot = (seq * NQ + qb) % 2
                if DEN == "mix":
                    a4 = acc4[:, oslot]
                    rd = [b("a4_%d_%d" % (oslot, j)) for j in range(4)]
                    st.op("dve", lambda h: h.tensor_tensor(out=a4[:, 0, :], in0=a4[:, 0, :], in1=a4[:, 1, :], op=ALU.add), reads=rd[0:2], writes=[rd[0]])
                    st.op("pool", lambda h: h.tensor_tensor(out=a4[:, 2, :], in0=a4[:, 2, :], in1=a4[:, 3, :], op=ALU.add), reads=rd[2:4], writes=[rd[2]])
                    st.op("pe", lambda h: h.matmul(pD[:, 0, :], lhsT=onesf[:], rhs=a4[:, 0, :], start=True, stop=False),
                          reads=[b("onesf"), rd[0]], writes=[b("pD0")])
                    st.op("pe", lambda h: h.matmul(pD[:, 0, :], lhsT=onesf[:], rhs=a4[:, 2, :], start=False, stop=True),
                          reads=[b("onesf"), rd[2]], writes=[b("pD0")])
                    st.op("act", lambda h: h.activation(out=rden[:, oslot, :], in_=pD[:, 0, :], func=AF.Ln), reads=[b("pD0")], writes=[b("rden%d" % oslot)])
                    st.op("act", lambda h: h.activation(out=rden[:, oslot, :], in_=rden[:, oslot, :], func=AF.Exp, scale=-1.0),
                          reads=[b("rden%d" % oslot)], writes=[b("rden%d" % oslot)])
                    st.op("dve", lambda h: h.tensor_tensor(out=oT[:, qslot, hh, :], in0=pO[:, oslot, :], in1=rden[:, oslot, :], op=ALU.mult),
                          reads=[b("pO%d" % oslot), b("rden%d" % oslot)], writes=[b("oT%d_%d" % (qslot, hh))])
                    return
                if DEN == "dve4":
                    a4 = acc4[:, oslot]
                    rd = [b("a4_%d_%d" % (oslot, j)) for j in range(4)]
                    st.op("dve", lambda h: h.tensor_tensor(out=a4[:, 0:2, :], in0=a4[:, 0:2, :], in1=a4[:, 2:4, :], op=ALU.add), reads=rd, writes=rd[0:2])
                    st.op("dve", lambda h: h.tensor_tensor(out=acc[:, oslot, :], in0=a4[:, 0, :], in1=a4[:, 1, :], op=ALU.add), reads=rd[0:2], writes=[b("acc%d" % oslot)])
                    st.op("pe", lambda h: h.matmul(pD[:, 0, :], lhsT=onesf[:], rhs=acc[:, oslot, :], start=True, stop=True),
                          reads=[b("onesf"), b("acc%d" % oslot)], writes=[b("pD0")])
                elif DEN == "dve":
                    st.op("pe", lambda h: h.matmul(pD[:, 0, :], lhsT=onesf[:], rhs=acc[:, oslot, :], start=True, stop=True),
                          reads=[b("onesf"), b("acc%d" % oslot)], writes=[b("pD0")])
                elif DEN == "split":
                    st.op("pe", lambda h: h.matmul(pD[:, 0, :], lhsT=onesf[:], rhs=acc[:, oslot, :], start=True, stop=False),
                          reads=[b("onesf"), b("acc%d" % oslot)], writes=[b("pD0")])
                    st.op("pe", lambda h: h.matmul(pD[:, 0, :], lhsT=onesf[:], rhs=accP[:, oslot, :], start=False, stop=True),
                          reads=[b("onesf"), b("accP%d" % oslot)], writes=[b("pD0")])
                st.op("dve", lambda h: h.reciprocal(out=rden[:, oslot, :], in_=pD[:, 0, :]), reads=[b("pD0")], writes=[b("rden%d" % oslot)])
                st.op("dve", lambda h: h.tensor_tensor(out=oT[:, qslot, hh, :], in0=pO[:, oslot, :], in1=rden[:, oslot, :], op=ALU.mult),
                      reads=[b("pO%d" % oslot), b("rden%d" % oslot)], writes=[b("oT%d_%d" % (qslot, hh))])

            def finish_q(seq, qb):
                qslot = (seq * NQ + qb) % 2
                tq = seq * L + qb * 512
                for hh in range(4):
                    st.op("act", lambda h, hh=hh: h.activation(out=sq[:, hh, :], in_=oT[:, qslot, hh, :], func=AF.Square),
                          reads=[b("oT%d_%d" % (qslot, hh))], writes=[b("sq%d" % hh)])
                for hh in range(4):
                    st.op("pe", lambda h, hh=hh: h.matmul(pSS[:], lhsT=ones[:], rhs=sq[:, hh, :], start=(hh == 0), stop=(hh == 3)),
                          reads=[b("ones"), b("sq%d" % hh)], writes=[b("pSS")])
                st.op("act", lambda h: h.activation(out=lnv[:], in_=pSS[:], func=AF.Ln, scale=1.0 / 512, bias=EPS), reads=[b("pSS")], writes=[b("lnv")])
                st.op("act", lambda h: h.activation(out=rstd[:], in_=lnv[:], func=AF.Exp, scale=-0.5), reads=[b("lnv")], writes=[b("rstd")])
                for hh in range(4):
                    yt, ybuf = ring_y.get()
                    st.op("dve", lambda h, hh=hh, yt=yt: h.tensor_tensor(out=yt, in0=oT[:, qslot, hh, :], in1=rstd[:], op=ALU.mult),
                          reads=[b("oT%d_%d" % (qslot, hh)), b("rstd")], writes=[ybuf])
                    st.dma(lambda h, hh=hh, yt=yt: h.dma_start(out=yT_s[4 + hh, :, tq:tq + 512], in_=yt), reads=[ybuf])

            import collections
            LA = int(os.environ.get("S2_LA", "2"))
            pending = collections.deque()
            nq = [0]

            def ensure(upto, seq):
                while nq[0] < min(upto, len(items)) and items[nq[0]][0] == seq:
                    pending.append(qk(items[nq[0]]))
                    nq[0] += 1

            ocnt = 0
            deferred = []
            DEFER = int(os.environ.get("S2_DEFER", "4"))
            for idx, item in enumerate(items):
                seq, qb, hh, kc = item
                if qb == 0 and hh == 0 and kc == 0:
                    loads_seq(seq)
                    loads_q(seq, 0)
                if hh == 0 and kc == 0 and qb + 1 < NQ:
                    loads_q(seq, qb + 1)
                ensure(idx + 1 + LA, seq)
                pt, pb = pending.popleft()
                oslot = ocnt % 2
                Pa, Pb = ring_P.get()
                st.op("act", lambda h, Pa=Pa, pt=pt: h.activation(out=Pa, in_=pt, func=AF.Exp, scale=SCALE), reads=[pb], writes=[Pb])
                st.op("pe", lambda h, Pa=Pa, oslot=oslot, hh=hh, kc=kc: h.matmul(pO[:, oslot, :], lhsT=vm[:, kc, hh * 128:(hh + 1) * 128], rhs=Pa,
                                                                           start=(kc == 0), stop=(kc == NK - 1)),
                      reads=[b("vm"), Pb], writes=[b("pO%d" % oslot)])
                if DEN == "pe":
                    st.op("pe", lambda h, Pa=Pa, kc=kc: h.matmul(pD[:, 0, :], lhsT=ones[:], rhs=Pa, start=(kc == 0), stop=(kc == NK - 1)),
                          reads=[b("ones"), Pb], writes=[b("pD0")])
                elif DEN in ("dve4", "mix"):
                    j4 = kc % 4
                    ab = b("a4_%d_%d" % (oslot, j4))
                    eng4 = "pool" if (DEN == "mix" and j4 == 3) else "dve"
                    if kc < 4:
                        st.op(eng4, lambda h, Pa=Pa, oslot=oslot, j4=j4: h.tensor_copy(out=acc4[:, oslot, j4, :], in_=Pa), reads=[Pb], writes=[ab])
                    else:
                        st.op(eng4, lambda h, Pa=Pa, oslot=oslot, j4=j4: h.tensor_tensor(out=acc4[:, oslot, j4, :], in0=acc4[:, oslot, j4, :], in1=Pa, op=ALU.add),
                              reads=[Pb, ab], writes=[ab])
                else:
                    on_pool = (DEN == "split" and kc % 4 == 3)
                    eng, at, an, first = ("pool", accP, "accP", kc == 3) if on_pool else ("dve", acc, "acc", kc == 0)
                    if first:
                        st.op(eng, lambda h, Pa=Pa, oslot=oslot, at=at: h.tensor_copy(out=at[:, oslot, :], in_=Pa), reads=[Pb], writes=[b("%s%d" % (an, oslot))])
                    else:
                        st.op(eng, lambda h, Pa=Pa, oslot=oslot, at=at: h.tensor_tensor(out=at[:, oslot, :], in0=at[:, oslot, :], in1=Pa, op=ALU.add),
                              reads=[Pb, b("%s%d" % (an, oslot))], writes=[b("%s%d" % (an, oslot))])
                if kc == NK - 1:
                    deferred.append((idx + DEFER, lambda seq=seq, qb=qb, hh=hh, oslot=oslot: finish_head(seq, qb, hh, oslot)))
                    ocnt += 1
                    if hh == 3:
                        deferred.append((idx + DEFER, lambda seq=seq, qb=qb: finish_q(seq, qb)))
                while deferred and deferred[0][0] <= idx:
                    deferred.pop(0)[1]()
            while deferred:
                deferred.pop(0)[1]()
            st.finish()
            st.emit()

    def stage3():
        NCs = L // 128
        NCHN = 2 * n_seq
        with contextlib.ExitStack() as es:
            def sb(name, shape, dt=F32):
                return es.enter_context(nc.sbuf_tensor("s3_" + name, list(shape), dt))

            def ps(name, shape, dt=F32):
                return es.enter_context(nc.psum_tensor("s3_" + name, list(shape), dt))

            st = Stage(nc, "s3")
            scal = sb("scal", [128, 2, 3, 4, NCH])
            masks = sb("masks", [128, 2, 128])
            ident = sb("ident", [128, 128], BF16)
            BD = sb("BD", [128, 4, 128])
            qd = sb("qd", [128, 2 * NCHN, 4, 256], BF16)
            ki = sb("ki", [128, 2 * NCHN, 4, 256], BF16)
            vt = sb("vt", [128, 2 * NCHN, 2, 512], BF16)
            kitok = sb("kitok", [128, NCHN, 512], BF16)
            scT = sb("scT", [128, NCHN, 8, 128], BF16)
            S_all = sb("S", [128, NCHN, 4, 128]); Sp_all = sb("Sp", [128, NCHN, 4, 128], BF16)
            t1_all = sb("t1", [128, NCHN, 4, 128]); t2_all = sb("t2", [128, NCHN, 4, 128])
            Bm_all = sb("Bm", [128, NCHN, 4, 128])
            oev = sb("oev", [128, 4, 512])
            cof = sb("cof", [128, 5, 512]); cob = sb("cob", [128, 5, 512]); cg_ = sb("cg", [128, 5, 512])
            osum_a = sb("osum", [128, 5, 512]); sqt_a = sb("sqt", [128, 5, 512]); yn_a = sb("yn", [128, 5, 512])
            ss8_a = sb("ss8", [128, 5, 8]); ln8_a = sb("ln8", [128, 5, 8]); rs8_a = sb("rs8", [128, 5, 8])
            ya = sb("ya", [128, 5, 512], BF16)
            yaT = sb("yaT", [128, 5, 4, 128], BF16)
            pT = ps("pT", [128, 1, 1024], BF16)
            pSc = ps("pSc", [128, 4, 4, 128])
            pOo = ps("pOo", [128, 2, 512])
            pU = ps("pU", [128, 1, 4, 128])
            pY = pU[:].bitcast(BF16).rearrange("p a b (c t) -> p (a b c) t", t=128)
            B = {}

            def b(name):
                if name not in B:
                    B[name] = Buf(name)
                return B[name]
            ring_oev = Ring(oev, 4, "oev")
            st.dma(lambda h: h.dma_start(out=scal[:], in_=scal_s), writes=[b("scal")])
            st.dma(lambda h: h.dma_start(out=masks[:, 0, :], in_=c_maskf), writes=[b("masks")])
            st.dma(lambda h: h.dma_start(out=masks[:, 1, :], in_=c_maskb), writes=[b("masks")])
            st.dma(lambda h: h.dma_start(out=ident[:], in_=c_ident), writes=[b("ident")])
            st.op("pool", lambda h: h.memset(BD[:], 0.0), writes=[b("BD")])
            st.op("pool", lambda h: h.memset(BD[0:64, :, 0:64], 1.0), writes=[b("BD")])
            st.op("pool", lambda h: h.memset(BD[64:128, :, 64:128], 1.0), writes=[b("BD")])
            rot = {"pT": 0, "pOo": 0, "pSc": 0}

            def chain(seq, d):
                ci = seq * 2 + d
                order = list(range(NCs)) if d == 0 else list(range(NCs - 1, -1, -1))
                S = S_all[:, ci]; Sp = Sp_all[:, ci]; t1 = t1_all[:, ci]; t2 = t2_all[:, ci]; Bm = Bm_all[:, ci]
                bS, bSp, bt1, bt2, bBm = (b("%s%d" % (nm, ci)) for nm in ("S", "Sp", "t1", "t2", "Bm"))
                st.op("pool", lambda h: h.memset(S, 0.0), writes=[bS])
                st.op("pool", lambda h: h.memset(Sp, 0.0), writes=[bSp])
                odst = of_s if d == 0 else ob_s
                def load_group(k):
                    gi = order[2 * k] // 2
                    slot = ci * 2 + k % 2
                    tg = seq * L + gi * 256
                    st.dma(lambda h: h.dma_start(out=qd[:, slot], in_=qdT_s[d, :, :, tg:tg + 256].rearrange("k p t -> p k t")), writes=[b("qd%d" % slot)])
                    st.dma(lambda h: h.dma_start(out=ki[:, slot], in_=kiT_s[d, :, :, tg:tg + 256].rearrange("k p t -> p k t")), writes=[b("ki%d" % slot)])
                    st.dma(lambda h: h.dma_start(out=vt[:, slot], in_=v_s[tg:tg + 256, :].rearrange("(j p) f -> p j f", p=128)), writes=[b("vt%d" % slot)])

                load_group(0)
                for pos, n in enumerate(order):
                    if pos % 2 == 0 and pos + 2 < len(order):
                        load_group(pos // 2 + 1)
                    slot = ci * 2 + (pos // 2) % 2
                    j = n % 2
                    cg = seq * NCs + n
                    t0 = cg * 128
                    qd_c = qd[:, slot, :, j * 128:(j + 1) * 128]
                    ki_c = ki[:, slot, :, j * 128:(j + 1) * 128]
                    v_c = vt[:, slot, j, :]
                    rq, rk, rv = b("qd%d" % slot), b("ki%d" % slot), b("vt%d" % slot)
                    has_next = pos + 1 < len(order)
                    tb = 0
                    sset = rot["pSc"] % 2
                    rot["pSc"] += 1
                    for bk in range(4):
                        st.op("pe", lambda h, bk=bk, tb=tb, ki_c=ki_c: h.transpose(out=pT[:, tb, bk * 128:(bk + 1) * 128], in_=ki_c[:, bk, :], identity=ident[:]),
                              reads=[rk, b("ident")], writes=[b("pT%d" % tb)])
                    st.op("act", lambda h, tb=tb: h.activation(out=kitok[:, ci, :], in_=pT[:, tb, 0:512], func=AF.Copy),
                          reads=[b("pT%d" % tb)], writes=[b("kitok%d" % ci)])
                    for hh in range(8):
                        bk, hp = hh // 2, hh % 2
                        st.op("pe", lambda h, bk=bk, hp=hp, ki_c=ki_c, qd_c=qd_c, sset=sset: h.matmul(pSc[:, sset * 2 + hp, bk, :], lhsT=ki_c[hp * 64:(hp + 1) * 64, bk, :],
                                                                                      rhs=qd_c[hp * 64:(hp + 1) * 64, bk, :], start=True, stop=True),
                              reads=[rk, rq], writes=[b("pSc%d" % (sset * 2 + hp))])
                    scv = scT[:, ci].rearrange("p (k two) t -> p k two t", two=2)
                    for half in range(2):
                        st.op("dve", lambda h, half=half, scv=scv, sset=sset: h.tensor_tensor(out=scv[:, :, half, :], in0=pSc[:, sset * 2 + half],
                                                                                             in1=bc(masks[:, d:d + 1, :], [128, 4, 128]), op=ALU.mult),
                              reads=[b("pSc%d" % (sset * 2 + half)), b("masks")], writes=[b("scT%d_%d" % (ci, half))])
                    if has_next:
                        st.op("pool", lambda h, cg=cg: h.tensor_tensor(out=Bm, in0=BD[:], in1=bc(scal[:, d, 1, :, cg:cg + 1], [128, 4, 128]), op=ALU.mult),
                              reads=[b("BD"), b("scal")], writes=[bBm])
                    yield
                    ob_ = rot["pOo"] % 2
                    rot["pOo"] += 1
                    for bk in range(4):
                        st.op("pe", lambda h, bk=bk, ob_=ob_, qd_c=qd_c: h.matmul(pOo[:, ob_, bk * 128:(bk + 1) * 128], lhsT=qd_c[:, bk, :], rhs=Sp[:, bk, :],
                                                                              start=True, stop=False),
                              reads=[rq, bSp], writes=[b("pOo%d" % ob_)])
                        for hp in range(2):
                            hh = 2 * bk + hp
                            st.op("pe", lambda h, hh=hh, hp=hp, ob_=ob_, v_c=v_c: h.matmul(pOo[:, ob_, hh * 64:(hh + 1) * 64], lhsT=scT[:, ci, hh, :],
                                                                                       rhs=v_c[:, hh * 64:(hh + 1) * 64], start=False, stop=(hp == 1)),
                                  reads=[b("scT%d_%d" % (ci, hh % 2)), rv], writes=[b("pOo%d" % ob_)])
                    ot, otb = ring_oev.get()
                    st.op("act", lambda h, ot=ot, ob_=ob_: h.activation(out=ot, in_=pOo[:, ob_, :], func=AF.Copy), reads=[b("pOo%d" % ob_)], writes=[otb])
                    st.dma(lambda h, ot=ot, t0=t0: h.dma_start(out=odst[t0:t0 + 128, :], in_=ot), reads=[otb], writes=[b("o%d_%d" % (d, cg))], queue="act")
                    yield
                    if has_next:
                        cgn = seq * NCs + order[pos + 1]
                        for bk in range(4):
                            st.op("pe", lambda h, bk=bk, v_c=v_c: h.matmul(pU[:, 0, bk, :], lhsT=kitok[:, ci, bk * 128:(bk + 1) * 128], rhs=v_c[:, bk * 128:(bk + 1) * 128],
                                                                          start=True, stop=True),
                                  reads=[b("kitok%d" % ci), rv], writes=[b("pU0")])
                        Abc = bc(scal[:, d, 0, :, cg:cg + 1], [128, 4, 128])
                        Cbc = bc(scal[:, d, 2, :, cgn:cgn + 1], [128, 4, 128])
                        st.op("pool", lambda h, Abc=Abc: h.tensor_tensor(out=t1, in0=S, in1=Abc, op=ALU.mult), reads=[bS, b("scal")], writes=[bt1])
                        st.op("dve", lambda h: h.tensor_tensor(out=t2, in0=pU[:, 0], in1=Bm, op=ALU.mult), reads=[b("pU0"), bBm], writes=[bt2])
                        st.op("dve", lambda h: h.tensor_tensor(out=S, in0=t1, in1=t2, op=ALU.add), reads=[bt1, bt2], writes=[bS])
                        st.op("dve", lambda h, Cbc=Cbc: h.tensor_tensor(out=Sp, in0=S, in1=Cbc, op=ALU.mult), reads=[bS, b("scal")], writes=[bSp])
                    yield

            gens = [chain(seq, d) for seq in range(n_seq) for d in range(2)]
            live = list(gens)
            while live:
                for g in list(live):
                    try:
                        next(g)
                    except StopIteration:
                        live.remove(g)

            NL = 5

            def combine_lane(cg):
                t0 = cg * 128
                k = cg % NL
                osum = osum_a[:, k]; sqt = sqt_a[:, k]; yn = yn_a[:, k]
                ss8 = ss8_a[:, k]; ln8 = ln8_a[:, k]; rs8 = rs8_a[:, k]
                bos, bsq, byn, bss, bln, brs = (b("%s%d" % (nm, k)) for nm in ("osum", "sqt", "yn", "ss8", "ln8", "rs8"))
                st.dma(lambda h: h.dma_start(out=cof[:, k], in_=of_s[t0:t0 + 128, :]), reads=[b("o0_%d" % cg)], writes=[b("cof%d" % k)])
                st.dma(lambda h: h.dma_start(out=cob[:, k], in_=ob_s[t0:t0 + 128, :]), reads=[b("o1_%d" % cg)], writes=[b("cob%d" % k)])
                st.dma(lambda h: h.dma_start(out=cg_[:, k], in_=g_s[t0:t0 + 128, :]), writes=[b("cg%d" % k)])
                yield
                st.op("dve", lambda h: h.tensor_tensor(out=osum, in0=cof[:, k], in1=cob[:, k], op=ALU.add),
                      reads=[b("cof%d" % k), b("cob%d" % k)], writes=[bos])
                st.op("act", lambda h: h.activation(out=sqt, in_=osum, func=AF.Square), reads=[bos], writes=[bsq])
                yield
                st.op("dve", lambda h: h.tensor_reduce(out=ss8, in_=sqt.rearrange("p (h e) -> p h e", e=64), axis=AX.X, op=ALU.add),
                      reads=[bsq], writes=[bss])
                st.op("act", lambda h: h.activation(out=ln8, in_=ss8, func=AF.Ln, scale=1.0 / 64, bias=EPS), reads=[bss], writes=[bln])
                st.op("act", lambda h: h.activation(out=rs8, in_=ln8, func=AF.Exp, scale=-0.5), reads=[bln], writes=[brs])
                yield
                st.op("dve", lambda h: h.tensor_tensor(out=yn.rearrange("p (h e) -> p h e", e=64), in0=osum.rearrange("p (h e) -> p h e", e=64),
                                                       in1=bc(rs8.unsqueeze(2), [128, 8, 64]), op=ALU.mult),
                      reads=[bos, brs], writes=[byn])
                st.op("pool", lambda h: h.tensor_tensor(out=ya[:, k, :], in0=yn, in1=cg_[:, k], op=ALU.mult),
                      reads=[byn, b("cg%d" % k)], writes=[b("ya%d" % k)])
                yield
                for bk in range(4):
                    st.op("pe", lambda h, bk=bk: h.transpose(out=pY[:, bk, :], in_=ya[:, k, bk * 128:(bk + 1) * 128], identity=ident[:]),
                          reads=[b("ya%d" % k), b("ident")], writes=[b("pU0")])
                st.op("act", lambda h: h.activation(out=yaT[:, k], in_=pY[:, 0:4, :], func=AF.Copy), reads=[b("pU0")], writes=[b("yaT%d" % k)])
                st.dma(lambda h: h.dma_start(out=yT_s[0:4, :, t0:t0 + 128].rearrange("k p t -> p k t"), in_=yaT[:, k]), reads=[b("yaT%d" % k)], queue="act")
                yield

            nxt_c = 0
            live = []
            while nxt_c < NCH or live:
                if nxt_c < NCH and len(live) < 4:
                    live.append(combine_lane(nxt_c))
                    nxt_c += 1
                for g in list(live):
                    try:
                        next(g)
                    except StopIteration:
                        live.remove(g)
            st.finish()
            st.emit()

    def stage4():
        TT = 256
        NTL = T // TT
        with contextlib.ExitStack() as es:
            def sb(name, shape, dt=F32):
                return es.enter_context(nc.sbuf_tensor("s4_" + name, list(shape), dt))

            def ps(name, shape, dt=F32):
                return es.enter_context(nc.psum_tensor("s4_" + name, list(shape), dt))

            st = Stage(nc, "s4")
            wo = sb("wo", [128, 8, D], BF16)
            wg = sb("wg", [128, 8, DFF], BF16)
            wu = sb("wu", [128, 8, DFF], BF16)
            wd = sb("wd", [128, NFB, D], BF16)
            goutt = sb("goutt", [128, 8]); g2t = sb("g2t", [128, 8])
            gfb = sb("gfb", [128, D])
            ident = sb("ident", [128, 128], BF16)
            xt = sb("xt", [128, 2, 2, D])
            yT = sb("yT", [128, 2, 8, TT], BF16)
            h2n = sb("h2n", [128, 2, D], BF16)
            h2T = sb("h2T", [128, 8, TT], BF16)
            aT = sb("aT", [128, NFB, TT], BF16)
            sg = sb("sg", [128, 2, TT])
            junk = sb("junk", [128, D], BF16)
            ss = sb("ss", [128, 4]); lnt = sb("lnt", [128, 4]); rs = sb("rs", [128, 4])
            pxt = ps("pxt", [128, 8, 128], BF16)
            pG = ps("pG", [128, 2, 512])
            pUu = ps("pUu", [128, 2, 512])
            pA = ps("pA", [128, 2, 512])
            B = {}

            def b(name):
                if name not in B:
                    B[name] = Buf(name)
                return B[name]
            ring_A = Ring(pA, 2, "pA")
            for (dst, src, nm) in ((goutt, gout, "goutt"), (g2t, g2, "g2t"), (ident, c_ident, "ident")):
                st.dma(lambda h, dst=dst, src=src: h.dma_start(out=dst[:], in_=src), writes=[b(nm)])
            if os.environ.get("S4_NOBC"):
                for pp in range(0, 128, 32):
                    pass
                st.op("pool", lambda h: h.memset(gfb[:], 1.0), writes=[b("gfb")])
            else:
                st.dma(lambda h: h.dma_start(out=gfb[:], in_=gfin.partition_broadcast(128)), writes=[b("gfb")])
            stg = xt[:].rearrange("p a b d -> p (a b) d")
            pcnt = [0]

            def prep(dst3, src3, nchunk, ncols, gain, name, gname):
                for c in range(nchunk):
                    for c0 in range(0, ncols, 1024):
                        c1 = min(ncols, c0 + 1024)
                        slot = pcnt[0] % 4
                        eng = ("dve", "act")[pcnt[0] % 2]
                        pcnt[0] += 1
                        sbuf = b("xt%d" % slot)
                        st.dma(lambda h, c=c, slot=slot, c0=c0, c1=c1: h.dma_start(out=stg[:, slot, 0:c1 - c0], in_=src3[:, c, c0:c1]), writes=[sbuf])
                        rd = [sbuf] + ([b(gname)] if gain is not None else [])
                        if eng == "act":
                            if gain is None:
                                st.op("act", lambda h, c=c, slot=slot, c0=c0, c1=c1: h.activation(out=dst3[:, c, c0:c1], in_=stg[:, slot, 0:c1 - c0], func=AF.Copy),
                                      reads=rd, writes=[b(name)])
                            else:
                                st.op("act", lambda h, c=c, slot=slot, c0=c0, c1=c1: h.activation(out=dst3[:, c, c0:c1], in_=stg[:, slot, 0:c1 - c0], func=AF.Copy,
                                                                                               scale=gain[:, c:c + 1]),
                                      reads=rd, writes=[b(name)])
                        else:
                            if gain is None:
                                st.op(eng, lambda h, c=c, slot=slot, c0=c0, c1=c1: h.tensor_copy(out=dst3[:, c, c0:c1], in_=stg[:, slot, 0:c1 - c0]),
                                      reads=rd, writes=[b(name)])
                            else:
                                st.op(eng, lambda h, c=c, slot=slot, c0=c0, c1=c1: h.tensor_scalar(out=dst3[:, c, c0:c1], in0=stg[:, slot, 0:c1 - c0],
                                                                                                scalar1=gain[:, c:c + 1], scalar2=None, op0=ALU.mult),
                                      reads=rd, writes=[b(name)])
            prep(wo, w_out, 8, D, goutt, "wo", "goutt")
            prep(wg, w_gate, 8, DFF, g2t, "wg", "g2t")
            prep(wu, w_up, 8, DFF, g2t, "wu", "g2t")
            prep(wd, w_down, NFB, D, None, "wd", None)

            def loads(i):
                slot = i % 2
                t0 = i * TT
                st.dma(lambda h: h.dma_start(out=xt[:, slot], in_=x[t0:t0 + TT, :].rearrange("(s p) d -> p s d", p=128)),
                       writes=[b("xt%d" % (slot * 2)), b("xt%d" % (slot * 2 + 1))])
                st.dma(lambda h: h.dma_start(out=yT[:, slot], in_=yT_s[:, :, t0:t0 + TT].rearrange("c p t -> p c t")), writes=[b("yT%d" % slot)])

            def rms_rstd(i, sub, which):
                slot = i % 2
                col = which * 2 + sub
                xb = b("xt%d" % (slot * 2 + sub))
                st.op("act", lambda h: h.activation(out=junk[:], in_=xt[:, slot, sub, :], func=AF.Square, accum_out=ss[:, col:col + 1]),
                      reads=[xb], writes=[b("junk"), b("ss%d" % col)])
                st.op("act", lambda h: h.activation(out=lnt[:, col:col + 1], in_=ss[:, col:col + 1], func=AF.Ln, scale=1.0 / D, bias=EPS),
                      reads=[b("ss%d" % col)], writes=[b("ln%d" % col)])
                st.op("act", lambda h: h.activation(out=rs[:, col:col + 1], in_=lnt[:, col:col + 1], func=AF.Exp, scale=-0.5),
                      reads=[b("ln%d" % col)], writes=[b("rs%d" % col)])
                return col

            def tile(i):
                slot = i % 2
                t0 = i * TT
                if i + 1 < NTL:
                    loads(i + 1)
                for sub in range(2):
                    xb = b("xt%d" % (slot * 2 + sub))
                    for half in range(2):
                        pt, pb = ring_A.get()
                        for c in range(8):
                            st.op("pe", lambda h, c=c, pt=pt, sub=sub, half=half: h.matmul(pt, lhsT=yT[:, slot, c, sub * 128:(sub + 1) * 128],
                                                                                          rhs=wo[:, c, half * 512:(half + 1) * 512], start=(c == 0), stop=(c == 7)),
                                  reads=[b("yT%d" % slot), b("wo")], writes=[pb])
                        st.op("dve", lambda h, pt=pt, sub=sub, half=half: h.tensor_tensor(out=xt[:, slot, sub, half * 512:(half + 1) * 512],
                                                                                         in0=xt[:, slot, sub, half * 512:(half + 1) * 512], in1=pt, op=ALU.add),
                              reads=[pb, xb], writes=[xb])
                for sub in range(2):
                    xb = b("xt%d" % (slot * 2 + sub))
                    col = rms_rstd(i, sub, 0)
                    st.op("dve", lambda h, sub=sub, col=col: h.tensor_scalar(out=h2n[:, sub, :], in0=xt[:, slot, sub, :], scalar1=rs[:, col:col + 1],
                                                                            scalar2=None, op0=ALU.mult),
                          reads=[xb, b("rs%d" % col)], writes=[b("h2n%d" % sub)])
                    for c in range(8):
                        st.op("pe", lambda h, sub=sub, c=c: h.transpose(out=pxt[:, c, :], in_=h2n[:, sub, c * 128:(c + 1) * 128], identity=ident[:]),
                              reads=[b("h2n%d" % sub), b("ident")], writes=[b("pxt")])
                    st.op("dve", lambda h, sub=sub: h.tensor_copy(out=h2T[:, :, sub * 128:(sub + 1) * 128], in_=pxt[:]), reads=[b("pxt")], writes=[b("h2T")])
                for fb in range(NFB):
                    gs = fb % 2
                    for c in range(8):
                        st.op("pe", lambda h, c=c, fb=fb, gs=gs: h.matmul(pG[:, gs, 0:TT], lhsT=wg[:, c, fb * 128:(fb + 1) * 128], rhs=h2T[:, c, :],
                                                                          start=(c == 0), stop=(c == 7)),
                              reads=[b("wg"), b("h2T")], writes=[b("pG%d" % gs)])
                    for c in range(8):
                        st.op("pe", lambda h, c=c, fb=fb, gs=gs: h.matmul(pUu[:, gs, 0:TT], lhsT=wu[:, c, fb * 128:(fb + 1) * 128], rhs=h2T[:, c, :],
                                                                          start=(c == 0), stop=(c == 7)),
                              reads=[b("wu"), b("h2T")], writes=[b("pU%d" % gs)])
                    st.op("act", lambda h, gs=gs: h.activation(out=sg[:, gs, :], in_=pG[:, gs, 0:TT], func=AF.Silu), reads=[b("pG%d" % gs)], writes=[b("sg%d" % gs)])
                    st.op("dve", lambda h, gs=gs, fb=fb: h.tensor_tensor(out=aT[:, fb, :], in0=sg[:, gs, :], in1=pUu[:, gs, 0:TT], op=ALU.mult),
                          reads=[b("sg%d" % gs), b("pU%d" % gs)], writes=[b("aT")])
                for sub in range(2):
                    xb = b("xt%d" % (slot * 2 + sub))
                    for half in range(2):
                        pt, pb = ring_A.get()
                        for fb in range(NFB):
                            st.op("pe", lambda h, fb=fb, pt=pt, sub=sub, half=half: h.matmul(pt, lhsT=aT[:, fb, sub * 128:(sub + 1) * 128],
                                                                                            rhs=wd[:, fb, half * 512:(half + 1) * 512], start=(fb == 0), stop=(fb == NFB - 1)),
                                  reads=[b("aT"), b("wd")], writes=[pb])
                        st.op("dve", lambda h, pt=pt, sub=sub, half=half: h.tensor_tensor(out=xt[:, slot, sub, half * 512:(half + 1) * 512],
                                                                                         in0=xt[:, slot, sub, half * 512:(half + 1) * 512], in1=pt, op=ALU.add),
                              reads=[pb, xb], writes=[xb])
                for sub in range(2):
                    xb = b("xt%d" % (slot * 2 + sub))
                    col = rms_rstd(i, sub, 1)
                    st.op("dve", lambda h, sub=sub, col=col: h.scalar_tensor_tensor(out=xt[:, slot, sub, :], in0=xt[:, slot, sub, :], scalar=rs[:, col:col + 1],
                                                                                   in1=gfb[:], op0=ALU.mult, op1=ALU.mult),
                          reads=[xb, b("rs%d" % col), b("gfb")], writes=[xb])
                    st.dma(lambda h, sub=sub: h.dma_start(out=out[t0 + sub * 128:t0 + (sub + 1) * 128, :], in_=xt[:, slot, sub, :]), reads=[xb])

            loads(0)
            for i in range(NTL):
                tile(i)
            st.finish()
            st.emit()

    if 1 in stages:
        stage1()
    if 2 in stages:
        stage2()
    if 3 in stages:
        stage3()
    if 4 in stages:
        stage4()
    return nc


def _pcn(w, rows):
    n = w.shape[1]
    return np.ascontiguousarray(w.reshape(rows // 128, 128, n).transpose(1, 0, 2))


def _pc(g):
    return np.ascontiguousarray(g.reshape(-1, 128).T)


def layout_inputs(inp, L):
    f32 = np.float32
    w_in = np.asarray(inp["w_in"][0], f32)
    hq, hi, hff, hfb, hg, cq, ckv, kr = np.split(w_in, np.cumsum([512, 512, 512, 512, 512, 384, 256])[:], axis=1)
    krot = np.concatenate([kr[:, 32:64], kr[:, 0:32]], axis=1)
    w1 = np.concatenate([hq, hff, hfb, hi, hg, cq, ckv, kr, kr, krot, krot], axis=1)
    assert w1.shape[1] == W1C
    wqb = np.asarray(inp["w_q_b"][0], f32).reshape(384, 4, 192)
    nope = wqb[:, :, 0:128].reshape(384, 512)
    rp = wqb[:, :, 128:192]
    rope = rp.reshape(384, 256)
    rot = np.concatenate([rp[:, :, 32:64], rp[:, :, 0:32]], axis=2).reshape(384, 256)
    wq = np.concatenate([nope, rope, rot], axis=1)
    wkvb = np.asarray(inp["w_kv_b"][0], f32).reshape(256, 4, 256)
    wkv = np.concatenate([wkvb[:, :, 0:128].reshape(256, 512), wkvb[:, :, 128:256].reshape(256, 512)], axis=1)
    lbl = np.asarray(inp["lb_logits"], f32)
    lbl_l = np.ascontiguousarray(lbl.reshape(2, 2, 4, 128).transpose(3, 0, 1, 2))
    gout = np.concatenate([np.asarray(inp["hgrn_norm_g"][0], f32), np.asarray(inp["mla_norm_g"][0], f32)])
    inv = 1.0 / (10000.0 ** (np.arange(0, 64, 2, dtype=np.float32) / 64.0))
    ang = np.arange(L, dtype=np.float32)[None, :] * inv[:, None].astype(np.float32)
    cos = np.cos(ang).astype(f32)
    sin = np.sin(ang).astype(f32)
    c_cos = np.ascontiguousarray(np.tile(cos, (4, 1)))
    c_sin = np.ascontiguousarray(np.tile(sin, (4, 1)))
    rmask = np.ones((128, 512), f32)
    rmask[:, 0::128] = 0.0
    jj = np.arange(128)[:, None]
    ii = np.arange(128)[None, :]
    d = {
        "w_in": _pcn(w1, 1024), "g1": _pc(np.asarray(inp["norm1_g"][0], f32)), "lbl": lbl_l,
        "w_qb": _pcn(wq, 384), "gqa": _pc(np.asarray(inp["q_a_norm_g"][0], f32)),
        "w_kvb": _pcn(wkv, 256), "gkva": _pc(np.asarray(inp["kv_a_norm_g"][0], f32)),
        "w_out": _pcn(np.asarray(inp["w_out"][0], f32), 1024), "gout": _pc(gout),
        "w_gate": _pcn(np.asarray(inp["w_gate"][0], f32), 1024), "w_up": _pcn(np.asarray(inp["w_up"][0], f32), 1024),
        "g2": _pc(np.asarray(inp["norm2_g"][0], f32)),
        "w_down": _pcn(np.asarray(inp["w_down"][0], f32), DFF),
        "gfin": np.asarray(inp["final_norm_g"], f32).reshape(1, D),
        "c_ident": np.eye(128).astype(ml_dtypes.bfloat16), "c_ones": np.ones((128, 128), ml_dtypes.bfloat16),
        "c_cos": c_cos, "c_sin": c_sin, "c_rmask": rmask,
        "c_maskf": (jj <= ii).astype(f32), "c_maskb": (jj >= ii).astype(f32),
    }
    return d


_NC_CACHE = {}


def kernel(**inputs):
    x = np.asarray(inputs["x"], np.float32)
    Bt, L, _ = x.shape
    n_seq = Bt // NCORES
    key = (n_seq, L)
    if key not in _NC_CACHE:
        _NC_CACHE[key] = build_nc(n_seq, L)
    nc = _NC_CACHE[key]
    shared = layout_inputs(inputs, L)
    in_maps = []
    for c in range(NCORES):
        m = dict(shared)
        m["x"] = np.ascontiguousarray(x[c * n_seq:(c + 1) * n_seq].reshape(n_seq * L, D))
        in_maps.append(m)
    res = run_bass_kernel_spmd(nc, in_maps, core_ids=list(range(NCORES)))
    out = np.stack([r["out"].reshape(n_seq, L, D) for r in res.results], axis=0)
    return out.reshape(Bt, L, D).astype(np.float32)
```

```python
import contextlib
import os
import numpy as np
import ml_dtypes
import concourse.bass as bass
import concourse.mybir as mybir
from concourse.bass_utils import run_bass_kernel_spmd

F32 = mybir.dt.float32
BF16 = mybir.dt.bfloat16
AF = mybir.ActivationFunctionType
ALU = mybir.AluOpType
AX = mybir.AxisListType

D = 1024
DFF = 2816
NFB = DFF // 128
EPS = 1e-6
NCORES = 8
W1C = 3456
SCALE = 192 ** -0.5


class Buf:
    __slots__ = ("name", "w", "r", "x")

    def __init__(self, name=""):
        self.name = name
        self.w = None
        self.r = {}
        self.x = len(name) > 1 and name[0] == "p" and (name[1].isupper() or name.startswith(("pmm", "pxt")))


class Stage:
    ENGS = ("pe", "act", "dve", "pool", "sp")

    def __init__(self, nc, name, n_dma_sems=16):
        self.nc = nc
        self.name = name
        self.ops = {e: [] for e in self.ENGS}
        self.cnt = {e: 0 for e in ("pe", "act", "dve", "pool")}
        self.waited = {e: {} for e in self.ENGS}
        self.n_dma = n_dma_sems
        self.dma_cnt = [0] * n_dma_sems
        self.dma_rr = 0
        self.sems = {}

    def _need(self, eng, ev, waits):
        if ev is None:
            return
        key, val = ev
        if key == "pe" and eng == "pe":
            return
        if self.waited[eng].get(key, 0) >= val:
            return
        self.waited[eng][key] = val
        waits.append((key, val))

    def _deps(self, eng, reads, writes):
        waits = []
        for b in reads:
            self._need(eng, b.w, waits)
            if b.x:
                for k, v in b.r.items():
                    if k != eng:
                        self._need(eng, (k, v), waits)
        for b in writes:
            self._need(eng, b.w, waits)
            for k, v in b.r.items():
                self._need(eng, (k, v), waits)
        return waits

    def _commit(self, ev, reads, writes):
        k, v = ev
        for b in reads:
            if b.r.get(k, 0) < v:
                b.r[k] = v
        for b in writes:
            b.w = ev
            b.r = {}

    def op(self, eng, fn, reads=(), writes=()):
        waits = self._deps(eng, reads, writes)
        self.cnt[eng] += 1
        ev = (eng, self.cnt[eng])
        self.ops[eng].append((waits, fn, (eng, 1)))
        self._commit(ev, reads, writes)
        return ev

    def dma(self, fn, reads=(), writes=(), queue="sp"):
        waits = self._deps(queue, reads, writes)
        k = self.dma_rr
        self.dma_rr = (self.dma_rr + 1) % self.n_dma
        key = "dma%d" % k
        if self.dma_cnt[k] > 0:
            self._need(queue, (key, self.dma_cnt[k]), waits)
        self.dma_cnt[k] += 16
        ev = (key, self.dma_cnt[k])
        self.ops[queue].append((waits, fn, (key, 16)))
        self._commit(ev, reads, writes)
        return ev

    def finish(self, eng="sp"):
        waits = []
        for k in range(self.n_dma):
            if self.dma_cnt[k] > 0:
                self._need(eng, ("dma%d" % k, self.dma_cnt[k]), waits)
        if waits:
            self.ops[eng].append((waits, None, None))

    def emit(self):
        nc = self.nc
        with contextlib.ExitStack() as st:
            for e in ("pe", "act", "dve", "pool"):
                self.sems[e] = st.enter_context(nc.semaphore("%s_%s" % (self.name, e)))
            for k in range(self.n_dma):
                self.sems["dma%d" % k] = st.enter_context(nc.semaphore("%s_d%d" % (self.name, k)))
            block = st.enter_context(nc.Block())
            sems = self.sems

            def run(h, lst):
                for waits, fn, inc in lst:
                    for key, val in waits:
                        h.wait_ge(sems[key], val)
                    if fn is not None:
                        fn(h).then_inc(sems[inc[0]], inc[1])

            if self.ops["sp"]:
                @block.sync
                def _(h):
                    run(h, self.ops["sp"])
            if self.ops["pe"]:
                @block.tensor
                def _(h):
                    run(h, self.ops["pe"])
            if self.ops["act"]:
                @block.scalar
                def _(h):
                    run(h, self.ops["act"])
            if self.ops["dve"]:
                @block.vector
                def _(h):
                    run(h, self.ops["dve"])
            if self.ops["pool"]:
                @block.gpsimd
                def _(h):
                    run(h, self.ops["pool"])


class Ring:
    def __init__(self, tens, n, name):
        self.t = tens
        self.n = n
        self.i = 0
        self.bufs = [Buf("%s%d" % (name, k)) for k in range(n)]

    def get(self):
        k = self.i
        self.i = (self.i + 1) % self.n
        return self.t[:, k], self.bufs[k]


def bc(ap, shape):
    return ap.to_broadcast(shape)


def build_nc(n_seq, L, debug=False, stages=(1, 2, 3, 4)):
    T = n_seq * L
    NT = T // 128
    NS = T // 512
    NCH = T // 128
    nc = bass.Bass("TRN2", target_bir_lowering=False)

    def din(name, shape, dt=F32):
        return nc.dram_tensor(name, list(shape), dt, kind="ExternalInput").ap()

    skind = "ExternalOutput" if debug else "Internal"

    def dscr(name, shape, dt):
        return nc.dram_tensor(name, list(shape), dt, kind=skind).ap()

    x = din("x", [T, D])
    out = nc.dram_tensor("out", [T, D], F32, kind="ExternalOutput").ap()
    w_in = din("w_in", [128, 8, W1C])
    g1 = din("g1", [128, 8])
    lbl = din("lbl", [128, 2, 2, 4])
    w_qb = din("w_qb", [128, 3, 1024])
    gqa = din("gqa", [128, 3])
    w_kvb = din("w_kvb", [128, 2, 1024])
    gkva = din("gkva", [128, 2])
    w_out = din("w_out", [128, 8, D])
    gout = din("gout", [128, 8])
    w_gate = din("w_gate", [128, 8, DFF])
    w_up = din("w_up", [128, 8, DFF])
    g2 = din("g2", [128, 8])
    w_down = din("w_down", [128, NFB, D])
    gfin = din("gfin", [1, D])
    c_ident = din("c_ident", [128, 128], BF16)
    c_ones = din("c_ones", [128, 128], BF16)
    c_cos = din("c_cos", [128, L])
    c_sin = din("c_sin", [128, L])
    c_rmask = din("c_rmask", [128, 512])
    c_maskf = din("c_maskf", [128, 128])
    c_maskb = din("c_maskb", [128, 128])

    qdT_s = dscr("qdT_s", [2, 4, 128, T], BF16)
    kiT_s = dscr("kiT_s", [2, 4, 128, T], BF16)
    scal_s = dscr("scal_s", [128, 2, 3, 4, NCH], F32)
    v_s = dscr("v_s", [T, 512], BF16)
    g_s = dscr("g_s", [T, 512], F32)
    qnT_s = dscr("qnT_s", [4, 128, T], BF16)
    qrT_s = dscr("qrT_s", [2, 128, T], BF16)
    knT_s = dscr("knT_s", [4, 128, T], BF16)
    krT_s = dscr("krT_s", [128, T], BF16)
    vm_s = dscr("vm_s", [T, 512], BF16)
    yT_s = dscr("yT_s", [8, 128, T], BF16)
    of_s = dscr("of_s", [T, 512], F32)
    ob_s = dscr("ob_s", [T, 512], F32)

    def stage1():
        with contextlib.ExitStack() as es:
            def sb(name, shape, dt=F32):
                return es.enter_context(nc.sbuf_tensor("s1_" + name, list(shape), dt))

            def ps(name, shape, dt=F32):
                return es.enter_context(nc.psum_tensor("s1_" + name, list(shape), dt))

            st = Stage(nc, "s1")
            w1 = sb("w1", [128, 8, W1C], BF16)
            wq = sb("wq", [128, 3, 1024], BF16)
            wkv = sb("wkv", [128, 2, 1024], BF16)
            g1t = sb("g1t", [128, 8]); gqat = sb("gqat", [128, 3]); gkvat = sb("gkvat", [128, 2])
            lblt = sb("lblt", [128, 2, 2, 4])
            lbt = sb("lbt", [128, 2, 4]); omlt = sb("omlt", [128, 2, 4])
            fa = sb("fa", [128, 2, 4]); fb_ = sb("fb", [128, 2, 4]); nfb = sb("nfb", [128, 2, 4]); lnfb = sb("lnfb", [128, 2, 4])
            ident = sb("ident", [128, 128], BF16)
            rmask = sb("rmask", [128, 512])
            xt = sb("xt", [128, 4, D]); xjunk = sb("xjunk", [128, 2, D], BF16)
            xn = sb("xn", [128, 2, D], BF16)
            hT = sb("hT", [128, 1, 8, 512], BF16)
            ssx = sb("ssx", [128, 2, 4]); lnx = sb("lnx", [128, 2, 4]); rsx = sb("rsx", [128, 2, 4])
            cst = sb("cst", [128, 2, 512]); snt = sb("snt", [128, 2, 512])
            qT = sb("qT", [128, 4, 512])
            th = sb("th", [128, 8, 512])
            tmpf = sb("tmpf", [128, 12, 512])
            stg = tmpf[:, 0:4, :].rearrange("p (a k) f -> p a (k f)", a=2)
            tmpb = sb("tmpb", [128, 8, 512], BF16)
            gt = sb("gt", [128, 4, 512])
            cqf = sb("cqf", [128, 4, 640]); cqn = sb("cqn", [128, 4, 640], BF16)
            ssq = sb("ssq", [128, 8]); lnq = sb("lnq", [128, 8]); rsq = sb("rsq", [128, 8])
            cqT = sb("cqT", [128, 3, 512], BF16); ckvT = sb("ckvT", [128, 2, 512], BF16)
            scal = sb("scal", [128, 2, 3, 4, NCH])
            pxt = ps("pxt", [128, 2, 8, 128], BF16)
            pmm = ps("pmm", [128, 6, 512])

            B = {}
            def b(name):
                if name not in B:
                    B[name] = Buf(name)
                return B[name]

            ring_f = Ring(tmpf, 12, "tmpf")
            ring_b = Ring(tmpb, 8, "tmpb")
            ring_p = Ring(pmm, 6, "pmm")
            ring_g = Ring(gt, 2, "gt")
            pxb = [Buf("pxt0"), Buf("pxt1")]
            hTb = [Buf("hT0"), Buf("hT1")]
            xtb = [Buf("xt%d" % i) for i in range(4)]
            xnb = [Buf("xn0"), Buf("xn1")]

            for (dst, src, nm) in ((g1t, g1, "g1t"), (gqat, gqa, "gqat"), (gkvat, gkva, "gkvat"),
                                   (lblt, lbl, "lblt"), (ident, c_ident, "ident"), (rmask, c_rmask, "rmask")):
                st.dma(lambda h, dst=dst, src=src: h.dma_start(out=dst[:], in_=src), writes=[b(nm)])
            st.op("dve", lambda h: h.tensor_tensor(out=lbt[:], in0=lblt[:, :, 1, :], in1=lblt[:, :, 0, :], op=ALU.subtract),
                  reads=[b("lblt")], writes=[b("lbt")])
            st.op("act", lambda h: h.activation(out=lbt[:], in_=lbt[:], func=AF.Exp), reads=[b("lbt")], writes=[b("lbt")])
            st.op("dve", lambda h: h.tensor_scalar(out=lbt[:], in0=lbt[:], scalar1=1.0, scalar2=None, op0=ALU.add),
                  reads=[b("lbt")], writes=[b("lbt")])
            st.op("dve", lambda h: h.reciprocal(out=lbt[:], in_=lbt[:]), reads=[b("lbt")], writes=[b("lbt")])
            st.op("dve", lambda h: h.tensor_scalar(out=omlt[:], in0=lbt[:], scalar1=-1.0, scalar2=1.0, op0=ALU.mult, op1=ALU.add),
                  reads=[b("lbt")], writes=[b("omlt")])
            st.op("dve", lambda h: h.tensor_scalar(out=fb_[:], in0=omlt[:], scalar1=0.5, scalar2=None, op0=ALU.mult),
                  reads=[b("omlt")], writes=[b("fb")])
            st.op("dve", lambda h: h.tensor_tensor(out=fa[:], in0=lbt[:], in1=fb_[:], op=ALU.add),
                  reads=[b("lbt"), b("fb")], writes=[b("fa")])
            st.op("dve", lambda h: h.tensor_scalar(out=nfb[:], in0=fb_[:], scalar1=-1.0, scalar2=None, op0=ALU.mult),
                  reads=[b("fb")], writes=[b("nfb")])
            st.op("act", lambda h: h.activation(out=lnfb[:], in_=fb_[:], func=AF.Ln), reads=[b("fb")], writes=[b("lnfb")])

            pcnt = [0]

            def prep(dst3, src3, nchunk, ncols, gain, eng_cycle, name):
                for c in range(nchunk):
                    for c0 in range(0, ncols, 1024):
                        c1 = min(ncols, c0 + 1024)
                        slot = pcnt[0] % 2
                        eng = eng_cycle[pcnt[0] % len(eng_cycle)]
                        pcnt[0] += 1
                        sbufs = [ring_f.bufs[2 * slot], ring_f.bufs[2 * slot + 1]]
                        st.dma(lambda h, c=c, slot=slot, c0=c0, c1=c1: h.dma_start(out=stg[:, slot, 0:c1 - c0], in_=src3[:, c, c0:c1]),
                               writes=sbufs)
                        st.op(eng, lambda h, c=c, slot=slot, c0=c0, c1=c1: h.tensor_scalar(
                            out=dst3[:, c, c0:c1], in0=stg[:, slot, 0:c1 - c0], scalar1=gain[:, c:c + 1], scalar2=0.0, op0=ALU.mult, op1=ALU.add),
                            reads=sbufs + [b(name + "_g")], writes=[b(name)])
            B["w1_g"] = b("g1t"); B["wq_g"] = b("gqat"); B["wkv_g"] = b("gkvat")
            prep(w1, w_in, 8, W1C, g1t, ["dve", "pool"], "w1")
            prep(wq, w_qb, 3, 1024, gqat, ["dve", "pool"], "wq")
            prep(wkv, w_kvb, 2, 1024, gkvat, ["dve", "pool"], "wkv")
            v1 = w1[:, :, 3328:3456].rearrange("p c (g r) -> p c g r", r=64)[:, :, :, 0:32]
            st.op("dve", lambda h: h.tensor_scalar(out=v1, in0=v1, scalar1=-1.0, scalar2=None, op0=ALU.mult),
                  reads=[b("w1")], writes=[b("w1")])
            v2 = wq[:, :, 768:1024].rearrange("p c (g r) -> p c g r", r=64)[:, :, :, 0:32]
            st.op("dve", lambda h: h.tensor_scalar(out=v2, in0=v2, scalar1=-1.0, scalar2=None, op0=ALU.mult),
                  reads=[b("wq")], writes=[b("wq")])

            w1b = b("w1")

            def mm_group(lhs_fn, rhs_fn, nk, reads, nfree=512):
                pt, pb = ring_p.get()
                pt = pt[:, 0:nfree]
                for c in range(nk):
                    st.op("pe", lambda h, c=c, pt=pt: h.matmul(pt, lhsT=lhs_fn(c), rhs=rhs_fn(c), start=(c == 0), stop=(c == nk - 1)),
                          reads=reads, writes=[pb])
                return pt, pb

            def xload(s):
                for j in range(4):
                    t0 = s * 512 + j * 128
                    st.dma(lambda h, j=j, t0=t0: h.dma_start(out=xt[:, j, :], in_=x[t0:t0 + 128, :]), writes=[xtb[j]])
                    st.op("act", lambda h, j=j: h.activation(out=xjunk[:, j % 2, :], in_=xt[:, j, :], func=AF.Square, accum_out=ssx[:, 0, j:j + 1]),
                          reads=[xtb[j]], writes=[b("ssx%d" % j), b("xjunk%d" % (j % 2))])

            def xnorm(s):
                st.op("act", lambda h: h.activation(out=lnx[:, 0, :], in_=ssx[:, 0, :], func=AF.Ln, scale=1.0 / D, bias=EPS),
                      reads=[b("ssx%d" % j) for j in range(4)], writes=[b("lnx")])
                st.op("act", lambda h: h.activation(out=rsx[:, 0, :], in_=lnx[:, 0, :], func=AF.Exp, scale=-0.5), reads=[b("lnx")], writes=[b("rsx")])
                for j in range(4):
                    n = j % 2
                    st.op("dve", lambda h, j=j, n=n: h.tensor_scalar(out=xn[:, n, :], in0=xt[:, j, :], scalar1=rsx[:, 0, j:j + 1], scalar2=None, op0=ALU.mult),
                          reads=[xtb[j], b("rsx")], writes=[xnb[n]])
                    for c in range(8):
                        st.op("pe", lambda h, n=n, c=c: h.transpose(out=pxt[:, n, c, :], in_=xn[:, n, c * 128:(c + 1) * 128], identity=ident[:]),
                              reads=[xnb[n], b("ident")], writes=[pxb[n]])
                    st.op("dve", lambda h, n=n, j=j: h.tensor_copy(out=hT[:, 0, :, j * 128:(j + 1) * 128], in_=pxt[:, n, :, :]),
                          reads=[pxb[n]], writes=[hTb[0]])

            def load_tables(s):
                slot = s % 2
                pos0 = (s * 512) % L
                st.dma(lambda h: h.dma_start(out=cst[:, slot, :], in_=c_cos[:, pos0:pos0 + 512]), writes=[b("cst%d" % slot)])
                st.dma(lambda h: h.dma_start(out=snt[:, slot, :], in_=c_sin[:, pos0:pos0 + 512]), writes=[b("snt%d" % slot)])

            def rope(s, pt1, pb1, pt2, pb2, dst):
                slot = s % 2
                f1, fb1 = ring_f.get()
                f2, fb2 = ring_f.get()
                ot, ob = ring_b.get()
                st.op("dve", lambda h: h.tensor_tensor(out=f1, in0=pt1, in1=cst[:, slot, :], op=ALU.mult), reads=[pb1, b("cst%d" % slot)], writes=[fb1])
                st.op("dve", lambda h: h.tensor_tensor(out=f2, in0=pt2, in1=snt[:, slot, :], op=ALU.mult), reads=[pb2, b("snt%d" % slot)], writes=[fb2])
                st.op("pool", lambda h: h.tensor_tensor(out=ot, in0=f1, in1=f2, op=ALU.add), reads=[fb1, fb2], writes=[ob])
                st.dma(lambda h: h.dma_start(out=dst, in_=ot), reads=[ob])

            def tm_piece(s, p):
                t0 = s * 512
                hs = hTb[0]
                if p == 8:
                    pt1, pb1 = mm_group(lambda c: w1[:, c, 3200:3328], lambda c: hT[:, 0, c, :], 8, [hs, w1b])
                    pt2, pb2 = mm_group(lambda c: w1[:, c, 3328:3456], lambda c: hT[:, 0, c, :], 8, [hs, w1b])
                    rope(s, pt1, pb1, pt2, pb2, krT_s[:, t0:t0 + 512])
                    return
                j = p // 2
                tt = t0 + j * 128
                lhs = lambda c: hT[:, 0, c, j * 128:(j + 1) * 128]
                if p % 2 == 0:
                    pt, pb = mm_group(lhs, lambda c: w1[:, c, 1536:2048], 8, [hs, w1b])
                    ot, ob = ring_b.get()
                    st.op("dve", lambda h: h.tensor_copy(out=ot, in_=pt), reads=[pb], writes=[ob])
                    st.dma(lambda h: h.dma_start(out=v_s[tt:tt + 128, :], in_=ot), reads=[ob])
                    pt2, pb2 = mm_group(lhs, lambda c: w1[:, c, 2048:2560], 8, [hs, w1b])
                    st.op("act", lambda h: h.activation(out=gt[:, j, :], in_=pt2, func=AF.Copy), reads=[pb2], writes=[b("gt%d" % j)])
                else:
                    pt, pb = mm_group(lhs, lambda c: w1[:, c, 2560:2944], 8, [hs, w1b], nfree=384)
                    st.op("act", lambda h: h.activation(out=xjunk[:, 0, 0:384], in_=pt[:, 0:384], func=AF.Square, accum_out=ssq[:, j:j + 1]),
                          reads=[pb], writes=[b("ssq%d" % j), b("xjunk0")])
                    st.op("dve", lambda h: h.tensor_copy(out=cqf[:, j, 0:384], in_=pt[:, 0:384]), reads=[pb], writes=[b("cqf%d" % j)])
                    pt2, pb2 = mm_group(lhs, lambda c: w1[:, c, 2944:3200], 8, [hs, w1b], nfree=256)
                    st.op("act", lambda h: h.activation(out=xjunk[:, 1, 0:256], in_=pt2[:, 0:256], func=AF.Square, accum_out=ssq[:, 4 + j:5 + j]),
                          reads=[pb2], writes=[b("ssq%d" % (4 + j)), b("xjunk1")])
                    st.op("dve", lambda h: h.tensor_copy(out=cqf[:, j, 384:640], in_=pt2[:, 0:256]), reads=[pb2], writes=[b("cqf%d" % j)])

            def fm_hf(s, db):
                col = 512 + db * 128
                pt, pb = mm_group(lambda c: w1[:, c, col:col + 128], lambda c: hT[:, 0, c, :], 8, [hTb[0], w1b])
                st.op("act", lambda h: h.activation(out=th[:, db, :], in_=pt, func=AF.Copy), reads=[pb], writes=[b("th%d" % db)])

            def fm_hq(s, blk):
                pt, pb = mm_group(lambda c: w1[:, c, blk * 128:(blk + 1) * 128], lambda c: hT[:, 0, c, :], 8, [hTb[0], w1b])
                st.op("dve", lambda h: h.tensor_copy(out=qT[:, blk, :], in_=pt), reads=[pb], writes=[b("qT%d" % blk)])

            def act18(s):
                t0 = s * 512
                for blk in range(4):
                    st.op("act", lambda h, blk=blk: h.activation(out=qT[:, blk, :], in_=qT[:, blk, :], func=AF.Silu),
                          reads=[b("qT%d" % blk)], writes=[b("qT%d" % blk)])
                for db in range(8):
                    st.op("act", lambda h, db=db: h.activation(out=th[:, db, :], in_=th[:, db, :], func=AF.Tanh, scale=-0.5),
                          reads=[b("th%d" % db)], writes=[b("th%d" % db)])
                for j in range(4):
                    tt = t0 + j * 128
                    st.op("act", lambda h, j=j: h.activation(out=gt[:, j, :], in_=gt[:, j, :], func=AF.Silu), reads=[b("gt%d" % j)], writes=[b("gt%d" % j)])
                    st.dma(lambda h, j=j, tt=tt: h.dma_start(out=g_s[tt:tt + 128, :], in_=gt[:, j, :]), reads=[b("gt%d" % j)])

            def gate_chain(s, db):
                d, blk = db // 4, db % 4
                t0 = s * 512
                c0 = s * 4
                thb = b("th%d" % db)
                lf, lfb = ring_f.get()
                st.op("pool", lambda h: h.tensor_scalar(out=lf, in0=th[:, db, :], scalar1=nfb[:, d, blk:blk + 1], scalar2=fa[:, d, blk:blk + 1],
                                                        op0=ALU.mult, op1=ALU.add),
                      reads=[thb, b("nfb"), b("fa")], writes=[lfb])
                st.op("act", lambda h: h.activation(out=lf, in_=lf, func=AF.Ln), reads=[lfb], writes=[lfb])
                yield
                cumt, cumb = ring_f.get()
                cct, ccb = ring_f.get()
                st.op("dve", lambda h: h.tensor_tensor_scan(out=cumt, data0=rmask[:], data1=lf, initial=0.0, op0=ALU.mult, op1=ALU.add),
                      reads=[lfb, b("rmask")], writes=[cumb])
                cumv = cumt.rearrange("p (c t) -> p c t", t=128)
                ccv = cct.rearrange("p (c t) -> p c t", t=128)
                st.op("dve", lambda h: h.tensor_tensor(out=ccv, in0=cumv, in1=bc(cumv[:, :, 63:64], [128, 4, 128]), op=ALU.subtract),
                      reads=[cumb], writes=[ccb])
                Xi, Yi = (1, 2) if d == 0 else (2, 1)
                st.op("act", lambda h: h.activation(out=scal[:, d, 0, blk, c0:c0 + 4], in_=cumv[:, :, 127], func=AF.Exp), reads=[cumb], writes=[b("scal")])
                st.op("act", lambda h: h.activation(out=scal[:, d, Xi, blk, c0:c0 + 4], in_=ccv[:, :, 127], func=AF.Exp), reads=[ccb], writes=[b("scal")])
                st.op("act", lambda h: h.activation(out=scal[:, d, Yi, blk, c0:c0 + 4], in_=cumv[:, :, 63], func=AF.Exp), reads=[cumb], writes=[b("scal")])
                if d == 0:
                    srct, srcb = cct, ccb
                else:
                    srct, srcb = ring_f.get()
                    st.op("pool", lambda h: h.tensor_tensor(out=srct, in0=lf, in1=cct, op=ALU.subtract), reads=[lfb, ccb], writes=[srcb])
                yield
                ea, eab = ring_f.get()
                eb, ebb = ring_f.get()
                st.op("act", lambda h: h.activation(out=ea, in_=srct, func=AF.Exp), reads=[srcb], writes=[eab])
                st.op("act", lambda h: h.activation(out=eb, in_=srct, func=AF.Exp, scale=-1.0, bias=lnfb[:, d, blk:blk + 1]),
                      reads=[srcb, b("lnfb")], writes=[ebb])
                o1, o1b = ring_b.get()
                o2, o2b = ring_b.get()
                st.op("dve", lambda h: h.tensor_tensor(out=o1, in0=qT[:, blk, :], in1=ea, op=ALU.mult), reads=[b("qT%d" % blk), eab], writes=[o1b])
                st.op("dve", lambda h: h.scalar_tensor_tensor(out=o2, in0=th[:, db, :], scalar=1.0, in1=eb, op0=ALU.add, op1=ALU.mult),
                      reads=[thb, ebb], writes=[o2b])
                st.dma(lambda h: h.dma_start(out=qdT_s[d, blk, :, t0:t0 + 512], in_=o1), reads=[o1b])
                st.dma(lambda h: h.dma_start(out=kiT_s[d, blk, :, t0:t0 + 512], in_=o2), reads=[o2b])
                yield

            def cq_rstd():
                st.op("act", lambda h: h.activation(out=lnq[:, 0:4], in_=ssq[:, 0:4], func=AF.Ln, scale=1.0 / 384, bias=EPS),
                      reads=[b("ssq%d" % i) for i in range(8)], writes=[b("lnq")])
                st.op("act", lambda h: h.activation(out=lnq[:, 4:8], in_=ssq[:, 4:8], func=AF.Ln, scale=1.0 / 256, bias=EPS),
                      reads=[b("ssq%d" % i) for i in range(8)], writes=[b("lnq")])
                st.op("act", lambda h: h.activation(out=rsq[:], in_=lnq[:], func=AF.Exp, scale=-0.5), reads=[b("lnq")], writes=[b("rsq")])

            def cq_transpose(s):
                for j in range(4):
                    st.op("dve", lambda h, j=j: h.tensor_scalar(out=cqn[:, j, 0:384], in0=cqf[:, j, 0:384], scalar1=rsq[:, j:j + 1], scalar2=None, op0=ALU.mult),
                          reads=[b("cqf%d" % j), b("rsq")], writes=[b("cqn%d" % j)])
                    st.op("dve", lambda h, j=j: h.tensor_scalar(out=cqn[:, j, 384:640], in0=cqf[:, j, 384:640], scalar1=rsq[:, 4 + j:5 + j], scalar2=None, op0=ALU.mult),
                          reads=[b("cqf%d" % j), b("rsq")], writes=[b("cqn%d" % j)])
                for j in range(4):
                    n = j % 2
                    for c in range(5):
                        st.op("pe", lambda h, n=n, c=c, j=j: h.transpose(out=pxt[:, n, c, :], in_=cqn[:, j, c * 128:(c + 1) * 128], identity=ident[:]),
                              reads=[b("cqn%d" % j), b("ident")], writes=[pxb[n]])
                    st.op("dve", lambda h, n=n, j=j: h.tensor_copy(out=cqT[:, :, j * 128:(j + 1) * 128], in_=pxt[:, n, 0:3, :]), reads=[pxb[n]], writes=[b("cqT")])
                    st.op("dve", lambda h, n=n, j=j: h.tensor_copy(out=ckvT[:, :, j * 128:(j + 1) * 128], in_=pxt[:, n, 3:5, :]), reads=[pxb[n]], writes=[b("ckvT")])

            def second_proj(s):
                t0 = s * 512
                out = []

                def qn(hh):
                    pt, pb = mm_group(lambda c: wq[:, c, hh * 128:(hh + 1) * 128], lambda c: cqT[:, c, :], 3, [b("cqT"), b("wq")])
                    ot, ob = ring_b.get()
                    st.op("dve", lambda h: h.tensor_copy(out=ot, in_=pt), reads=[pb], writes=[ob])
                    st.dma(lambda h: h.dma_start(out=qnT_s[hh, :, t0:t0 + 512], in_=ot), reads=[ob])

                def qr_(pr):
                    pt1, pb1 = mm_group(lambda c: wq[:, c, 512 + pr * 128:640 + pr * 128], lambda c: cqT[:, c, :], 3, [b("cqT"), b("wq")])
                    pt2, pb2 = mm_group(lambda c: wq[:, c, 768 + pr * 128:896 + pr * 128], lambda c: cqT[:, c, :], 3, [b("cqT"), b("wq")])
                    rope(s, pt1, pb1, pt2, pb2, qrT_s[pr, :, t0:t0 + 512])

                def kn(hh):
                    pt, pb = mm_group(lambda c: wkv[:, c, hh * 128:(hh + 1) * 128], lambda c: ckvT[:, c, :], 2, [b("ckvT"), b("wkv")])
                    ot, ob = ring_b.get()
                    st.op("dve", lambda h: h.tensor_copy(out=ot, in_=pt), reads=[pb], writes=[ob])
                    st.dma(lambda h: h.dma_start(out=knT_s[hh, :, t0:t0 + 512], in_=ot), reads=[ob])

                def vv(j):
                    tt = t0 + j * 128
                    pt, pb = mm_group(lambda c: ckvT[:, c, j * 128:(j + 1) * 128], lambda c: wkv[:, c, 512:1024], 2, [b("ckvT"), b("wkv")])
                    ot, ob = ring_b.get()
                    st.op("dve", lambda h: h.tensor_copy(out=ot, in_=pt), reads=[pb], writes=[ob])
                    st.dma(lambda h: h.dma_start(out=vm_s[tt:tt + 128, :], in_=ot), reads=[ob])
                for hh in range(4):
                    out.append(lambda hh=hh: qn(hh))
                for pr in range(2):
                    out.append(lambda pr=pr: qr_(pr))
                for hh in range(4):
                    out.append(lambda hh=hh: kn(hh))
                for j in range(4):
                    out.append(lambda j=j: vv(j))
                return out

            xload(0)
            load_tables(0)
            xnorm(0)
            if NS > 1:
                xload(1)
            for p in range(9):
                tm_piece(0, p)
            for blk in range(4):
                fm_hq(0, blk)
            for db in range(8):
                fm_hf(0, db)
            cq_rstd()
            for s in range(NS):
                nxt = s + 1 < NS
                if nxt:
                    load_tables(s + 1)
                    xnorm(s + 1)
                    if s + 2 < NS:
                        xload(s + 2)
                cq_transpose(s)
                act18(s)
                for f_ in second_proj(s):
                    f_()
                rest = []
                for db0 in range(0, 8, 2):
                    live = [gate_chain(s, db0), gate_chain(s, db0 + 1)]
                    while live:
                        for g_ in list(live):
                            try:
                                next(g_)
                            except StopIteration:
                                live.remove(g_)
                    for db in (db0, db0 + 1):
                        if nxt:
                            tm_piece(s + 1, db)
                            if db == 7:
                                tm_piece(s + 1, 8)
                            fm_hf(s + 1, db)
                            if db >= 4:
                                fm_hq(s + 1, db - 4)
                        if rest:
                            rest.pop(0)()
                while rest:
                    rest.pop(0)()
                if nxt:
                    cq_rstd()
            st.dma(lambda h: h.dma_start(out=scal_s, in_=scal[:]), reads=[b("scal")])
            st.finish()
            st.emit()

    def stage2():
        NK = L // 128
        NQ = L // 512
        DEN = os.environ.get("S2_DEN", "mix")
        ROPE128 = os.environ.get("S2_ROPE", "k128") == "k128"
        NPS = int(os.environ.get("S2_NPS", "4"))
        NP = int(os.environ.get("S2_NP", "6"))
        with contextlib.ExitStack() as es:
            def sb(name, shape, dt=F32):
                return es.enter_context(nc.sbuf_tensor("s2_" + name, list(shape), dt))

            def ps(name, shape, dt=F32):
                return es.enter_context(nc.psum_tensor("s2_" + name, list(shape), dt))

            st = Stage(nc, "s2")
            knT = sb("knT", [128, 4, L], BF16)
            vm = sb("vm", [128, NK, 512], BF16)
            krT = sb("krT", [128, L], BF16)
            qn = sb("qn", [128, 2, 4, 512], BF16)
            qr = sb("qr", [128, 2, 2, 512], BF16)
            ones = sb("ones", [128, 128], BF16)
            Pt = sb("Pt", [128, NP, 512], BF16)
            qrz = sb("qrz", [128, 2, 4, 512], BF16)
            hm = sb("hm", [128, 2])
            accP = sb("accP", [128, 2, 512])
            oT = sb("oT", [128, 2, 4, 512])
            rden = sb("rden", [128, 2, 512])
            sq = sb("sq", [128, 4, 512], BF16)
            lnv = sb("lnv", [128, 512]); rstd = sb("rstd", [128, 512])
            yb = sb("yb", [128, 4, 512], BF16)
            pS = ps("pS", [128, NPS, 512])
            pO = ps("pO", [128, 2, 512])
            pD = ps("pD", [128, 1, 512])
            pSS = ps("pSS", [128, 512])
            acc = sb("acc", [128, 2, 512])
            acc4 = sb("acc4", [128, 2, 4, 512])
            onesf = sb("onesf", [128, 128])
            B = {}

            def b(name):
                if name not in B:
                    B[name] = Buf(name)
                return B[name]
            ring_P = Ring(Pt, NP, "Pt")
            ring_S = Ring(pS, NPS, "pS")
            ring_y = Ring(yb, 4, "yb")
            st.dma(lambda h: h.dma_start(out=ones[:], in_=c_ones), writes=[b("ones")])
            st.op("pool", lambda h: h.memset(onesf[:], 1.0), writes=[b("onesf")])
            st.op("pool", lambda h: h.memset(hm[:], 0.0), writes=[b("hm")])
            st.op("pool", lambda h: h.memset(hm[0:64, 0:1], 1.0), writes=[b("hm")])
            st.op("pool", lambda h: h.memset(hm[64:128, 1:2], 1.0), writes=[b("hm")])

            items = []
            for seq in range(n_seq):
                for qb in range(NQ):
                    for hh in range(4):
                        for kc in range(NK):
                            items.append((seq, qb, hh, kc))

            def loads_seq(seq):
                s0 = seq * L
                for hh in range(4):
                    st.dma(lambda h, hh=hh: h.dma_start(out=knT[:, hh, :], in_=knT_s[hh, :, s0:s0 + L]), writes=[b("knT")])
                st.dma(lambda h: h.dma_start(out=krT[:], in_=krT_s[:, s0:s0 + L]), writes=[b("krT")])
                st.dma(lambda h: h.dma_start(out=vm[:], in_=vm_s[s0:s0 + L, :].rearrange("(k p) f -> p k f", p=128)), writes=[b("vm")])

            def loads_q(seq, qb):
                slot = (seq * NQ + qb) % 2
                tq = seq * L + qb * 512
                st.dma(lambda h: h.dma_start(out=qn[:, slot], in_=qnT_s[:, :, tq:tq + 512].rearrange("h p t -> p h t")), writes=[b("qn%d" % slot)])
                st.dma(lambda h: h.dma_start(out=qr[:, slot], in_=qrT_s[:, :, tq:tq + 512].rearrange("h p t -> p h t")), writes=[b("qr%d" % slot)])
                if ROPE128:
                    for hh in range(4):
                        st.op("pool", lambda h, hh=hh: h.tensor_scalar(out=qrz[:, slot, hh, :], in0=qr[:, slot, hh // 2, :], scalar1=hm[:, hh % 2:hh % 2 + 1],
                                                                        scalar2=0.0, op0=ALU.mult, op1=ALU.add),
                              reads=[b("qr%d" % slot), b("hm")], writes=[b("qrz%d" % slot)])

            def qk(item):
                seq, qb, hh, kc = item
                slot = (seq * NQ + qb) % 2
                hp, pr = hh % 2, hh // 2
                pt, pb = ring_S.get()
                st.op("pe", lambda h: h.matmul(pt, lhsT=knT[:, hh, kc * 128:(kc + 1) * 128], rhs=qn[:, slot, hh, :], start=True, stop=False),
                      reads=[b("knT"), b("qn%d" % slot)], writes=[pb])
                if ROPE128:
                    st.op("pe", lambda h: h.matmul(pt, lhsT=krT[:, kc * 128:(kc + 1) * 128], rhs=qrz[:, slot, hh, :], start=False, stop=True),
                          reads=[b("krT"), b("qrz%d" % slot)], writes=[pb])
                else:
                    st.op("pe", lambda h: h.matmul(pt, lhsT=krT[hp * 64:(hp + 1) * 64, kc * 128:(kc + 1) * 128],
                                                   rhs=qr[hp * 64:(hp + 1) * 64, slot, pr, :], start=False, stop=True),
                          reads=[b("krT"), b("qr%d" % slot)], writes=[pb])
                return pt, pb

            def finish_head(seq, qb, hh, oslot):
                qslot = (seq * NQ + qb) % 2
                if DEN == "mix":
                    a4 = acc4[:, oslot]
                    rd = [b("a4_%d_%d" % (oslot, j)) for j in range(4)]
                    st.op("dve", lambda h: h.tensor_tensor(out=a4[:, 0, :], in0=a4[:, 0, :], in1=a4[:, 1, :], op=ALU.add), reads=rd[0:2], writes=[rd[0]])
                    st.op("dve", lambda h: h.tensor_tensor(out=a4[:, 2, :], in0=a4[:, 2, :], in1=a4[:, 3, :], op=ALU.add), reads=rd[2:4], writes=[rd[2]])
                    st.op("pe", lambda h: h.matmul(pD[:, 0, :], lhsT=onesf[:], rhs=a4[:, 0, :], start=True, stop=False),
                          reads=[b("onesf"), rd[0]], writes=[b("pD0")])
                    st.op("pe", lambda h: h.matmul(pD[:, 0, :], lhsT=onesf[:], rhs=a4[:, 2, :], start=False, stop=True),
                          reads=[b("onesf"), rd[2]], writes=[b("pD0")])
                    st.op("act", lambda h: h.activation(out=rden[:, oslot, :], in_=pD[:, 0, :], func=AF.Ln), reads=[b("pD0")], writes=[b("rden%d" % oslot)])
                    st.op("act", lambda h: h.activation(out=rden[:, oslot, :], in_=rden[:, oslot, :], func=AF.Exp, scale=-1.0),
                          reads=[b("rden%d" % oslot)], writes=[b("rden%d" % oslot)])
                    st.op("dve", lambda h: h.tensor_tensor(out=oT[:, qslot, hh, :], in0=pO[:, oslot, :], in1=rden[:, oslot, :], op=ALU.mult),
                          reads=[b("pO%d" % oslot), b("rden%d" % oslot)], writes=[b("oT%d_%d" % (qslot, hh))])
                    return
                if DEN == "dve4":
                    a4 = acc4[:, oslot]
                    rd = [b("a4_%d_%d" % (oslot, j)) for j in range(4)]
                    st.op("dve", lambda h: h.tensor_tensor(out=a4[:, 0:2, :], in0=a4[:, 0:2, :], in1=a4[:, 2:4, :], op=ALU.add), reads=rd, writes=rd[0:2])
                    st.op("dve", lambda h: h.tensor_tensor(out=acc[:, oslot, :], in0=a4[:, 0, :], in1=a4[:, 1, :], op=ALU.add), reads=rd[0:2], writes=[b("acc%d" % oslot)])
                    st.op("pe", lambda h: h.matmul(pD[:, 0, :], lhsT=onesf[:], rhs=acc[:, oslot, :], start=True, stop=True),
                          reads=[b("onesf"), b("acc%d" % oslot)], writes=[b("pD0")])
                elif DEN == "dve":
                    st.op("pe", lambda h: h.matmul(pD[:, 0, :], lhsT=onesf[:], rhs=acc[:, oslot, :], start=True, stop=True),
                          reads=[b("onesf"), b("acc%d" % oslot)], writes=[b("pD0")])
                elif DEN == "split":
                    st.op("pe", lambda h: h.matmul(pD[:, 0, :], lhsT=onesf[:], rhs=acc[:, oslot, :], start=True, stop=False),
                          reads=[b("onesf"), b("acc%d" % oslot)], writes=[b("pD0")])
                    st.op("pe", lambda h: h.matmul(pD[:, 0, :], lhsT=onesf[:], rhs=accP[:, oslot, :], start=False, stop=True),
                          reads=[b("onesf"), b("accP%d" % oslot)], writes=[b("pD0")])
                st.op("dve", lambda h: h.reciprocal(out=rden[:, oslot, :], in_=pD[:, 0, :]), reads=[b("pD0")], writes=[b("rden%d" % oslot)])
                st.op("dve", lambda h: h.tensor_tensor(out=oT[:, qslot, hh, :], in0=pO[:, oslot, :], in1=rden[:, oslot, :], op=ALU.mult),
                      reads=[b("pO%d" % oslot), b("rden%d" % oslot)], writes=[b("oT%d_%d" % (qslot, hh))])

            def finish_q(seq, qb):
                qslot = (seq * NQ + qb) % 2
                tq = seq * L + qb * 512
                for hh in range(4):
                    st.op("act", lambda h, hh=hh: h.activation(out=sq[:, hh, :], in_=oT[:, qslot, hh, :], func=AF.Square),
                          reads=[b("oT%d_%d" % (qslot, hh))], writes=[b("sq%d" % hh)])
                for hh in range(4):
                    st.op("pe", lambda h, hh=hh: h.matmul(pSS[:], lhsT=ones[:], rhs=sq[:, hh, :], start=(hh == 0), stop=(hh == 3)),
                          reads=[b("ones"), b("sq%d" % hh)], writes=[b("pSS")])
                st.op("act", lambda h: h.activation(out=lnv[:], in_=pSS[:], func=AF.Ln, scale=1.0 / 512, bias=EPS), reads=[b("pSS")], writes=[b("lnv")])
                st.op("act", lambda h: h.activation(out=rstd[:], in_=lnv[:], func=AF.Exp, scale=-0.5), reads=[b("lnv")], writes=[b("rstd")])
                for hh in range(4):
                    yt, ybuf = ring_y.get()
                    st.op("dve", lambda h, hh=hh, yt=yt: h.tensor_tensor(out=yt, in0=oT[:, qslot, hh, :], in1=rstd[:], op=ALU.mult),
                          reads=[b("oT%d_%d" % (qslot, hh)), b("rstd")], writes=[ybuf])
                    st.dma(lambda h, hh=hh, yt=yt: h.dma_start(out=yT_s[4 + hh, :, tq:tq + 512], in_=yt), reads=[ybuf])

            import collections
            LA = int(os.environ.get("S2_LA", "2"))
            pending = collections.deque()
            nq = [0]

            def ensure(upto, seq):
                while nq[0] < min(upto, len(items)) and items[nq[0]][0] == seq:
                    pending.append(qk(items[nq[0]]))
                    nq[0] += 1

            ocnt = 0
            deferred = []
            DEFER = int(os.environ.get("S2_DEFER", "4"))
            for idx, item in enumerate(items):
                seq, qb, hh, kc = item
                if qb == 0 and hh == 0 and kc == 0:
                    loads_seq(seq)
                    loads_q(seq, 0)
                if hh == 0 and kc == 0 and qb + 1 < NQ:
                    loads_q(seq, qb + 1)
                ensure(idx + 1 + LA, seq)
                pt, pb = pending.popleft()
                oslot = ocnt % 2
                Pa, Pb = ring_P.get()
                st.op("act", lambda h, Pa=Pa, pt=pt: h.activation(out=Pa, in_=pt, func=AF.Exp, scale=SCALE), reads=[pb], writes=[Pb])
                st.op("pe", lambda h, Pa=Pa, oslot=oslot, hh=hh, kc=kc: h.matmul(pO[:, oslot, :], lhsT=vm[:, kc, hh * 128:(hh + 1) * 128], rhs=Pa,
                                                                           start=(kc == 0), stop=(kc == NK - 1)),
                      reads=[b("vm"), Pb], writes=[b("pO%d" % oslot)])
                if DEN == "pe":
                    st.op("pe", lambda h, Pa=Pa, kc=kc: h.matmul(pD[:, 0, :], lhsT=ones[:], rhs=Pa, start=(kc == 0), stop=(kc == NK - 1)),
                          reads=[b("ones"), Pb], writes=[b("pD0")])
                elif DEN in ("dve4", "mix"):
                    j4 = kc % 4
                    ab = b("a4_%d_%d" % (oslot, j4))
                    eng4 = "pool" if (DEN == "mix" and j4 == 3 and kc >= 4) else "dve"
                    if kc < 4:
                        st.op(eng4, lambda h, Pa=Pa, oslot=oslot, j4=j4: h.tensor_copy(out=acc4[:, oslot, j4, :], in_=Pa), reads=[Pb], writes=[ab])
                    else:
                        st.op(eng4, lambda h, Pa=Pa, oslot=oslot, j4=j4: h.tensor_tensor(out=acc4[:, oslot, j4, :], in0=acc4[:, oslot, j4, :], in1=Pa, op=ALU.add),
                              reads=[Pb, ab], writes=[ab])
                else:
                    on_pool = (DEN == "split" and kc % 4 == 3)
                    eng, at, an, first = ("pool", accP, "accP", kc == 3) if on_pool else ("dve", acc, "acc", kc == 0)
                    if first:
                        st.op(eng, lambda h, Pa=Pa, oslot=oslot, at=at: h.tensor_copy(out=at[:, oslot, :], in_=Pa), reads=[Pb], writes=[b("%s%d" % (an, oslot))])
                    else:
                        st.op(eng, lambda h, Pa=Pa, oslot=oslot, at=at: h.tensor_tensor(out=at[:, oslot, :], in0=at[:, oslot, :], in1=Pa, op=ALU.add),
                              reads=[Pb, b("%s%d" % (an, oslot))], writes=[b("%s%d" % (an, oslot))])
                if kc == NK - 1:
                    deferred.append((idx + DEFER, lambda seq=seq, qb=qb, hh=hh, oslot=oslot: finish_head(seq, qb, hh, oslot)))
                    ocnt += 1
                    if hh == 3:
                        deferred.append((idx + DEFER, lambda seq=seq, qb=qb: finish_q(seq, qb)))
                while deferred and deferred[0][0] <= idx:
                    deferred.pop(0)[1]()
            while deferred:
                deferred.pop(0)[1]()
            st.finish()
            st.emit()

    def stage3():
        NCs = L // 128
        NCHN = 2 * n_seq
        with contextlib.ExitStack() as es:
            def sb(name, shape, dt=F32):
                return es.enter_context(nc.sbuf_tensor("s3_" + name, list(shape), dt))

            def ps(name, shape, dt=F32):
                return es.enter_context(nc.psum_tensor("s3_" + name, list(shape), dt))

            st = Stage(nc, "s3")
            scal = sb("scal", [128, 2, 3, 4, NCH])
            masks = sb("masks", [128, 2, 128])
            ident = sb("ident", [128, 128], BF16)
            BD = sb("BD", [128, 4, 128])
            qd = sb("qd", [128, 2 * NCHN, 4, 256], BF16)
            ki = sb("ki", [128, 2 * NCHN, 4, 256], BF16)
            vt = sb("vt", [128, 2 * NCHN, 2, 512], BF16)
            kitok = sb("kitok", [128, NCHN, 512], BF16)
            scT = sb("scT", [128, NCHN, 8, 128], BF16)
            S_all = sb("S", [128, NCHN, 4, 128]); Sp_all = sb("Sp", [128, NCHN, 4, 128], BF16)
            t1_all = sb("t1", [128, NCHN, 4, 128]); t2_all = sb("t2", [128, NCHN, 4, 128])
            Bm_all = sb("Bm", [128, NCHN, 4, 128])
            oev = sb("oev", [128, 4, 512])
            cof = sb("cof", [128, 5, 512]); cob = sb("cob", [128, 5, 512]); cg_ = sb("cg", [128, 5, 512])
            osum_a = sb("osum", [128, 5, 512]); sqt_a = sb("sqt", [128, 5, 512]); yn_a = sb("yn", [128, 5, 512])
            ss8_a = sb("ss8", [128, 5, 8]); ln8_a = sb("ln8", [128, 5, 8]); rs8_a = sb("rs8", [128, 5, 8])
            ya = sb("ya", [128, 5, 512], BF16)
            yaT = sb("yaT", [128, 5, 4, 128], BF16)
            pT = ps("pT", [128, 1, 1024], BF16)
            pSc = ps("pSc", [128, 4, 4, 128])
            pOo = ps("pOo", [128, 2, 512])
            pU = ps("pU", [128, 1, 4, 128])
            pY = pU[:].bitcast(BF16).rearrange("p a b (c t) -> p (a b c) t", t=128)
            B = {}

            def b(name):
                if name not in B:
                    B[name] = Buf(name)
                return B[name]
            ring_oev = Ring(oev, 4, "oev")
            st.dma(lambda h: h.dma_start(out=scal[:], in_=scal_s), writes=[b("scal")])
            st.dma(lambda h: h.dma_start(out=masks[:, 0, :], in_=c_maskf), writes=[b("masks")])
            st.dma(lambda h: h.dma_start(out=masks[:, 1, :], in_=c_maskb), writes=[b("masks")])
            st.dma(lambda h: h.dma_start(out=ident[:], in_=c_ident), writes=[b("ident")])
            st.op("pool", lambda h: h.memset(BD[:], 0.0), writes=[b("BD")])
            st.op("pool", lambda h: h.memset(BD[0:64, :, 0:64], 1.0), writes=[b("BD")])
            st.op("pool", lambda h: h.memset(BD[64:128, :, 64:128], 1.0), writes=[b("BD")])
            rot = {"pT": 0, "pOo": 0, "pSc": 0}

            def chain(seq, d):
                ci = seq * 2 + d
                order = list(range(NCs)) if d == 0 else list(range(NCs - 1, -1, -1))
                S = S_all[:, ci]; Sp = Sp_all[:, ci]; t1 = t1_all[:, ci]; t2 = t2_all[:, ci]; Bm = Bm_all[:, ci]
                bS, bSp, bt1, bt2, bBm = (b("%s%d" % (nm, ci)) for nm in ("S", "Sp", "t1", "t2", "Bm"))
                st.op("pool", lambda h: h.memset(S, 0.0), writes=[bS])
                st.op("pool", lambda h: h.memset(Sp, 0.0), writes=[bSp])
                odst = of_s if d == 0 else ob_s
                def load_group(k):
                    gi = order[2 * k] // 2
                    slot = ci * 2 + k % 2
                    tg = seq * L + gi * 256
                    st.dma(lambda h: h.dma_start(out=qd[:, slot], in_=qdT_s[d, :, :, tg:tg + 256].rearrange("k p t -> p k t")), writes=[b("qd%d" % slot)])
                    st.dma(lambda h: h.dma_start(out=ki[:, slot], in_=kiT_s[d, :, :, tg:tg + 256].rearrange("k p t -> p k t")), writes=[b("ki%d" % slot)])
                    st.dma(lambda h: h.dma_start(out=vt[:, slot], in_=v_s[tg:tg + 256, :].rearrange("(j p) f -> p j f", p=128)), writes=[b("vt%d" % slot)])

                load_group(0)
                for pos, n in enumerate(order):
                    if pos % 2 == 0 and pos + 2 < len(order):
                        load_group(pos // 2 + 1)
                    slot = ci * 2 + (pos // 2) % 2
                    j = n % 2
                    cg = seq * NCs + n
                    t0 = cg * 128
                    qd_c = qd[:, slot, :, j * 128:(j + 1) * 128]
                    ki_c = ki[:, slot, :, j * 128:(j + 1) * 128]
                    v_c = vt[:, slot, j, :]
                    rq, rk, rv = b("qd%d" % slot), b("ki%d" % slot), b("vt%d" % slot)
                    has_next = pos + 1 < len(order)
                    tb = 0
                    sset = rot["pSc"] % 2
                    rot["pSc"] += 1
                    for bk in range(4):
                        st.op("pe", lambda h, bk=bk, tb=tb, ki_c=ki_c: h.transpose(out=pT[:, tb, bk * 128:(bk + 1) * 128], in_=ki_c[:, bk, :], identity=ident[:]),
                              reads=[rk, b("ident")], writes=[b("pT%d" % tb)])
                    st.op("act", lambda h, tb=tb: h.activation(out=kitok[:, ci, :], in_=pT[:, tb, 0:512], func=AF.Copy),
                          reads=[b("pT%d" % tb)], writes=[b("kitok%d" % ci)])
                    for hh in range(8):
                        bk, hp = hh // 2, hh % 2
                        st.op("pe", lambda h, bk=bk, hp=hp, ki_c=ki_c, qd_c=qd_c, sset=sset: h.matmul(pSc[:, sset * 2 + hp, bk, :], lhsT=ki_c[hp * 64:(hp + 1) * 64, bk, :],
                                                                                      rhs=qd_c[hp * 64:(hp + 1) * 64, bk, :], start=True, stop=True),
                              reads=[rk, rq], writes=[b("pSc%d" % (sset * 2 + hp))])
                    scv = scT[:, ci].rearrange("p (k two) t -> p k two t", two=2)
                    for half in range(2):
                        st.op("dve", lambda h, half=half, scv=scv, sset=sset: h.tensor_tensor(out=scv[:, :, half, :], in0=pSc[:, sset * 2 + half],
                                                                                             in1=bc(masks[:, d:d + 1, :], [128, 4, 128]), op=ALU.mult),
                              reads=[b("pSc%d" % (sset * 2 + half)), b("masks")], writes=[b("scT%d_%d" % (ci, half))])
                    if has_next:
                        st.op("pool", lambda h, cg=cg: h.tensor_tensor(out=Bm, in0=BD[:], in1=bc(scal[:, d, 1, :, cg:cg + 1], [128, 4, 128]), op=ALU.mult),
                              reads=[b("BD"), b("scal")], writes=[bBm])
                    yield
                    ob_ = rot["pOo"] % 2
                    rot["pOo"] += 1
                    for bk in range(4):
                        st.op("pe", lambda h, bk=bk, ob_=ob_, qd_c=qd_c: h.matmul(pOo[:, ob_, bk * 128:(bk + 1) * 128], lhsT=qd_c[:, bk, :], rhs=Sp[:, bk, :],
                                                                              start=True, stop=False),
                              reads=[rq, bSp], writes=[b("pOo%d" % ob_)])
                        for hp in range(2):
                            hh = 2 * bk + hp
                            st.op("pe", lambda h, hh=hh, hp=hp, ob_=ob_, v_c=v_c: h.matmul(pOo[:, ob_, hh * 64:(hh + 1) * 64], lhsT=scT[:, ci, hh, :],
                                                                                       rhs=v_c[:, hh * 64:(hh + 1) * 64], start=False, stop=(hp == 1)),
                                  reads=[b("scT%d_%d" % (ci, hh % 2)), rv], writes=[b("pOo%d" % ob_)])
                    ot, otb = ring_oev.get()
                    st.op("act", lambda h, ot=ot, ob_=ob_: h.activation(out=ot, in_=pOo[:, ob_, :], func=AF.Copy), reads=[b("pOo%d" % ob_)], writes=[otb])
                    st.dma(lambda h, ot=ot, t0=t0: h.dma_start(out=odst[t0:t0 + 128, :], in_=ot), reads=[otb], writes=[b("o%d_%d" % (d, cg))], queue="act")
                    yield
                    if has_next:
                        cgn = seq * NCs + order[pos + 1]
                        for bk in range(4):
                            st.op("pe", lambda h, bk=bk, v_c=v_c: h.matmul(pU[:, 0, bk, :], lhsT=kitok[:, ci, bk * 128:(bk + 1) * 128], rhs=v_c[:, bk * 128:(bk + 1) * 128],
                                                                          start=True, stop=True),
                                  reads=[b("kitok%d" % ci), rv], writes=[b("pU0")])
                        Abc = bc(scal[:, d, 0, :, cg:cg + 1], [128, 4, 128])
                        Cbc = bc(scal[:, d, 2, :, cgn:cgn + 1], [128, 4, 128])
                        st.op("pool", lambda h, Abc=Abc: h.tensor_tensor(out=t1, in0=S, in1=Abc, op=ALU.mult), reads=[bS, b("scal")], writes=[bt1])
                        st.op("dve", lambda h: h.tensor_tensor(out=t2, in0=pU[:, 0], in1=Bm, op=ALU.mult), reads=[b("pU0"), bBm], writes=[bt2])
                        st.op("dve", lambda h: h.tensor_tensor(out=S, in0=t1, in1=t2, op=ALU.add), reads=[bt1, bt2], writes=[bS])
                        st.op("dve", lambda h, Cbc=Cbc: h.tensor_tensor(out=Sp, in0=S, in1=Cbc, op=ALU.mult), reads=[bS, b("scal")], writes=[bSp])
                    yield

            gens = [chain(seq, d) for seq in range(n_seq) for d in range(2)]
            live = list(gens)
            while live:
                for g in list(live):
                    try:
                        next(g)
                    except StopIteration:
                        live.remove(g)

            NL = 5

            def combine_lane(cg):
                t0 = cg * 128
                k = cg % NL
                osum = osum_a[:, k]; sqt = sqt_a[:, k]; yn = yn_a[:, k]
                ss8 = ss8_a[:, k]; ln8 = ln8_a[:, k]; rs8 = rs8_a[:, k]
                bos, bsq, byn, bss, bln, brs = (b("%s%d" % (nm, k)) for nm in ("osum", "sqt", "yn", "ss8", "ln8", "rs8"))
                st.dma(lambda h: h.dma_start(out=cof[:, k], in_=of_s[t0:t0 + 128, :]), reads=[b("o0_%d" % cg)], writes=[b("cof%d" % k)])
                st.dma(lambda h: h.dma_start(out=cob[:, k], in_=ob_s[t0:t0 + 128, :]), reads=[b("o1_%d" % cg)], writes=[b("cob%d" % k)])
                st.dma(lambda h: h.dma_start(out=cg_[:, k], in_=g_s[t0:t0 + 128, :]), writes=[b("cg%d" % k)])
                yield
                st.op("dve", lambda h: h.tensor_tensor(out=osum, in0=cof[:, k], in1=cob[:, k], op=ALU.add),
                      reads=[b("cof%d" % k), b("cob%d" % k)], writes=[bos])
                st.op("act", lambda h: h.activation(out=sqt, in_=osum, func=AF.Square), reads=[bos], writes=[bsq])
                yield
                st.op("dve", lambda h: h.tensor_reduce(out=ss8, in_=sqt.rearrange("p (h e) -> p h e", e=64), axis=AX.X, op=ALU.add),
                      reads=[bsq], writes=[bss])
                st.op("act", lambda h: h.activation(out=ln8, in_=ss8, func=AF.Ln, scale=1.0 / 64, bias=EPS), reads=[bss], writes=[bln])
                st.op("act", lambda h: h.activation(out=rs8, in_=ln8, func=AF.Exp, scale=-0.5), reads=[bln], writes=[brs])
                yield
                st.op("dve", lambda h: h.tensor_tensor(out=yn.rearrange("p (h e) -> p h e", e=64), in0=osum.rearrange("p (h e) -> p h e", e=64),
                                                       in1=bc(rs8.unsqueeze(2), [128, 8, 64]), op=ALU.mult),
                      reads=[bos, brs], writes=[byn])
                st.op("pool", lambda h: h.tensor_tensor(out=ya[:, k, :], in0=yn, in1=cg_[:, k], op=ALU.mult),
                      reads=[byn, b("cg%d" % k)], writes=[b("ya%d" % k)])
                yield
                for bk in range(4):
                    st.op("pe", lambda h, bk=bk: h.transpose(out=pY[:, bk, :], in_=ya[:, k, bk * 128:(bk + 1) * 128], identity=ident[:]),
                          reads=[b("ya%d" % k), b("ident")], writes=[b("pU0")])
                st.op("act", lambda h: h.activation(out=yaT[:, k], in_=pY[:, 0:4, :], func=AF.Copy), reads=[b("pU0")], writes=[b("yaT%d" % k)])
                st.dma(lambda h: h.dma_start(out=yT_s[0:4, :, t0:t0 + 128].rearrange("k p t -> p k t"), in_=yaT[:, k]), reads=[b("yaT%d" % k)], queue="act")
                yield

            nxt_c = 0
            live = []
            while nxt_c < NCH or live:
                if nxt_c < NCH and len(live) < 4:
                    live.append(combine_lane(nxt_c))
                    nxt_c += 1
                for g in list(live):
                    try:
                        next(g)
                    except StopIteration:
                        live.remove(g)
            st.finish()
            st.emit()

    def stage4():
        TT = 256
        NTL = T // TT
        with contextlib.ExitStack() as es:
            def sb(name, shape, dt=F32):
                return es.enter_context(nc.sbuf_tensor("s4_" + name, list(shape), dt))

            def ps(name, shape, dt=F32):
                return es.enter_context(nc.psum_tensor("s4_" + name, list(shape), dt))

            st = Stage(nc, "s4")
            wo = sb("wo", [128, 8, D], BF16)
            wg = sb("wg", [128, 8, DFF], BF16)
            wu = sb("wu", [128, 8, DFF], BF16)
            wd = sb("wd", [128, NFB, D], BF16)
            goutt = sb("goutt", [128, 8]); g2t = sb("g2t", [128, 8])
            gfb = sb("gfb", [128, D])
            ident = sb("ident", [128, 128], BF16)
            xt = sb("xt", [128, 2, 2, D])
            yT = sb("yT", [128, 2, 8, TT], BF16)
            h2n = sb("h2n", [128, 2, D], BF16)
            h2T = sb("h2T", [128, 8, TT], BF16)
            aT = sb("aT", [128, NFB, TT], BF16)
            sg = sb("sg", [128, 2, TT])
            junk = sb("junk", [128, D], BF16)
            ss = sb("ss", [128, 4]); lnt = sb("lnt", [128, 4]); rs = sb("rs", [128, 4])
            pxt = ps("pxt", [128, 8, 128], BF16)
            pG = ps("pG", [128, 2, 512])
            pUu = ps("pUu", [128, 2, 512])
            pA = ps("pA", [128, 2, 512])
            B = {}

            def b(name):
                if name not in B:
                    B[name] = Buf(name)
                return B[name]
            ring_A = Ring(pA, 2, "pA")
            for (dst, src, nm) in ((goutt, gout, "goutt"), (g2t, g2, "g2t"), (ident, c_ident, "ident")):
                st.dma(lambda h, dst=dst, src=src: h.dma_start(out=dst[:], in_=src), writes=[b(nm)])
            if os.environ.get("S4_NOBC"):
                for pp in range(0, 128, 32):
                    pass
                st.op("pool", lambda h: h.memset(gfb[:], 1.0), writes=[b("gfb")])
            else:
                st.dma(lambda h: h.dma_start(out=gfb[:], in_=gfin.partition_broadcast(128)), writes=[b("gfb")])
            stg = xt[:].rearrange("p a b d -> p (a b) d")
            pcnt = [0]

            def prep(dst3, src3, nchunk, ncols, gain, name, gname):
                for c in range(nchunk):
                    for c0 in range(0, ncols, 1024):
                        c1 = min(ncols, c0 + 1024)
                        slot = pcnt[0] % 4
                        eng = ("dve", "act")[pcnt[0] % 2]
                        pcnt[0] += 1
                        sbuf = b("xt%d" % slot)
                        st.dma(lambda h, c=c, slot=slot, c0=c0, c1=c1: h.dma_start(out=stg[:, slot, 0:c1 - c0], in_=src3[:, c, c0:c1]), writes=[sbuf])
                        rd = [sbuf] + ([b(gname)] if gain is not None else [])
                        if eng == "act":
                            if gain is None:
                                st.op("act", lambda h, c=c, slot=slot, c0=c0, c1=c1: h.activation(out=dst3[:, c, c0:c1], in_=stg[:, slot, 0:c1 - c0], func=AF.Copy),
                                      reads=rd, writes=[b(name)])
                            else:
                                st.op("act", lambda h, c=c, slot=slot, c0=c0, c1=c1: h.activation(out=dst3[:, c, c0:c1], in_=stg[:, slot, 0:c1 - c0], func=AF.Copy,
                                                                                               scale=gain[:, c:c + 1]),
                                      reads=rd, writes=[b(name)])
                        else:
                            if gain is None:
                                st.op(eng, lambda h, c=c, slot=slot, c0=c0, c1=c1: h.tensor_copy(out=dst3[:, c, c0:c1], in_=stg[:, slot, 0:c1 - c0]),
                                      reads=rd, writes=[b(name)])
                            else:
                                st.op(eng, lambda h, c=c, slot=slot, c0=c0, c1=c1: h.tensor_scalar(out=dst3[:, c, c0:c1], in0=stg[:, slot, 0:c1 - c0],
                                                                                                scalar1=gain[:, c:c + 1], scalar2=None, op0=ALU.mult),
                                      reads=rd, writes=[b(name)])
            prep(wo, w_out, 8, D, goutt, "wo", "goutt")
            prep(wg, w_gate, 8, DFF, g2t, "wg", "g2t")
            prep(wu, w_up, 8, DFF, g2t, "wu", "g2t")
            prep(wd, w_down, NFB, D, None, "wd", None)

            def loads(i):
                slot = i % 2
                t0 = i * TT
                st.dma(lambda h: h.dma_start(out=xt[:, slot], in_=x[t0:t0 + TT, :].rearrange("(s p) d -> p s d", p=128)),
                       writes=[b("xt%d" % (slot * 2)), b("xt%d" % (slot * 2 + 1))])
                st.dma(lambda h: h.dma_start(out=yT[:, slot], in_=yT_s[:, :, t0:t0 + TT].rearrange("c p t -> p c t")), writes=[b("yT%d" % slot)])

            def rms_rstd(i, sub, which):
                slot = i % 2
                col = which * 2 + sub
                xb = b("xt%d" % (slot * 2 + sub))
                st.op("act", lambda h: h.activation(out=junk[:], in_=xt[:, slot, sub, :], func=AF.Square, accum_out=ss[:, col:col + 1]),
                      reads=[xb], writes=[b("junk"), b("ss%d" % col)])
                st.op("act", lambda h: h.activation(out=lnt[:, col:col + 1], in_=ss[:, col:col + 1], func=AF.Ln, scale=1.0 / D, bias=EPS),
                      reads=[b("ss%d" % col)], writes=[b("ln%d" % col)])
                st.op("act", lambda h: h.activation(out=rs[:, col:col + 1], in_=lnt[:, col:col + 1], func=AF.Exp, scale=-0.5),
                      reads=[b("ln%d" % col)], writes=[b("rs%d" % col)])
                return col

            def tile(i):
                slot = i % 2
                t0 = i * TT
                if i + 1 < NTL:
                    loads(i + 1)
                for sub in range(2):
                    xb = b("xt%d" % (slot * 2 + sub))
                    for half in range(2):
                        pt, pb = ring_A.get()
                        for c in range(8):
                            st.op("pe", lambda h, c=c, pt=pt, sub=sub, half=half: h.matmul(pt, lhsT=yT[:, slot, c, sub * 128:(sub + 1) * 128],
                                                                                          rhs=wo[:, c, half * 512:(half + 1) * 512], start=(c == 0), stop=(c == 7)),
                                  reads=[b("yT%d" % slot), b("wo")], writes=[pb])
                        st.op("dve", lambda h, pt=pt, sub=sub, half=half: h.tensor_tensor(out=xt[:, slot, sub, half * 512:(half + 1) * 512],
                                                                                         in0=xt[:, slot, sub, half * 512:(half + 1) * 512], in1=pt, op=ALU.add),
                              reads=[pb, xb], writes=[xb])
                for sub in range(2):
                    xb = b("xt%d" % (slot * 2 + sub))
                    col = rms_rstd(i, sub, 0)
                    st.op("dve", lambda h, sub=sub, col=col: h.tensor_scalar(out=h2n[:, sub, :], in0=xt[:, slot, sub, :], scalar1=rs[:, col:col + 1],
                                                                            scalar2=None, op0=ALU.mult),
                          reads=[xb, b("rs%d" % col)], writes=[b("h2n%d" % sub)])
                    for c in range(8):
                        st.op("pe", lambda h, sub=sub, c=c: h.transpose(out=pxt[:, c, :], in_=h2n[:, sub, c * 128:(c + 1) * 128], identity=ident[:]),
                              reads=[b("h2n%d" % sub), b("ident")], writes=[b("pxt")])
                    st.op("dve", lambda h, sub=sub: h.tensor_copy(out=h2T[:, :, sub * 128:(sub + 1) * 128], in_=pxt[:]), reads=[b("pxt")], writes=[b("h2T")])
                for fb in range(NFB):
                    gs = fb % 2
                    for c in range(8):
                        st.op("pe", lambda h, c=c, fb=fb, gs=gs: h.matmul(pG[:, gs, 0:TT], lhsT=wg[:, c, fb * 128:(fb + 1) * 128], rhs=h2T[:, c, :],
                                                                          start=(c == 0), stop=(c == 7)),
                              reads=[b("wg"), b("h2T")], writes=[b("pG%d" % gs)])
                    for c in range(8):
                        st.op("pe", lambda h, c=c, fb=fb, gs=gs: h.matmul(pUu[:, gs, 0:TT], lhsT=wu[:, c, fb * 128:(fb + 1) * 128], rhs=h2T[:, c, :],
                                                                          start=(c == 0), stop=(c == 7)),
                              reads=[b("wu"), b("h2T")], writes=[b("pU%d" % gs)])
                    st.op("act", lambda h, gs=gs: h.activation(out=sg[:, gs, :], in_=pG[:, gs, 0:TT], func=AF.Silu), reads=[b("pG%d" % gs)], writes=[b("sg%d" % gs)])
                    st.op("dve", lambda h, gs=gs, fb=fb: h.tensor_tensor(out=aT[:, fb, :], in0=sg[:, gs, :], in1=pUu[:, gs, 0:TT], op=ALU.mult),
                          reads=[b("sg%d" % gs), b("pU%d" % gs)], writes=[b("aT")])
                for sub in range(2):
                    xb = b("xt%d" % (slot * 2 + sub))
                    for half in range(2):
                        pt, pb = ring_A.get()
                        for fb in range(NFB):
                            st.op("pe", lambda h, fb=fb, pt=pt, sub=sub, half=half: h.matmul(pt, lhsT=aT[:, fb, sub * 128:(sub + 1) * 128],
                                                                                            rhs=wd[:, fb, half * 512:(half + 1) * 512], start=(fb == 0), stop=(fb == NFB - 1)),
                                  reads=[b("aT"), b("wd")], writes=[pb])
                        st.op("dve", lambda h, pt=pt, sub=sub, half=half: h.tensor_tensor(out=xt[:, slot, sub, half * 512:(half + 1) * 512],
                                                                                         in0=xt[:, slot, sub, half * 512:(half + 1) * 512], in1=pt, op=ALU.add),
                              reads=[pb, xb], writes=[xb])
                for sub in range(2):
                    xb = b("xt%d" % (slot * 2 + sub))
                    col = rms_rstd(i, sub, 1)
                    st.op("dve", lambda h, sub=sub, col=col: h.scalar_tensor_tensor(out=xt[:, slot, sub, :], in0=xt[:, slot, sub, :], scalar=rs[:, col:col + 1],
                                                                                   in1=gfb[:], op0=ALU.mult, op1=ALU.mult),
                          reads=[xb, b("rs%d" % col), b("gfb")], writes=[xb])
                    st.dma(lambda h, sub=sub: h.dma_start(out=out[t0 + sub * 128:t0 + (sub + 1) * 128, :], in_=xt[:, slot, sub, :]), reads=[xb])

            loads(0)
            for i in range(NTL):
                tile(i)
            st.finish()
            st.emit()

    if 1 in stages:
        stage1()
    if 2 in stages:
        stage2()
    if 3 in stages:
        stage3()
    if 4 in stages:
        stage4()
    return nc


def _pcn(w, rows):
    n = w.shape[1]
    return np.ascontiguousarray(w.reshape(rows // 128, 128, n).transpose(1, 0, 2))


def _pc(g):
    return np.ascontiguousarray(g.reshape(-1, 128).T)


def layout_inputs(inp, L):
    f32 = np.float32
    w_in = np.asarray(inp["w_in"][0], f32)
    hq, hi, hff, hfb, hg, cq, ckv, kr = np.split(w_in, np.cumsum([512, 512, 512, 512, 512, 384, 256])[:], axis=1)
    krot = np.concatenate([kr[:, 32:64], kr[:, 0:32]], axis=1)
    w1 = np.concatenate([hq, hff, hfb, hi, hg, cq, ckv, kr, kr, krot, krot], axis=1)
    assert w1.shape[1] == W1C
    wqb = np.asarray(inp["w_q_b"][0], f32).reshape(384, 4, 192)
    nope = wqb[:, :, 0:128].reshape(384, 512)
    rp = wqb[:, :, 128:192]
    rope = rp.reshape(384, 256)
    rot = np.concatenate([rp[:, :, 32:64], rp[:, :, 0:32]], axis=2).reshape(384, 256)
    wq = np.concatenate([nope, rope, rot], axis=1)
    wkvb = np.asarray(inp["w_kv_b"][0], f32).reshape(256, 4, 256)
    wkv = np.concatenate([wkvb[:, :, 0:128].reshape(256, 512), wkvb[:, :, 128:256].reshape(256, 512)], axis=1)
    lbl = np.asarray(inp["lb_logits"], f32)
    lbl_l = np.ascontiguousarray(lbl.reshape(2, 2, 4, 128).transpose(3, 0, 1, 2))
    gout = np.concatenate([np.asarray(inp["hgrn_norm_g"][0], f32), np.asarray(inp["mla_norm_g"][0], f32)])
    inv = 1.0 / (10000.0 ** (np.arange(0, 64, 2, dtype=np.float32) / 64.0))
    ang = np.arange(L, dtype=np.float32)[None, :] * inv[:, None].astype(np.float32)
    cos = np.cos(ang).astype(f32)
    sin = np.sin(ang).astype(f32)
    c_cos = np.ascontiguousarray(np.tile(cos, (4, 1)))
    c_sin = np.ascontiguousarray(np.tile(sin, (4, 1)))
    rmask = np.ones((128, 512), f32)
    rmask[:, 0::128] = 0.0
    jj = np.arange(128)[:, None]
    ii = np.arange(128)[None, :]
    d = {
        "w_in": _pcn(w1, 1024), "g1": _pc(np.asarray(inp["norm1_g"][0], f32)), "lbl": lbl_l,
        "w_qb": _pcn(wq, 384), "gqa": _pc(np.asarray(inp["q_a_norm_g"][0], f32)),
        "w_kvb": _pcn(wkv, 256), "gkva": _pc(np.asarray(inp["kv_a_norm_g"][0], f32)),
        "w_out": _pcn(np.asarray(inp["w_out"][0], f32), 1024), "gout": _pc(gout),
        "w_gate": _pcn(np.asarray(inp["w_gate"][0], f32), 1024), "w_up": _pcn(np.asarray(inp["w_up"][0], f32), 1024),
        "g2": _pc(np.asarray(inp["norm2_g"][0], f32)),
        "w_down": _pcn(np.asarray(inp["w_down"][0], f32), DFF),
        "gfin": np.asarray(inp["final_norm_g"], f32).reshape(1, D),
        "c_ident": np.eye(128).astype(ml_dtypes.bfloat16), "c_ones": np.ones((128, 128), ml_dtypes.bfloat16),
        "c_cos": c_cos, "c_sin": c_sin, "c_rmask": rmask,
        "c_maskf": (jj <= ii).astype(f32), "c_maskb": (jj >= ii).astype(f32),
    }
    return d


_NC_CACHE = {}


def kernel(**inputs):
    x = np.asarray(inputs["x"], np.float32)
    Bt, L, _ = x.shape
    n_seq = Bt // NCORES
    key = (n_seq, L)
    if key not in _NC_CACHE:
        _NC_CACHE[key] = build_nc(n_seq, L)
    nc = _NC_CACHE[key]
    shared = layout_inputs(inputs, L)
    in_maps = []
    for c in range(NCORES):
        m = dict(shared)
        m["x"] = np.ascontiguousarray(x[c * n_seq:(c + 1) * n_seq].reshape(n_seq * L, D))
        in_maps.append(m)
    res = run_bass_kernel_spmd(nc, in_maps, core_ids=list(range(NCORES)))
    out = np.stack([r["out"].reshape(n_seq, L, D) for r in res.results], axis=0)
    return out.reshape(Bt, L, D).astype(np.float32)
```
